# Optimizing a Trainium2 kernel written in Bass

```python
import jax, jax.numpy as jnp
from jax import lax
import numpy as np

D_MODEL = 1024
BATCH = 16
SEQ = 2048
DEPTH = 2
DEC_BATCH = 8
DEC_SEQ = 32
PAST_LEN = 2048

CHUNK = 64
HEAD_DIM = 64
H_A = 8
H_B = 8
LEFT_CHUNKS_A = 8
REL_CLIP_A = 128
N_REL_A = 2 * REL_CLIP_A + 1
SB_BLOCK = 128
W_A = H_A * HEAD_DIM
W_B = H_B * HEAD_DIM
W_AB = W_A + W_B
IN_AB = 4 * W_A + 4 * W_B
AB_SPLITS = (W_A, 2 * W_A, 3 * W_A, 4 * W_A, 4 * W_A + W_B, 4 * W_A + 2 * W_B, 4 * W_A + 3 * W_B)
H_C = 16
KV_C = 4
G_C = H_C // KV_C
WINDOW_C = 128
LEFT_CHUNKS_C = WINDOW_C // CHUNK
W_C = H_C * HEAD_DIM
KVW_C = KV_C * HEAD_DIM
IN_C = 2 * W_C + 2 * KVW_C
C_SPLITS = (W_C, W_C + KVW_C, W_C + 2 * KVW_C)
ROPE_THETA = 10000.0
RMS_EPS = 1e-6
NEG_INF = -1e30
N_AB = (DEPTH + 1) // 2
N_C = DEPTH // 2

kernel_name = 'hybrid_chunk_streaming_encoder_step'


def _rmsnorm(x, g):
    xf = x.astype(jnp.float32)
    r = lax.rsqrt(jnp.mean(xf * xf, axis=-1, keepdims=True) + RMS_EPS)
    return (xf * r * g.astype(jnp.float32)).astype(x.dtype)


def _rope(x, pos):
    half = HEAD_DIM // 2
    inv = ROPE_THETA ** (-jnp.arange(half, dtype=jnp.float32) * (2.0 / HEAD_DIM))
    ang = pos.astype(jnp.float32)[:, None] * inv[None, :]
    cos, sin = jnp.cos(ang)[:, None, :], jnp.sin(ang)[:, None, :]
    xf = x.astype(jnp.float32)
    x1, x2 = xf[..., :half], xf[..., half:]
    return jnp.concatenate([x1 * cos - x2 * sin, x2 * cos + x1 * sin], axis=-1).astype(x.dtype)


def _chunk_band(x, n_left):
    B, S = x.shape[:2]
    nC = S // CHUNK
    xp = jnp.pad(x, ((0, 0), (n_left * CHUNK, 0), (0, 0), (0, 0)))
    xp = xp.reshape(B, nC + n_left, CHUNK, *x.shape[2:])
    return jnp.concatenate([xp[:, i:i + nC] for i in range(n_left + 1)], axis=2)


def _chunk_band_mask(q_pos, k_pos, n_left):
    qc = q_pos[:, :, None] // CHUNK
    kc = k_pos[:, None, :] // CHUNK
    return (k_pos[:, None, :] >= 0) & (kc <= qc) & (kc >= qc - n_left)


def _band_attention(q, k, v, mask, bias=None, sinks=None):
    s = jnp.einsum('bcqhgd,bckhd->bchgqk', q, k).astype(jnp.float32) * (HEAD_DIM ** -0.5)
    if bias is not None:
        s = s + bias
    s = jnp.where(mask[None, :, None, None], s, NEG_INF)
    m = jnp.max(s, axis=-1, keepdims=True)
    if sinks is not None:
        sk = sinks.astype(jnp.float32)[:, :, None, None]
        m = jnp.maximum(m, sk)
        p = jnp.exp(s - m)
        denom = jnp.sum(p, axis=-1, keepdims=True) + jnp.exp(sk - m)
    else:
        p = jnp.exp(s - m)
        denom = jnp.sum(p, axis=-1, keepdims=True)
    return jnp.einsum('bchgqk,bckhd->bcqhgd', (p / denom).astype(v.dtype), v)


def _band_prompt(q, k, v, pos, n_left, groups):
    B, S, Hq, d = q.shape
    nC = S // CHUNK
    qc = q.reshape(B, nC, CHUNK, Hq // groups, groups, d)
    q_pos = pos.reshape(nC, CHUNK)
    k_pos = (jnp.arange(nC) * CHUNK)[:, None] + jnp.arange(-n_left * CHUNK, CHUNK)[None, :]
    return qc, _chunk_band(k, n_left), _chunk_band(v, n_left), q_pos, k_pos


def _band_sample(q, k, v, ck, cv, past, groups):
    B, T, Hq, d = q.shape
    L = ck.shape[1]
    qc = q.reshape(B, 1, T, Hq // groups, groups, d)
    k_all = jnp.concatenate([ck, k], axis=1)[:, None]
    v_all = jnp.concatenate([cv, v], axis=1)[:, None]
    q_pos = (past + jnp.arange(T))[None]
    k_pos = (past - L + jnp.arange(L + T))[None]
    return qc, k_all, v_all, q_pos, k_pos


def _mixer_a(q, k, v, q_pos, k_pos, table):
    mask = _chunk_band_mask(q_pos, k_pos, LEFT_CHUNKS_A)
    rel = jnp.clip(q_pos[:, :, None] - k_pos[:, None, :], -REL_CLIP_A, REL_CLIP_A) + REL_CLIP_A
    bias = jnp.moveaxis(jnp.take(table, rel, axis=1), 0, 1)[:, :, None].astype(jnp.float32)
    return _band_attention(q, k, v, mask, bias=bias)


def _mixer_c(q, k, v, q_pos, k_pos, sinks):
    mask = _chunk_band_mask(q_pos, k_pos, LEFT_CHUNKS_C)
    return _band_attention(q, k, v, mask, sinks=sinks.reshape(KV_C, G_C))


def _sb_block(q, k, v, q_pos, k_pos):
    z = jnp.einsum('bqhd,bkhd->bhqk', q, k).astype(jnp.float32) * (HEAD_DIM ** -0.5)
    causal = k_pos[None, :] < q_pos[:, None]
    log_1m = jnp.where(causal, jax.nn.log_sigmoid(-z), 0.0)
    later = lax.cumsum(log_1m, axis=3, reverse=True) - log_1m
    a = jnp.where(causal, jnp.exp(jax.nn.log_sigmoid(z) + later), 0.0)
    return jnp.einsum('bhqk,bkhd->bqhd', a.astype(v.dtype), v)


def _stick_breaking_prompt(q, k, v):
    B, S, H, d = q.shape
    nb = S // SB_BLOCK
    pos = jnp.arange(S)
    qb = q.reshape(B, nb, SB_BLOCK, H, d).transpose(1, 0, 2, 3, 4)
    pb = pos.reshape(nb, SB_BLOCK)
    out = lax.map(lambda a: _sb_block(a[0], k, v, a[1], pos), (qb, pb))
    return out.transpose(1, 0, 2, 3, 4).reshape(B, S, H, d)


def _merge(x, parts, w_out, g_post):
    y = jnp.concatenate(parts, axis=-1) @ w_out
    return x + _rmsnorm(y, g_post)


def _ab_project(x, g_pre, w_in):
    B, S, _ = x.shape
    u = _rmsnorm(x, g_pre) @ w_in
    qa, ka, va, ga, qb, kb, vb, gb = jnp.split(u, AB_SPLITS, axis=-1)
    hd = lambda t, h: t.reshape(B, S, h, HEAD_DIM)
    return hd(qa, H_A), hd(ka, H_A), hd(va, H_A), ga, hd(qb, H_B), hd(kb, H_B), hd(vb, H_B), gb


def _c_project(x, g_pre, w_in):
    B, S, _ = x.shape
    u = _rmsnorm(x, g_pre) @ w_in
    q, k, v, g = jnp.split(u, C_SPLITS, axis=-1)
    return (q.reshape(B, S, H_C, HEAD_DIM), k.reshape(B, S, KV_C, HEAD_DIM),
            v.reshape(B, S, KV_C, HEAD_DIM), g)


def _ab_layer_prompt(x, g_pre, w_in, w_out, g_post, rel_table):
    B, S, _ = x.shape
    qa, ka, va, ga, qb, kb, vb, gb = _ab_project(x, g_pre, w_in)
    pos = jnp.arange(S)
    oa = _mixer_a(*_band_prompt(qa, ka, va, pos, LEFT_CHUNKS_A, 1), rel_table).reshape(B, S, W_A)
    ob = _stick_breaking_prompt(qb, kb, vb).reshape(B, S, W_B)
    y = _merge(x, [oa * jax.nn.silu(ga), ob * jax.nn.silu(gb)], w_out, g_post)
    la = min(LEFT_CHUNKS_A * CHUNK, S)
    return y, (ka[:, S - la:], va[:, S - la:], kb, vb)


def _ab_layer_sample(x, ca_k, ca_v, cb_k, cb_v, past, g_pre, w_in, w_out, g_post, rel_table):
    B, T, _ = x.shape
    qa, ka, va, ga, qb, kb, vb, gb = _ab_project(x, g_pre, w_in)
    oa = _mixer_a(*_band_sample(qa, ka, va, ca_k, ca_v, past, 1), rel_table).reshape(B, T, W_A)
    kb_all = jnp.concatenate([cb_k, kb], axis=1)
    vb_all = jnp.concatenate([cb_v, vb], axis=1)
    ob = _sb_block(qb, kb_all, vb_all, past + jnp.arange(T), jnp.arange(past + T)).reshape(B, T, W_B)
    y = _merge(x, [oa * jax.nn.silu(ga), ob * jax.nn.silu(gb)], w_out, g_post)
    return y, (ka, va, kb, vb)


def _c_layer_prompt(x, g_pre, w_in, sinks, w_out, g_post):
    B, S, _ = x.shape
    q, k, v, g = _c_project(x, g_pre, w_in)
    pos = jnp.arange(S)
    q, k = _rope(q, pos), _rope(k, pos)
    o = _mixer_c(*_band_prompt(q, k, v, pos, LEFT_CHUNKS_C, G_C), sinks).reshape(B, S, W_C)
    y = _merge(x, [o * jax.nn.silu(g)], w_out, g_post)
    lc = min(WINDOW_C, S)
    return y, (k[:, S - lc:], v[:, S - lc:])


def _c_layer_sample(x, cc_k, cc_v, past, g_pre, w_in, sinks, w_out, g_post):
    B, T, _ = x.shape
    q, k, v, g = _c_project(x, g_pre, w_in)
    pos = past + jnp.arange(T)
    q, k = _rope(q, pos), _rope(k, pos)
    o = _mixer_c(*_band_sample(q, k, v, cc_k, cc_v, past, G_C), sinks).reshape(B, T, W_C)
    y = _merge(x, [o * jax.nn.silu(g)], w_out, g_post)
    return y, (k, v)


def setup_inputs(seed: int = 0) -> dict:
    key = jax.random.key(seed)
    ks = jax.random.split(key, 20)
    f32 = jnp.float32
    nrm = lambda k, shape, scale=1.0: scale * jax.random.normal(k, shape, f32)
    la = min(LEFT_CHUNKS_A * CHUNK, PAST_LEN)
    lc = min(WINDOW_C, PAST_LEN)
    return {
        'x_prompt': nrm(ks[0], (BATCH, SEQ, D_MODEL)),
        'x_sample': nrm(ks[1], (DEC_BATCH, DEC_SEQ, D_MODEL)),
        'cache_a_k': nrm(ks[2], (N_AB, DEC_BATCH, la, H_A, HEAD_DIM)),
        'cache_a_v': nrm(ks[3], (N_AB, DEC_BATCH, la, H_A, HEAD_DIM)),
        'cache_b_k': nrm(ks[4], (N_AB, DEC_BATCH, PAST_LEN, H_B, HEAD_DIM)),
        'cache_b_v': nrm(ks[5], (N_AB, DEC_BATCH, PAST_LEN, H_B, HEAD_DIM)),
        'cache_c_k': nrm(ks[6], (N_C, DEC_BATCH, lc, KV_C, HEAD_DIM)),
        'cache_c_v': nrm(ks[7], (N_C, DEC_BATCH, lc, KV_C, HEAD_DIM)),
        'ab_norm_pre': 1.0 + nrm(ks[8], (N_AB, D_MODEL), 0.1),
        'ab_w_in': nrm(ks[9], (N_AB, D_MODEL, IN_AB), D_MODEL ** -0.5),
        'ab_w_out': nrm(ks[10], (N_AB, W_AB, D_MODEL), W_AB ** -0.5),
        'ab_norm_post': 1.0 + nrm(ks[11], (N_AB, D_MODEL), 0.1),
        'a_rel_bias': nrm(ks[12], (N_AB, H_A, N_REL_A), 0.5),
        'c_norm_pre': 1.0 + nrm(ks[13], (N_C, D_MODEL), 0.1),
        'c_w_in': nrm(ks[14], (N_C, D_MODEL, IN_C), D_MODEL ** -0.5),
        'c_sinks': nrm(ks[15], (N_C, H_C)),
        'c_w_out': nrm(ks[16], (N_C, W_C, D_MODEL), W_C ** -0.5),
        'c_norm_post': 1.0 + nrm(ks[17], (N_C, D_MODEL), 0.1),
    }


def reference(x_prompt, x_sample, cache_a_k, cache_a_v, cache_b_k, cache_b_v, cache_c_k, cache_c_v,
              ab_norm_pre, ab_w_in, ab_w_out, ab_norm_post, a_rel_bias,
              c_norm_pre, c_w_in, c_sinks, c_w_out, c_norm_post):
    past = cache_b_k.shape[2]
    yp, ys = x_prompt, x_sample
    ab_p, ab_s, c_p, c_s = [], [], [], []
    for layer in range(DEPTH):
        i = layer // 2
        if layer % 2 == 0:
            yp, rows = _ab_layer_prompt(yp, ab_norm_pre[i], ab_w_in[i], ab_w_out[i], ab_norm_post[i], a_rel_bias[i])
            ab_p.append(rows)
            ys, rows = _ab_layer_sample(ys, cache_a_k[i], cache_a_v[i], cache_b_k[i], cache_b_v[i], past,
                                        ab_norm_pre[i], ab_w_in[i], ab_w_out[i], ab_norm_post[i], a_rel_bias[i])
            ab_s.append(rows)
        else:
            yp, rows = _c_layer_prompt(yp, c_norm_pre[i], c_w_in[i], c_sinks[i], c_w_out[i], c_norm_post[i])
            c_p.append(rows)
            ys, rows = _c_layer_sample(ys, cache_c_k[i], cache_c_v[i], past,
                                       c_norm_pre[i], c_w_in[i], c_sinks[i], c_w_out[i], c_norm_post[i])
            c_s.append(rows)
    st = lambda rows, j: jnp.stack([r[j] for r in rows])
    return (yp, ys,
            st(ab_p, 0), st(ab_p, 1), st(ab_p, 2), st(ab_p, 3), st(c_p, 0), st(c_p, 1),
            st(ab_s, 0), st(ab_s, 1), st(ab_s, 2), st(ab_s, 3), st(c_s, 0), st(c_s, 1))
```

```python
import numpy as np
import concourse.bass as bass
import concourse.mybir as mybir
from concourse.bass_utils import run_bass_kernel_spmd

F32 = mybir.dt.float32
BF16 = mybir.dt.bfloat16
AF = mybir.ActivationFunctionType
ALU = mybir.AluOpType

D = 1024
S = 2048
TS = 32
PAST = 2048
EPS = 1e-6
NEG = -30000.0
SEM_LIM = 20000


class Res:
    __slots__ = ("w", "r", "name", "excl")

    def __init__(self, name="", excl=False):
        self.w = None
        self.r = {}
        self.name = name
        self.excl = excl


class _Rec:
    def __init__(self):
        self.call = None

    def __getattr__(self, name):
        def f(*args, **kwargs):
            self.call = (name, args, kwargs)
            return None
        return f


class Prog:
    ENG = ("pe", "act", "dve", "pool", "sp")

    def __init__(self, nc):
        self.nc = nc
        self.ops = {e: [] for e in self.ENG}
        self.waited = {e: {} for e in self.ENG}
        self.ndma_sems = 12
        self.ndma_q = {"sp": 12, "pool": 6}
        self.dma_cnt = {q: [0] * self.ndma_sems for q in ("sp", "pool")}
        self.dma_next = {"sp": 0, "pool": 0}
        self.dma_last = {q: [None] * self.ndma_sems for q in ("sp", "pool")}
        self.out_dma_refs = []
        self.last_pe = None

    def _need(self, eng, ref, waits):
        if ref is None:
            return
        if ref[0] == "op":
            _, e2, idx = ref
            if e2 == eng and eng == "pe":
                return
            if self.waited[eng].get(e2, -1) >= idx:
                return
            if e2 == eng and idx >= len(self.ops[eng]):
                return
            self.waited[eng][e2] = idx
            self.ops[e2][idx]["inc"] = True
            waits.append(ref)
        else:
            _, q, slot, val = ref
            key = ("dma", q, slot)
            if self.waited[eng].get(key, -1) >= val:
                return
            self.waited[eng][key] = val
            waits.append(ref)

    def _deps(self, eng, reads, writes, same_engine_war=False):
        waits = []
        for r in reads:
            self._need(eng, r.w, waits)
        for w in writes:
            self._need(eng, w.w, waits)
            for e2, ref in w.r.items():
                self._need(eng, ref, waits)
        return waits

    def _commit(self, ref, reads, writes):
        for r in reads:
            r.r[ref[1] if ref[0] == "op" else ("dma", ref[1], ref[2])] = ref
        for w in writes:
            w.w = ref
            w.r = {}

    def op(self, eng, fn, reads=(), writes=()):
        ex = [r for r in reads if r.excl]
        if ex:
            reads = [r for r in reads if not r.excl]
            writes = list(writes) + ex
        waits = self._deps(eng, reads, writes)
        idx = len(self.ops[eng])
        rec = _Rec()
        fn(rec)
        assert rec.call is not None
        self.ops[eng].append({"fn": rec.call, "waits": waits, "inc": False, "dma": None})
        self._commit(("op", eng, idx), reads, writes)
        return ("op", eng, idx)

    def dma(self, q, out, in_, reads=(), writes=(), is_output=False):
        waits = self._deps(q, reads, writes)
        slot = self.dma_next[q]
        self.dma_next[q] = (slot + 1) % self.ndma_q[q]
        prev = self.dma_last[q][slot]
        if prev is not None:
            self._need(q, prev, waits)
        self.dma_cnt[q][slot] += 1
        val = self.dma_cnt[q][slot] * 16
        ref = ("dma", q, slot, val)
        self.dma_last[q][slot] = ref
        self.ops[q].append({"fn": ("dma_start", (), {"out": out, "in_": in_}), "waits": waits,
                            "inc": False, "dma": (q, slot)})
        self._commit(ref, reads, writes)
        if is_output:
            self.out_dma_refs.append(ref)
        return ref

    def barrier(self):
        refs = []
        for e in self.ENG:
            for idx in range(len(self.ops[e]) - 1, -1, -1):
                o = self.ops[e][idx]
                if o["fn"] is not None and o["dma"] is None:
                    refs.append(("op", e, idx))
                    break
        for q in ("sp", "pool"):
            for slot in range(self.ndma_sems):
                if self.dma_last[q][slot] is not None:
                    refs.append(self.dma_last[q][slot])
        for e in self.ENG:
            waits = []
            for ref in refs:
                if ref[0] == "op" and ref[1] == e:
                    continue
                self._need(e, ref, waits)
            self.ops[e].append({"fn": None, "waits": waits, "inc": False, "dma": None})

    def finalize(self, block, sems, dma_sems):
        nc = self.nc
        marks = {}
        for e in self.ENG:
            c = 0
            m = []
            for o in self.ops[e]:
                if o["inc"] and o["fn"] is not None and o["dma"] is None:
                    c += 1
                m.append(c)
            marks[e] = m
            assert c <= SEM_LIM * len(sems[e]), (e, c)

        def sem_of(e, idx):
            m = marks[e][idx]
            assert m >= 1
            k = (m - 1) // SEM_LIM
            return sems[e][k], (m - 1) % SEM_LIM + 1

        engs = {"pe": nc.tensor, "act": nc.scalar, "dve": nc.vector, "pool": nc.gpsimd, "sp": nc.sync}

        def _pinfo(ap):
            fs = 1
            for s_ in list(ap.tensor.shape)[1:]:
                fs *= int(s_)
            p0 = int(ap.offset) // fs
            col = int(ap.offset) % fs
            return p0, int(ap.ap[0][1]), col, fs
        prev = None
        nviol = 0
        for o in self.ops["pe"]:
            if o["fn"] is None:
                continue
            name_, args_, kw_ = o["fn"]
            out_ap = args_[0] if args_ else kw_["out"]
            l_ap = kw_.get("lhsT", kw_.get("in_"))
            p0, kk, _, _ = _pinfo(l_ap)
            _, _, col, fs = _pinfo(out_ap)
            esz = 4 if fs in (512, 1024) and out_ap.tensor.name.startswith("pb") else 2
            bank_id = (out_ap.tensor.name, (col * esz) // 2048)
            rows = (p0, p0 + kk)
            cur = (rows, bank_id)
            if prev is not None and kk < 128 and (prev[0][1] - prev[0][0]) < 128:
                disjoint = rows[0] >= prev[0][1] or prev[0][0] >= rows[1]
                if disjoint and prev[1] == bank_id:
                    nviol += 1
            prev = cur
        assert nviol == 0, f"row-tile bank violations: {nviol}"

        def run(e, eng):
            for idx, o in enumerate(self.ops[e]):
                for ref in o["waits"]:
                    if ref[0] == "op":
                        s_, v_ = sem_of(ref[1], ref[2])
                        eng.wait_ge(s_, v_)
                    else:
                        eng.wait_ge(dma_sems[ref[1]][ref[2]], ref[3])
                if o["fn"] is None:
                    continue
                name_, args_, kw_ = o["fn"]
                ins = getattr(eng, name_)(*args_, **kw_)
                if o["dma"] is not None:
                    ins.then_inc(dma_sems[o["dma"][0]][o["dma"][1]], 16)
                elif o["inc"]:
                    s_, _ = sem_of(e, idx)
                    ins.then_inc(s_, 1)

        @block.tensor
        def _(eng):
            run("pe", eng)

        @block.scalar
        def _(eng):
            run("act", eng)

        @block.vector
        def _(eng):
            run("dve", eng)

        @block.gpsimd
        def _(eng):
            run("pool", eng)

        @block.sync
        def _(eng):
            run("sp", eng)


def _consts():
    c = {}
    i = np.arange(128)
    c["ident"] = np.eye(128, dtype=np.float32)
    c["tri"] = (i[:, None] >= i[None, :]).astype(np.float32)
    c["ones"] = np.ones((128, 128), np.float32)
    c["lmask"] = (i[:, None] < i[None, :]).astype(np.float32)
    rot = np.zeros((128, 128), np.float32)
    for p in range(128):
        if p % 64 < 32:
            rot[p + 32, p] = 1.0
        else:
            rot[p - 32, p] = 1.0
    c["rot"] = rot
    dsel = np.zeros((2, 128, 128), np.float32)
    for a in range(2):
        for p in range(128):
            dsel[a, a * 64 + (p % 64), p] = 1.0
    c["dsel"] = dsel
    c["dselrot"] = np.stack([dsel[a] @ rot for a in range(2)])
    half = 32
    inv = (10000.0 ** (-np.arange(half, dtype=np.float32) * np.float32(2.0 / 64))).astype(np.float32)
    pos = np.arange(S + TS, dtype=np.float32)
    ang = (pos[:, None] * inv[None, :]).astype(np.float32)
    cos, sin = np.cos(ang).astype(np.float32), np.sin(ang).astype(np.float32)
    pidx = np.arange(128) % 32
    sign = np.where((np.arange(128) % 64) < 32, -1.0, 1.0).astype(np.float32)
    c["cosT"] = np.ascontiguousarray(cos[:, pidx].T)
    c["sinT"] = np.ascontiguousarray((sin[:, pidx] * sign[None, :]).T)
    fidx = np.arange(256) % 32
    fsign = np.where((np.arange(256) % 64) < 32, -1.0, 1.0).astype(np.float32)
    c["cosTM"] = np.ascontiguousarray(cos[:, fidx])
    c["sinTM"] = np.ascontiguousarray(sin[:, fidx] * fsign[None, :])
    return c


def _bias_tiles(table):
    k = np.arange(128)[:, None]
    q = np.arange(128)[None, :]
    bp = np.zeros((8, 128, 640), np.float32)
    for slot in range(5):
        rel = (4 - slot) * 128 + (q - k)
        idx = np.clip(rel, -128, 128) + 128
        bp[:, :, slot * 128:(slot + 1) * 128] = table[:, idx]
    qs = PAST + np.arange(TS)[None, :]
    bs = np.zeros((8, 128, 160), np.float32)
    for slot in range(5):
        kpos = PAST - 512 + slot * 128 + np.arange(128)[:, None]
        idx = np.clip(qs - kpos, -128, 128) + 128
        bs[:, :, slot * 32:(slot + 1) * 32] = table[:, idx]
    return bp, bs


def build():
    nc = bass.Bass("TRN2", target_bir_lowering=False)
    P = Prog(nc)

    def din(name, shape):
        return nc.dram_tensor(name, list(shape), F32, kind="ExternalInput").ap()

    def dout(name, shape):
        return nc.dram_tensor(name, list(shape), F32, kind="ExternalOutput").ap()

    xp = din("xp", [2, S, D])
    xs = din("xs", [TS, D])
    ca_k = din("ca_k", [512, 512]); ca_v = din("ca_v", [512, 512])
    cb_k = din("cb_k", [PAST, 512]); cb_v = din("cb_v", [PAST, 512])
    cc_k = din("cc_k", [128, 256]); cc_v = din("cc_v", [128, 256])
    w_ab = din("w_ab", [8, D, 512])
    w_oab = din("w_oab", [D, D])
    w_c = din("w_c", [D, 2560])
    w_oc = din("w_oc", [D, D])
    gpre_ab = din("gpre_ab", [128, D]); gpost_ab = din("gpost_ab", [128, D])
    gpre_c = din("gpre_c", [128, D]); gpost_c = din("gpost_c", [128, D])
    biasP = din("biasP", [8, 128, 640]); biasS = din("biasS", [8, 128, 160])
    sinks = din("sinks", [128, 8])
    c_ident = din("c_ident", [128, 128]); c_tri = din("c_tri", [128, 128]); c_ones = din("c_ones", [128, 128])
    c_lmask = din("c_lmask", [128, 128]); c_rot = din("c_rot", [128, 128])
    c_dsel = din("c_dsel", [2, 128, 128]); c_dselrot = din("c_dselrot", [2, 128, 128])
    c_cosT = din("c_cosT", [128, S + TS]); c_sinT = din("c_sinT", [128, S + TS])
    c_cosTM = din("c_cosTM", [S + TS, 256]); c_sinTM = din("c_sinTM", [S + TS, 256])

    yp = dout("yp", [2, S, D]); ys = dout("ys", [TS, D])
    o_akp = dout("o_akp", [2, 512, 512]); o_avp = dout("o_avp", [2, 512, 512])
    o_bkp = dout("o_bkp", [2, S, 512]); o_bvp = dout("o_bvp", [2, S, 512])
    o_ckp = dout("o_ckp", [2, 128, 256]); o_cvp = dout("o_cvp", [2, 128, 256])
    o_aks = dout("o_aks", [TS, 512]); o_avs = dout("o_avs", [TS, 512])
    o_bks = dout("o_bks", [TS, 512]); o_bvs = dout("o_bvs", [TS, 512])
    o_cks = dout("o_cks", [TS, 256]); o_cvs = dout("o_cvs", [TS, 256])

    from contextlib import ExitStack
    es = ExitStack()

    def sb(name, shape, dt):
        return es.enter_context(nc.sbuf_tensor(name, list(shape), dt))

    def ps(name, shape, dt):
        return es.enter_context(nc.psum_tensor(name, list(shape), dt))

    with es:
        sems = {e: [es.enter_context(nc.semaphore(f"s_{e}{k}")) for k in range(2)] for e in Prog.ENG}
        dma_sems = {q: [es.enter_context(nc.semaphore(f"d_{q}{k}")) for k in range(P.ndma_sems)]
                    for q in ("sp", "pool")}

        ident = sb("ident", [128, 128], BF16); tri = sb("tri", [128, 128], BF16)
        ones = sb("ones", [128, 128], BF16); lmask = sb("lmask", [128, 128], BF16)
        rot = sb("rot", [128, 128], BF16)
        dsel = sb("dsel", [128, 2, 128], BF16); dselrot = sb("dselrot", [128, 2, 128], BF16)
        esink = sb("esink", [128, 8], F32)
        R_const = Res("const")
        for t_, src in ((ident, c_ident), (tri, c_tri), (ones, c_ones), (lmask, c_lmask), (rot, c_rot)):
            P.dma("pool", t_[:], src, writes=[R_const])
        for a in range(2):
            P.dma("pool", dsel[:, a, :], c_dsel[a], writes=[R_const])
            P.dma("pool", dselrot[:, a, :], c_dselrot[a], writes=[R_const])
        for t_, src in ((esink, sinks),):
            P.dma("sp", t_[:], src, writes=[R_const])
        P.op("act", lambda e: e.activation(out=esink[:], in_=esink[:], func=AF.Exp), reads=[R_const], writes=[R_const])

        pbank = [ps(f"pb{i}", [128, 1024], F32) for i in range(3)]
        ptps = [ps(f"ptp{i}", [128, 1024], BF16) for i in range(2)]
        ptp = ptps[0]
        R_bank = [Res(f"bank{i}", excl=True) for i in range(6)]
        R_tps = [Res(f"tp{i}", excl=True) for i in range(2)]
        R_tp = R_tps[0]

        def bank(i):
            return pbank[i // 2][:, (i % 2) * 512:(i % 2 + 1) * 512]

        oT = sb("oT", [128, 8, S], BF16)
        R_oT = [[Res(f"oT{c}_{g}") for g in range(4)] for c in range(8)]
        xst = [sb(f"xst{i}", [128, D], F32) for i in range(2)]
        R_xst = [Res(f"xst{i}") for i in range(2)]
        junk2 = [sb("junk2_0", [128, D], BF16)] * 2; R_junk2 = [Res()] * 2
        stat2 = [sb(f"stat2_{i}", [128, 2], F32) for i in range(2)]; R_stat2 = [Res() for _ in range(2)]
        xsb = [sb(f"xsb{i}", [128, D], BF16) for i in range(2)]
        R_xsb = [Res(f"xsb{i}") for i in range(2)]
        cnt = {"x": 0, "stg": 0, "pj": 0}

        seqs = [("p", 0), ("p", 1), ("s", 0)]

        def x_src(kind, si, t0, n):
            return xp[si, t0:t0 + n, :] if kind == "p" else xs[t0:t0 + n, :]

        def rstd_from(ss_ap, out_ap, n, reads, writes):
            P.op("act", lambda e: e.activation(out=out_ap, in_=ss_ap, func=AF.Ln, scale=1.0 / D, bias=EPS),
                 reads=reads, writes=writes)
            P.op("act", lambda e: e.activation(out=out_ap, in_=out_ap, func=AF.Exp, scale=-0.5),
                 reads=writes, writes=writes)

        def norm_part1(src_ap, src_res, ts, gain, gain_res=None):
            gain_res = gain_res or R_const
            k = cnt["x"]; cnt["x"] += 1
            xb = xsb[k % 2]; Rxb = R_xsb[k % 2]
            st = stat2[k % 2]; Rst = R_stat2[k % 2]
            jk = junk2[k % 2]; Rjk = R_junk2[k % 2]
            P.op("act", lambda e: e.activation(out=jk[:ts, :], in_=src_ap, func=AF.Square,
                                               accum_out=st[:ts, 0:1]),
                 reads=[src_res], writes=[Rjk, Rst])
            rstd_from(st[:ts, 0:1], st[:ts, 1:2], ts, [Rst], [Rst])
            P.op("dve", lambda e: e.scalar_tensor_tensor(out=xb[:ts, :], in0=src_ap, scalar=st[:ts, 1:2],
                                                         in1=gain[:ts, :], op0=ALU.mult, op1=ALU.mult),
                 reads=[src_res, Rst, gain_res], writes=[Rxb])
            return k

        def norm_part2(k, ts, dst_ap3, dst_res):
            xb = xsb[k % 2]; Rxb = R_xsb[k % 2]
            tp = ptps[k % 2]; Rtp = R_tps[k % 2]
            for c in range(8):
                P.op("pe", lambda e, c=c: e.transpose(out=tp[:, c * ts:(c + 1) * ts],
                                                      in_=xb[:ts, c * 128:(c + 1) * 128], identity=ident[:ts, :ts]),
                     reads=[Rxb, R_const], writes=[Rtp])
            P.op("dve", lambda e: e.tensor_copy(out=dst_ap3,
                                                in_=tp[:, 0:8 * ts].rearrange("p (c t) -> p c t", c=8)),
                 reads=[Rtp], writes=[dst_res])

        def norm_transpose(src_ap, src_res, ts, gain, dst_ap3, dst_res, gain_res=None):
            k = norm_part1(src_ap, src_res, ts, gain, gain_res)
            norm_part2(k, ts, dst_ap3, dst_res)

        for (kind, si) in seqs:
            T = S if kind == "p" else TS
            ts = min(T, 128)
            nt = T // ts
            gs = min(T, 512)
            ng = T // gs
            tpg = gs // ts
            pos0 = 0 if kind == "p" else PAST

            with ExitStack() as l0:
                def sb0(name, shape, dt):
                    return l0.enter_context(nc.sbuf_tensor(f"{name}_{kind}{si}", list(shape), dt))

                def mk0(name, shape, dt, n):
                    return [sb0(f"{name}{i}", shape, dt) for i in range(n)]

                gpreab = sb0("gpreab", [128, D], F32); R_gab = Res()
                P.dma("sp", gpreab[:], gpre_ab, writes=[R_gab])
                xnT = sb0("xnT", [128, 8, T], BF16)
                R_xnT = [Res(f"xnT{g}") for g in range(ng)]
                wring = mk0("wr", [128, 8, 512], BF16, 2)
                R_wring = [Res(f"wr{i}") for i in range(2)]
                qTs = mk0("qT", [128, T], BF16, 2); kTs = mk0("kT", [128, T], BF16, 2); gTs = mk0("gT", [128, T], BF16, 2)
                merged = (kind == "p")
                if merged:
                    Vs = mk0("V", [128, nt, 2, 128], BF16, 2)
                else:
                    Vs = mk0("V", [128, nt, 128], BF16, 2)
                R_qs = [[Res() for _ in range(ng)] for _ in range(2)]
                R_ks = [[Res() for _ in range(ng)] for _ in range(2)]
                R_gs = [[Res() for _ in range(ng)] for _ in range(2)]
                R_Vs = [[Res() for _ in range(nt)] for _ in range(2)]
                stg = mk0("stg", [128, 256], F32, 2); R_stg = [Res() for _ in range(2)]
                biasT = sb0("biasT", [128, 8, 640], F32); R_bias = Res("bias")
                Ssb = mk0("Ssb", [128, 640], F32, 2); R_Ssb = [Res() for _ in range(2)]
                PT = mk0("PT", [128, 640], BF16, 2); R_PT = [Res() for _ in range(2)]
                rec2 = mk0("rec", [128, 128], F32, 2); R_rec2 = [Res() for _ in range(2)]
                tmpo2 = mk0("tmpo", [128, 128], F32, 2); R_tmpo2 = [Res() for _ in range(2)]
                EH = [mk0(f"E{h}_", [128, 512], F32, 2) for h in range(2)]; R_EH = [[Res() for _ in range(2)] for _ in range(2)]
                SPH = [mk0(f"SP{h}_", [128, 512], BF16, 2) for h in range(2)]; R_SPH = [[Res() for _ in range(2)] for _ in range(2)]
                ATH = [mk0(f"AT{h}_", [128, 512], BF16, 2) for h in range(2)]; R_ATH = [[Res() for _ in range(2)] for _ in range(2)]
                sgt = sb0("sgt", [128, 512], F32); R_sgt = Res()
                SaccH = mk0("Sacc", [128, 512], F32, 2); R_SaccH = [Res() for _ in range(2)]
                SaccBH = [mk0(f"SaccB{h}_", [128, 512], BF16, 2) for h in range(2)]; R_SaccBH = [[Res() for _ in range(2)] for _ in range(2)]
                nqT = mk0("nq", [128, 512], BF16, 2); R_nq = [Res() for _ in range(2)]
                if kind == "s":
                    kcache = sb0("kcache", [128, 16, 128], BF16); R_kc = Res()
                    kTcs = mk0("kTc", [128, PAST], BF16, 2); R_kTcs = [Res() for _ in range(2)]
                    Vcs = mk0("Vc", [128, 16, 128], BF16, 2); R_Vcs = [Res() for _ in range(2)]

                if merged:
                    for s_ in range(2):
                        for ti_ in range(nt):
                            P.op("pool", lambda e, s_=s_, ti_=ti_: e.memset(Vs[s_][:, ti_, :, :], 1.0), writes=[R_Vs[s_][ti_]])
                if kind == "p":
                    for h in range(8):
                        P.dma("sp", biasT[:, h, :], biasP[h], writes=[R_bias])
                    for h in range(8):
                        P.op("pool", lambda e, h=h: e.memset(biasT[0:64, h, 64:128], NEG), writes=[R_bias])
                        P.op("pool", lambda e, h=h: e.memset(biasT[64:128, h, 512:576], NEG), writes=[R_bias])
                else:
                    for h in range(8):
                        P.dma("sp", biasT[:, h, 0:160], biasS[h], writes=[R_bias])

                p0flags = {}

                def gen_phase0():
                    for ti in range(nt):
                        k = cnt["x"]
                        xt = xst[k % 2]; Rxt = R_xst[k % 2]
                        P.dma("sp", xt[:ts, :], x_src(kind, si, ti * ts, ts), writes=[Rxt])
                        g = ti // tpg
                        norm_transpose(xt[:ts, :], Rxt, ts, gpreab, xnT[:, :, ti * ts:(ti + 1) * ts], R_xnT[g],
                                       gain_res=R_gab)
                        if (ti + 1) % tpg == 0:
                            p0flags[g] = True
                        yield

                def load_w(pi):
                    P.dma("pool", wring[pi % 2][:], w_ab[pi].rearrange("(c p) n -> p c n", p=128),
                          writes=[R_wring[pi % 2]])

                PJB = 5

                def gen_proj(pi):
                    isA = pi < 4
                    hp = pi % 4
                    st = pi % 2
                    W = wring[st]; RW = R_wring[st]
                    qT, kT, gT, V = qTs[st], kTs[st], gTs[st], Vs[st]
                    R_q, R_k, R_g, R_V = R_qs[st], R_ks[st], R_gs[st], R_Vs[st]
                    if kind == "s":
                        kTc, R_kTc, Vc, R_Vc = kTcs[st], R_kTcs[st], Vcs[st], R_Vcs[st]
                        csrc_k, csrc_v, nck = (ca_k, ca_v, 4) if isA else (cb_k, cb_v, 16)
                        P.dma("pool", kcache[:, 0:nck, :],
                              csrc_k[:, hp * 128:(hp + 1) * 128].rearrange("(t p) f -> p t f", p=128), writes=[R_kc])
                        P.dma("pool", Vc[:, 0:nck, :],
                              csrc_v[:, hp * 128:(hp + 1) * 128].rearrange("(t p) f -> p t f", p=128), writes=[R_Vc])
                        for t0 in range(0, nck, 8):
                            nb_ = min(8, nck - t0)
                            for t_ in range(nb_):
                                P.op("pe", lambda e, t_=t_: e.transpose(
                                    out=ptp[:, t_ * 128:(t_ + 1) * 128], in_=kcache[:, t0 + t_, :], identity=ident[:, :]),
                                    reads=[R_kc, R_const], writes=[R_tp])
                            P.op("dve", lambda e: e.tensor_copy(
                                out=kTc[:, t0 * 128:(t0 + nb_) * 128], in_=ptp[:, 0:nb_ * 128]),
                                reads=[R_tp], writes=[R_kTc])
                            yield
                    pjbanks = [5, 4] if pi == 0 else [5]
                    pjc = [0]

                    def nextpj():
                        b_ = pjbanks[pjc[0] % len(pjbanks)]; pjc[0] += 1
                        return b_
                    for g in range(ng):
                        t0 = g * gs
                        while pi == 0 and not p0flags.get(g):
                            yield
                        for (fc, kindf) in ((0, "q"), (2, "k"), (1, "g")):
                            bi = nextpj()
                            for kc in range(8):
                                P.op("pe", lambda e, kc=kc: e.matmul(
                                    bank(bi)[:, 0:gs], lhsT=W[:, kc, fc * 128:(fc + 1) * 128],
                                    rhs=xnT[:, kc, t0:t0 + gs], start=(kc == 0), stop=(kc == 7)),
                                    reads=[RW, R_xnT[g]], writes=[R_bank[bi]])
                            if kindf == "q":
                                P.op("dve", lambda e: e.tensor_scalar(
                                    out=qT[:, t0:t0 + gs], in0=bank(bi)[:, 0:gs], scalar1=0.125, scalar2=None,
                                    op0=ALU.mult), reads=[R_bank[bi]], writes=[R_q[g]])
                            elif kindf == "k":
                                P.op("dve", lambda e: e.tensor_copy(
                                    out=kT[:, t0:t0 + gs], in_=bank(bi)[:, 0:gs]), reads=[R_bank[bi]], writes=[R_k[g]])
                            else:
                                P.op("act", lambda e: e.activation(out=sgt[:, 0:gs], in_=bank(bi)[:, 0:gs], func=AF.Exp,
                                                                   scale=-1.0), reads=[R_bank[bi]], writes=[R_sgt])
                                P.op("act", lambda e: e.activation(out=sgt[:, 0:gs], in_=sgt[:, 0:gs], func=AF.Ln, bias=1.0),
                                     reads=[R_sgt], writes=[R_sgt])
                                P.op("act", lambda e: e.activation(out=sgt[:, 0:gs], in_=sgt[:, 0:gs], func=AF.Exp,
                                                                   scale=-1.0), reads=[R_sgt], writes=[R_sgt])
                                P.op("dve", lambda e: e.tensor_tensor(out=gT[:, t0:t0 + gs], in0=bank(bi)[:, 0:gs],
                                                                      in1=sgt[:, 0:gs], op=ALU.mult),
                                     reads=[R_bank[bi], R_sgt], writes=[R_g[g]])
                            yield
                        for tt in range(tpg):
                            ti = g * tpg + tt
                            bi = nextpj()
                            for kc in range(8):
                                P.op("pe", lambda e, kc=kc: e.matmul(
                                    bank(bi)[:ts, 0:256], lhsT=xnT[:, kc, ti * ts:(ti + 1) * ts],
                                    rhs=W[:, kc, 256:512], start=(kc == 0), stop=(kc == 7)),
                                    reads=[RW, R_xnT[g]], writes=[R_bank[bi]])
                            if merged:
                                P.op("dve", lambda e: e.tensor_copy(
                                    out=V[:ts, ti, 0, 0:64], in_=bank(bi)[:ts, 128:192]), reads=[R_bank[bi]], writes=[R_V[ti]])
                                P.op("dve", lambda e: e.tensor_copy(
                                    out=V[:ts, ti, 1, 64:128], in_=bank(bi)[:ts, 192:256]), reads=[R_bank[bi]], writes=[R_V[ti]])
                            else:
                                P.op("dve", lambda e: e.tensor_copy(
                                    out=V[:ts, ti, :], in_=bank(bi)[:ts, 128:256]), reads=[R_bank[bi]], writes=[R_V[ti]])
                            if kind == "p":
                                if isA:
                                    need = ti * ts >= S - 512
                                    dk, dv, r0 = o_akp, o_avp, ti * ts - (S - 512)
                                else:
                                    need = True
                                    dk, dv, r0 = o_bkp, o_bvp, ti * ts
                                dk_ap = dk[si, r0:r0 + ts, hp * 128:(hp + 1) * 128] if need else None
                                dv_ap = dv[si, r0:r0 + ts, hp * 128:(hp + 1) * 128] if need else None
                            else:
                                need = True
                                dk, dv = (o_aks, o_avs) if isA else (o_bks, o_bvs)
                                dk_ap = dk[0:ts, hp * 128:(hp + 1) * 128]
                                dv_ap = dv[0:ts, hp * 128:(hp + 1) * 128]
                            if need:
                                sk = cnt["stg"] % 2; cnt["stg"] += 1
                                P.op("dve", lambda e: e.tensor_copy(
                                    out=stg[sk][:ts, :], in_=bank(bi)[:ts, 0:256]),
                                    reads=[R_bank[bi]], writes=[R_stg[sk]])
                                P.dma("pool", dk_ap, stg[sk][:ts, 0:128], reads=[R_stg[sk]], is_output=True)
                                P.dma("pool", dv_ap, stg[sk][:ts, 128:256], reads=[R_stg[sk]], is_output=True)
                            yield

                def gen_attnA(pi):
                    hp = pi % 4
                    st = pi % 2
                    qT, kT, gT, V = qTs[st], kTs[st], gTs[st], Vs[st]
                    R_q, R_k, R_g, R_V = R_qs[st], R_ks[st], R_gs[st], R_Vs[st]
                    if kind == "s":
                        kTc, R_kTc, Vc, R_Vc = kTcs[st], R_kTcs[st], Vcs[st], R_Vcs[st]
                    nqb = T // ts
                    qw = ts
                    its = [(j, hh) for j in range(nqb) for hh in range(2)]

                    def a_blocks(j, hh):
                        po = hh * 64
                        blocks = []
                        if kind == "p":
                            for slot in range(5):
                                kb = j - 4 + slot
                                if kb < 0:
                                    continue
                                blocks.append((kT[po:po + 64, kb * 128:(kb + 1) * 128], 128,
                                               V[:, kb, hh, :], slot,
                                               [R_k[kb // 4]], [R_V[kb]]))
                        else:
                            for slot in range(4):
                                blocks.append((kTc[po:po + 64, slot * 128:(slot + 1) * 128], 128,
                                               Vc[:, slot, hh * 64:(hh + 1) * 64], slot, [R_kTc], [R_Vc]))
                            blocks.append((kT[po:po + 64, 0:TS], TS, V[0:TS, 0, hh * 64:(hh + 1) * 64], 4,
                                           [R_k[0]], [R_V[0]]))
                        return blocks

                    def a_stage1(n):
                        j, hh = its[n]
                        h = hp * 2 + hh
                        po = hh * 64
                        sbk = n % 2
                        RS = [R_bank[2 * sbk], R_bank[2 * sbk + 1]]
                        Sps = pbank[sbk]
                        blocks = a_blocks(j, hh)
                        q_ap = qT[po:po + 64, j * qw:(j + 1) * qw]
                        gq = (j * qw) // gs
                        for (k_ap, nk, v_ap, slot, rk, rv) in blocks:
                            P.op("pe", lambda e, k_ap=k_ap, nk=nk, slot=slot: e.matmul(
                                Sps[:nk, slot * qw:(slot + 1) * qw], lhsT=k_ap, rhs=q_ap, start=True, stop=True),
                                reads=rk + [R_q[gq]], writes=RS)
                        s0 = blocks[0][3]
                        full = [b for b in blocks if b[1] == 128]
                        part = [b for b in blocks if b[1] != 128]
                        sk = n % 2
                        lo, hi = s0 * qw, (full[-1][3] + 1) * qw
                        P.op("dve", lambda e: e.tensor_tensor(
                            out=Ssb[sk][:, lo:hi], in0=Sps[:, lo:hi], in1=biasT[:, h, lo:hi], op=ALU.add),
                            reads=RS + [R_bias], writes=[R_Ssb[sk]])
                        P.op("act", lambda e: e.activation(
                            out=PT[sk][:, lo:hi], in_=Ssb[sk][:, lo:hi], func=AF.Exp),
                            reads=[R_Ssb[sk]], writes=[R_PT[sk]])
                        for (k_ap, nk, v_ap, slot, rk, rv) in part:
                            lo2, hi2 = slot * qw, (slot + 1) * qw
                            P.op("dve", lambda e, lo2=lo2, hi2=hi2, nk=nk: e.tensor_tensor(
                                out=Ssb[sk][:nk, lo2:hi2], in0=Sps[:nk, lo2:hi2], in1=biasT[:nk, h, lo2:hi2],
                                op=ALU.add), reads=RS + [R_bias], writes=[R_Ssb[sk]])
                            P.op("act", lambda e, lo2=lo2, hi2=hi2, nk=nk: e.activation(
                                out=PT[sk][:nk, lo2:hi2], in_=Ssb[sk][:nk, lo2:hi2], func=AF.Exp),
                                reads=[R_Ssb[sk]], writes=[R_PT[sk]])

                    def a_stage2m(n):
                        j, hh = its[n]
                        po = hh * 64
                        pd = 64 - po
                        sk = n % 2
                        odb = 4
                        OD = bank(4)[:, (n % 2) * 128:(n % 2) * 128 + 128]
                        blocks = a_blocks(j, hh)
                        nb = len(blocks)
                        gq = (j * qw) // gs
                        for bi_, (k_ap, nk, v_ap, slot, rk, rv) in enumerate(blocks):
                            P.op("pe", lambda e, v_ap=v_ap, nk=nk, slot=slot, bi_=bi_: e.matmul(
                                OD[:, 0:qw], lhsT=v_ap, rhs=PT[sk][:nk, slot * qw:(slot + 1) * qw],
                                start=(bi_ == 0), stop=(bi_ == nb - 1)),
                                reads=rv + [R_PT[sk]], writes=[R_bank[odb]])
                        rc = rec2[hh]; Rrc = R_rec2[hh]
                        tm = tmpo2[hh]; Rtm = R_tmpo2[hh]
                        P.op("act", lambda e: e.activation(
                            out=rc[po:po + 64, 0:qw], in_=OD[pd:pd + 64, 0:qw], func=AF.Ln),
                            reads=[R_bank[odb]], writes=[Rrc])
                        P.op("act", lambda e: e.activation(
                            out=rc[po:po + 64, 0:qw], in_=rc[po:po + 64, 0:qw], func=AF.Exp, scale=-1.0),
                            reads=[Rrc], writes=[Rrc])
                        P.op("dve", lambda e: e.tensor_tensor(
                            out=tm[po:po + 64, 0:qw], in0=OD[po:po + 64, 0:qw], in1=rc[po:po + 64, 0:qw],
                            op=ALU.mult), reads=[R_bank[odb], Rrc], writes=[Rtm])
                        P.op("pool", lambda e: e.tensor_tensor(
                            out=oT[po:po + 64, pi, j * qw:(j + 1) * qw], in0=tm[po:po + 64, 0:qw],
                            in1=gT[po:po + 64, j * qw:(j + 1) * qw], op=ALU.mult),
                            reads=[Rtm, R_g[gq]], writes=[R_oT[pi][gq]])

                    def a_stage2(n):
                        if merged:
                            return a_stage2m(n)
                        j, hh = its[n]
                        po = hh * 64
                        sk = n % 2
                        odb = 4
                        OD = bank(odb)
                        blocks = a_blocks(j, hh)
                        nb = len(blocks)
                        gq = (j * qw) // gs
                        for bi_, (k_ap, nk, v_ap, slot, rk, rv) in enumerate(blocks):
                            P.op("pe", lambda e, v_ap=v_ap, nk=nk, slot=slot, bi_=bi_: e.matmul(
                                OD[po:po + 64, 0:qw], lhsT=v_ap, rhs=PT[sk][:nk, slot * qw:(slot + 1) * qw],
                                start=(bi_ == 0), stop=(bi_ == nb - 1)),
                                reads=rv + [R_PT[sk]], writes=[R_bank[odb]])
                        for bi_, (k_ap, nk, v_ap, slot, rk, rv) in enumerate(blocks):
                            P.op("pe", lambda e, nk=nk, slot=slot, bi_=bi_: e.matmul(
                                OD[po:po + 64, 128:128 + qw], lhsT=ones[:nk, 0:64],
                                rhs=PT[sk][:nk, slot * qw:(slot + 1) * qw],
                                start=(bi_ == 0), stop=(bi_ == nb - 1)),
                                reads=[R_PT[sk], R_const], writes=[R_bank[odb]])
                        rc = rec2[hh]; Rrc = R_rec2[hh]
                        tm = tmpo2[hh]; Rtm = R_tmpo2[hh]
                        P.op("act", lambda e: e.activation(
                            out=rc[po:po + 64, 0:qw], in_=OD[po:po + 64, 128:128 + qw], func=AF.Ln),
                            reads=[R_bank[odb]], writes=[Rrc])
                        P.op("act", lambda e: e.activation(
                            out=rc[po:po + 64, 0:qw], in_=rc[po:po + 64, 0:qw], func=AF.Exp, scale=-1.0),
                            reads=[Rrc], writes=[Rrc])
                        P.op("dve", lambda e: e.tensor_tensor(
                            out=tm[po:po + 64, 0:qw], in0=OD[po:po + 64, 0:qw], in1=rc[po:po + 64, 0:qw],
                            op=ALU.mult), reads=[R_bank[odb], Rrc], writes=[Rtm])
                        P.op("pool", lambda e: e.tensor_tensor(
                            out=oT[po:po + 64, pi, j * qw:(j + 1) * qw], in0=tm[po:po + 64, 0:qw],
                            in1=gT[po:po + 64, j * qw:(j + 1) * qw], op=ALU.mult),
                            reads=[Rtm, R_g[gq]], writes=[R_oT[pi][gq]])

                    a_stage1(0)
                    yield
                    for n in range(len(its)):
                        if n + 1 < len(its):
                            a_stage1(n + 1)
                            yield
                        a_stage2(n)
                        yield

                def gen_attnB(pi):
                    st = pi % 2
                    qT, kT, gT, V = qTs[st], kTs[st], gTs[st], Vs[st]
                    R_q, R_k, R_g, R_V = R_qs[st], R_ks[st], R_gs[st], R_Vs[st]
                    if kind == "s":
                        kTc, R_kTc, Vc, R_Vc = kTcs[st], R_kTcs[st], Vcs[st], R_Vcs[st]
                    cw = gs
                    ob = 4
                    OB = bank(ob)
                    dq = min(128, cw)
                    allsteps = {}
                    G = []
                    for c in range(ng):
                        for hh in range(2):
                            po = hh * 64
                            lst = []
                            if kind == "p":
                                for kb in range(4 * c + 3, -1, -1):
                                    q0 = max(0, kb * 128 - c * 512)
                                    lst.append((kT[po:po + 64, kb * 128:(kb + 1) * 128], 128,
                                                V[:, kb, hh, hh * 64:(hh + 1) * 64], q0, kb >= 4 * c,
                                                [R_k[kb // 4]], [R_V[kb]]))
                            else:
                                lst.append((kT[po:po + 64, 0:TS], TS, V[0:TS, 0, hh * 64:(hh + 1) * 64], 0, True,
                                            [R_k[0]], [R_V[0]]))
                                for kb in range(15, -1, -1):
                                    lst.append((kTc[po:po + 64, kb * 128:(kb + 1) * 128], 128,
                                                Vc[:, kb, hh * 64:(hh + 1) * 64], 0, False, [R_kTc], [R_Vc]))
                            allsteps[(c, hh)] = lst
                        for i in range(len(allsteps[(c, 0)])):
                            G.append((c, i))

                    def prep(c, s):
                        for hh in range(2):
                            po = hh * 64
                            P.op("dve", lambda e: e.tensor_scalar(
                                out=nqT[hh][po:po + 64, 0:cw], in0=qT[po:po + 64, c * cw:(c + 1) * cw],
                                scalar1=-1.0, scalar2=None, op0=ALU.mult), reads=[R_q[c]], writes=[R_nq[hh]])
                            P.op("pool", lambda e: e.memset(SaccH[hh][:, 0:cw], 0.0), writes=[R_SaccH[hh]])
                            P.op("pool", lambda e: e.memset(SaccBH[hh][s][:, 0:cw], 0.0), writes=[R_SaccBH[hh][s]])

                    def stage1(gi):
                        c, i = G[gi]
                        s = gi % 2
                        for hh in range(2):
                            po = hh * 64
                            k_ap, nk, v_ap, q0, diag, rk, rv = allsteps[(c, hh)][i]
                            z = bank(hh); Rz = R_bank[hh]
                            P.op("pe", lambda e: e.matmul(z[:nk, q0:cw], lhsT=k_ap,
                                                          rhs=qT[po:po + 64, c * cw + q0:(c + 1) * cw],
                                                          start=True, stop=True),
                                 reads=rk + [R_q[c]], writes=[Rz])
                        for hh in range(2):
                            k_ap, nk, v_ap, q0, diag, rk, rv = allsteps[(c, hh)][i]
                            z = bank(hh); Rz = R_bank[hh]
                            E = EH[hh][s]; RE = R_EH[hh][s]
                            P.op("act", lambda e: e.activation(out=E[:nk, q0:cw], in_=z[:nk, q0:cw], func=AF.Exp),
                                 reads=[Rz], writes=[RE])
                            if diag:
                                P.op("dve", lambda e: e.tensor_tensor(
                                    out=E[:nk, q0:q0 + dq], in0=E[:nk, q0:q0 + dq], in1=lmask[:nk, 0:dq],
                                    op=ALU.mult), reads=[RE, R_const], writes=[RE])
                        for hh in range(2):
                            k_ap, nk, v_ap, q0, diag, rk, rv = allsteps[(c, hh)][i]
                            E = EH[hh][s]; RE = R_EH[hh][s]
                            SP = SPH[hh][s]; RSP = R_SPH[hh][s]
                            P.op("act", lambda e: e.activation(out=SP[:nk, q0:cw], in_=E[:nk, q0:cw], func=AF.Ln,
                                                               bias=1.0), reads=[RE], writes=[RSP])

                    def stage2(gi):
                        c, i = G[gi]
                        s = gi % 2
                        ns = len(allsteps[(c, 0)])
                        for hh in range(2):
                            k_ap, nk, v_ap, q0, diag, rk, rv = allsteps[(c, hh)][i]
                            cps = bank(2 + hh); Rc = R_bank[2 + hh]
                            SP = SPH[hh][s]; RSP = R_SPH[hh][s]
                            P.op("pe", lambda e: e.matmul(cps[:nk, q0:cw], lhsT=tri[:nk, :nk], rhs=SP[:nk, q0:cw],
                                                          start=True, stop=False),
                                 reads=[RSP, R_const], writes=[Rc])
                            P.op("pe", lambda e: e.matmul(cps[:nk, q0:cw], lhsT=ones[:, :nk],
                                                          rhs=SaccBH[hh][s][:, q0:cw], start=False, stop=False),
                                 reads=[R_SaccBH[hh][s], R_const], writes=[Rc])
                        for hh in range(2):
                            po = hh * 64
                            k_ap, nk, v_ap, q0, diag, rk, rv = allsteps[(c, hh)][i]
                            cps = bank(2 + hh); Rc = R_bank[2 + hh]
                            P.op("pe", lambda e: e.matmul(cps[:nk, q0:cw], lhsT=k_ap,
                                                          rhs=nqT[hh][po:po + 64, q0:cw], start=False, stop=True),
                                 reads=rk + [R_nq[hh]], writes=[Rc])
                        if i + 1 < ns:
                            for hh in range(2):
                                k_ap, nk, v_ap, q0, diag, rk, rv = allsteps[(c, hh)][i]
                                SP = SPH[hh][s]; RSP = R_SPH[hh][s]
                                P.op("dve", lambda e: e.tensor_tensor(
                                    out=SaccH[hh][:nk, q0:cw], in0=SaccH[hh][:nk, q0:cw], in1=SP[:nk, q0:cw], op=ALU.add),
                                    reads=[RSP, R_SaccH[hh]], writes=[R_SaccH[hh]])
                                P.op("dve", lambda e: e.tensor_copy(out=SaccBH[hh][1 - s][:, 0:cw],
                                                                    in_=SaccH[hh][:, 0:cw]),
                                     reads=[R_SaccH[hh]], writes=[R_SaccBH[hh][1 - s]])
                        for hh in range(2):
                            k_ap, nk, v_ap, q0, diag, rk, rv = allsteps[(c, hh)][i]
                            cps = bank(2 + hh); Rc = R_bank[2 + hh]
                            AT = ATH[hh][s]; RAT = R_ATH[hh][s]
                            P.op("act", lambda e: e.activation(out=AT[:nk, q0:cw], in_=cps[:nk, q0:cw], func=AF.Exp,
                                                               scale=-1.0), reads=[Rc], writes=[RAT])
                            if q0 > 0:
                                P.op("pool", lambda e: e.memset(AT[:nk, 0:q0], 0.0), writes=[RAT])
                            if diag:
                                P.op("dve", lambda e: e.tensor_tensor(
                                    out=AT[:nk, q0:q0 + dq], in0=AT[:nk, q0:q0 + dq], in1=lmask[:nk, 0:dq],
                                    op=ALU.mult), reads=[RAT, R_const], writes=[RAT])

                    def stage3(gi):
                        c, i = G[gi]
                        s = gi % 2
                        ns = len(allsteps[(c, 0)])
                        for hh in range(2):
                            po = hh * 64
                            k_ap, nk, v_ap, q0, diag, rk, rv = allsteps[(c, hh)][i]
                            AT = ATH[hh][s]; RAT = R_ATH[hh][s]
                            P.op("pe", lambda e: e.matmul(OB[po:po + 64, 0:cw], lhsT=v_ap, rhs=AT[:nk, 0:cw],
                                                          start=(i == 0), stop=(i == ns - 1)),
                                 reads=rv + [RAT], writes=[R_bank[ob]])
                        if i == ns - 1:
                            P.op("dve", lambda e: e.tensor_tensor(
                                out=oT[:, pi, c * cw:(c + 1) * cw], in0=OB[:, 0:cw],
                                in1=gT[:, c * cw:(c + 1) * cw], op=ALU.mult),
                                reads=[R_bank[ob], R_g[c]], writes=[R_oT[pi][c]])

                    NG = len(G)
                    stage1(0)
                    yield
                    for gi in range(NG):
                        if gi + 1 < NG:
                            stage1(gi + 1)
                        if G[gi][1] == 0:
                            prep(G[gi][0], gi % 2)
                        stage2(gi)
                        if gi > 0:
                            stage3(gi - 1)
                        yield
                    stage3(NG - 1)
                    yield

                def run_weighted(ga, na, gb, nb_):
                    da = db = 0
                    a_alive, b_alive = True, gb is not None
                    while a_alive or b_alive:
                        pick_b = b_alive and (not a_alive or (db + 1) * na <= (da + 1) * nb_)
                        if pick_b:
                            try:
                                next(gb); db += 1
                            except StopIteration:
                                b_alive = False
                        else:
                            try:
                                next(ga); da += 1
                            except StopIteration:
                                a_alive = False

                n_proj = ng * (3 + tpg) + (3 if kind == "s" else 0)
                load_w(0)
                gp0, gj0 = gen_phase0(), gen_proj(0)
                alive = [gp0, gj0]
                while alive:
                    for s_ in list(alive):
                        try:
                            next(s_)
                        except StopIteration:
                            alive.remove(s_)
                for pi in range(8):
                    if pi + 1 < 8:
                        load_w(pi + 1)
                    isA = pi < 4
                    if isA:
                        ga = gen_attnA(pi); na = 2 * (T // ts) * 2
                    else:
                        ga = gen_attnB(pi)
                        na = sum((4 * c + 4 + 2) for c in range(ng)) if kind == "p" else 19
                    gb = gen_proj(pi + 1) if pi + 1 < 8 else None
                    run_weighted(ga, na, gb, n_proj)
            P.barrier()

            with ExitStack() as l1:
                def sb1(name, shape, dt):
                    return l1.enter_context(nc.sbuf_tensor(f"{name}_{kind}{si}", list(shape), dt))

                gs1 = min(T, 256)
                ng1 = T // gs1
                tpg1 = gs1 // ts
                qw = ts
                woab = sb1("woab", [128, 8, D], BF16); R_woab = Res()
                wc = sb1("wc", [128, 8, 2560], BF16); R_wc = Res()
                woc = sb1("woc", [128, 8, D], BF16); R_woc = Res()
                gpostab = sb1("gpostab", [128, D], F32); gprec = sb1("gprec", [128, D], F32)
                gpostc = sb1("gpostc", [128, D], F32); R_gn = Res()
                for t_, src in ((gpostab, gpost_ab), (gprec, gpre_c), (gpostc, gpost_c)):
                    P.dma("sp", t_[:], src, writes=[R_gn])
                P.dma("pool", woab[:], w_oab.rearrange("(c p) n -> p c n", p=128), writes=[R_woab])
                for q4 in range(4):
                    P.dma("pool", wc[:, :, q4 * 640:(q4 + 1) * 640],
                          w_c[:, q4 * 640:(q4 + 1) * 640].rearrange("(c p) n -> p c n", p=128), writes=[R_wc])
                P.dma("pool", woc[:], w_oc.rearrange("(c p) n -> p c n", p=128), writes=[R_woc])

                def mk(name, shape, dt, n):
                    return [sb1(f"{name}{i}", shape, dt) for i in range(n)], [Res() for _ in range(n)]

                Y0, R_Y0 = mk("Y0", [128, tpg1, D], F32, 2)
                t1b, R_t1 = mk("t1b", [128, D], F32, 2)
                statY, R_statY = mk("statY", [128, 4], F32, 2)
                xn1T, R_xn1 = mk("xn1T", [128, 8, gs1], BF16, 1)
                xn1T, R_xn1 = xn1T * 2, R_xn1 * 2
                qbf, R_qbf = mk("qbf", [128, gs1], BF16, 2)
                qr, R_qr = mk("qr", [128, 8, gs1], BF16, 2)
                kbf, R_kbf = mk("kbf", [128, 2, gs1], BF16, 1)
                kr, R_kr = mk("kr", [128, 4, 128 + gs1], BF16, 2)
                g1, R_g1 = mk("g1", [128, 8, gs1], BF16, 2)
                V1, R_V1 = mk("V1", [128, 1 + tpg1, 256], BF16, 2)
                cosg, R_cos = mk("cosg", [128, gs1], F32, 1)
                sing, R_sin = mk("sing", [128, gs1], F32, 1)
                cosg, R_cos, sing, R_sin = cosg * 2, R_cos * 2, sing * 2, R_sin * 2
                ta, R_ta = mk("ta", [128, 256], F32, 2)
                tb, R_tb = mk("tb", [128, 256], F32, 2)
                tcb, R_tcb = mk("tcb", [128, 256], F32, 2)
                PTc, R_PTc = mk("PTc", [128, 512], BF16, 4)
                recc, R_recc = mk("recc", [128, 256], F32, 1)
                tmpc, R_tmpc = mk("tmpc", [128, 256], F32, 1)
                recc, R_recc, tmpc, R_tmpc = recc * 2, R_recc * 2, tmpc * 2, R_tmpc * 2
                kst = sb1("kst", [128, 256], F32); R_kst = Res()
                ksw = sb1("ksw", [128, 256], F32); R_ksw = Res()
                ctm = sb1("ctm", [128, 256], F32); stm = sb1("stm", [128, 256], F32); R_ctm = Res()
                kvst = sb1("kvst", [128, 512], F32); R_kvst = Res()
                if kind == "s":
                    kcc = sb1("kcc", [128, 256], BF16); R_kcc = Res()
                    kcd = sb1("kcd", [128, 4, 128], BF16); R_kcd = Res()
                    krc = sb1("krc", [128, 4, 128], BF16); R_krc = Res()
                    Vcc = sb1("Vcc", [128, 256], BF16); R_Vcc = Res()
                    P.dma("pool", kcc[:], cc_k, writes=[R_kcc])
                    P.dma("pool", Vcc[:], cc_v, writes=[R_Vcc])
                    for a in range(4):
                        for d2 in range(2):
                            P.op("pool", lambda e, a=a, d2=d2: e.tensor_copy(
                                out=kcd[:, a, d2 * 64:(d2 + 1) * 64], in_=kcc[:, a * 64:(a + 1) * 64]),
                                reads=[R_kcc], writes=[R_kcd])
                    for a in range(4):
                        P.op("pe", lambda e, a=a: e.transpose(out=ptp[:, a * 128:(a + 1) * 128], in_=kcd[:, a, :],
                                                              identity=ident[:, :]),
                             reads=[R_kcd, R_const], writes=[R_tp])
                    for a in range(4):
                        P.op("dve", lambda e, a=a: e.tensor_copy(out=krc[:, a, :], in_=ptp[:, a * 128:(a + 1) * 128]),
                             reads=[R_tp], writes=[R_krc])
                lt0 = pos0 + T - ts
                P.dma("sp", ctm[:ts, :], c_cosTM[lt0:lt0 + ts, :], writes=[R_ctm])
                P.dma("sp", stm[:ts, :], c_sinTM[lt0:lt0 + ts, :], writes=[R_ctm])

                yc = {"n": 0, "bx": 0, "t": 0}
                LAG_A = 1
                XB = [0, 1, 2]
                YB = [3, 4, 5]

                def nbx():
                    b_ = XB[yc["bx"] % 3]; yc["bx"] += 1
                    return b_

                def post_norm_residual(bk0, bk1, gain, res_ap, res_r, out_ap, out_r):
                    k = yc["t"]; yc["t"] += 1
                    st = statY[k % 2]; Rst = R_statY[k % 2]
                    jk = junk2[k % 2]; Rjk = R_junk2[k % 2]
                    for half, bk in enumerate((bk0, bk1)):
                        P.op("act", lambda e, half=half, bk=bk: e.activation(
                            out=jk[:ts, half * 512:(half + 1) * 512], in_=bank(bk)[:ts, :], func=AF.Square,
                            accum_out=st[:ts, half:half + 1]), reads=[R_bank[bk]], writes=[Rjk, Rst])
                    P.op("dve", lambda e: e.tensor_tensor(out=st[:ts, 2:3], in0=st[:ts, 0:1], in1=st[:ts, 1:2],
                                                          op=ALU.add), reads=[Rst], writes=[Rst])
                    rstd_from(st[:ts, 2:3], st[:ts, 3:4], ts, [Rst], [Rst])
                    for half, bk in enumerate((bk0, bk1)):
                        P.op("dve", lambda e, half=half, bk=bk: e.scalar_tensor_tensor(
                            out=out_ap[:, half * 512:(half + 1) * 512], in0=bank(bk)[:ts, :], scalar=st[:ts, 3:4],
                            in1=gain[:ts, half * 512:(half + 1) * 512], op0=ALU.mult, op1=ALU.mult),
                            reads=[R_bank[bk], Rst, R_gn], writes=[out_r])
                    P.op("pool", lambda e: e.tensor_tensor(out=out_ap, in0=out_ap, in1=res_ap, op=ALU.add),
                         reads=[res_r, out_r], writes=[out_r])

                def gen_a(g):
                    gb = g % 2
                    t0 = g * gs1
                    g0 = t0 // gs
                    P.dma("sp", cosg[gb][:, :], c_cosT[:, pos0 + t0:pos0 + t0 + gs1], writes=[R_cos[gb]])
                    P.dma("sp", sing[gb][:, :], c_sinT[:, pos0 + t0:pos0 + t0 + gs1], writes=[R_sin[gb]])
                    for tt in range(tpg1):
                        ti = g * tpg1 + tt
                        k = cnt["x"]
                        xt = xst[k % 2]; Rxt = R_xst[k % 2]
                        P.dma("sp", xt[:ts, :], x_src(kind, si, ti * ts, ts), writes=[Rxt])
                        bks = (nbx(), nbx())
                        for half in range(2):
                            for c in range(8):
                                P.op("pe", lambda e, c=c, half=half: e.matmul(
                                    bank(bks[half])[:ts, :], lhsT=oT[:, c, ti * ts:(ti + 1) * ts],
                                    rhs=woab[:, c, half * 512:(half + 1) * 512], start=(c == 0), stop=(c == 7)),
                                    reads=[R_oT[c][g0], R_woab], writes=[R_bank[bks[half]]])
                        yield
                        post_norm_residual(bks[0], bks[1], gpostab, xt[:ts, :], Rxt, Y0[gb][:ts, tt, :], R_Y0[gb])
                        kk = norm_part1(Y0[gb][:ts, tt, :], R_Y0[gb], ts, gprec, gain_res=R_gn)
                        for _ in range(LAG_A):
                            yield
                        norm_part2(kk, ts, xn1T[gb][:, :, tt * ts:(tt + 1) * ts], R_xn1[gb])
                        yield

                def gen_b(g):
                    gb = g % 2
                    xn = xn1T[gb]; Rxn = R_xn1[gb]
                    cs, sn = cosg[gb], sing[gb]
                    if g > 0:
                        P.op("pool", lambda e: e.tensor_copy(out=kr[gb][:, :, 0:128], in_=kr[1 - gb][:, :, gs1:gs1 + 128]),
                             reads=[R_kr[1 - gb]], writes=[R_kr[gb]])
                        P.op("pool", lambda e: e.tensor_copy(out=V1[gb][:, 0, :], in_=V1[1 - gb][:, tpg1, :]),
                             reads=[R_V1[1 - gb]], writes=[R_V1[gb]])
                    for fc in range(8):
                        b1 = nbx()
                        for kc in range(8):
                            P.op("pe", lambda e, kc=kc: e.matmul(
                                bank(b1)[:, 0:gs1], lhsT=wc[:, kc, fc * 128:(fc + 1) * 128], rhs=xn[:, kc, :],
                                start=(kc == 0), stop=(kc == 7)), reads=[R_wc, Rxn], writes=[R_bank[b1]])
                        s2 = fc % 2
                        P.op("act", lambda e: e.activation(out=qbf[s2][:, :], in_=bank(b1)[:, 0:gs1], func=AF.Copy, scale=0.125),
                             reads=[R_bank[b1]], writes=[R_qbf[s2]])
                        P.op("dve", lambda e: e.scalar_tensor_tensor(
                            out=ta[s2][:, 0:gs1], in0=bank(b1)[:, 0:gs1], scalar=0.125, in1=cs[:, :], op0=ALU.mult,
                            op1=ALU.mult), reads=[R_bank[b1], R_cos[gb]], writes=[R_ta[s2]])
                        b2 = nbx()
                        P.op("pe", lambda e: e.matmul(bank(b2)[:, 0:gs1], lhsT=rot[:, :], rhs=qbf[s2][:, :], start=True, stop=True),
                             reads=[R_qbf[s2], R_const], writes=[R_bank[b2]])
                        P.op("dve", lambda e: e.tensor_tensor(out=tb[s2][:, 0:gs1], in0=bank(b2)[:, 0:gs1], in1=sn[:, :],
                                                              op=ALU.mult), reads=[R_bank[b2], R_sin[gb]], writes=[R_tb[s2]])
                        P.op("pool", lambda e: e.tensor_tensor(out=qr[gb][:, fc, :], in0=ta[s2][:, 0:gs1], in1=tb[s2][:, 0:gs1],
                                                               op=ALU.add), reads=[R_ta[s2], R_tb[s2]], writes=[R_qr[gb]])
                        yield
                def gen_b2(g):
                    gb = g % 2
                    xn = xn1T[gb]; Rxn = R_xn1[gb]
                    cs, sn = cosg[gb], sing[gb]
                    for kc2 in range(2):
                        b1 = nbx()
                        for kc in range(8):
                            P.op("pe", lambda e, kc=kc: e.matmul(
                                bank(b1)[:, 0:gs1], lhsT=wc[:, kc, 1024 + kc2 * 128:1024 + (kc2 + 1) * 128],
                                rhs=xn[:, kc, :], start=(kc == 0), stop=(kc == 7)),
                                reads=[R_wc, Rxn], writes=[R_bank[b1]])
                        P.op("act", lambda e: e.activation(out=kbf[0][:, kc2, :], in_=bank(b1)[:, 0:gs1], func=AF.Copy),
                             reads=[R_bank[b1]], writes=[R_kbf[0]])
                    yield
                    for a in range(4):
                        s2 = a % 2
                        b1 = nbx()
                        P.op("pe", lambda e: e.matmul(bank(b1)[:, 0:gs1], lhsT=dsel[:, a % 2, :], rhs=kbf[0][:, a // 2, :],
                                                      start=True, stop=True),
                             reads=[R_kbf[0], R_const], writes=[R_bank[b1]])
                        b2 = nbx()
                        P.op("pe", lambda e: e.matmul(bank(b2)[:, 0:gs1], lhsT=dselrot[:, a % 2, :], rhs=kbf[0][:, a // 2, :],
                                                      start=True, stop=True),
                             reads=[R_kbf[0], R_const], writes=[R_bank[b2]])
                        P.op("dve", lambda e: e.tensor_tensor(out=ta[s2][:, 0:gs1], in0=bank(b1)[:, 0:gs1], in1=cs[:, :],
                                                              op=ALU.mult), reads=[R_bank[b1], R_cos[gb]], writes=[R_ta[s2]])
                        P.op("dve", lambda e: e.tensor_tensor(out=tb[s2][:, 0:gs1], in0=bank(b2)[:, 0:gs1], in1=sn[:, :],
                                                              op=ALU.mult), reads=[R_bank[b2], R_sin[gb]], writes=[R_tb[s2]])
                        P.op("pool", lambda e: e.tensor_tensor(
                            out=kr[gb][:, a, 128:128 + gs1], in0=ta[s2][:, 0:gs1], in1=tb[s2][:, 0:gs1], op=ALU.add),
                            reads=[R_ta[s2], R_tb[s2]], writes=[R_kr[gb]])
                        yield
                    for fc in range(8):
                        b1 = nbx()
                        for kc in range(8):
                            P.op("pe", lambda e, kc=kc: e.matmul(
                                bank(b1)[:, 0:gs1], lhsT=wc[:, kc, 1536 + fc * 128:1536 + (fc + 1) * 128],
                                rhs=xn[:, kc, :], start=(kc == 0), stop=(kc == 7)),
                                reads=[R_wc, Rxn], writes=[R_bank[b1]])
                        s2 = fc % 2
                        P.op("act", lambda e: e.activation(out=tcb[s2][:, 0:gs1], in_=bank(b1)[:, 0:gs1], func=AF.Exp,
                                                           scale=-1.0), reads=[R_bank[b1]], writes=[R_tcb[s2]])
                        P.op("act", lambda e: e.activation(out=tcb[s2][:, 0:gs1], in_=tcb[s2][:, 0:gs1], func=AF.Ln, bias=1.0),
                             reads=[R_tcb[s2]], writes=[R_tcb[s2]])
                        P.op("act", lambda e: e.activation(out=tcb[s2][:, 0:gs1], in_=tcb[s2][:, 0:gs1], func=AF.Exp,
                                                           scale=-1.0), reads=[R_tcb[s2]], writes=[R_tcb[s2]])
                        P.op("dve", lambda e: e.tensor_tensor(out=g1[gb][:, fc, :], in0=bank(b1)[:, 0:gs1],
                                                              in1=tcb[s2][:, 0:gs1], op=ALU.mult),
                             reads=[R_bank[b1], R_tcb[s2]], writes=[R_g1[gb]])
                        yield
                    for tt in range(tpg1):
                        ti = g * tpg1 + tt
                        b1 = nbx()
                        for kc in range(8):
                            P.op("pe", lambda e, kc=kc: e.matmul(
                                bank(b1)[:ts, :], lhsT=xn[:, kc, tt * ts:(tt + 1) * ts], rhs=wc[:, kc, 1024:1536],
                                start=(kc == 0), stop=(kc == 7)), reads=[R_wc, Rxn], writes=[R_bank[b1]])
                        P.op("dve", lambda e: e.tensor_copy(out=V1[gb][:ts, 1 + tt, :], in_=bank(b1)[:ts, 256:512]),
                             reads=[R_bank[b1]], writes=[R_V1[gb]])
                        if ti == nt - 1:
                            ysk = kvst; Rysk = R_kvst
                            P.op("dve", lambda e: e.tensor_copy(out=ysk[:ts, 0:256], in_=bank(b1)[:ts, 256:512]),
                                 reads=[R_bank[b1]], writes=[Rysk])
                            dv_ap = o_cvp[si, :, :] if kind == "p" else o_cvs[:, :]
                            dk_ap = o_ckp[si, :, :] if kind == "p" else o_cks[:, :]
                            P.dma("pool", dv_ap, ysk[:ts, 0:256], reads=[Rysk], is_output=True)
                            P.op("dve", lambda e: e.tensor_copy(out=kst[:ts, :], in_=bank(b1)[:ts, 0:256]),
                                 reads=[R_bank[b1]], writes=[R_kst])
                            for hk in range(4):
                                for b2_ in range(2):
                                    P.op("dve", lambda e, hk=hk, b2_=b2_: e.tensor_copy(
                                        out=ksw[:ts, hk * 64 + b2_ * 32:hk * 64 + b2_ * 32 + 32],
                                        in_=kst[:ts, hk * 64 + (1 - b2_) * 32:hk * 64 + (1 - b2_) * 32 + 32]),
                                        reads=[R_kst], writes=[R_ksw])
                            P.op("dve", lambda e: e.tensor_tensor(out=kst[:ts, :], in0=kst[:ts, :], in1=ctm[:ts, :],
                                                                  op=ALU.mult), reads=[R_kst, R_ctm], writes=[R_kst])
                            P.op("dve", lambda e: e.tensor_tensor(out=ksw[:ts, :], in0=ksw[:ts, :], in1=stm[:ts, :],
                                                                  op=ALU.mult), reads=[R_ksw, R_ctm], writes=[R_ksw])
                            P.op("dve", lambda e: e.tensor_tensor(out=ysk[:ts, 256:512], in0=kst[:ts, :],
                                                                  in1=ksw[:ts, :], op=ALU.add),
                                 reads=[R_kst, R_ksw], writes=[Rysk])
                            P.dma("pool", dk_ap, ysk[:ts, 256:512], reads=[Rysk], is_output=True)
                        yield

                def c_blocks(g, j, a):
                    gb = g % 2
                    J = g * tpg1 + j
                    blocks = []
                    if kind == "p":
                        if J > 0:
                            blocks.append((kr[gb][:, a, j * 128:(j + 1) * 128], 128,
                                           V1[gb][:, j, a * 64:(a + 1) * 64], "prev", [R_kr[gb]], [R_V1[gb]]))
                        blocks.append((kr[gb][:, a, (j + 1) * 128:(j + 2) * 128], 128,
                                       V1[gb][:, j + 1, a * 64:(a + 1) * 64], "diag", [R_kr[gb]], [R_V1[gb]]))
                    else:
                        blocks.append((krc[:, a, :], 128, Vcc[:, a * 64:(a + 1) * 64], "c", [R_krc], [R_Vcc]))
                        blocks.append((kr[gb][:, a, 128:128 + TS], TS, V1[gb][0:TS, 1, a * 64:(a + 1) * 64], "n",
                                       [R_kr[gb]], [R_V1[gb]]))
                    return blocks

                def c_stage1(g, n):
                    gb = g % 2
                    j, a = n // 4, n % 4
                    blocks = c_blocks(g, j, a)
                    for par in range(2):
                        sbk = YB[par]
                        Sps = bank(sbk)
                        po = par * 64
                        pt = PTc[(n % 2) * 2 + par]; Rpt = R_PTc[(n % 2) * 2 + par]
                        for bi_, (k_ap, nk, v_ap, tag, rk, rv) in enumerate(blocks):
                            if qw == 128:
                                col = bi_ * 2 * qw
                                P.op("pe", lambda e, k_ap=k_ap, nk=nk, col=col: e.matmul(
                                    Sps[:nk, col:col + 2 * qw].rearrange("p (h q) -> p h q", h=2),
                                    lhsT=k_ap[po:po + 64, :],
                                    rhs=qr[gb][po:po + 64, 2 * a:2 * a + 2, j * qw:(j + 1) * qw],
                                    start=True, stop=True), reads=rk + [R_qr[gb]], writes=[R_bank[sbk]])
                                continue
                            for hi in range(2):
                                fc = 2 * a + hi
                                col = (bi_ * 2 + hi) * qw
                                P.op("pe", lambda e, k_ap=k_ap, nk=nk, fc=fc, col=col: e.matmul(
                                    Sps[:nk, col:col + qw], lhsT=k_ap[po:po + 64, :],
                                    rhs=qr[gb][po:po + 64, fc, j * qw:(j + 1) * qw],
                                    start=True, stop=True), reads=rk + [R_qr[gb]], writes=[R_bank[sbk]])
                        for bi_, (k_ap, nk, v_ap, tag, rk, rv) in enumerate(blocks):
                            c0 = bi_ * 2 * qw
                            P.op("act", lambda e, nk=nk, c0=c0: e.activation(
                                out=pt[:nk, c0:c0 + 2 * qw], in_=Sps[:nk, c0:c0 + 2 * qw], func=AF.Exp),
                                reads=[R_bank[sbk]], writes=[Rpt])
                            if tag == "prev":
                                P.op("pool", lambda e, c0=c0: e.memset(
                                    pt[0:64, c0:c0 + 2 * qw].rearrange("p (h q) -> p h q", h=2)[:, :, 64:128], 0.0),
                                    writes=[Rpt])
                            if tag == "diag":
                                P.op("pool", lambda e, c0=c0: e.memset(
                                    pt[64:128, c0:c0 + 2 * qw].rearrange("p (h q) -> p h q", h=2)[:, :, 0:64], 0.0),
                                    writes=[Rpt])

                def c_stage2(g, n):
                    gb = g % 2
                    j, a = n // 4, n % 4
                    blocks = c_blocks(g, j, a)
                    nb = len(blocks)
                    ocb = YB[2]
                    OC = bank(ocb)
                    for par in range(2):
                        po = par * 64
                        pt = PTc[(n % 2) * 2 + par]; Rpt = R_PTc[(n % 2) * 2 + par]
                        if qw == 128:
                            for bi_, (k_ap, nk, v_ap, tag, rk, rv) in enumerate(blocks):
                                col = bi_ * 2 * qw
                                P.op("pe", lambda e, v_ap=v_ap, nk=nk, col=col, bi_=bi_: e.matmul(
                                    OC[po:po + 64, 0:256], lhsT=v_ap, rhs=pt[:nk, col:col + 256],
                                    start=(bi_ == 0), stop=(bi_ == nb - 1)),
                                    reads=rv + [Rpt], writes=[R_bank[ocb]])
                            for bi_, (k_ap, nk, v_ap, tag, rk, rv) in enumerate(blocks):
                                col = bi_ * 2 * qw
                                P.op("pe", lambda e, nk=nk, col=col, bi_=bi_: e.matmul(
                                    OC[po:po + 64, 256:512], lhsT=ones[:nk, 0:64],
                                    rhs=pt[:nk, col:col + 256], start=(bi_ == 0), stop=(bi_ == nb - 1)),
                                    reads=[Rpt, R_const], writes=[R_bank[ocb]])
                            continue
                        for hi in range(2):
                            for bi_, (k_ap, nk, v_ap, tag, rk, rv) in enumerate(blocks):
                                col = (bi_ * 2 + hi) * qw
                                P.op("pe", lambda e, v_ap=v_ap, nk=nk, col=col, bi_=bi_: e.matmul(
                                    OC[po:po + 64, hi * 128:hi * 128 + qw], lhsT=v_ap, rhs=pt[:nk, col:col + qw],
                                    start=(bi_ == 0), stop=(bi_ == nb - 1)),
                                    reads=rv + [Rpt], writes=[R_bank[ocb]])
                            for bi_, (k_ap, nk, v_ap, tag, rk, rv) in enumerate(blocks):
                                col = (bi_ * 2 + hi) * qw
                                P.op("pe", lambda e, nk=nk, col=col, bi_=bi_: e.matmul(
                                    OC[po:po + 64, 256 + hi * 128:256 + hi * 128 + qw], lhsT=ones[:nk, 0:64],
                                    rhs=pt[:nk, col:col + qw], start=(bi_ == 0), stop=(bi_ == nb - 1)),
                                    reads=[Rpt, R_const], writes=[R_bank[ocb]])
                    s2 = n % 2
                    rc = recc[s2]; Rrc = R_recc[s2]
                    tm = tmpc[s2]; Rtm = R_tmpc[s2]
                    for hi in range(2):
                        fc = 2 * a + hi
                        P.op("act", lambda e, hi=hi, fc=fc: e.activation(
                            out=rc[:, hi * 128:hi * 128 + qw], in_=OC[:, 256 + hi * 128:256 + hi * 128 + qw],
                            func=AF.Ln, bias=esink[:, fc:fc + 1]),
                            reads=[R_bank[ocb], R_const], writes=[Rrc])
                    if qw == 128:
                        P.op("act", lambda e: e.activation(out=rc[:, 0:256], in_=rc[:, 0:256], func=AF.Exp, scale=-1.0),
                             reads=[Rrc], writes=[Rrc])
                        P.op("dve", lambda e: e.tensor_tensor(out=tm[:, 0:256], in0=OC[:, 0:256], in1=rc[:, 0:256],
                                                              op=ALU.mult), reads=[R_bank[ocb], Rrc], writes=[Rtm])
                        P.op("pool", lambda e: e.tensor_tensor(
                            out=qr[gb][:, 2 * a:2 * a + 2, j * qw:(j + 1) * qw],
                            in0=tm[:, 0:256].rearrange("p (h q) -> p h q", h=2),
                            in1=g1[gb][:, 2 * a:2 * a + 2, j * qw:(j + 1) * qw], op=ALU.mult),
                            reads=[Rtm, R_g1[gb]], writes=[R_qr[gb]])
                    else:
                        for hi in range(2):
                            fc = 2 * a + hi
                            P.op("act", lambda e, hi=hi: e.activation(out=rc[:, hi * 128:hi * 128 + qw],
                                                                      in_=rc[:, hi * 128:hi * 128 + qw], func=AF.Exp,
                                                                      scale=-1.0),
                                 reads=[Rrc], writes=[Rrc])
                            P.op("dve", lambda e, hi=hi: e.tensor_tensor(
                                out=tm[:, hi * 128:hi * 128 + qw], in0=OC[:, hi * 128:hi * 128 + qw],
                                in1=rc[:, hi * 128:hi * 128 + qw], op=ALU.mult),
                                reads=[R_bank[ocb], Rrc], writes=[Rtm])
                            P.op("pool", lambda e, hi=hi, fc=fc: e.tensor_tensor(
                                out=qr[gb][:, fc, j * qw:(j + 1) * qw], in0=tm[:, hi * 128:hi * 128 + qw],
                                in1=g1[gb][:, fc, j * qw:(j + 1) * qw], op=ALU.mult),
                                reads=[Rtm, R_g1[gb]], writes=[R_qr[gb]])

                def gen_c(g):
                    nn = tpg1 * 4
                    c_stage1(g, 0)
                    yield
                    for n in range(nn):
                        if n + 1 < nn:
                            c_stage1(g, n + 1)
                            yield
                        c_stage2(g, n)
                        yield

                def gen_d(g):
                    gb = g % 2
                    for tt in range(tpg1):
                        ti = g * tpg1 + tt
                        bks = (nbx(), nbx())
                        for half in range(2):
                            for c in range(8):
                                P.op("pe", lambda e, c=c, half=half: e.matmul(
                                    bank(bks[half])[:ts, :], lhsT=qr[gb][:, c, tt * ts:(tt + 1) * ts],
                                    rhs=woc[:, c, half * 512:(half + 1) * 512], start=(c == 0), stop=(c == 7)),
                                    reads=[R_qr[gb], R_woc], writes=[R_bank[bks[half]]])
                        yield
                        sk = yc["n"] % 2; yc["n"] += 1
                        ysk = t1b[sk]; Rysk = R_t1[sk]
                        post_norm_residual(bks[0], bks[1], gpostc, Y0[gb][:ts, tt, :], R_Y0[gb], ysk[:ts, :], Rysk)
                        dst = yp[si, ti * ts:(ti + 1) * ts, :] if kind == "p" else ys[0:ts, :]
                        P.dma("pool", dst, ysk[:ts, :], reads=[Rysk], is_output=True)
                        yield

                def chain(*gens):
                    for g_ in gens:
                        yield from g_

                def run_streams(streams):
                    alive = list(streams)
                    while alive:
                        for s_ in list(alive):
                            try:
                                next(s_)
                            except StopIteration:
                                alive.remove(s_)

                flags = {}

                def wait_for(*keys):
                    while not all(flags.get(k) for k in keys):
                        yield

                def SA():
                    for g in range(ng1):
                        if g >= 2:
                            yield from wait_for(("d", g - 2))
                        if g >= 1:
                            yield from wait_for(("b2", g - 1))
                        yield from gen_a(g)
                        flags[("a", g)] = True
                        first = True
                        for _ in gen_b(g):
                            if first:
                                flags[("halo", g)] = True
                                first = False
                            yield
                        flags[("halo", g)] = True
                        flags[("bq", g)] = True

                def SB():
                    for g in range(ng1):
                        yield from wait_for(("a", g), ("halo", g))
                        yield from gen_b2(g)
                        flags[("b2", g)] = True

                def SC():
                    for g in range(ng1):
                        yield from wait_for(("bq", g), ("b2", g))
                        yield from gen_c(g)
                        flags[("c", g)] = True

                def SD():
                    for g in range(ng1):
                        yield from wait_for(("c", g))
                        yield from gen_d(g)
                        flags[("d", g)] = True

                run_streams([SA(), SB(), SC(), SD()])
            P.barrier()


        with nc.Block() as block:
            P.finalize(block, sems, dma_sems)
    return nc


_NC_CACHE = {}


def _prep(x_prompt, x_sample, cache_a_k, cache_a_v, cache_b_k, cache_b_v, cache_c_k, cache_c_v,
          ab_norm_pre, ab_w_in, ab_w_out, ab_norm_post, a_rel_bias,
          c_norm_pre, c_w_in, c_sinks, c_w_out, c_norm_post):
    f32 = np.float32
    A = lambda a: np.ascontiguousarray(np.asarray(a, dtype=f32))
    x_prompt, x_sample = A(x_prompt), A(x_sample)
    ncore = 8
    cst = _consts()
    w = A(ab_w_in)[0]
    w_ab = np.zeros((8, D, 512), f32)
    for pi in range(8):
        base = 0 if pi < 4 else 2048
        hp = pi % 4
        sl = lambda blk: w[:, base + blk * 512 + hp * 128: base + blk * 512 + (hp + 1) * 128]
        w_ab[pi, :, 0:128] = sl(0)
        w_ab[pi, :, 128:256] = sl(3)
        w_ab[pi, :, 256:384] = sl(1)
        w_ab[pi, :, 384:512] = sl(2)
    rep = lambda v: np.ascontiguousarray(np.broadcast_to(A(v).reshape(1, D), (128, D)))
    bp, bs = _bias_tiles(A(a_rel_bias)[0])
    sk = A(c_sinks)[0]
    sinks_l = np.zeros((128, 8), f32)
    for fc in range(8):
        sinks_l[0:64, fc] = sk[2 * fc]
        sinks_l[64:128, fc] = sk[2 * fc + 1]
    common = {
        "w_ab": w_ab, "w_oab": A(ab_w_out)[0], "w_c": A(c_w_in)[0], "w_oc": A(c_w_out)[0],
        "gpre_ab": rep(ab_norm_pre[0]), "gpost_ab": rep(ab_norm_post[0]),
        "gpre_c": rep(c_norm_pre[0]), "gpost_c": rep(c_norm_post[0]),
        "biasP": bp, "biasS": bs, "sinks": sinks_l,
        "c_ident": cst["ident"], "c_tri": cst["tri"], "c_ones": cst["ones"], "c_lmask": cst["lmask"],
        "c_rot": cst["rot"], "c_dsel": cst["dsel"], "c_dselrot": cst["dselrot"],
        "c_cosT": cst["cosT"], "c_sinT": cst["sinT"], "c_cosTM": cst["cosTM"], "c_sinTM": cst["sinTM"],
    }
    cak, cav = A(cache_a_k)[0], A(cache_a_v)[0]
    cbk, cbv = A(cache_b_k)[0], A(cache_b_v)[0]
    cck, ccv = A(cache_c_k)[0], A(cache_c_v)[0]
    in_maps = []
    for i in range(ncore):
        m = dict(common)
        m["xp"] = np.ascontiguousarray(x_prompt[2 * i:2 * i + 2])
        m["xs"] = np.ascontiguousarray(x_sample[i])
        m["ca_k"] = cak[i].reshape(512, 512); m["ca_v"] = cav[i].reshape(512, 512)
        m["cb_k"] = cbk[i].reshape(PAST, 512); m["cb_v"] = cbv[i].reshape(PAST, 512)
        m["cc_k"] = cck[i].reshape(128, 256); m["cc_v"] = ccv[i].reshape(128, 256)
        in_maps.append(m)
    return in_maps


def kernel(**inputs):
    ncore = 8
    in_maps = _prep(**inputs)
    if "nc" not in _NC_CACHE:
        _NC_CACHE["nc"] = build()
    nc = _NC_CACHE["nc"]
    res = run_bass_kernel_spmd(nc, in_maps, core_ids=list(range(ncore)))
    return _gather(res.results)


def _gather(R):
    ncore = len(R)
    cat = lambda name: np.concatenate([R[i][name] for i in range(ncore)], axis=0)
    stk = lambda name: np.stack([R[i][name] for i in range(ncore)], axis=0)
    y_prompt = cat("yp")
    y_sample = stk("ys")
    out = (
        y_prompt, y_sample,
        cat("o_akp").reshape(1, 16, 512, 8, 64), cat("o_avp").reshape(1, 16, 512, 8, 64),
        cat("o_bkp").reshape(1, 16, S, 8, 64), cat("o_bvp").reshape(1, 16, S, 8, 64),
        cat("o_ckp").reshape(1, 16, 128, 4, 64), cat("o_cvp").reshape(1, 16, 128, 4, 64),
        stk("o_aks").reshape(1, 8, TS, 8, 64), stk("o_avs").reshape(1, 8, TS, 8, 64),
        stk("o_bks").reshape(1, 8, TS, 8, 64), stk("o_bvs").reshape(1, 8, TS, 8, 64),
        stk("o_cks").reshape(1, 8, TS, 4, 64), stk("o_cvs").reshape(1, 8, TS, 4, 64),
    )
    return tuple(np.ascontiguousarray(o.astype(np.float32)) for o in out)
```

```python
import numpy as np
import concourse.bass as bass
import concourse.mybir as mybir
from concourse.bass_utils import run_bass_kernel_spmd

F32 = mybir.dt.float32
BF16 = mybir.dt.bfloat16
AF = mybir.ActivationFunctionType
ALU = mybir.AluOpType

D = 1024
S = 2048
TS = 32
PAST = 2048
EPS = 1e-6
NEG = -30000.0
SEM_LIM = 20000


class Res:
    __slots__ = ("w", "r", "name", "excl")

    def __init__(self, name="", excl=False):
        self.w = None
        self.r = {}
        self.name = name
        self.excl = excl


class _Rec:
    def __init__(self):
        self.call = None

    def __getattr__(self, name):
        def f(*args, **kwargs):
            self.call = (name, args, kwargs)
            return None
        return f


class Prog:
    ENG = ("pe", "act", "dve", "pool", "sp")

    def __init__(self, nc):
        self.nc = nc
        self.ops = {e: [] for e in self.ENG}
        self.waited = {e: {} for e in self.ENG}
        self.ndma_sems = 12
        self.ndma_q = {"sp": 12, "pool": 6}
        self.dma_cnt = {q: [0] * self.ndma_sems for q in ("sp", "pool")}
        self.dma_next = {"sp": 0, "pool": 0}
        self.dma_last = {q: [None] * self.ndma_sems for q in ("sp", "pool")}
        self.out_dma_refs = []
        self.last_pe = None

    def _need(self, eng, ref, waits):
        if ref is None:
            return
        if ref[0] == "op":
            _, e2, idx = ref
            if e2 == eng and eng == "pe":
                return
            if self.waited[eng].get(e2, -1) >= idx:
                return
            if e2 == eng and idx >= len(self.ops[eng]):
                return
            self.waited[eng][e2] = idx
            self.ops[e2][idx]["inc"] = True
            waits.append(ref)
        else:
            _, q, slot, val = ref
            key = ("dma", q, slot)
            if self.waited[eng].get(key, -1) >= val:
                return
            self.waited[eng][key] = val
            waits.append(ref)

    def _deps(self, eng, reads, writes, same_engine_war=False):
        waits = []
        for r in reads:
            self._need(eng, r.w, waits)
        for w in writes:
            self._need(eng, w.w, waits)
            for e2, ref in w.r.items():
                self._need(eng, ref, waits)
        return waits

    def _commit(self, ref, reads, writes):
        for r in reads:
            r.r[ref[1] if ref[0] == "op" else ("dma", ref[1], ref[2])] = ref
        for w in writes:
            w.w = ref
            w.r = {}

    def op(self, eng, fn, reads=(), writes=()):
        ex = [r for r in reads if r.excl]
        if ex:
            reads = [r for r in reads if not r.excl]
            writes = list(writes) + ex
        waits = self._deps(eng, reads, writes)
        idx = len(self.ops[eng])
        rec = _Rec()
        fn(rec)
        assert rec.call is not None
        self.ops[eng].append({"fn": rec.call, "waits": waits, "inc": False, "dma": None})
        self._commit(("op", eng, idx), reads, writes)
        return ("op", eng, idx)

    def dma(self, q, out, in_, reads=(), writes=(), is_output=False):
        waits = self._deps(q, reads, writes)
        slot = self.dma_next[q]
        self.dma_next[q] = (slot + 1) % self.ndma_q[q]
        prev = self.dma_last[q][slot]
        if prev is not None:
            self._need(q, prev, waits)
        self.dma_cnt[q][slot] += 1
        val = self.dma_cnt[q][slot] * 16
        ref = ("dma", q, slot, val)
        self.dma_last[q][slot] = ref
        self.ops[q].append({"fn": ("dma_start", (), {"out": out, "in_": in_}), "waits": waits,
                            "inc": False, "dma": (q, slot)})
        self._commit(ref, reads, writes)
        if is_output:
            self.out_dma_refs.append(ref)
        return ref

    def barrier(self):
        refs = []
        for e in self.ENG:
            for idx in range(len(self.ops[e]) - 1, -1, -1):
                o = self.ops[e][idx]
                if o["fn"] is not None and o["dma"] is None:
                    refs.append(("op", e, idx))
                    break
        for q in ("sp", "pool"):
            for slot in range(self.ndma_sems):
                if self.dma_last[q][slot] is not None:
                    refs.append(self.dma_last[q][slot])
        for e in self.ENG:
            waits = []
            for ref in refs:
                if ref[0] == "op" and ref[1] == e:
                    continue
                self._need(e, ref, waits)
            self.ops[e].append({"fn": None, "waits": waits, "inc": False, "dma": None})

    def finalize(self, block, sems, dma_sems):
        nc = self.nc
        marks = {}
        for e in self.ENG:
            c = 0
            m = []
            for o in self.ops[e]:
                if o["inc"] and o["fn"] is not None and o["dma"] is None:
                    c += 1
                m.append(c)
            marks[e] = m
            assert c <= SEM_LIM * len(sems[e]), (e, c)

        def sem_of(e, idx):
            m = marks[e][idx]
            assert m >= 1
            k = (m - 1) // SEM_LIM
            return sems[e][k], (m - 1) % SEM_LIM + 1

        engs = {"pe": nc.tensor, "act": nc.scalar, "dve": nc.vector, "pool": nc.gpsimd, "sp": nc.sync}

        def _pinfo(ap):
            fs = 1
            for s_ in list(ap.tensor.shape)[1:]:
                fs *= int(s_)
            p0 = int(ap.offset) // fs
            col = int(ap.offset) % fs
            return p0, int(ap.ap[0][1]), col, fs
        prev = None
        nviol = 0
        for o in self.ops["pe"]:
            if o["fn"] is None:
                continue
            name_, args_, kw_ = o["fn"]
            out_ap = args_[0] if args_ else kw_["out"]
            l_ap = kw_.get("lhsT", kw_.get("in_"))
            p0, kk, _, _ = _pinfo(l_ap)
            _, _, col, fs = _pinfo(out_ap)
            esz = 4 if fs in (512, 1024) and out_ap.tensor.name.startswith("pb") else 2
            bank_id = (out_ap.tensor.name, (col * esz) // 2048)
            rows = (p0, p0 + kk)
            cur = (rows, bank_id)
            if prev is not None and kk < 128 and (prev[0][1] - prev[0][0]) < 128:
                disjoint = rows[0] >= prev[0][1] or prev[0][0] >= rows[1]
                if disjoint and prev[1] == bank_id:
                    nviol += 1
            prev = cur
        assert nviol == 0, f"row-tile bank violations: {nviol}"

        def run(e, eng):
            for idx, o in enumerate(self.ops[e]):
                for ref in o["waits"]:
                    if ref[0] == "op":
                        s_, v_ = sem_of(ref[1], ref[2])
                        eng.wait_ge(s_, v_)
                    else:
                        eng.wait_ge(dma_sems[ref[1]][ref[2]], ref[3])
                if o["fn"] is None:
                    continue
                name_, args_, kw_ = o["fn"]
                ins = getattr(eng, name_)(*args_, **kw_)
                if o["dma"] is not None:
                    ins.then_inc(dma_sems[o["dma"][0]][o["dma"][1]], 16)
                elif o["inc"]:
                    s_, _ = sem_of(e, idx)
                    ins.then_inc(s_, 1)

        @block.tensor
        def _(eng):
            run("pe", eng)

        @block.scalar
        def _(eng):
            run("act", eng)

        @block.vector
        def _(eng):
            run("dve", eng)

        @block.gpsimd
        def _(eng):
            run("pool", eng)

        @block.sync
        def _(eng):
            run("sp", eng)


def _consts():
    c = {}
    i = np.arange(128)
    c["ident"] = np.eye(128, dtype=np.float32)
    c["tri"] = (i[:, None] >= i[None, :]).astype(np.float32)
    c["ones"] = np.ones((128, 128), np.float32)
    c["lmask"] = (i[:, None] < i[None, :]).astype(np.float32)
    rot = np.zeros((128, 128), np.float32)
    for p in range(128):
        if p % 64 < 32:
            rot[p + 32, p] = 1.0
        else:
            rot[p - 32, p] = 1.0
    c["rot"] = rot
    dsel = np.zeros((2, 128, 128), np.float32)
    for a in range(2):
        for p in range(128):
            dsel[a, a * 64 + (p % 64), p] = 1.0
    c["dsel"] = dsel
    c["dselrot"] = np.stack([dsel[a] @ rot for a in range(2)])
    half = 32
    inv = (10000.0 ** (-np.arange(half, dtype=np.float32) * np.float32(2.0 / 64))).astype(np.float32)
    pos = np.arange(S + TS, dtype=np.float32)
    ang = (pos[:, None] * inv[None, :]).astype(np.float32)
    cos, sin = np.cos(ang).astype(np.float32), np.sin(ang).astype(np.float32)
    pidx = np.arange(128) % 32
    sign = np.where((np.arange(128) % 64) < 32, -1.0, 1.0).astype(np.float32)
    c["cosT"] = np.ascontiguousarray(cos[:, pidx].T)
    c["sinT"] = np.ascontiguousarray((sin[:, pidx] * sign[None, :]).T)
    fidx = np.arange(256) % 32
    fsign = np.where((np.arange(256) % 64) < 32, -1.0, 1.0).astype(np.float32)
    c["cosTM"] = np.ascontiguousarray(cos[:, fidx])
    c["sinTM"] = np.ascontiguousarray(sin[:, fidx] * fsign[None, :])
    return c


def _bias_tiles(table):
    k = np.arange(128)[:, None]
    q = np.arange(128)[None, :]
    bp = np.zeros((8, 128, 640), np.float32)
    for slot in range(5):
        rel = (4 - slot) * 128 + (q - k)
        idx = np.clip(rel, -128, 128) + 128
        bp[:, :, slot * 128:(slot + 1) * 128] = table[:, idx]
    qs = PAST + np.arange(TS)[None, :]
    bs = np.zeros((8, 128, 160), np.float32)
    for slot in range(5):
        kpos = PAST - 512 + slot * 128 + np.arange(128)[:, None]
        idx = np.clip(qs - kpos, -128, 128) + 128
        bs[:, :, slot * 32:(slot + 1) * 32] = table[:, idx]
    return bp, bs


def build():
    nc = bass.Bass("TRN2", target_bir_lowering=False)
    P = Prog(nc)

    def din(name, shape):
        return nc.dram_tensor(name, list(shape), F32, kind="ExternalInput").ap()

    def dout(name, shape):
        return nc.dram_tensor(name, list(shape), F32, kind="ExternalOutput").ap()

    xp = din("xp", [2, S, D])
    xs = din("xs", [TS, D])
    ca_k = din("ca_k", [512, 512]); ca_v = din("ca_v", [512, 512])
    cb_k = din("cb_k", [PAST, 512]); cb_v = din("cb_v", [PAST, 512])
    cc_k = din("cc_k", [128, 256]); cc_v = din("cc_v", [128, 256])
    w_ab = din("w_ab", [8, D, 512])
    w_oab = din("w_oab", [D, D])
    w_c = din("w_c", [D, 2560])
    w_oc = din("w_oc", [D, D])
    gpre_ab = din("gpre_ab", [128, D]); gpost_ab = din("gpost_ab", [128, D])
    gpre_c = din("gpre_c", [128, D]); gpost_c = din("gpost_c", [128, D])
    biasP = din("biasP", [8, 128, 640]); biasS = din("biasS", [8, 128, 160])
    sinks = din("sinks", [128, 8])
    c_ident = din("c_ident", [128, 128]); c_tri = din("c_tri", [128, 128]); c_ones = din("c_ones", [128, 128])
    c_lmask = din("c_lmask", [128, 128]); c_rot = din("c_rot", [128, 128])
    c_dsel = din("c_dsel", [2, 128, 128]); c_dselrot = din("c_dselrot", [2, 128, 128])
    c_cosT = din("c_cosT", [128, S + TS]); c_sinT = din("c_sinT", [128, S + TS])
    c_cosTM = din("c_cosTM", [S + TS, 256]); c_sinTM = din("c_sinTM", [S + TS, 256])

    yp = dout("yp", [2, S, D]); ys = dout("ys", [TS, D])
    o_akp = dout("o_akp", [2, 512, 512]); o_avp = dout("o_avp", [2, 512, 512])
    o_bkp = dout("o_bkp", [2, S, 512]); o_bvp = dout("o_bvp", [2, S, 512])
    o_ckp = dout("o_ckp", [2, 128, 256]); o_cvp = dout("o_cvp", [2, 128, 256])
    o_aks = dout("o_aks", [TS, 512]); o_avs = dout("o_avs", [TS, 512])
    o_bks = dout("o_bks", [TS, 512]); o_bvs = dout("o_bvs", [TS, 512])
    o_cks = dout("o_cks", [TS, 256]); o_cvs = dout("o_cvs", [TS, 256])

    from contextlib import ExitStack
    es = ExitStack()

    def sb(name, shape, dt):
        return es.enter_context(nc.sbuf_tensor(name, list(shape), dt))

    def ps(name, shape, dt):
        return es.enter_context(nc.psum_tensor(name, list(shape), dt))

    with es:
        sems = {e: [es.enter_context(nc.semaphore(f"s_{e}{k}")) for k in range(2)] for e in Prog.ENG}
        dma_sems = {q: [es.enter_context(nc.semaphore(f"d_{q}{k}")) for k in range(P.ndma_sems)]
                    for q in ("sp", "pool")}

        ident = sb("ident", [128, 128], BF16); tri = sb("tri", [128, 128], BF16)
        ones = sb("ones", [128, 128], BF16); lmask = sb("lmask", [128, 128], BF16)
        rot = sb("rot", [128, 128], BF16)
        dsel = sb("dsel", [128, 2, 128], BF16); dselrot = sb("dselrot", [128, 2, 128], BF16)
        esink = sb("esink", [128, 8], F32)
        R_const = Res("const")
        for t_, src in ((ident, c_ident), (tri, c_tri), (ones, c_ones), (lmask, c_lmask), (rot, c_rot)):
            P.dma("pool", t_[:], src, writes=[R_const])
        for a in range(2):
            P.dma("pool", dsel[:, a, :], c_dsel[a], writes=[R_const])
            P.dma("pool", dselrot[:, a, :], c_dselrot[a], writes=[R_const])
        for t_, src in ((esink, sinks),):
            P.dma("sp", t_[:], src, writes=[R_const])
        P.op("act", lambda e: e.activation(out=esink[:], in_=esink[:], func=AF.Exp), reads=[R_const], writes=[R_const])

        pbank = [ps(f"pb{i}", [128, 1024], F32) for i in range(3)]
        ptps = [ps(f"ptp{i}", [128, 1024], BF16) for i in range(2)]
        ptp = ptps[0]
        R_bank = [Res(f"bank{i}", excl=True) for i in range(6)]
        R_tps = [Res(f"tp{i}", excl=True) for i in range(2)]
        R_tp = R_tps[0]

        def bank(i):
            return pbank[i // 2][:, (i % 2) * 512:(i % 2 + 1) * 512]

        oT = sb("oT", [128, 8, S], BF16)
        R_oT = [[Res(f"oT{c}_{g}") for g in range(4)] for c in range(8)]
        xst = [sb(f"xst{i}", [128, D], F32) for i in range(2)]
        R_xst = [Res(f"xst{i}") for i in range(2)]
        junk2 = [sb("junk2_0", [128, D], BF16)] * 2; R_junk2 = [Res()] * 2
        stat2 = [sb(f"stat2_{i}", [128, 2], F32) for i in range(2)]; R_stat2 = [Res() for _ in range(2)]
        xsb = [sb(f"xsb{i}", [128, D], BF16) for i in range(2)]
        R_xsb = [Res(f"xsb{i}") for i in range(2)]
        cnt = {"x": 0, "stg": 0, "pj": 0}

        seqs = [("p", 0), ("p", 1), ("s", 0)]

        def x_src(kind, si, t0, n):
            return xp[si, t0:t0 + n, :] if kind == "p" else xs[t0:t0 + n, :]

        def rstd_from(ss_ap, out_ap, n, reads, writes):
            P.op("act", lambda e: e.activation(out=out_ap, in_=ss_ap, func=AF.Ln, scale=1.0 / D, bias=EPS),
                 reads=reads, writes=writes)
            P.op("act", lambda e: e.activation(out=out_ap, in_=out_ap, func=AF.Exp, scale=-0.5),
                 reads=writes, writes=writes)

        def norm_part1(src_ap, src_res, ts, gain, gain_res=None):
            gain_res = gain_res or R_const
            k = cnt["x"]; cnt["x"] += 1
            xb = xsb[k % 2]; Rxb = R_xsb[k % 2]
            st = stat2[k % 2]; Rst = R_stat2[k % 2]
            jk = junk2[k % 2]; Rjk = R_junk2[k % 2]
            P.op("act", lambda e: e.activation(out=jk[:ts, :], in_=src_ap, func=AF.Square,
                                               accum_out=st[:ts, 0:1]),
                 reads=[src_res], writes=[Rjk, Rst])
            rstd_from(st[:ts, 0:1], st[:ts, 1:2], ts, [Rst], [Rst])
            P.op("dve", lambda e: e.scalar_tensor_tensor(out=xb[:ts, :], in0=src_ap, scalar=st[:ts, 1:2],
                                                         in1=gain[:ts, :], op0=ALU.mult, op1=ALU.mult),
                 reads=[src_res, Rst, gain_res], writes=[Rxb])
            return k

        def norm_part2(k, ts, dst_ap3, dst_res):
            xb = xsb[k % 2]; Rxb = R_xsb[k % 2]
            tp = ptps[k % 2]; Rtp = R_tps[k % 2]
            for c in range(8):
                P.op("pe", lambda e, c=c: e.transpose(out=tp[:, c * ts:(c + 1) * ts],
                                                      in_=xb[:ts, c * 128:(c + 1) * 128], identity=ident[:ts, :ts]),
                     reads=[Rxb, R_const], writes=[Rtp])
            P.op("dve", lambda e: e.tensor_copy(out=dst_ap3,
                                                in_=tp[:, 0:8 * ts].rearrange("p (c t) -> p c t", c=8)),
                 reads=[Rtp], writes=[dst_res])

        def norm_transpose(src_ap, src_res, ts, gain, dst_ap3, dst_res, gain_res=None):
            k = norm_part1(src_ap, src_res, ts, gain, gain_res)
            norm_part2(k, ts, dst_ap3, dst_res)

        for (kind, si) in seqs:
            T = S if kind == "p" else TS
            ts = min(T, 128)
            nt = T // ts
            gs = min(T, 512)
            ng = T // gs
            tpg = gs // ts
            pos0 = 0 if kind == "p" else PAST

            with ExitStack() as l0:
                def sb0(name, shape, dt):
                    return l0.enter_context(nc.sbuf_tensor(f"{name}_{kind}{si}", list(shape), dt))

                def mk0(name, shape, dt, n):
                    return [sb0(f"{name}{i}", shape, dt) for i in range(n)]

                gpreab = sb0("gpreab", [128, D], F32); R_gab = Res()
                P.dma("sp", gpreab[:], gpre_ab, writes=[R_gab])
                xnT = sb0("xnT", [128, 8, T], BF16)
                R_xnT = [Res(f"xnT{g}") for g in range(ng)]
                wring = mk0("wr", [128, 8, 512], BF16, 2)
                R_wring = [Res(f"wr{i}") for i in range(2)]
                qTs = mk0("qT", [128, T], BF16, 2); kTs = mk0("kT", [128, T], BF16, 2); gTs = mk0("gT", [128, T], BF16, 2)
                merged = (kind == "p")
                if merged:
                    Vs = mk0("V", [128, nt, 2, 128], BF16, 2)
                else:
                    Vs = mk0("V", [128, nt, 128], BF16, 2)
                R_qs = [[Res() for _ in range(ng)] for _ in range(2)]
                R_ks = [[Res() for _ in range(ng)] for _ in range(2)]
                R_gs = [[Res() for _ in range(ng)] for _ in range(2)]
                R_Vs = [[Res() for _ in range(nt)] for _ in range(2)]
                stg = mk0("stg", [128, 256], F32, 2); R_stg = [Res() for _ in range(2)]
                biasT = sb0("biasT", [128, 8, 640], F32); R_bias = Res("bias")
                Ssb = mk0("Ssb", [128, 640], F32, 2); R_Ssb = [Res() for _ in range(2)]
                PT = mk0("PT", [128, 640], BF16, 2); R_PT = [Res() for _ in range(2)]
                rec2 = mk0("rec", [128, 128], F32, 2); R_rec2 = [Res() for _ in range(2)]
                tmpo2 = mk0("tmpo", [128, 128], F32, 2); R_tmpo2 = [Res() for _ in range(2)]
                EH = [mk0(f"E{h}_", [128, 512], F32, 2) for h in range(2)]; R_EH = [[Res() for _ in range(2)] for _ in range(2)]
                SPH = [mk0(f"SP{h}_", [128, 512], BF16, 2) for h in range(2)]; R_SPH = [[Res() for _ in range(2)] for _ in range(2)]
                ATH = [mk0(f"AT{h}_", [128, 512], BF16, 2) for h in range(2)]; R_ATH = [[Res() for _ in range(2)] for _ in range(2)]
                sgt = sb0("sgt", [128, 512], F32); R_sgt = Res()
                SaccH = mk0("Sacc", [128, 512], F32, 2); R_SaccH = [Res() for _ in range(2)]
                SaccBH = [mk0(f"SaccB{h}_", [128, 512], BF16, 2) for h in range(2)]; R_SaccBH = [[Res() for _ in range(2)] for _ in range(2)]
                nqT = mk0("nq", [128, 512], BF16, 2); R_nq = [Res() for _ in range(2)]
                if kind == "s":
                    kcache = sb0("kcache", [128, 16, 128], BF16); R_kc = Res()
                    kTcs = mk0("kTc", [128, PAST], BF16, 2); R_kTcs = [Res() for _ in range(2)]
                    Vcs = mk0("Vc", [128, 16, 128], BF16, 2); R_Vcs = [Res() for _ in range(2)]

                if merged:
                    for s_ in range(2):
                        for ti_ in range(nt):
                            P.op("pool", lambda e, s_=s_, ti_=ti_: e.memset(Vs[s_][:, ti_, :, :], 1.0), writes=[R_Vs[s_][ti_]])
                if kind == "p":
                    for h in range(8):
                        P.dma("sp", biasT[:, h, :], biasP[h], writes=[R_bias])
                    for h in range(8):
                        P.op("pool", lambda e, h=h: e.memset(biasT[0:64, h, 64:128], NEG), writes=[R_bias])
                        P.op("pool", lambda e, h=h: e.memset(biasT[64:128, h, 512:576], NEG), writes=[R_bias])
                else:
                    for h in range(8):
                        P.dma("sp", biasT[:, h, 0:160], biasS[h], writes=[R_bias])

                p0flags = {}

                def gen_phase0():
                    for ti in range(nt):
                        k = cnt["x"]
                        xt = xst[k % 2]; Rxt = R_xst[k % 2]
                        P.dma("sp", xt[:ts, :], x_src(kind, si, ti * ts, ts), writes=[Rxt])
                        g = ti // tpg
                        norm_transpose(xt[:ts, :], Rxt, ts, gpreab, xnT[:, :, ti * ts:(ti + 1) * ts], R_xnT[g],
                                       gain_res=R_gab)
                        if (ti + 1) % tpg == 0:
                            p0flags[g] = True
                        yield

                def load_w(pi):
                    P.dma("pool", wring[pi % 2][:], w_ab[pi].rearrange("(c p) n -> p c n", p=128),
                          writes=[R_wring[pi % 2]])

                PJB = 5

                def gen_proj(pi):
                    isA = pi < 4
                    hp = pi % 4
                    st = pi % 2
                    W = wring[st]; RW = R_wring[st]
                    qT, kT, gT, V = qTs[st], kTs[st], gTs[st], Vs[st]
                    R_q, R_k, R_g, R_V = R_qs[st], R_ks[st], R_gs[st], R_Vs[st]
                    if kind == "s":
                        kTc, R_kTc, Vc, R_Vc = kTcs[st], R_kTcs[st], Vcs[st], R_Vcs[st]
                        csrc_k, csrc_v, nck = (ca_k, ca_v, 4) if isA else (cb_k, cb_v, 16)
                        P.dma("pool", kcache[:, 0:nck, :],
                              csrc_k[:, hp * 128:(hp + 1) * 128].rearrange("(t p) f -> p t f", p=128), writes=[R_kc])
                        P.dma("pool", Vc[:, 0:nck, :],
                              csrc_v[:, hp * 128:(hp + 1) * 128].rearrange("(t p) f -> p t f", p=128), writes=[R_Vc])
                        for t0 in range(0, nck, 8):
                            nb_ = min(8, nck - t0)
                            for t_ in range(nb_):
                                P.op("pe", lambda e, t_=t_: e.transpose(
                                    out=ptp[:, t_ * 128:(t_ + 1) * 128], in_=kcache[:, t0 + t_, :], identity=ident[:, :]),
                                    reads=[R_kc, R_const], writes=[R_tp])
                            P.op("dve", lambda e: e.tensor_copy(
                                out=kTc[:, t0 * 128:(t0 + nb_) * 128], in_=ptp[:, 0:nb_ * 128]),
                                reads=[R_tp], writes=[R_kTc])
                            yield
                    pjbanks = [5, 4] if pi == 0 else [5]
                    pjc = [0]

                    def nextpj():
                        b_ = pjbanks[pjc[0] % len(pjbanks)]; pjc[0] += 1
                        return b_
                    for g in range(ng):
                        t0 = g * gs
                        while pi == 0 and not p0flags.get(g):
                            yield
                        for (fc, kindf) in ((0, "q"), (2, "k"), (1, "g")):
                            bi = nextpj()
                            for kc in range(8):
                                P.op("pe", lambda e, kc=kc: e.matmul(
                                    bank(bi)[:, 0:gs], lhsT=W[:, kc, fc * 128:(fc + 1) * 128],
                                    rhs=xnT[:, kc, t0:t0 + gs], start=(kc == 0), stop=(kc == 7)),
                                    reads=[RW, R_xnT[g]], writes=[R_bank[bi]])
                            if kindf == "q":
                                P.op("dve", lambda e: e.tensor_scalar(
                                    out=qT[:, t0:t0 + gs], in0=bank(bi)[:, 0:gs], scalar1=0.125, scalar2=None,
                                    op0=ALU.mult), reads=[R_bank[bi]], writes=[R_q[g]])
                            elif kindf == "k":
                                P.op("dve", lambda e: e.tensor_copy(
                                    out=kT[:, t0:t0 + gs], in_=bank(bi)[:, 0:gs]), reads=[R_bank[bi]], writes=[R_k[g]])
                            else:
                                P.op("act", lambda e: e.activation(out=sgt[:, 0:gs], in_=bank(bi)[:, 0:gs], func=AF.Exp,
                                                                   scale=-1.0), reads=[R_bank[bi]], writes=[R_sgt])
                                P.op("act", lambda e: e.activation(out=sgt[:, 0:gs], in_=sgt[:, 0:gs], func=AF.Ln, bias=1.0),
                                     reads=[R_sgt], writes=[R_sgt])
                                P.op("act", lambda e: e.activation(out=sgt[:, 0:gs], in_=sgt[:, 0:gs], func=AF.Exp,
                                                                   scale=-1.0), reads=[R_sgt], writes=[R_sgt])
                                P.op("dve", lambda e: e.tensor_tensor(out=gT[:, t0:t0 + gs], in0=bank(bi)[:, 0:gs],
                                                                      in1=sgt[:, 0:gs], op=ALU.mult),
                                     reads=[R_bank[bi], R_sgt], writes=[R_g[g]])
                            yield
                        for tt in range(tpg):
                            ti = g * tpg + tt
                            bi = nextpj()
                            for kc in range(8):
                                P.op("pe", lambda e, kc=kc: e.matmul(
                                    bank(bi)[:ts, 0:256], lhsT=xnT[:, kc, ti * ts:(ti + 1) * ts],
                                    rhs=W[:, kc, 256:512], start=(kc == 0), stop=(kc == 7)),
                                    reads=[RW, R_xnT[g]], writes=[R_bank[bi]])
                            if merged:
                                P.op("dve", lambda e: e.tensor_copy(
                                    out=V[:ts, ti, 0, 0:64], in_=bank(bi)[:ts, 128:192]), reads=[R_bank[bi]], writes=[R_V[ti]])
                                P.op("dve", lambda e: e.tensor_copy(
                                    out=V[:ts, ti, 1, 64:128], in_=bank(bi)[:ts, 192:256]), reads=[R_bank[bi]], writes=[R_V[ti]])
                            else:
                                P.op("dve", lambda e: e.tensor_copy(
                                    out=V[:ts, ti, :], in_=bank(bi)[:ts, 128:256]), reads=[R_bank[bi]], writes=[R_V[ti]])
                            if kind == "p":
                                if isA:
                                    need = ti * ts >= S - 512
                                    dk, dv, r0 = o_akp, o_avp, ti * ts - (S - 512)
                                else:
                                    need = True
                                    dk, dv, r0 = o_bkp, o_bvp, ti * ts
                                dk_ap = dk[si, r0:r0 + ts, hp * 128:(hp + 1) * 128] if need else None
                                dv_ap = dv[si, r0:r0 + ts, hp * 128:(hp + 1) * 128] if need else None
                            else:
                                need = True
                                dk, dv = (o_aks, o_avs) if isA else (o_bks, o_bvs)
                                dk_ap = dk[0:ts, hp * 128:(hp + 1) * 128]
                                dv_ap = dv[0:ts, hp * 128:(hp + 1) * 128]
                            if need:
                                sk = cnt["stg"] % 2; cnt["stg"] += 1
                                P.op("dve", lambda e: e.tensor_copy(
                                    out=stg[sk][:ts, :], in_=bank(bi)[:ts, 0:256]),
                                    reads=[R_bank[bi]], writes=[R_stg[sk]])
                                P.dma("pool", dk_ap, stg[sk][:ts, 0:128], reads=[R_stg[sk]], is_output=True)
                                P.dma("pool", dv_ap, stg[sk][:ts, 128:256], reads=[R_stg[sk]], is_output=True)
                            yield

                def gen_attnA(pi):
                    hp = pi % 4
                    st = pi % 2
                    qT, kT, gT, V = qTs[st], kTs[st], gTs[st], Vs[st]
                    R_q, R_k, R_g, R_V = R_qs[st], R_ks[st], R_gs[st], R_Vs[st]
                    if kind == "s":
                        kTc, R_kTc, Vc, R_Vc = kTcs[st], R_kTcs[st], Vcs[st], R_Vcs[st]
                    nqb = T // ts
                    qw = ts
                    its = [(j, hh) for j in range(nqb) for hh in range(2)]

                    def a_blocks(j, hh):
                        po = hh * 64
                        blocks = []
                        if kind == "p":
                            for slot in range(5):
                                kb = j - 4 + slot
                                if kb < 0:
                                    continue
                                blocks.append((kT[po:po + 64, kb * 128:(kb + 1) * 128], 128,
                                               V[:, kb, hh, :], slot,
                                               [R_k[kb // 4]], [R_V[kb]]))
                        else:
                            for slot in range(4):
                                blocks.append((kTc[po:po + 64, slot * 128:(slot + 1) * 128], 128,
                                               Vc[:, slot, hh * 64:(hh + 1) * 64], slot, [R_kTc], [R_Vc]))
                            blocks.append((kT[po:po + 64, 0:TS], TS, V[0:TS, 0, hh * 64:(hh + 1) * 64], 4,
                                           [R_k[0]], [R_V[0]]))
                        return blocks

                    def a_stage1(n):
                        j, hh = its[n]
                        h = hp * 2 + hh
                        po = hh * 64
                        sbk = n % 2
                        RS = [R_bank[2 * sbk], R_bank[2 * sbk + 1]]
                        Sps = pbank[sbk]
                        blocks = a_blocks(j, hh)
                        q_ap = qT[po:po + 64, j * qw:(j + 1) * qw]
                        gq = (j * qw) // gs
                        for (k_ap, nk, v_ap, slot, rk, rv) in blocks:
                            P.op("pe", lambda e, k_ap=k_ap, nk=nk, slot=slot: e.matmul(
                                Sps[:nk, slot * qw:(slot + 1) * qw], lhsT=k_ap, rhs=q_ap, start=True, stop=True),
                                reads=rk + [R_q[gq]], writes=RS)
                        s0 = blocks[0][3]
                        full = [b for b in blocks if b[1] == 128]
                        part = [b for b in blocks if b[1] != 128]
                        sk = n % 2
                        lo, hi = s0 * qw, (full[-1][3] + 1) * qw
                        P.op("dve", lambda e: e.tensor_tensor(
                            out=Ssb[sk][:, lo:hi], in0=Sps[:, lo:hi], in1=biasT[:, h, lo:hi], op=ALU.add),
                            reads=RS + [R_bias], writes=[R_Ssb[sk]])
                        P.op("act", lambda e: e.activation(
                            out=PT[sk][:, lo:hi], in_=Ssb[sk][:, lo:hi], func=AF.Exp),
                            reads=[R_Ssb[sk]], writes=[R_PT[sk]])
                        for (k_ap, nk, v_ap, slot, rk, rv) in part:
                            lo2, hi2 = slot * qw, (slot + 1) * qw
                            P.op("dve", lambda e, lo2=lo2, hi2=hi2, nk=nk: e.tensor_tensor(
                                out=Ssb[sk][:nk, lo2:hi2], in0=Sps[:nk, lo2:hi2], in1=biasT[:nk, h, lo2:hi2],
                                op=ALU.add), reads=RS + [R_bias], writes=[R_Ssb[sk]])
                            P.op("act", lambda e, lo2=lo2, hi2=hi2, nk=nk: e.activation(
                                out=PT[sk][:nk, lo2:hi2], in_=Ssb[sk][:nk, lo2:hi2], func=AF.Exp),
                                reads=[R_Ssb[sk]], writes=[R_PT[sk]])

                    def a_stage2m(n):
                        j, hh = its[n]
                        po = hh * 64
                        pd = 64 - po
                        sk = n % 2
                        odb = 4
                        OD = bank(4)[:, (n % 2) * 128:(n % 2) * 128 + 128]
                        blocks = a_blocks(j, hh)
                        nb = len(blocks)
                        gq = (j * qw) // gs
                        for bi_, (k_ap, nk, v_ap, slot, rk, rv) in enumerate(blocks):
                            P.op("pe", lambda e, v_ap=v_ap, nk=nk, slot=slot, bi_=bi_: e.matmul(
                                OD[:, 0:qw], lhsT=v_ap, rhs=PT[sk][:nk, slot * qw:(slot + 1) * qw],
                                start=(bi_ == 0), stop=(bi_ == nb - 1)),
                                reads=rv + [R_PT[sk]], writes=[R_bank[odb]])
                        rc = rec2[hh]; Rrc = R_rec2[hh]
                        tm = tmpo2[hh]; Rtm = R_tmpo2[hh]
                        P.op("act", lambda e: e.activation(
                            out=rc[po:po + 64, 0:qw], in_=OD[pd:pd + 64, 0:qw], func=AF.Ln),
                            reads=[R_bank[odb]], writes=[Rrc])
                        P.op("act", lambda e: e.activation(
                            out=rc[po:po + 64, 0:qw], in_=rc[po:po + 64, 0:qw], func=AF.Exp, scale=-1.0),
                            reads=[Rrc], writes=[Rrc])
                        P.op("dve", lambda e: e.tensor_tensor(
                            out=tm[po:po + 64, 0:qw], in0=OD[po:po + 64, 0:qw], in1=rc[po:po + 64, 0:qw],
                            op=ALU.mult), reads=[R_bank[odb], Rrc], writes=[Rtm])
                        P.op("pool", lambda e: e.tensor_tensor(
                            out=oT[po:po + 64, pi, j * qw:(j + 1) * qw], in0=tm[po:po + 64, 0:qw],
                            in1=gT[po:po + 64, j * qw:(j + 1) * qw], op=ALU.mult),
                            reads=[Rtm, R_g[gq]], writes=[R_oT[pi][gq]])

                    def a_stage2(n):
                        if merged:
                            return a_stage2m(n)
                        j, hh = its[n]
                        po = hh * 64
                        sk = n % 2
                        odb = 4
                        OD = bank(odb)
                        blocks = a_blocks(j, hh)
                        nb = len(blocks)
                        gq = (j * qw) // gs
                        for bi_, (k_ap, nk, v_ap, slot, rk, rv) in enumerate(blocks):
                            P.op("pe", lambda e, v_ap=v_ap, nk=nk, slot=slot, bi_=bi_: e.matmul(
                                OD[po:po + 64, 0:qw], lhsT=v_ap, rhs=PT[sk][:nk, slot * qw:(slot + 1) * qw],
                                start=(bi_ == 0), stop=(bi_ == nb - 1)),
                                reads=rv + [R_PT[sk]], writes=[R_bank[odb]])
                        for bi_, (k_ap, nk, v_ap, slot, rk, rv) in enumerate(blocks):
                            P.op("pe", lambda e, nk=nk, slot=slot, bi_=bi_: e.matmul(
                                OD[po:po + 64, 128:128 + qw], lhsT=ones[:nk, 0:64],
                                rhs=PT[sk][:nk, slot * qw:(slot + 1) * qw],
                                start=(bi_ == 0), stop=(bi_ == nb - 1)),
                                reads=[R_PT[sk], R_const], writes=[R_bank[odb]])
                        rc = rec2[hh]; Rrc = R_rec2[hh]
                        tm = tmpo2[hh]; Rtm = R_tmpo2[hh]
                        P.op("act", lambda e: e.activation(
                            out=rc[po:po + 64, 0:qw], in_=OD[po:po + 64, 128:128 + qw], func=AF.Ln),
                            reads=[R_bank[odb]], writes=[Rrc])
                        P.op("act", lambda e: e.activation(
                            out=rc[po:po + 64, 0:qw], in_=rc[po:po + 64, 0:qw], func=AF.Exp, scale=-1.0),
                            reads=[Rrc], writes=[Rrc])
                        P.op("dve", lambda e: e.tensor_tensor(
                            out=tm[po:po + 64, 0:qw], in0=OD[po:po + 64, 0:qw], in1=rc[po:po + 64, 0:qw],
                            op=ALU.mult), reads=[R_bank[odb], Rrc], writes=[Rtm])
                        P.op("pool", lambda e: e.tensor_tensor(
                            out=oT[po:po + 64, pi, j * qw:(j + 1) * qw], in0=tm[po:po + 64, 0:qw],
                            in1=gT[po:po + 64, j * qw:(j + 1) * qw], op=ALU.mult),
                            reads=[Rtm, R_g[gq]], writes=[R_oT[pi][gq]])

                    a_stage1(0)
                    yield
                    for n in range(len(its)):
                        if n + 1 < len(its):
                            a_stage1(n + 1)
                            yield
                        a_stage2(n)
                        yield

                def gen_attnB(pi):
                    st = pi % 2
                    qT, kT, gT, V = qTs[st], kTs[st], gTs[st], Vs[st]
                    R_q, R_k, R_g, R_V = R_qs[st], R_ks[st], R_gs[st], R_Vs[st]
                    if kind == "s":
                        kTc, R_kTc, Vc, R_Vc = kTcs[st], R_kTcs[st], Vcs[st], R_Vcs[st]
                    cw = gs
                    ob = 4
                    OB = bank(ob)
                    dq = min(128, cw)
                    for c in range(ng):
                        steps = [[], []]
                        for hh in range(2):
                            po = hh * 64
                            P.op("dve", lambda e: e.tensor_scalar(
                                out=nqT[hh][po:po + 64, 0:cw], in0=qT[po:po + 64, c * cw:(c + 1) * cw],
                                scalar1=-1.0, scalar2=None, op0=ALU.mult), reads=[R_q[c]], writes=[R_nq[hh]])
                            P.op("pool", lambda e: e.memset(SaccH[hh][:, 0:cw], 0.0), writes=[R_SaccH[hh]])
                            P.op("pool", lambda e: e.memset(SaccBH[hh][0][:, 0:cw], 0.0), writes=[R_SaccBH[hh][0]])
                            if kind == "p":
                                for kb in range(4 * c + 3, -1, -1):
                                    q0 = max(0, kb * 128 - c * 512)
                                    steps[hh].append((kT[po:po + 64, kb * 128:(kb + 1) * 128], 128,
                                                      V[:, kb, hh, hh * 64:(hh + 1) * 64], q0, kb >= 4 * c,
                                                      [R_k[kb // 4]], [R_V[kb]]))
                            else:
                                steps[hh].append((kT[po:po + 64, 0:TS], TS, V[0:TS, 0, hh * 64:(hh + 1) * 64], 0, True,
                                                  [R_k[0]], [R_V[0]]))
                                for kb in range(15, -1, -1):
                                    steps[hh].append((kTc[po:po + 64, kb * 128:(kb + 1) * 128], 128,
                                                      Vc[:, kb, hh * 64:(hh + 1) * 64], 0, False, [R_kTc], [R_Vc]))
                        ns = len(steps[0])

                        def stage1(i):
                            for hh in range(2):
                                po = hh * 64
                                k_ap, nk, v_ap, q0, diag, rk, rv = steps[hh][i]
                                z = bank(hh); Rz = R_bank[hh]
                                P.op("pe", lambda e: e.matmul(z[:nk, q0:cw], lhsT=k_ap,
                                                              rhs=qT[po:po + 64, c * cw + q0:(c + 1) * cw],
                                                              start=True, stop=True),
                                     reads=rk + [R_q[c]], writes=[Rz])
                            for hh in range(2):
                                k_ap, nk, v_ap, q0, diag, rk, rv = steps[hh][i]
                                z = bank(hh); Rz = R_bank[hh]
                                E = EH[hh][i % 2]; RE = R_EH[hh][i % 2]
                                P.op("act", lambda e: e.activation(out=E[:nk, q0:cw], in_=z[:nk, q0:cw], func=AF.Exp),
                                     reads=[Rz], writes=[RE])
                                if diag:
                                    P.op("dve", lambda e: e.tensor_tensor(
                                        out=E[:nk, q0:q0 + dq], in0=E[:nk, q0:q0 + dq], in1=lmask[:nk, 0:dq],
                                        op=ALU.mult), reads=[RE, R_const], writes=[RE])
                            for hh in range(2):
                                k_ap, nk, v_ap, q0, diag, rk, rv = steps[hh][i]
                                E = EH[hh][i % 2]; RE = R_EH[hh][i % 2]
                                SP = SPH[hh][i % 2]; RSP = R_SPH[hh][i % 2]
                                P.op("act", lambda e: e.activation(out=SP[:nk, q0:cw], in_=E[:nk, q0:cw], func=AF.Ln,
                                                                   bias=1.0), reads=[RE], writes=[RSP])

                        def stage2(i):
                            for hh in range(2):
                                k_ap, nk, v_ap, q0, diag, rk, rv = steps[hh][i]
                                cps = bank(2 + hh); Rc = R_bank[2 + hh]
                                SP = SPH[hh][i % 2]; RSP = R_SPH[hh][i % 2]
                                P.op("pe", lambda e: e.matmul(cps[:nk, q0:cw], lhsT=tri[:nk, :nk], rhs=SP[:nk, q0:cw],
                                                              start=True, stop=False),
                                     reads=[RSP, R_const], writes=[Rc])
                                P.op("pe", lambda e: e.matmul(cps[:nk, q0:cw], lhsT=ones[:, :nk],
                                                              rhs=SaccBH[hh][i % 2][:, q0:cw], start=False, stop=False),
                                     reads=[R_SaccBH[hh][i % 2], R_const], writes=[Rc])
                            for hh in range(2):
                                po = hh * 64
                                k_ap, nk, v_ap, q0, diag, rk, rv = steps[hh][i]
                                cps = bank(2 + hh); Rc = R_bank[2 + hh]
                                P.op("pe", lambda e: e.matmul(cps[:nk, q0:cw], lhsT=k_ap,
                                                              rhs=nqT[hh][po:po + 64, q0:cw], start=False, stop=True),
                                     reads=rk + [R_nq[hh]], writes=[Rc])
                            if i + 1 < ns:
                                for hh in range(2):
                                    k_ap, nk, v_ap, q0, diag, rk, rv = steps[hh][i]
                                    SP = SPH[hh][i % 2]; RSP = R_SPH[hh][i % 2]
                                    P.op("dve", lambda e: e.tensor_tensor(
                                        out=SaccH[hh][:nk, q0:cw], in0=SaccH[hh][:nk, q0:cw], in1=SP[:nk, q0:cw], op=ALU.add),
                                        reads=[RSP, R_SaccH[hh]], writes=[R_SaccH[hh]])
                                    P.op("dve", lambda e: e.tensor_copy(out=SaccBH[hh][(i + 1) % 2][:, 0:cw],
                                                                        in_=SaccH[hh][:, 0:cw]),
                                         reads=[R_SaccH[hh]], writes=[R_SaccBH[hh][(i + 1) % 2]])
                            for hh in range(2):
                                k_ap, nk, v_ap, q0, diag, rk, rv = steps[hh][i]
                                cps = bank(2 + hh); Rc = R_bank[2 + hh]
                                AT = ATH[hh][i % 2]; RAT = R_ATH[hh][i % 2]
                                P.op("act", lambda e: e.activation(out=AT[:nk, q0:cw], in_=cps[:nk, q0:cw], func=AF.Exp,
                                                                   scale=-1.0), reads=[Rc], writes=[RAT])
                                if q0 > 0:
                                    P.op("pool", lambda e: e.memset(AT[:nk, 0:q0], 0.0), writes=[RAT])
                                if diag:
                                    P.op("dve", lambda e: e.tensor_tensor(
                                        out=AT[:nk, q0:q0 + dq], in0=AT[:nk, q0:q0 + dq], in1=lmask[:nk, 0:dq],
                                        op=ALU.mult), reads=[RAT, R_const], writes=[RAT])

                        def stage3(i):
                            for hh in range(2):
                                po = hh * 64
                                k_ap, nk, v_ap, q0, diag, rk, rv = steps[hh][i]
                                AT = ATH[hh][i % 2]; RAT = R_ATH[hh][i % 2]
                                P.op("pe", lambda e: e.matmul(OB[po:po + 64, 0:cw], lhsT=v_ap, rhs=AT[:nk, 0:cw],
                                                              start=(i == 0), stop=(i == ns - 1)),
                                     reads=rv + [RAT], writes=[R_bank[ob]])

                        stage1(0)
                        yield
                        for i in range(ns):
                            if i + 1 < ns:
                                stage1(i + 1)
                            stage2(i)
                            if i > 0:
                                stage3(i - 1)
                            yield
                        stage3(ns - 1)
                        P.op("dve", lambda e: e.tensor_tensor(
                            out=oT[:, pi, c * cw:(c + 1) * cw], in0=OB[:, 0:cw],
                            in1=gT[:, c * cw:(c + 1) * cw], op=ALU.mult),
                            reads=[R_bank[ob], R_g[c]], writes=[R_oT[pi][c]])
                        yield

                def run_weighted(ga, na, gb, nb_):
                    da = db = 0
                    a_alive, b_alive = True, gb is not None
                    while a_alive or b_alive:
                        pick_b = b_alive and (not a_alive or (db + 1) * na <= (da + 1) * nb_)
                        if pick_b:
                            try:
                                next(gb); db += 1
                            except StopIteration:
                                b_alive = False
                        else:
                            try:
                                next(ga); da += 1
                            except StopIteration:
                                a_alive = False

                n_proj = ng * (3 + tpg) + (3 if kind == "s" else 0)
                load_w(0)
                load_w(1)
                gp0, gj0 = gen_phase0(), gen_proj(0)
                alive = [gp0, gj0]
                while alive:
                    for s_ in list(alive):
                        try:
                            next(s_)
                        except StopIteration:
                            alive.remove(s_)
                for pi in range(8):
                    if pi + 2 < 8:
                        load_w(pi + 2)
                    isA = pi < 4
                    if isA:
                        ga = gen_attnA(pi); na = 2 * (T // ts) * 2
                    else:
                        ga = gen_attnB(pi)
                        na = sum((4 * c + 4 + 2) for c in range(ng)) if kind == "p" else 19
                    gb = gen_proj(pi + 1) if pi + 1 < 8 else None
                    run_weighted(ga, na, gb, n_proj)
            P.barrier()

            with ExitStack() as l1:
                def sb1(name, shape, dt):
                    return l1.enter_context(nc.sbuf_tensor(f"{name}_{kind}{si}", list(shape), dt))

                gs1 = min(T, 256)
                ng1 = T // gs1
                tpg1 = gs1 // ts
                qw = ts
                woab = sb1("woab", [128, 8, D], BF16); R_woab = Res()
                wc = sb1("wc", [128, 8, 2560], BF16); R_wc = Res()
                woc = sb1("woc", [128, 8, D], BF16); R_woc = Res()
                gpostab = sb1("gpostab", [128, D], F32); gprec = sb1("gprec", [128, D], F32)
                gpostc = sb1("gpostc", [128, D], F32); R_gn = Res()
                for t_, src in ((gpostab, gpost_ab), (gprec, gpre_c), (gpostc, gpost_c)):
                    P.dma("sp", t_[:], src, writes=[R_gn])
                P.dma("pool", woab[:], w_oab.rearrange("(c p) n -> p c n", p=128), writes=[R_woab])
                for q4 in range(4):
                    P.dma("pool", wc[:, :, q4 * 640:(q4 + 1) * 640],
                          w_c[:, q4 * 640:(q4 + 1) * 640].rearrange("(c p) n -> p c n", p=128), writes=[R_wc])
                P.dma("pool", woc[:], w_oc.rearrange("(c p) n -> p c n", p=128), writes=[R_woc])

                def mk(name, shape, dt, n):
                    return [sb1(f"{name}{i}", shape, dt) for i in range(n)], [Res() for _ in range(n)]

                Y0, R_Y0 = mk("Y0", [128, tpg1, D], F32, 2)
                t1b, R_t1 = mk("t1b", [128, D], F32, 2)
                statY, R_statY = mk("statY", [128, 4], F32, 2)
                xn1T, R_xn1 = mk("xn1T", [128, 8, gs1], BF16, 1)
                xn1T, R_xn1 = xn1T * 2, R_xn1 * 2
                qbf, R_qbf = mk("qbf", [128, gs1], BF16, 2)
                qr, R_qr = mk("qr", [128, 8, gs1], BF16, 2)
                kbf, R_kbf = mk("kbf", [128, 2, gs1], BF16, 1)
                kr, R_kr = mk("kr", [128, 4, 128 + gs1], BF16, 2)
                g1, R_g1 = mk("g1", [128, 8, gs1], BF16, 2)
                V1, R_V1 = mk("V1", [128, 1 + tpg1, 256], BF16, 2)
                cosg, R_cos = mk("cosg", [128, gs1], F32, 1)
                sing, R_sin = mk("sing", [128, gs1], F32, 1)
                cosg, R_cos, sing, R_sin = cosg * 2, R_cos * 2, sing * 2, R_sin * 2
                ta, R_ta = mk("ta", [128, 256], F32, 2)
                tb, R_tb = mk("tb", [128, 256], F32, 2)
                tcb, R_tcb = mk("tcb", [128, 256], F32, 2)
                PTc, R_PTc = mk("PTc", [128, 512], BF16, 4)
                recc, R_recc = mk("recc", [128, 256], F32, 1)
                tmpc, R_tmpc = mk("tmpc", [128, 256], F32, 1)
                recc, R_recc, tmpc, R_tmpc = recc * 2, R_recc * 2, tmpc * 2, R_tmpc * 2
                kst = sb1("kst", [128, 256], F32); R_kst = Res()
                ksw = sb1("ksw", [128, 256], F32); R_ksw = Res()
                ctm = sb1("ctm", [128, 256], F32); stm = sb1("stm", [128, 256], F32); R_ctm = Res()
                kvst = sb1("kvst", [128, 512], F32); R_kvst = Res()
                if kind == "s":
                    kcc = sb1("kcc", [128, 256], BF16); R_kcc = Res()
                    kcd = sb1("kcd", [128, 4, 128], BF16); R_kcd = Res()
                    krc = sb1("krc", [128, 4, 128], BF16); R_krc = Res()
                    Vcc = sb1("Vcc", [128, 256], BF16); R_Vcc = Res()
                    P.dma("pool", kcc[:], cc_k, writes=[R_kcc])
                    P.dma("pool", Vcc[:], cc_v, writes=[R_Vcc])
                    for a in range(4):
                        for d2 in range(2):
                            P.op("pool", lambda e, a=a, d2=d2: e.tensor_copy(
                                out=kcd[:, a, d2 * 64:(d2 + 1) * 64], in_=kcc[:, a * 64:(a + 1) * 64]),
                                reads=[R_kcc], writes=[R_kcd])
                    for a in range(4):
                        P.op("pe", lambda e, a=a: e.transpose(out=ptp[:, a * 128:(a + 1) * 128], in_=kcd[:, a, :],
                                                              identity=ident[:, :]),
                             reads=[R_kcd, R_const], writes=[R_tp])
                    for a in range(4):
                        P.op("dve", lambda e, a=a: e.tensor_copy(out=krc[:, a, :], in_=ptp[:, a * 128:(a + 1) * 128]),
                             reads=[R_tp], writes=[R_krc])
                lt0 = pos0 + T - ts
                P.dma("sp", ctm[:ts, :], c_cosTM[lt0:lt0 + ts, :], writes=[R_ctm])
                P.dma("sp", stm[:ts, :], c_sinTM[lt0:lt0 + ts, :], writes=[R_ctm])

                yc = {"n": 0, "bx": 0, "t": 0}
                LAG_A = 1
                XB = [0, 1, 2]
                YB = [3, 4, 5]

                def nbx():
                    b_ = XB[yc["bx"] % 3]; yc["bx"] += 1
                    return b_

                def post_norm_residual(bk0, bk1, gain, res_ap, res_r, out_ap, out_r):
                    k = yc["t"]; yc["t"] += 1
                    st = statY[k % 2]; Rst = R_statY[k % 2]
                    jk = junk2[k % 2]; Rjk = R_junk2[k % 2]
                    for half, bk in enumerate((bk0, bk1)):
                        P.op("act", lambda e, half=half, bk=bk: e.activation(
                            out=jk[:ts, half * 512:(half + 1) * 512], in_=bank(bk)[:ts, :], func=AF.Square,
                            accum_out=st[:ts, half:half + 1]), reads=[R_bank[bk]], writes=[Rjk, Rst])
                    P.op("dve", lambda e: e.tensor_tensor(out=st[:ts, 2:3], in0=st[:ts, 0:1], in1=st[:ts, 1:2],
                                                          op=ALU.add), reads=[Rst], writes=[Rst])
                    rstd_from(st[:ts, 2:3], st[:ts, 3:4], ts, [Rst], [Rst])
                    for half, bk in enumerate((bk0, bk1)):
                        P.op("dve", lambda e, half=half, bk=bk: e.scalar_tensor_tensor(
                            out=out_ap[:, half * 512:(half + 1) * 512], in0=bank(bk)[:ts, :], scalar=st[:ts, 3:4],
                            in1=gain[:ts, half * 512:(half + 1) * 512], op0=ALU.mult, op1=ALU.mult),
                            reads=[R_bank[bk], Rst, R_gn], writes=[out_r])
                    P.op("pool", lambda e: e.tensor_tensor(out=out_ap, in0=out_ap, in1=res_ap, op=ALU.add),
                         reads=[res_r, out_r], writes=[out_r])

                def gen_a(g):
                    gb = g % 2
                    t0 = g * gs1
                    g0 = t0 // gs
                    P.dma("sp", cosg[gb][:, :], c_cosT[:, pos0 + t0:pos0 + t0 + gs1], writes=[R_cos[gb]])
                    P.dma("sp", sing[gb][:, :], c_sinT[:, pos0 + t0:pos0 + t0 + gs1], writes=[R_sin[gb]])
                    for tt in range(tpg1):
                        ti = g * tpg1 + tt
                        k = cnt["x"]
                        xt = xst[k % 2]; Rxt = R_xst[k % 2]
                        P.dma("sp", xt[:ts, :], x_src(kind, si, ti * ts, ts), writes=[Rxt])
                        bks = (nbx(), nbx())
                        for half in range(2):
                            for c in range(8):
                                P.op("pe", lambda e, c=c, half=half: e.matmul(
                                    bank(bks[half])[:ts, :], lhsT=oT[:, c, ti * ts:(ti + 1) * ts],
                                    rhs=woab[:, c, half * 512:(half + 1) * 512], start=(c == 0), stop=(c == 7)),
                                    reads=[R_oT[c][g0], R_woab], writes=[R_bank[bks[half]]])
                        yield
                        post_norm_residual(bks[0], bks[1], gpostab, xt[:ts, :], Rxt, Y0[gb][:ts, tt, :], R_Y0[gb])
                        kk = norm_part1(Y0[gb][:ts, tt, :], R_Y0[gb], ts, gprec, gain_res=R_gn)
                        for _ in range(LAG_A):
                            yield
                        norm_part2(kk, ts, xn1T[gb][:, :, tt * ts:(tt + 1) * ts], R_xn1[gb])
                        yield

                def gen_b(g):
                    gb = g % 2
                    xn = xn1T[gb]; Rxn = R_xn1[gb]
                    cs, sn = cosg[gb], sing[gb]
                    if g > 0:
                        P.op("pool", lambda e: e.tensor_copy(out=kr[gb][:, :, 0:128], in_=kr[1 - gb][:, :, gs1:gs1 + 128]),
                             reads=[R_kr[1 - gb]], writes=[R_kr[gb]])
                        P.op("pool", lambda e: e.tensor_copy(out=V1[gb][:, 0, :], in_=V1[1 - gb][:, tpg1, :]),
                             reads=[R_V1[1 - gb]], writes=[R_V1[gb]])
                    for fc in range(8):
                        b1 = nbx()
                        for kc in range(8):
                            P.op("pe", lambda e, kc=kc: e.matmul(
                                bank(b1)[:, 0:gs1], lhsT=wc[:, kc, fc * 128:(fc + 1) * 128], rhs=xn[:, kc, :],
                                start=(kc == 0), stop=(kc == 7)), reads=[R_wc, Rxn], writes=[R_bank[b1]])
                        s2 = fc % 2
                        P.op("act", lambda e: e.activation(out=qbf[s2][:, :], in_=bank(b1)[:, 0:gs1], func=AF.Copy, scale=0.125),
                             reads=[R_bank[b1]], writes=[R_qbf[s2]])
                        P.op("dve", lambda e: e.scalar_tensor_tensor(
                            out=ta[s2][:, 0:gs1], in0=bank(b1)[:, 0:gs1], scalar=0.125, in1=cs[:, :], op0=ALU.mult,
                            op1=ALU.mult), reads=[R_bank[b1], R_cos[gb]], writes=[R_ta[s2]])
                        b2 = nbx()
                        P.op("pe", lambda e: e.matmul(bank(b2)[:, 0:gs1], lhsT=rot[:, :], rhs=qbf[s2][:, :], start=True, stop=True),
                             reads=[R_qbf[s2], R_const], writes=[R_bank[b2]])
                        P.op("dve", lambda e: e.tensor_tensor(out=tb[s2][:, 0:gs1], in0=bank(b2)[:, 0:gs1], in1=sn[:, :],
                                                              op=ALU.mult), reads=[R_bank[b2], R_sin[gb]], writes=[R_tb[s2]])
                        P.op("pool", lambda e: e.tensor_tensor(out=qr[gb][:, fc, :], in0=ta[s2][:, 0:gs1], in1=tb[s2][:, 0:gs1],
                                                               op=ALU.add), reads=[R_ta[s2], R_tb[s2]], writes=[R_qr[gb]])
                        yield
                def gen_b2(g):
                    gb = g % 2
                    xn = xn1T[gb]; Rxn = R_xn1[gb]
                    cs, sn = cosg[gb], sing[gb]
                    for kc2 in range(2):
                        b1 = nbx()
                        for kc in range(8):
                            P.op("pe", lambda e, kc=kc: e.matmul(
                                bank(b1)[:, 0:gs1], lhsT=wc[:, kc, 1024 + kc2 * 128:1024 + (kc2 + 1) * 128],
                                rhs=xn[:, kc, :], start=(kc == 0), stop=(kc == 7)),
                                reads=[R_wc, Rxn], writes=[R_bank[b1]])
                        P.op("act", lambda e: e.activation(out=kbf[0][:, kc2, :], in_=bank(b1)[:, 0:gs1], func=AF.Copy),
                             reads=[R_bank[b1]], writes=[R_kbf[0]])
                    yield
                    for a in range(4):
                        s2 = a % 2
                        b1 = nbx()
                        P.op("pe", lambda e: e.matmul(bank(b1)[:, 0:gs1], lhsT=dsel[:, a % 2, :], rhs=kbf[0][:, a // 2, :],
                                                      start=True, stop=True),
                             reads=[R_kbf[0], R_const], writes=[R_bank[b1]])
                        b2 = nbx()
                        P.op("pe", lambda e: e.matmul(bank(b2)[:, 0:gs1], lhsT=dselrot[:, a % 2, :], rhs=kbf[0][:, a // 2, :],
                                                      start=True, stop=True),
                             reads=[R_kbf[0], R_const], writes=[R_bank[b2]])
                        P.op("dve", lambda e: e.tensor_tensor(out=ta[s2][:, 0:gs1], in0=bank(b1)[:, 0:gs1], in1=cs[:, :],
                                                              op=ALU.mult), reads=[R_bank[b1], R_cos[gb]], writes=[R_ta[s2]])
                        P.op("dve", lambda e: e.tensor_tensor(out=tb[s2][:, 0:gs1], in0=bank(b2)[:, 0:gs1], in1=sn[:, :],
                                                              op=ALU.mult), reads=[R_bank[b2], R_sin[gb]], writes=[R_tb[s2]])
                        P.op("pool", lambda e: e.tensor_tensor(
                            out=kr[gb][:, a, 128:128 + gs1], in0=ta[s2][:, 0:gs1], in1=tb[s2][:, 0:gs1], op=ALU.add),
                            reads=[R_ta[s2], R_tb[s2]], writes=[R_kr[gb]])
                        yield
                    for fc in range(8):
                        b1 = nbx()
                        for kc in range(8):
                            P.op("pe", lambda e, kc=kc: e.matmul(
                                bank(b1)[:, 0:gs1], lhsT=wc[:, kc, 1536 + fc * 128:1536 + (fc + 1) * 128],
                                rhs=xn[:, kc, :], start=(kc == 0), stop=(kc == 7)),
                                reads=[R_wc, Rxn], writes=[R_bank[b1]])
                        s2 = fc % 2
                        P.op("act", lambda e: e.activation(out=tcb[s2][:, 0:gs1], in_=bank(b1)[:, 0:gs1], func=AF.Exp,
                                                           scale=-1.0), reads=[R_bank[b1]], writes=[R_tcb[s2]])
                        P.op("act", lambda e: e.activation(out=tcb[s2][:, 0:gs1], in_=tcb[s2][:, 0:gs1], func=AF.Ln, bias=1.0),
                             reads=[R_tcb[s2]], writes=[R_tcb[s2]])
                        P.op("act", lambda e: e.activation(out=tcb[s2][:, 0:gs1], in_=tcb[s2][:, 0:gs1], func=AF.Exp,
                                                           scale=-1.0), reads=[R_tcb[s2]], writes=[R_tcb[s2]])
                        P.op("dve", lambda e: e.tensor_tensor(out=g1[gb][:, fc, :], in0=bank(b1)[:, 0:gs1],
                                                              in1=tcb[s2][:, 0:gs1], op=ALU.mult),
                             reads=[R_bank[b1], R_tcb[s2]], writes=[R_g1[gb]])
                        yield
                    for tt in range(tpg1):
                        ti = g * tpg1 + tt
                        b1 = nbx()
                        for kc in range(8):
                            P.op("pe", lambda e, kc=kc: e.matmul(
                                bank(b1)[:ts, :], lhsT=xn[:, kc, tt * ts:(tt + 1) * ts], rhs=wc[:, kc, 1024:1536],
                                start=(kc == 0), stop=(kc == 7)), reads=[R_wc, Rxn], writes=[R_bank[b1]])
                        P.op("dve", lambda e: e.tensor_copy(out=V1[gb][:ts, 1 + tt, :], in_=bank(b1)[:ts, 256:512]),
                             reads=[R_bank[b1]], writes=[R_V1[gb]])
                        if ti == nt - 1:
                            ysk = kvst; Rysk = R_kvst
                            P.op("dve", lambda e: e.tensor_copy(out=ysk[:ts, 0:256], in_=bank(b1)[:ts, 256:512]),
                                 reads=[R_bank[b1]], writes=[Rysk])
                            dv_ap = o_cvp[si, :, :] if kind == "p" else o_cvs[:, :]
                            dk_ap = o_ckp[si, :, :] if kind == "p" else o_cks[:, :]
                            P.dma("pool", dv_ap, ysk[:ts, 0:256], reads=[Rysk], is_output=True)
                            P.op("dve", lambda e: e.tensor_copy(out=kst[:ts, :], in_=bank(b1)[:ts, 0:256]),
                                 reads=[R_bank[b1]], writes=[R_kst])
                            for hk in range(4):
                                for b2_ in range(2):
                                    P.op("dve", lambda e, hk=hk, b2_=b2_: e.tensor_copy(
                                        out=ksw[:ts, hk * 64 + b2_ * 32:hk * 64 + b2_ * 32 + 32],
                                        in_=kst[:ts, hk * 64 + (1 - b2_) * 32:hk * 64 + (1 - b2_) * 32 + 32]),
                                        reads=[R_kst], writes=[R_ksw])
                            P.op("dve", lambda e: e.tensor_tensor(out=kst[:ts, :], in0=kst[:ts, :], in1=ctm[:ts, :],
                                                                  op=ALU.mult), reads=[R_kst, R_ctm], writes=[R_kst])
                            P.op("dve", lambda e: e.tensor_tensor(out=ksw[:ts, :], in0=ksw[:ts, :], in1=stm[:ts, :],
                                                                  op=ALU.mult), reads=[R_ksw, R_ctm], writes=[R_ksw])
                            P.op("dve", lambda e: e.tensor_tensor(out=ysk[:ts, 256:512], in0=kst[:ts, :],
                                                                  in1=ksw[:ts, :], op=ALU.add),
                                 reads=[R_kst, R_ksw], writes=[Rysk])
                            P.dma("pool", dk_ap, ysk[:ts, 256:512], reads=[Rysk], is_output=True)
                        yield

                def c_blocks(g, j, a):
                    gb = g % 2
                    J = g * tpg1 + j
                    blocks = []
                    if kind == "p":
                        if J > 0:
                            blocks.append((kr[gb][:, a, j * 128:(j + 1) * 128], 128,
                                           V1[gb][:, j, a * 64:(a + 1) * 64], "prev", [R_kr[gb]], [R_V1[gb]]))
                        blocks.append((kr[gb][:, a, (j + 1) * 128:(j + 2) * 128], 128,
                                       V1[gb][:, j + 1, a * 64:(a + 1) * 64], "diag", [R_kr[gb]], [R_V1[gb]]))
                    else:
                        blocks.append((krc[:, a, :], 128, Vcc[:, a * 64:(a + 1) * 64], "c", [R_krc], [R_Vcc]))
                        blocks.append((kr[gb][:, a, 128:128 + TS], TS, V1[gb][0:TS, 1, a * 64:(a + 1) * 64], "n",
                                       [R_kr[gb]], [R_V1[gb]]))
                    return blocks

                def c_stage1(g, n):
                    gb = g % 2
                    j, a = n // 4, n % 4
                    blocks = c_blocks(g, j, a)
                    for par in range(2):
                        sbk = YB[par]
                        Sps = bank(sbk)
                        po = par * 64
                        pt = PTc[(n % 2) * 2 + par]; Rpt = R_PTc[(n % 2) * 2 + par]
                        for bi_, (k_ap, nk, v_ap, tag, rk, rv) in enumerate(blocks):
                            if qw == 128:
                                col = bi_ * 2 * qw
                                P.op("pe", lambda e, k_ap=k_ap, nk=nk, col=col: e.matmul(
                                    Sps[:nk, col:col + 2 * qw].rearrange("p (h q) -> p h q", h=2),
                                    lhsT=k_ap[po:po + 64, :],
                                    rhs=qr[gb][po:po + 64, 2 * a:2 * a + 2, j * qw:(j + 1) * qw],
                                    start=True, stop=True), reads=rk + [R_qr[gb]], writes=[R_bank[sbk]])
                                continue
                            for hi in range(2):
                                fc = 2 * a + hi
                                col = (bi_ * 2 + hi) * qw
                                P.op("pe", lambda e, k_ap=k_ap, nk=nk, fc=fc, col=col: e.matmul(
                                    Sps[:nk, col:col + qw], lhsT=k_ap[po:po + 64, :],
                                    rhs=qr[gb][po:po + 64, fc, j * qw:(j + 1) * qw],
                                    start=True, stop=True), reads=rk + [R_qr[gb]], writes=[R_bank[sbk]])
                        for bi_, (k_ap, nk, v_ap, tag, rk, rv) in enumerate(blocks):
                            c0 = bi_ * 2 * qw
                            P.op("act", lambda e, nk=nk, c0=c0: e.activation(
                                out=pt[:nk, c0:c0 + 2 * qw], in_=Sps[:nk, c0:c0 + 2 * qw], func=AF.Exp),
                                reads=[R_bank[sbk]], writes=[Rpt])
                            if tag == "prev":
                                P.op("pool", lambda e, c0=c0: e.memset(
                                    pt[0:64, c0:c0 + 2 * qw].rearrange("p (h q) -> p h q", h=2)[:, :, 64:128], 0.0),
                                    writes=[Rpt])
                            if tag == "diag":
                                P.op("pool", lambda e, c0=c0: e.memset(
                                    pt[64:128, c0:c0 + 2 * qw].rearrange("p (h q) -> p h q", h=2)[:, :, 0:64], 0.0),
                                    writes=[Rpt])

                def c_stage2(g, n):
                    gb = g % 2
                    j, a = n // 4, n % 4
                    blocks = c_blocks(g, j, a)
                    nb = len(blocks)
                    ocb = YB[2]
                    OC = bank(ocb)
                    for par in range(2):
                        po = par * 64
                        pt = PTc[(n % 2) * 2 + par]; Rpt = R_PTc[(n % 2) * 2 + par]
                        if qw == 128:
                            for bi_, (k_ap, nk, v_ap, tag, rk, rv) in enumerate(blocks):
                                col = bi_ * 2 * qw
                                P.op("pe", lambda e, v_ap=v_ap, nk=nk, col=col, bi_=bi_: e.matmul(
                                    OC[po:po + 64, 0:256], lhsT=v_ap, rhs=pt[:nk, col:col + 256],
                                    start=(bi_ == 0), stop=(bi_ == nb - 1)),
                                    reads=rv + [Rpt], writes=[R_bank[ocb]])
                            for bi_, (k_ap, nk, v_ap, tag, rk, rv) in enumerate(blocks):
                                col = bi_ * 2 * qw
                                P.op("pe", lambda e, nk=nk, col=col, bi_=bi_: e.matmul(
                                    OC[po:po + 64, 256:512], lhsT=ones[:nk, 0:64],
                                    rhs=pt[:nk, col:col + 256], start=(bi_ == 0), stop=(bi_ == nb - 1)),
                                    reads=[Rpt, R_const], writes=[R_bank[ocb]])
                            continue
                        for hi in range(2):
                            for bi_, (k_ap, nk, v_ap, tag, rk, rv) in enumerate(blocks):
                                col = (bi_ * 2 + hi) * qw
                                P.op("pe", lambda e, v_ap=v_ap, nk=nk, col=col, bi_=bi_: e.matmul(
                                    OC[po:po + 64, hi * 128:hi * 128 + qw], lhsT=v_ap, rhs=pt[:nk, col:col + qw],
                                    start=(bi_ == 0), stop=(bi_ == nb - 1)),
                                    reads=rv + [Rpt], writes=[R_bank[ocb]])
                            for bi_, (k_ap, nk, v_ap, tag, rk, rv) in enumerate(blocks):
                                col = (bi_ * 2 + hi) * qw
                                P.op("pe", lambda e, nk=nk, col=col, bi_=bi_: e.matmul(
                                    OC[po:po + 64, 256 + hi * 128:256 + hi * 128 + qw], lhsT=ones[:nk, 0:64],
                                    rhs=pt[:nk, col:col + qw], start=(bi_ == 0), stop=(bi_ == nb - 1)),
                                    reads=[Rpt, R_const], writes=[R_bank[ocb]])
                    s2 = n % 2
                    rc = recc[s2]; Rrc = R_recc[s2]
                    tm = tmpc[s2]; Rtm = R_tmpc[s2]
                    for hi in range(2):
                        fc = 2 * a + hi
                        P.op("act", lambda e, hi=hi, fc=fc: e.activation(
                            out=rc[:, hi * 128:hi * 128 + qw], in_=OC[:, 256 + hi * 128:256 + hi * 128 + qw],
                            func=AF.Ln, bias=esink[:, fc:fc + 1]),
                            reads=[R_bank[ocb], R_const], writes=[Rrc])
                    if qw == 128:
                        P.op("act", lambda e: e.activation(out=rc[:, 0:256], in_=rc[:, 0:256], func=AF.Exp, scale=-1.0),
                             reads=[Rrc], writes=[Rrc])
                        P.op("dve", lambda e: e.tensor_tensor(out=tm[:, 0:256], in0=OC[:, 0:256], in1=rc[:, 0:256],
                                                              op=ALU.mult), reads=[R_bank[ocb], Rrc], writes=[Rtm])
                        P.op("pool", lambda e: e.tensor_tensor(
                            out=qr[gb][:, 2 * a:2 * a + 2, j * qw:(j + 1) * qw],
                            in0=tm[:, 0:256].rearrange("p (h q) -> p h q", h=2),
                            in1=g1[gb][:, 2 * a:2 * a + 2, j * qw:(j + 1) * qw], op=ALU.mult),
                            reads=[Rtm, R_g1[gb]], writes=[R_qr[gb]])
                    else:
                        for hi in range(2):
                            fc = 2 * a + hi
                            P.op("act", lambda e, hi=hi: e.activation(out=rc[:, hi * 128:hi * 128 + qw],
                                                                      in_=rc[:, hi * 128:hi * 128 + qw], func=AF.Exp,
                                                                      scale=-1.0),
                                 reads=[Rrc], writes=[Rrc])
                            P.op("dve", lambda e, hi=hi: e.tensor_tensor(
                                out=tm[:, hi * 128:hi * 128 + qw], in0=OC[:, hi * 128:hi * 128 + qw],
                                in1=rc[:, hi * 128:hi * 128 + qw], op=ALU.mult),
                                reads=[R_bank[ocb], Rrc], writes=[Rtm])
                            P.op("pool", lambda e, hi=hi, fc=fc: e.tensor_tensor(
                                out=qr[gb][:, fc, j * qw:(j + 1) * qw], in0=tm[:, hi * 128:hi * 128 + qw],
                                in1=g1[gb][:, fc, j * qw:(j + 1) * qw], op=ALU.mult),
                                reads=[Rtm, R_g1[gb]], writes=[R_qr[gb]])

                def gen_c(g):
                    nn = tpg1 * 4
                    c_stage1(g, 0)
                    yield
                    for n in range(nn):
                        if n + 1 < nn:
                            c_stage1(g, n + 1)
                            yield
                        c_stage2(g, n)
                        yield

                def gen_d(g):
                    gb = g % 2
                    for tt in range(tpg1):
                        ti = g * tpg1 + tt
                        bks = (nbx(), nbx())
                        for half in range(2):
                            for c in range(8):
                                P.op("pe", lambda e, c=c, half=half: e.matmul(
                                    bank(bks[half])[:ts, :], lhsT=qr[gb][:, c, tt * ts:(tt + 1) * ts],
                                    rhs=woc[:, c, half * 512:(half + 1) * 512], start=(c == 0), stop=(c == 7)),
                                    reads=[R_qr[gb], R_woc], writes=[R_bank[bks[half]]])
                        yield
                        sk = yc["n"] % 2; yc["n"] += 1
                        ysk = t1b[sk]; Rysk = R_t1[sk]
                        post_norm_residual(bks[0], bks[1], gpostc, Y0[gb][:ts, tt, :], R_Y0[gb], ysk[:ts, :], Rysk)
                        dst = yp[si, ti * ts:(ti + 1) * ts, :] if kind == "p" else ys[0:ts, :]
                        P.dma("pool", dst, ysk[:ts, :], reads=[Rysk], is_output=True)
                        yield

                def chain(*gens):
                    for g_ in gens:
                        yield from g_

                def run_streams(streams):
                    alive = list(streams)
                    while alive:
                        for s_ in list(alive):
                            try:
                                next(s_)
                            except StopIteration:
                                alive.remove(s_)

                flags = {}

                def wait_for(*keys):
                    while not all(flags.get(k) for k in keys):
                        yield

                def SA():
                    for g in range(ng1):
                        if g >= 2:
                            yield from wait_for(("d", g - 2))
                        if g >= 1:
                            yield from wait_for(("b2", g - 1))
                        yield from gen_a(g)
                        flags[("a", g)] = True
                        first = True
                        for _ in gen_b(g):
                            if first:
                                flags[("halo", g)] = True
                                first = False
                            yield
                        flags[("halo", g)] = True
                        flags[("bq", g)] = True

                def SB():
                    for g in range(ng1):
                        yield from wait_for(("a", g), ("halo", g))
                        yield from gen_b2(g)
                        flags[("b2", g)] = True

                def SC():
                    for g in range(ng1):
                        yield from wait_for(("bq", g), ("b2", g))
                        yield from gen_c(g)
                        flags[("c", g)] = True

                def SD():
                    for g in range(ng1):
                        yield from wait_for(("c", g))
                        yield from gen_d(g)
                        flags[("d", g)] = True

                run_streams([SA(), SB(), SC(), SD()])
            P.barrier()


        with nc.Block() as block:
            P.finalize(block, sems, dma_sems)
    return nc


_NC_CACHE = {}


def _prep(x_prompt, x_sample, cache_a_k, cache_a_v, cache_b_k, cache_b_v, cache_c_k, cache_c_v,
          ab_norm_pre, ab_w_in, ab_w_out, ab_norm_post, a_rel_bias,
          c_norm_pre, c_w_in, c_sinks, c_w_out, c_norm_post):
    f32 = np.float32
    A = lambda a: np.ascontiguousarray(np.asarray(a, dtype=f32))
    x_prompt, x_sample = A(x_prompt), A(x_sample)
    ncore = 8
    cst = _consts()
    w = A(ab_w_in)[0]
    w_ab = np.zeros((8, D, 512), f32)
    for pi in range(8):
        base = 0 if pi < 4 else 2048
        hp = pi % 4
        sl = lambda blk: w[:, base + blk * 512 + hp * 128: base + blk * 512 + (hp + 1) * 128]
        w_ab[pi, :, 0:128] = sl(0)
        w_ab[pi, :, 128:256] = sl(3)
        w_ab[pi, :, 256:384] = sl(1)
        w_ab[pi, :, 384:512] = sl(2)
    rep = lambda v: np.ascontiguousarray(np.broadcast_to(A(v).reshape(1, D), (128, D)))
    bp, bs = _bias_tiles(A(a_rel_bias)[0])
    sk = A(c_sinks)[0]
    sinks_l = np.zeros((128, 8), f32)
    for fc in range(8):
        sinks_l[0:64, fc] = sk[2 * fc]
        sinks_l[64:128, fc] = sk[2 * fc + 1]
    common = {
        "w_ab": w_ab, "w_oab": A(ab_w_out)[0], "w_c": A(c_w_in)[0], "w_oc": A(c_w_out)[0],
        "gpre_ab": rep(ab_norm_pre[0]), "gpost_ab": rep(ab_norm_post[0]),
        "gpre_c": rep(c_norm_pre[0]), "gpost_c": rep(c_norm_post[0]),
        "biasP": bp, "biasS": bs, "sinks": sinks_l,
        "c_ident": cst["ident"], "c_tri": cst["tri"], "c_ones": cst["ones"], "c_lmask": cst["lmask"],
        "c_rot": cst["rot"], "c_dsel": cst["dsel"], "c_dselrot": cst["dselrot"],
        "c_cosT": cst["cosT"], "c_sinT": cst["sinT"], "c_cosTM": cst["cosTM"], "c_sinTM": cst["sinTM"],
    }
    cak, cav = A(cache_a_k)[0], A(cache_a_v)[0]
    cbk, cbv = A(cache_b_k)[0], A(cache_b_v)[0]
    cck, ccv = A(cache_c_k)[0], A(cache_c_v)[0]
    in_maps = []
    for i in range(ncore):
        m = dict(common)
        m["xp"] = np.ascontiguousarray(x_prompt[2 * i:2 * i + 2])
        m["xs"] = np.ascontiguousarray(x_sample[i])
        m["ca_k"] = cak[i].reshape(512, 512); m["ca_v"] = cav[i].reshape(512, 512)
        m["cb_k"] = cbk[i].reshape(PAST, 512); m["cb_v"] = cbv[i].reshape(PAST, 512)
        m["cc_k"] = cck[i].reshape(128, 256); m["cc_v"] = ccv[i].reshape(128, 256)
        in_maps.append(m)
    return in_maps


def kernel(**inputs):
    ncore = 8
    in_maps = _prep(**inputs)
    if "nc" not in _NC_CACHE:
        _NC_CACHE["nc"] = build()
    nc = _NC_CACHE["nc"]
    res = run_bass_kernel_spmd(nc, in_maps, core_ids=list(range(ncore)))
    return _gather(res.results)


def _gather(R):
    ncore = len(R)
    cat = lambda name: np.concatenate([R[i][name] for i in range(ncore)], axis=0)
    stk = lambda name: np.stack([R[i][name] for i in range(ncore)], axis=0)
    y_prompt = cat("yp")
    y_sample = stk("ys")
    out = (
        y_prompt, y_sample,
        cat("o_akp").reshape(1, 16, 512, 8, 64), cat("o_avp").reshape(1, 16, 512, 8, 64),
        cat("o_bkp").reshape(1, 16, S, 8, 64), cat("o_bvp").reshape(1, 16, S, 8, 64),
        cat("o_ckp").reshape(1, 16, 128, 4, 64), cat("o_cvp").reshape(1, 16, 128, 4, 64),
        stk("o_aks").reshape(1, 8, TS, 8, 64), stk("o_avs").reshape(1, 8, TS, 8, 64),
        stk("o_bks").reshape(1, 8, TS, 8, 64), stk("o_bvs").reshape(1, 8, TS, 8, 64),
        stk("o_cks").reshape(1, 8, TS, 4, 64), stk("o_cvs").reshape(1, 8, TS, 4, 64),
    )
    return tuple(np.ascontiguousarray(o.astype(np.float32)) for o in out)
```

```python
import numpy as np
import concourse.bass as bass
import concourse.mybir as mybir
from concourse.bass_utils import run_bass_kernel_spmd

F32 = mybir.dt.float32
BF16 = mybir.dt.bfloat16
AF = mybir.ActivationFunctionType
ALU = mybir.AluOpType

D = 1024
S = 2048
TS = 32
PAST = 2048
EPS = 1e-6
NEG = -30000.0
SEM_LIM = 20000


class Res:
    __slots__ = ("w", "r", "name", "excl")

    def __init__(self, name="", excl=False):
        self.w = None
        self.r = {}
        self.name = name
        self.excl = excl


class _Rec:
    def __init__(self):
        self.call = None

    def __getattr__(self, name):
        def f(*args, **kwargs):
            self.call = (name, args, kwargs)
            return None
        return f


class Prog:
    ENG = ("pe", "act", "dve", "pool", "sp")

    def __init__(self, nc):
        self.nc = nc
        self.ops = {e: [] for e in self.ENG}
        self.waited = {e: {} for e in self.ENG}
        self.ndma_sems = 12
        self.ndma_q = {"sp": 12, "pool": 6}
        self.dma_cnt = {q: [0] * self.ndma_sems for q in ("sp", "pool")}
        self.dma_next = {"sp": 0, "pool": 0}
        self.dma_last = {q: [None] * self.ndma_sems for q in ("sp", "pool")}
        self.out_dma_refs = []
        self.last_pe = None

    def _need(self, eng, ref, waits):
        if ref is None:
            return
        if ref[0] == "op":
            _, e2, idx = ref
            if e2 == eng and eng == "pe":
                return
            if self.waited[eng].get(e2, -1) >= idx:
                return
            if e2 == eng and idx >= len(self.ops[eng]):
                return
            self.waited[eng][e2] = idx
            self.ops[e2][idx]["inc"] = True
            waits.append(ref)
        else:
            _, q, slot, val = ref
            key = ("dma", q, slot)
            if self.waited[eng].get(key, -1) >= val:
                return
            self.waited[eng][key] = val
            waits.append(ref)

    def _deps(self, eng, reads, writes, same_engine_war=False):
        waits = []
        for r in reads:
            self._need(eng, r.w, waits)
        for w in writes:
            self._need(eng, w.w, waits)
            for e2, ref in w.r.items():
                self._need(eng, ref, waits)
        return waits

    def _commit(self, ref, reads, writes):
        for r in reads:
            r.r[ref[1] if ref[0] == "op" else ("dma", ref[1], ref[2])] = ref
        for w in writes:
            w.w = ref
            w.r = {}

    def op(self, eng, fn, reads=(), writes=()):
        ex = [r for r in reads if r.excl]
        if ex:
            reads = [r for r in reads if not r.excl]
            writes = list(writes) + ex
        waits = self._deps(eng, reads, writes)
        idx = len(self.ops[eng])
        rec = _Rec()
        fn(rec)
        assert rec.call is not None
        self.ops[eng].append({"fn": rec.call, "waits": waits, "inc": False, "dma": None})
        self._commit(("op", eng, idx), reads, writes)
        return ("op", eng, idx)

    def dma(self, q, out, in_, reads=(), writes=(), is_output=False):
        waits = self._deps(q, reads, writes)
        slot = self.dma_next[q]
        self.dma_next[q] = (slot + 1) % self.ndma_q[q]
        prev = self.dma_last[q][slot]
        if prev is not None:
            self._need(q, prev, waits)
        self.dma_cnt[q][slot] += 1
        val = self.dma_cnt[q][slot] * 16
        ref = ("dma", q, slot, val)
        self.dma_last[q][slot] = ref
        self.ops[q].append({"fn": ("dma_start", (), {"out": out, "in_": in_}), "waits": waits,
                            "inc": False, "dma": (q, slot)})
        self._commit(ref, reads, writes)
        if is_output:
            self.out_dma_refs.append(ref)
        return ref

    def barrier(self):
        refs = []
        for e in self.ENG:
            for idx in range(len(self.ops[e]) - 1, -1, -1):
                o = self.ops[e][idx]
                if o["fn"] is not None and o["dma"] is None:
                    refs.append(("op", e, idx))
                    break
        for q in ("sp", "pool"):
            for slot in range(self.ndma_sems):
                if self.dma_last[q][slot] is not None:
                    refs.append(self.dma_last[q][slot])
        for e in self.ENG:
            waits = []
            for ref in refs:
                if ref[0] == "op" and ref[1] == e:
                    continue
                self._need(e, ref, waits)
            self.ops[e].append({"fn": None, "waits": waits, "inc": False, "dma": None})

    def finalize(self, block, sems, dma_sems):
        nc = self.nc
        marks = {}
        for e in self.ENG:
            c = 0
            m = []
            for o in self.ops[e]:
                if o["inc"] and o["fn"] is not None and o["dma"] is None:
                    c += 1
                m.append(c)
            marks[e] = m
            assert c <= SEM_LIM * len(sems[e]), (e, c)

        def sem_of(e, idx):
            m = marks[e][idx]
            assert m >= 1
            k = (m - 1) // SEM_LIM
            return sems[e][k], (m - 1) % SEM_LIM + 1

        engs = {"pe": nc.tensor, "act": nc.scalar, "dve": nc.vector, "pool": nc.gpsimd, "sp": nc.sync}

        def _pinfo(ap):
            fs = 1
            for s_ in list(ap.tensor.shape)[1:]:
                fs *= int(s_)
            p0 = int(ap.offset) // fs
            col = int(ap.offset) % fs
            return p0, int(ap.ap[0][1]), col, fs
        prev = None
        nviol = 0
        for o in self.ops["pe"]:
            if o["fn"] is None:
                continue
            name_, args_, kw_ = o["fn"]
            out_ap = args_[0] if args_ else kw_["out"]
            l_ap = kw_.get("lhsT", kw_.get("in_"))
            p0, kk, _, _ = _pinfo(l_ap)
            _, _, col, fs = _pinfo(out_ap)
            esz = 4 if fs in (512, 1024) and out_ap.tensor.name.startswith("pb") else 2
            bank_id = (out_ap.tensor.name, (col * esz) // 2048)
            rows = (p0, p0 + kk)
            cur = (rows, bank_id)
            if prev is not None and kk < 128 and (prev[0][1] - prev[0][0]) < 128:
                disjoint = rows[0] >= prev[0][1] or prev[0][0] >= rows[1]
                if disjoint and prev[1] == bank_id:
                    nviol += 1
            prev = cur
        assert nviol == 0, f"row-tile bank violations: {nviol}"

        def run(e, eng):
            for idx, o in enumerate(self.ops[e]):
                for ref in o["waits"]:
                    if ref[0] == "op":
                        s_, v_ = sem_of(ref[1], ref[2])
                        eng.wait_ge(s_, v_)
                    else:
                        eng.wait_ge(dma_sems[ref[1]][ref[2]], ref[3])
                if o["fn"] is None:
                    continue
                name_, args_, kw_ = o["fn"]
                ins = getattr(eng, name_)(*args_, **kw_)
                if o["dma"] is not None:
                    ins.then_inc(dma_sems[o["dma"][0]][o["dma"][1]], 16)
                elif o["inc"]:
                    s_, _ = sem_of(e, idx)
                    ins.then_inc(s_, 1)

        @block.tensor
        def _(eng):
            run("pe", eng)

        @block.scalar
        def _(eng):
            run("act", eng)

        @block.vector
        def _(eng):
            run("dve", eng)

        @block.gpsimd
        def _(eng):
            run("pool", eng)

        @block.sync
        def _(eng):
            run("sp", eng)


def _consts():
    c = {}
    i = np.arange(128)
    c["ident"] = np.eye(128, dtype=np.float32)
    c["tri"] = (i[:, None] >= i[None, :]).astype(np.float32)
    c["ones"] = np.ones((128, 128), np.float32)
    c["lmask"] = (i[:, None] < i[None, :]).astype(np.float32)
    rot = np.zeros((128, 128), np.float32)
    for p in range(128):
        if p % 64 < 32:
            rot[p + 32, p] = 1.0
        else:
            rot[p - 32, p] = 1.0
    c["rot"] = rot
    dsel = np.zeros((2, 128, 128), np.float32)
    for a in range(2):
        for p in range(128):
            dsel[a, a * 64 + (p % 64), p] = 1.0
    c["dsel"] = dsel
    c["dselrot"] = np.stack([dsel[a] @ rot for a in range(2)])
    half = 32
    inv = (10000.0 ** (-np.arange(half, dtype=np.float32) * np.float32(2.0 / 64))).astype(np.float32)
    pos = np.arange(S + TS, dtype=np.float32)
    ang = (pos[:, None] * inv[None, :]).astype(np.float32)
    cos, sin = np.cos(ang).astype(np.float32), np.sin(ang).astype(np.float32)
    pidx = np.arange(128) % 32
    sign = np.where((np.arange(128) % 64) < 32, -1.0, 1.0).astype(np.float32)
    c["cosT"] = np.ascontiguousarray(cos[:, pidx].T)
    c["sinT"] = np.ascontiguousarray((sin[:, pidx] * sign[None, :]).T)
    fidx = np.arange(256) % 32
    fsign = np.where((np.arange(256) % 64) < 32, -1.0, 1.0).astype(np.float32)
    c["cosTM"] = np.ascontiguousarray(cos[:, fidx])
    c["sinTM"] = np.ascontiguousarray(sin[:, fidx] * fsign[None, :])
    return c


def _bias_tiles(table):
    k = np.arange(128)[:, None]
    q = np.arange(128)[None, :]
    bp = np.zeros((8, 128, 640), np.float32)
    for slot in range(5):
        rel = (4 - slot) * 128 + (q - k)
        idx = np.clip(rel, -128, 128) + 128
        bp[:, :, slot * 128:(slot + 1) * 128] = table[:, idx]
    qs = PAST + np.arange(TS)[None, :]
    bs = np.zeros((8, 128, 160), np.float32)
    for slot in range(5):
        kpos = PAST - 512 + slot * 128 + np.arange(128)[:, None]
        idx = np.clip(qs - kpos, -128, 128) + 128
        bs[:, :, slot * 32:(slot + 1) * 32] = table[:, idx]
    return bp, bs


def build():
    nc = bass.Bass("TRN2", target_bir_lowering=False)
    P = Prog(nc)

    def din(name, shape):
        return nc.dram_tensor(name, list(shape), F32, kind="ExternalInput").ap()

    def dout(name, shape):
        return nc.dram_tensor(name, list(shape), F32, kind="ExternalOutput").ap()

    xp = din("xp", [2, S, D])
    xs = din("xs", [TS, D])
    ca_k = din("ca_k", [512, 512]); ca_v = din("ca_v", [512, 512])
    cb_k = din("cb_k", [PAST, 512]); cb_v = din("cb_v", [PAST, 512])
    cc_k = din("cc_k", [128, 256]); cc_v = din("cc_v", [128, 256])
    w_ab = din("w_ab", [8, D, 512])
    w_oab = din("w_oab", [D, D])
    w_c = din("w_c", [D, 2560])
    w_oc = din("w_oc", [D, D])
    gpre_ab = din("gpre_ab", [128, D]); gpost_ab = din("gpost_ab", [128, D])
    gpre_c = din("gpre_c", [128, D]); gpost_c = din("gpost_c", [128, D])
    biasP = din("biasP", [8, 128, 640]); biasS = din("biasS", [8, 128, 160])
    sinks = din("sinks", [128, 8])
    c_ident = din("c_ident", [128, 128]); c_tri = din("c_tri", [128, 128]); c_ones = din("c_ones", [128, 128])
    c_lmask = din("c_lmask", [128, 128]); c_rot = din("c_rot", [128, 128])
    c_dsel = din("c_dsel", [2, 128, 128]); c_dselrot = din("c_dselrot", [2, 128, 128])
    c_cosT = din("c_cosT", [128, S + TS]); c_sinT = din("c_sinT", [128, S + TS])
    c_cosTM = din("c_cosTM", [S + TS, 256]); c_sinTM = din("c_sinTM", [S + TS, 256])

    yp = dout("yp", [2, S, D]); ys = dout("ys", [TS, D])
    o_akp = dout("o_akp", [2, 512, 512]); o_avp = dout("o_avp", [2, 512, 512])
    o_bkp = dout("o_bkp", [2, S, 512]); o_bvp = dout("o_bvp", [2, S, 512])
    o_ckp = dout("o_ckp", [2, 128, 256]); o_cvp = dout("o_cvp", [2, 128, 256])
    o_aks = dout("o_aks", [TS, 512]); o_avs = dout("o_avs", [TS, 512])
    o_bks = dout("o_bks", [TS, 512]); o_bvs = dout("o_bvs", [TS, 512])
    o_cks = dout("o_cks", [TS, 256]); o_cvs = dout("o_cvs", [TS, 256])

    from contextlib import ExitStack
    es = ExitStack()

    def sb(name, shape, dt):
        return es.enter_context(nc.sbuf_tensor(name, list(shape), dt))

    def ps(name, shape, dt):
        return es.enter_context(nc.psum_tensor(name, list(shape), dt))

    with es:
        sems = {e: [es.enter_context(nc.semaphore(f"s_{e}{k}")) for k in range(2)] for e in Prog.ENG}
        dma_sems = {q: [es.enter_context(nc.semaphore(f"d_{q}{k}")) for k in range(P.ndma_sems)]
                    for q in ("sp", "pool")}

        ident = sb("ident", [128, 128], BF16); tri = sb("tri", [128, 128], BF16)
        ones = sb("ones", [128, 128], BF16); lmask = sb("lmask", [128, 128], BF16)
        rot = sb("rot", [128, 128], BF16)
        dsel = sb("dsel", [128, 2, 128], BF16); dselrot = sb("dselrot", [128, 2, 128], BF16)
        esink = sb("esink", [128, 8], F32)
        R_const = Res("const")
        for t_, src in ((ident, c_ident), (tri, c_tri), (ones, c_ones), (lmask, c_lmask), (rot, c_rot)):
            P.dma("pool", t_[:], src, writes=[R_const])
        for a in range(2):
            P.dma("pool", dsel[:, a, :], c_dsel[a], writes=[R_const])
            P.dma("pool", dselrot[:, a, :], c_dselrot[a], writes=[R_const])
        for t_, src in ((esink, sinks),):
            P.dma("sp", t_[:], src, writes=[R_const])
        P.op("act", lambda e: e.activation(out=esink[:], in_=esink[:], func=AF.Exp), reads=[R_const], writes=[R_const])

        pbank = [ps(f"pb{i}", [128, 1024], F32) for i in range(3)]
        ptps = [ps(f"ptp{i}", [128, 1024], BF16) for i in range(2)]
        ptp = ptps[0]
        R_bank = [Res(f"bank{i}", excl=True) for i in range(6)]
        R_tps = [Res(f"tp{i}", excl=True) for i in range(2)]
        R_tp = R_tps[0]

        def bank(i):
            return pbank[i // 2][:, (i % 2) * 512:(i % 2 + 1) * 512]

        oT = sb("oT", [128, 8, S], BF16)
        R_oT = [[Res(f"oT{c}_{g}") for g in range(4)] for c in range(8)]
        xst = [sb(f"xst{i}", [128, D], F32) for i in range(2)]
        R_xst = [Res(f"xst{i}") for i in range(2)]
        junk2 = [sb("junk2_0", [128, D], BF16)] * 2; R_junk2 = [Res()] * 2
        stat2 = [sb(f"stat2_{i}", [128, 2], F32) for i in range(2)]; R_stat2 = [Res() for _ in range(2)]
        xsb = [sb(f"xsb{i}", [128, D], BF16) for i in range(2)]
        R_xsb = [Res(f"xsb{i}") for i in range(2)]
        cnt = {"x": 0, "stg": 0, "pj": 0}

        seqs = [("p", 0), ("p", 1), ("s", 0)]

        def x_src(kind, si, t0, n):
            return xp[si, t0:t0 + n, :] if kind == "p" else xs[t0:t0 + n, :]

        def rstd_from(ss_ap, out_ap, n, reads, writes):
            P.op("act", lambda e: e.activation(out=out_ap, in_=ss_ap, func=AF.Ln, scale=1.0 / D, bias=EPS),
                 reads=reads, writes=writes)
            P.op("act", lambda e: e.activation(out=out_ap, in_=out_ap, func=AF.Exp, scale=-0.5),
                 reads=writes, writes=writes)

        def norm_part1(src_ap, src_res, ts, gain, gain_res=None):
            gain_res = gain_res or R_const
            k = cnt["x"]; cnt["x"] += 1
            xb = xsb[k % 2]; Rxb = R_xsb[k % 2]
            st = stat2[k % 2]; Rst = R_stat2[k % 2]
            jk = junk2[k % 2]; Rjk = R_junk2[k % 2]
            P.op("act", lambda e: e.activation(out=jk[:ts, :], in_=src_ap, func=AF.Square,
                                               accum_out=st[:ts, 0:1]),
                 reads=[src_res], writes=[Rjk, Rst])
            rstd_from(st[:ts, 0:1], st[:ts, 1:2], ts, [Rst], [Rst])
            P.op("dve", lambda e: e.scalar_tensor_tensor(out=xb[:ts, :], in0=src_ap, scalar=st[:ts, 1:2],
                                                         in1=gain[:ts, :], op0=ALU.mult, op1=ALU.mult),
                 reads=[src_res, Rst, gain_res], writes=[Rxb])
            return k

        def norm_part2(k, ts, dst_ap3, dst_res):
            xb = xsb[k % 2]; Rxb = R_xsb[k % 2]
            tp = ptps[k % 2]; Rtp = R_tps[k % 2]
            for c in range(8):
                P.op("pe", lambda e, c=c: e.transpose(out=tp[:, c * ts:(c + 1) * ts],
                                                      in_=xb[:ts, c * 128:(c + 1) * 128], identity=ident[:ts, :ts]),
                     reads=[Rxb, R_const], writes=[Rtp])
            P.op("dve", lambda e: e.tensor_copy(out=dst_ap3,
                                                in_=tp[:, 0:8 * ts].rearrange("p (c t) -> p c t", c=8)),
                 reads=[Rtp], writes=[dst_res])

        def norm_transpose(src_ap, src_res, ts, gain, dst_ap3, dst_res, gain_res=None):
            k = norm_part1(src_ap, src_res, ts, gain, gain_res)
            norm_part2(k, ts, dst_ap3, dst_res)

        for (kind, si) in seqs:
            T = S if kind == "p" else TS
            ts = min(T, 128)
            nt = T // ts
            gs = min(T, 512)
            ng = T // gs
            tpg = gs // ts
            pos0 = 0 if kind == "p" else PAST

            with ExitStack() as l0:
                def sb0(name, shape, dt):
                    return l0.enter_context(nc.sbuf_tensor(f"{name}_{kind}{si}", list(shape), dt))

                def mk0(name, shape, dt, n):
                    return [sb0(f"{name}{i}", shape, dt) for i in range(n)]

                gpreab = sb0("gpreab", [128, D], F32); R_gab = Res()
                P.dma("sp", gpreab[:], gpre_ab, writes=[R_gab])
                xnT = sb0("xnT", [128, 8, T], BF16)
                R_xnT = [Res(f"xnT{g}") for g in range(ng)]
                wring = mk0("wr", [128, 8, 512], BF16, 2)
                R_wring = [Res(f"wr{i}") for i in range(2)]
                qTs = mk0("qT", [128, T], BF16, 2); kTs = mk0("kT", [128, T], BF16, 2); gTs = mk0("gT", [128, T], BF16, 2)
                merged = (kind == "p")
                if merged:
                    Vs = mk0("V", [128, nt, 2, 128], BF16, 2)
                else:
                    Vs = mk0("V", [128, nt, 128], BF16, 2)
                R_qs = [[Res() for _ in range(ng)] for _ in range(2)]
                R_ks = [[Res() for _ in range(ng)] for _ in range(2)]
                R_gs = [[Res() for _ in range(ng)] for _ in range(2)]
                R_Vs = [[Res() for _ in range(nt)] for _ in range(2)]
                stg = mk0("stg", [128, 256], F32, 2); R_stg = [Res() for _ in range(2)]
                biasT = sb0("biasT", [128, 8, 640], F32); R_bias = Res("bias")
                Ssb = mk0("Ssb", [128, 640], F32, 2); R_Ssb = [Res() for _ in range(2)]
                PT = mk0("PT", [128, 640], BF16, 2); R_PT = [Res() for _ in range(2)]
                rec2 = mk0("rec", [128, 128], F32, 2); R_rec2 = [Res() for _ in range(2)]
                tmpo2 = mk0("tmpo", [128, 128], F32, 2); R_tmpo2 = [Res() for _ in range(2)]
                EH = [mk0(f"E{h}_", [128, 512], F32, 2) for h in range(2)]; R_EH = [[Res() for _ in range(2)] for _ in range(2)]
                SPH = [mk0(f"SP{h}_", [128, 512], BF16, 2) for h in range(2)]; R_SPH = [[Res() for _ in range(2)] for _ in range(2)]
                ATH = [mk0(f"AT{h}_", [128, 512], BF16, 2) for h in range(2)]; R_ATH = [[Res() for _ in range(2)] for _ in range(2)]
                sgt = sb0("sgt", [128, 512], F32); R_sgt = Res()
                SaccH = mk0("Sacc", [128, 512], F32, 2); R_SaccH = [Res() for _ in range(2)]
                SaccBH = [mk0(f"SaccB{h}_", [128, 512], BF16, 2) for h in range(2)]; R_SaccBH = [[Res() for _ in range(2)] for _ in range(2)]
                nqT = mk0("nq", [128, 512], BF16, 2); R_nq = [Res() for _ in range(2)]
                if kind == "s":
                    kcache = sb0("kcache", [128, 16, 128], BF16); R_kc = Res()
                    kTcs = mk0("kTc", [128, PAST], BF16, 2); R_kTcs = [Res() for _ in range(2)]
                    Vcs = mk0("Vc", [128, 16, 128], BF16, 2); R_Vcs = [Res() for _ in range(2)]

                def load_w(pi):
                    P.dma("pool", wring[pi % 2][:], w_ab[pi].rearrange("(c p) n -> p c n", p=128),
                          writes=[R_wring[pi % 2]])

                load_w(0)
                load_w(1)
                if merged:
                    for s_ in range(2):
                        for ti_ in range(nt):
                            P.op("pool", lambda e, s_=s_, ti_=ti_: e.memset(Vs[s_][:, ti_, :, :], 1.0), writes=[R_Vs[s_][ti_]])
                if kind == "p":
                    for h in range(8):
                        P.dma("sp", biasT[:, h, :], biasP[h], writes=[R_bias])
                    for h in range(8):
                        P.op("pool", lambda e, h=h: e.memset(biasT[0:64, h, 64:128], NEG), writes=[R_bias])
                        P.op("pool", lambda e, h=h: e.memset(biasT[64:128, h, 512:576], NEG), writes=[R_bias])
                else:
                    for h in range(8):
                        P.dma("sp", biasT[:, h, 0:160], biasS[h], writes=[R_bias])

                p0flags = {}

                def gen_phase0():
                    for ti in range(nt):
                        k = cnt["x"]
                        xt = xst[k % 2]; Rxt = R_xst[k % 2]
                        P.dma("sp", xt[:ts, :], x_src(kind, si, ti * ts, ts), writes=[Rxt])
                        g = ti // tpg
                        norm_transpose(xt[:ts, :], Rxt, ts, gpreab, xnT[:, :, ti * ts:(ti + 1) * ts], R_xnT[g],
                                       gain_res=R_gab)
                        if (ti + 1) % tpg == 0:
                            p0flags[g] = True
                        yield


                PJB = 5

                def gen_proj(pi):
                    isA = pi < 4
                    hp = pi % 4
                    st = pi % 2
                    W = wring[st]; RW = R_wring[st]
                    qT, kT, gT, V = qTs[st], kTs[st], gTs[st], Vs[st]
                    R_q, R_k, R_g, R_V = R_qs[st], R_ks[st], R_gs[st], R_Vs[st]
                    if kind == "s":
                        kTc, R_kTc, Vc, R_Vc = kTcs[st], R_kTcs[st], Vcs[st], R_Vcs[st]
                        csrc_k, csrc_v, nck = (ca_k, ca_v, 4) if isA else (cb_k, cb_v, 16)
                        if pi == 0:
                            P.dma("pool", kcache[:, 0:nck, :],
                                  csrc_k[:, hp * 128:(hp + 1) * 128].rearrange("(t p) f -> p t f", p=128), writes=[R_kc])
                        P.dma("pool", Vc[:, 0:nck, :],
                              csrc_v[:, hp * 128:(hp + 1) * 128].rearrange("(t p) f -> p t f", p=128), writes=[R_Vc])
                        for t0 in range(0, nck, 8):
                            nb_ = min(8, nck - t0)
                            for t_ in range(nb_):
                                P.op("pe", lambda e, t_=t_: e.transpose(
                                    out=ptp[:, t_ * 128:(t_ + 1) * 128], in_=kcache[:, t0 + t_, :], identity=ident[:, :]),
                                    reads=[R_kc, R_const], writes=[R_tp])
                            P.op("dve", lambda e: e.tensor_copy(
                                out=kTc[:, t0 * 128:(t0 + nb_) * 128], in_=ptp[:, 0:nb_ * 128]),
                                reads=[R_tp], writes=[R_kTc])
                            yield
                        if pi + 1 < 8:
                            pn = pi + 1
                            nsrc, nn = (ca_k, 4) if pn < 4 else (cb_k, 16)
                            P.dma("pool", kcache[:, 0:nn, :],
                                  nsrc[:, (pn % 4) * 128:(pn % 4 + 1) * 128].rearrange("(t p) f -> p t f", p=128),
                                  writes=[R_kc])
                    pjbanks = [5, 4] if pi == 0 else [5]
                    pjc = [0]

                    def nextpj():
                        b_ = pjbanks[pjc[0] % len(pjbanks)]; pjc[0] += 1
                        return b_
                    for g in range(ng):
                        t0 = g * gs
                        while pi == 0 and not p0flags.get(g):
                            yield
                        for (fc, kindf) in ((0, "q"), (2, "k"), (1, "g")):
                            bi = nextpj()
                            for kc in range(8):
                                P.op("pe", lambda e, kc=kc: e.matmul(
                                    bank(bi)[:, 0:gs], lhsT=W[:, kc, fc * 128:(fc + 1) * 128],
                                    rhs=xnT[:, kc, t0:t0 + gs], start=(kc == 0), stop=(kc == 7)),
                                    reads=[RW, R_xnT[g]], writes=[R_bank[bi]])
                            if kindf == "q":
                                P.op("dve", lambda e: e.tensor_scalar(
                                    out=qT[:, t0:t0 + gs], in0=bank(bi)[:, 0:gs], scalar1=0.125, scalar2=None,
                                    op0=ALU.mult), reads=[R_bank[bi]], writes=[R_q[g]])
                            elif kindf == "k":
                                P.op("dve", lambda e: e.tensor_copy(
                                    out=kT[:, t0:t0 + gs], in_=bank(bi)[:, 0:gs]), reads=[R_bank[bi]], writes=[R_k[g]])
                            else:
                                P.op("act", lambda e: e.activation(out=sgt[:, 0:gs], in_=bank(bi)[:, 0:gs], func=AF.Exp,
                                                                   scale=-1.0), reads=[R_bank[bi]], writes=[R_sgt])
                                P.op("act", lambda e: e.activation(out=sgt[:, 0:gs], in_=sgt[:, 0:gs], func=AF.Ln, bias=1.0),
                                     reads=[R_sgt], writes=[R_sgt])
                                P.op("act", lambda e: e.activation(out=sgt[:, 0:gs], in_=sgt[:, 0:gs], func=AF.Exp,
                                                                   scale=-1.0), reads=[R_sgt], writes=[R_sgt])
                                P.op("dve", lambda e: e.tensor_tensor(out=gT[:, t0:t0 + gs], in0=bank(bi)[:, 0:gs],
                                                                      in1=sgt[:, 0:gs], op=ALU.mult),
                                     reads=[R_bank[bi], R_sgt], writes=[R_g[g]])
                            yield
                        for tt in range(tpg):
                            ti = g * tpg + tt
                            bi = nextpj()
                            for kc in range(8):
                                P.op("pe", lambda e, kc=kc: e.matmul(
                                    bank(bi)[:ts, 0:256], lhsT=xnT[:, kc, ti * ts:(ti + 1) * ts],
                                    rhs=W[:, kc, 256:512], start=(kc == 0), stop=(kc == 7)),
                                    reads=[RW, R_xnT[g]], writes=[R_bank[bi]])
                            if merged:
                                P.op("dve", lambda e: e.tensor_copy(
                                    out=V[:ts, ti, 0, 0:64], in_=bank(bi)[:ts, 128:192]), reads=[R_bank[bi]], writes=[R_V[ti]])
                                P.op("dve", lambda e: e.tensor_copy(
                                    out=V[:ts, ti, 1, 64:128], in_=bank(bi)[:ts, 192:256]), reads=[R_bank[bi]], writes=[R_V[ti]])
                            else:
                                P.op("dve", lambda e: e.tensor_copy(
                                    out=V[:ts, ti, :], in_=bank(bi)[:ts, 128:256]), reads=[R_bank[bi]], writes=[R_V[ti]])
                            if kind == "p":
                                if isA:
                                    need = ti * ts >= S - 512
                                    dk, dv, r0 = o_akp, o_avp, ti * ts - (S - 512)
                                else:
                                    need = True
                                    dk, dv, r0 = o_bkp, o_bvp, ti * ts
                                dk_ap = dk[si, r0:r0 + ts, hp * 128:(hp + 1) * 128] if need else None
                                dv_ap = dv[si, r0:r0 + ts, hp * 128:(hp + 1) * 128] if need else None
                            else:
                                need = True
                                dk, dv = (o_aks, o_avs) if isA else (o_bks, o_bvs)
                                dk_ap = dk[0:ts, hp * 128:(hp + 1) * 128]
                                dv_ap = dv[0:ts, hp * 128:(hp + 1) * 128]
                            if need:
                                sk = cnt["stg"] % 2; cnt["stg"] += 1
                                P.op("dve", lambda e: e.tensor_copy(
                                    out=stg[sk][:ts, :], in_=bank(bi)[:ts, 0:256]),
                                    reads=[R_bank[bi]], writes=[R_stg[sk]])
                                P.dma("pool", dk_ap, stg[sk][:ts, 0:128], reads=[R_stg[sk]], is_output=True)
                                P.dma("pool", dv_ap, stg[sk][:ts, 128:256], reads=[R_stg[sk]], is_output=True)
                            yield

                def gen_attnA(pi):
                    hp = pi % 4
                    st = pi % 2
                    qT, kT, gT, V = qTs[st], kTs[st], gTs[st], Vs[st]
                    R_q, R_k, R_g, R_V = R_qs[st], R_ks[st], R_gs[st], R_Vs[st]
                    if kind == "s":
                        kTc, R_kTc, Vc, R_Vc = kTcs[st], R_kTcs[st], Vcs[st], R_Vcs[st]
                    nqb = T // ts
                    qw = ts
                    its = [(j, hh) for j in range(nqb) for hh in range(2)]

                    def a_blocks(j, hh):
                        po = hh * 64
                        blocks = []
                        if kind == "p":
                            for slot in range(5):
                                kb = j - 4 + slot
                                if kb < 0:
                                    continue
                                blocks.append((kT[po:po + 64, kb * 128:(kb + 1) * 128], 128,
                                               V[:, kb, hh, :], slot,
                                               [R_k[kb // 4]], [R_V[kb]]))
                        else:
                            for slot in range(4):
                                blocks.append((kTc[po:po + 64, slot * 128:(slot + 1) * 128], 128,
                                               Vc[:, slot, hh * 64:(hh + 1) * 64], slot, [R_kTc], [R_Vc]))
                            blocks.append((kT[po:po + 64, 0:TS], TS, V[0:TS, 0, hh * 64:(hh + 1) * 64], 4,
                                           [R_k[0]], [R_V[0]]))
                        return blocks

                    def a_stage1(n):
                        j, hh = its[n]
                        h = hp * 2 + hh
                        po = hh * 64
                        sbk = n % 2
                        RS = [R_bank[2 * sbk], R_bank[2 * sbk + 1]]
                        Sps = pbank[sbk]
                        blocks = a_blocks(j, hh)
                        q_ap = qT[po:po + 64, j * qw:(j + 1) * qw]
                        gq = (j * qw) // gs
                        for (k_ap, nk, v_ap, slot, rk, rv) in blocks:
                            P.op("pe", lambda e, k_ap=k_ap, nk=nk, slot=slot: e.matmul(
                                Sps[:nk, slot * qw:(slot + 1) * qw], lhsT=k_ap, rhs=q_ap, start=True, stop=True),
                                reads=rk + [R_q[gq]], writes=RS)
                        s0 = blocks[0][3]
                        full = [b for b in blocks if b[1] == 128]
                        part = [b for b in blocks if b[1] != 128]
                        sk = n % 2
                        lo, hi = s0 * qw, (full[-1][3] + 1) * qw
                        P.op("dve", lambda e: e.tensor_tensor(
                            out=Ssb[sk][:, lo:hi], in0=Sps[:, lo:hi], in1=biasT[:, h, lo:hi], op=ALU.add),
                            reads=RS + [R_bias], writes=[R_Ssb[sk]])
                        P.op("act", lambda e: e.activation(
                            out=PT[sk][:, lo:hi], in_=Ssb[sk][:, lo:hi], func=AF.Exp),
                            reads=[R_Ssb[sk]], writes=[R_PT[sk]])
                        for (k_ap, nk, v_ap, slot, rk, rv) in part:
                            lo2, hi2 = slot * qw, (slot + 1) * qw
                            P.op("dve", lambda e, lo2=lo2, hi2=hi2, nk=nk: e.tensor_tensor(
                                out=Ssb[sk][:nk, lo2:hi2], in0=Sps[:nk, lo2:hi2], in1=biasT[:nk, h, lo2:hi2],
                                op=ALU.add), reads=RS + [R_bias], writes=[R_Ssb[sk]])
                            P.op("act", lambda e, lo2=lo2, hi2=hi2, nk=nk: e.activation(
                                out=PT[sk][:nk, lo2:hi2], in_=Ssb[sk][:nk, lo2:hi2], func=AF.Exp),
                                reads=[R_Ssb[sk]], writes=[R_PT[sk]])

                    def a_stage2m(n):
                        j, hh = its[n]
                        po = hh * 64
                        pd = 64 - po
                        sk = n % 2
                        odb = 4
                        OD = bank(4)[:, (n % 2) * 128:(n % 2) * 128 + 128]
                        blocks = a_blocks(j, hh)
                        nb = len(blocks)
                        gq = (j * qw) // gs
                        for bi_, (k_ap, nk, v_ap, slot, rk, rv) in enumerate(blocks):
                            P.op("pe", lambda e, v_ap=v_ap, nk=nk, slot=slot, bi_=bi_: e.matmul(
                                OD[:, 0:qw], lhsT=v_ap, rhs=PT[sk][:nk, slot * qw:(slot + 1) * qw],
                                start=(bi_ == 0), stop=(bi_ == nb - 1)),
                                reads=rv + [R_PT[sk]], writes=[R_bank[odb]])
                        rc = rec2[hh]; Rrc = R_rec2[hh]
                        tm = tmpo2[hh]; Rtm = R_tmpo2[hh]
                        P.op("act", lambda e: e.activation(
                            out=rc[po:po + 64, 0:qw], in_=OD[pd:pd + 64, 0:qw], func=AF.Ln),
                            reads=[R_bank[odb]], writes=[Rrc])
                        P.op("act", lambda e: e.activation(
                            out=rc[po:po + 64, 0:qw], in_=rc[po:po + 64, 0:qw], func=AF.Exp, scale=-1.0),
                            reads=[Rrc], writes=[Rrc])
                        P.op("dve", lambda e: e.tensor_tensor(
                            out=tm[po:po + 64, 0:qw], in0=OD[po:po + 64, 0:qw], in1=rc[po:po + 64, 0:qw],
                            op=ALU.mult), reads=[R_bank[odb], Rrc], writes=[Rtm])
                        P.op("pool", lambda e: e.tensor_tensor(
                            out=oT[po:po + 64, pi, j * qw:(j + 1) * qw], in0=tm[po:po + 64, 0:qw],
                            in1=gT[po:po + 64, j * qw:(j + 1) * qw], op=ALU.mult),
                            reads=[Rtm, R_g[gq]], writes=[R_oT[pi][gq]])

                    def a_stage2(n):
                        if merged:
                            return a_stage2m(n)
                        j, hh = its[n]
                        po = hh * 64
                        sk = n % 2
                        odb = 4
                        OD = bank(odb)
                        blocks = a_blocks(j, hh)
                        nb = len(blocks)
                        gq = (j * qw) // gs
                        for bi_, (k_ap, nk, v_ap, slot, rk, rv) in enumerate(blocks):
                            P.op("pe", lambda e, v_ap=v_ap, nk=nk, slot=slot, bi_=bi_: e.matmul(
                                OD[po:po + 64, 0:qw], lhsT=v_ap, rhs=PT[sk][:nk, slot * qw:(slot + 1) * qw],
                                start=(bi_ == 0), stop=(bi_ == nb - 1)),
                                reads=rv + [R_PT[sk]], writes=[R_bank[odb]])
                        for bi_, (k_ap, nk, v_ap, slot, rk, rv) in enumerate(blocks):
                            P.op("pe", lambda e, nk=nk, slot=slot, bi_=bi_: e.matmul(
                                OD[po:po + 64, 128:128 + qw], lhsT=ones[:nk, 0:64],
                                rhs=PT[sk][:nk, slot * qw:(slot + 1) * qw],
                                start=(bi_ == 0), stop=(bi_ == nb - 1)),
                                reads=[R_PT[sk], R_const], writes=[R_bank[odb]])
                        rc = rec2[hh]; Rrc = R_rec2[hh]
                        tm = tmpo2[hh]; Rtm = R_tmpo2[hh]
                        P.op("act", lambda e: e.activation(
                            out=rc[po:po + 64, 0:qw], in_=OD[po:po + 64, 128:128 + qw], func=AF.Ln),
                            reads=[R_bank[odb]], writes=[Rrc])
                        P.op("act", lambda e: e.activation(
                            out=rc[po:po + 64, 0:qw], in_=rc[po:po + 64, 0:qw], func=AF.Exp, scale=-1.0),
                            reads=[Rrc], writes=[Rrc])
                        P.op("dve", lambda e: e.tensor_tensor(
                            out=tm[po:po + 64, 0:qw], in0=OD[po:po + 64, 0:qw], in1=rc[po:po + 64, 0:qw],
                            op=ALU.mult), reads=[R_bank[odb], Rrc], writes=[Rtm])
                        P.op("pool", lambda e: e.tensor_tensor(
                            out=oT[po:po + 64, pi, j * qw:(j + 1) * qw], in0=tm[po:po + 64, 0:qw],
                            in1=gT[po:po + 64, j * qw:(j + 1) * qw], op=ALU.mult),
                            reads=[Rtm, R_g[gq]], writes=[R_oT[pi][gq]])

                    a_stage1(0)
                    yield
                    for n in range(len(its)):
                        if n + 1 < len(its):
                            a_stage1(n + 1)
                            yield
                        a_stage2(n)
                        yield

                def gen_attnB(pi):
                    st = pi % 2
                    qT, kT, gT, V = qTs[st], kTs[st], gTs[st], Vs[st]
                    R_q, R_k, R_g, R_V = R_qs[st], R_ks[st], R_gs[st], R_Vs[st]
                    if kind == "s":
                        kTc, R_kTc, Vc, R_Vc = kTcs[st], R_kTcs[st], Vcs[st], R_Vcs[st]
                    cw = gs
                    ob = 4
                    OB = bank(ob)
                    dq = min(128, cw)
                    for c in range(ng):
                        steps = [[], []]
                        for hh in range(2):
                            po = hh * 64
                            P.op("dve", lambda e: e.tensor_scalar(
                                out=nqT[hh][po:po + 64, 0:cw], in0=qT[po:po + 64, c * cw:(c + 1) * cw],
                                scalar1=-1.0, scalar2=None, op0=ALU.mult), reads=[R_q[c]], writes=[R_nq[hh]])
                            P.op("pool", lambda e: e.memset(SaccH[hh][:, 0:cw], 0.0), writes=[R_SaccH[hh]])
                            P.op("pool", lambda e: e.memset(SaccBH[hh][0][:, 0:cw], 0.0), writes=[R_SaccBH[hh][0]])
                            if kind == "p":
                                for kb in range(4 * c + 3, -1, -1):
                                    q0 = max(0, kb * 128 - c * 512)
                                    steps[hh].append((kT[po:po + 64, kb * 128:(kb + 1) * 128], 128,
                                                      V[:, kb, hh, hh * 64:(hh + 1) * 64], q0, kb >= 4 * c,
                                                      [R_k[kb // 4]], [R_V[kb]]))
                            else:
                                steps[hh].append((kT[po:po + 64, 0:TS], TS, V[0:TS, 0, hh * 64:(hh + 1) * 64], 0, True,
                                                  [R_k[0]], [R_V[0]]))
                                for kb in range(15, -1, -1):
                                    steps[hh].append((kTc[po:po + 64, kb * 128:(kb + 1) * 128], 128,
                                                      Vc[:, kb, hh * 64:(hh + 1) * 64], 0, False, [R_kTc], [R_Vc]))
                        ns = len(steps[0])

                        def stage1(i):
                            for hh in range(2):
                                po = hh * 64
                                k_ap, nk, v_ap, q0, diag, rk, rv = steps[hh][i]
                                z = bank(hh); Rz = R_bank[hh]
                                P.op("pe", lambda e: e.matmul(z[:nk, q0:cw], lhsT=k_ap,
                                                              rhs=qT[po:po + 64, c * cw + q0:(c + 1) * cw],
                                                              start=True, stop=True),
                                     reads=rk + [R_q[c]], writes=[Rz])
                            for hh in range(2):
                                k_ap, nk, v_ap, q0, diag, rk, rv = steps[hh][i]
                                z = bank(hh); Rz = R_bank[hh]
                                E = EH[hh][i % 2]; RE = R_EH[hh][i % 2]
                                P.op("act", lambda e: e.activation(out=E[:nk, q0:cw], in_=z[:nk, q0:cw], func=AF.Exp),
                                     reads=[Rz], writes=[RE])
                                if diag:
                                    P.op("dve", lambda e: e.tensor_tensor(
                                        out=E[:nk, q0:q0 + dq], in0=E[:nk, q0:q0 + dq], in1=lmask[:nk, 0:dq],
                                        op=ALU.mult), reads=[RE, R_const], writes=[RE])
                            for hh in range(2):
                                k_ap, nk, v_ap, q0, diag, rk, rv = steps[hh][i]
                                E = EH[hh][i % 2]; RE = R_EH[hh][i % 2]
                                SP = SPH[hh][i % 2]; RSP = R_SPH[hh][i % 2]
                                P.op("act", lambda e: e.activation(out=SP[:nk, q0:cw], in_=E[:nk, q0:cw], func=AF.Ln,
                                                                   bias=1.0), reads=[RE], writes=[RSP])

                        def stage2(i):
                            for hh in range(2):
                                k_ap, nk, v_ap, q0, diag, rk, rv = steps[hh][i]
                                cps = bank(2 + hh); Rc = R_bank[2 + hh]
                                SP = SPH[hh][i % 2]; RSP = R_SPH[hh][i % 2]
                                P.op("pe", lambda e: e.matmul(cps[:nk, q0:cw], lhsT=tri[:nk, :nk], rhs=SP[:nk, q0:cw],
                                                              start=True, stop=False),
                                     reads=[RSP, R_const], writes=[Rc])
                                P.op("pe", lambda e: e.matmul(cps[:nk, q0:cw], lhsT=ones[:, :nk],
                                                              rhs=SaccBH[hh][i % 2][:, q0:cw], start=False, stop=False),
                                     reads=[R_SaccBH[hh][i % 2], R_const], writes=[Rc])
                            for hh in range(2):
                                po = hh * 64
                                k_ap, nk, v_ap, q0, diag, rk, rv = steps[hh][i]
                                cps = bank(2 + hh); Rc = R_bank[2 + hh]
                                P.op("pe", lambda e: e.matmul(cps[:nk, q0:cw], lhsT=k_ap,
                                                              rhs=nqT[hh][po:po + 64, q0:cw], start=False, stop=True),
                                     reads=rk + [R_nq[hh]], writes=[Rc])
                            if i + 1 < ns:
                                for hh in range(2):
                                    k_ap, nk, v_ap, q0, diag, rk, rv = steps[hh][i]
                                    SP = SPH[hh][i % 2]; RSP = R_SPH[hh][i % 2]
                                    P.op("dve", lambda e: e.tensor_tensor(
                                        out=SaccH[hh][:nk, q0:cw], in0=SaccH[hh][:nk, q0:cw], in1=SP[:nk, q0:cw], op=ALU.add),
                                        reads=[RSP, R_SaccH[hh]], writes=[R_SaccH[hh]])
                                    P.op("dve", lambda e: e.tensor_copy(out=SaccBH[hh][(i + 1) % 2][:, 0:cw],
                                                                        in_=SaccH[hh][:, 0:cw]),
                                         reads=[R_SaccH[hh]], writes=[R_SaccBH[hh][(i + 1) % 2]])
                            for hh in range(2):
                                k_ap, nk, v_ap, q0, diag, rk, rv = steps[hh][i]
                                cps = bank(2 + hh); Rc = R_bank[2 + hh]
                                AT = ATH[hh][i % 2]; RAT = R_ATH[hh][i % 2]
                                P.op("act", lambda e: e.activation(out=AT[:nk, q0:cw], in_=cps[:nk, q0:cw], func=AF.Exp,
                                                                   scale=-1.0), reads=[Rc], writes=[RAT])
                                if q0 > 0:
                                    P.op("pool", lambda e: e.memset(AT[:nk, 0:q0], 0.0), writes=[RAT])
                                if diag:
                                    P.op("dve", lambda e: e.tensor_tensor(
                                        out=AT[:nk, q0:q0 + dq], in0=AT[:nk, q0:q0 + dq], in1=lmask[:nk, 0:dq],
                                        op=ALU.mult), reads=[RAT, R_const], writes=[RAT])

                        def stage3(i):
                            for hh in range(2):
                                po = hh * 64
                                k_ap, nk, v_ap, q0, diag, rk, rv = steps[hh][i]
                                AT = ATH[hh][i % 2]; RAT = R_ATH[hh][i % 2]
                                P.op("pe", lambda e: e.matmul(OB[po:po + 64, 0:cw], lhsT=v_ap, rhs=AT[:nk, 0:cw],
                                                              start=(i == 0), stop=(i == ns - 1)),
                                     reads=rv + [RAT], writes=[R_bank[ob]])

                        stage1(0)
                        yield
                        for i in range(ns):
                            if i + 1 < ns:
                                stage1(i + 1)
                            stage2(i)
                            if i > 0:
                                stage3(i - 1)
                            yield
                        stage3(ns - 1)
                        P.op("dve", lambda e: e.tensor_tensor(
                            out=oT[:, pi, c * cw:(c + 1) * cw], in0=OB[:, 0:cw],
                            in1=gT[:, c * cw:(c + 1) * cw], op=ALU.mult),
                            reads=[R_bank[ob], R_g[c]], writes=[R_oT[pi][c]])
                        yield

                def run_weighted(ga, na, gb, nb_):
                    da = db = 0
                    a_alive, b_alive = True, gb is not None
                    while a_alive or b_alive:
                        pick_b = b_alive and (not a_alive or (db + 1) * na <= (da + 1) * nb_)
                        if pick_b:
                            try:
                                next(gb); db += 1
                            except StopIteration:
                                b_alive = False
                        else:
                            try:
                                next(ga); da += 1
                            except StopIteration:
                                a_alive = False

                n_proj = ng * (3 + tpg) + (3 if kind == "s" else 0)
                gp0, gj0 = gen_phase0(), gen_proj(0)
                alive = [gp0, gj0]
                while alive:
                    for s_ in list(alive):
                        try:
                            next(s_)
                        except StopIteration:
                            alive.remove(s_)
                for pi in range(8):
                    if pi + 2 < 8:
                        load_w(pi + 2)
                    isA = pi < 4
                    if isA:
                        ga = gen_attnA(pi); na = 2 * (T // ts) * 2
                    else:
                        ga = gen_attnB(pi)
                        na = sum((4 * c + 4 + 2) for c in range(ng)) if kind == "p" else 19
                    gb = gen_proj(pi + 1) if pi + 1 < 8 else None
                    run_weighted(ga, na, gb, n_proj)
            P.barrier()

            with ExitStack() as l1:
                def sb1(name, shape, dt):
                    return l1.enter_context(nc.sbuf_tensor(f"{name}_{kind}{si}", list(shape), dt))

                gs1 = min(T, 256)
                ng1 = T // gs1
                tpg1 = gs1 // ts
                qw = ts
                woab = sb1("woab", [128, 8, D], BF16); R_woab = Res()
                wc = sb1("wc", [128, 8, 2560], BF16); R_wc = Res()
                woc = sb1("woc", [128, 8, D], BF16); R_woc = Res()
                gpostab = sb1("gpostab", [128, D], F32); gprec = sb1("gprec", [128, D], F32)
                gpostc = sb1("gpostc", [128, D], F32); R_gn = Res()
                for t_, src in ((gpostab, gpost_ab), (gprec, gpre_c), (gpostc, gpost_c)):
                    P.dma("sp", t_[:], src, writes=[R_gn])
                P.dma("pool", woab[:], w_oab.rearrange("(c p) n -> p c n", p=128), writes=[R_woab])
                for q4 in range(4):
                    P.dma("pool", wc[:, :, q4 * 640:(q4 + 1) * 640],
                          w_c[:, q4 * 640:(q4 + 1) * 640].rearrange("(c p) n -> p c n", p=128), writes=[R_wc])
                P.dma("pool", woc[:], w_oc.rearrange("(c p) n -> p c n", p=128), writes=[R_woc])

                def mk(name, shape, dt, n):
                    return [sb1(f"{name}{i}", shape, dt) for i in range(n)], [Res() for _ in range(n)]

                Y0, R_Y0 = mk("Y0", [128, tpg1, D], F32, 2)
                t1b, R_t1 = mk("t1b", [128, D], F32, 2)
                statY, R_statY = mk("statY", [128, 4], F32, 2)
                xn1T, R_xn1 = mk("xn1T", [128, 8, gs1], BF16, 1)
                xn1T, R_xn1 = xn1T * 2, R_xn1 * 2
                qbf, R_qbf = mk("qbf", [128, gs1], BF16, 2)
                qr, R_qr = mk("qr", [128, 8, gs1], BF16, 2)
                kbf, R_kbf = mk("kbf", [128, 2, gs1], BF16, 1)
                kr, R_kr = mk("kr", [128, 4, 128 + gs1], BF16, 2)
                g1, R_g1 = mk("g1", [128, 8, gs1], BF16, 2)
                V1, R_V1 = mk("V1", [128, 1 + tpg1, 256], BF16, 2)
                cosg, R_cos = mk("cosg", [128, gs1], F32, 1)
                sing, R_sin = mk("sing", [128, gs1], F32, 1)
                cosg, R_cos, sing, R_sin = cosg * 2, R_cos * 2, sing * 2, R_sin * 2
                ta, R_ta = mk("ta", [128, 256], F32, 2)
                tb, R_tb = mk("tb", [128, 256], F32, 2)
                tcb, R_tcb = mk("tcb", [128, 256], F32, 2)
                PTc, R_PTc = mk("PTc", [128, 512], BF16, 4)
                recc, R_recc = mk("recc", [128, 256], F32, 1)
                tmpc, R_tmpc = mk("tmpc", [128, 256], F32, 1)
                recc, R_recc, tmpc, R_tmpc = recc * 2, R_recc * 2, tmpc * 2, R_tmpc * 2
                kst = sb1("kst", [128, 256], F32); R_kst = Res()
                ksw = sb1("ksw", [128, 256], F32); R_ksw = Res()
                ctm = sb1("ctm", [128, 256], F32); stm = sb1("stm", [128, 256], F32); R_ctm = Res()
                kvst = sb1("kvst", [128, 512], F32); R_kvst = Res()
                if kind == "s":
                    kcc = sb1("kcc", [128, 256], BF16); R_kcc = Res()
                    kcd = sb1("kcd", [128, 4, 128], BF16); R_kcd = Res()
                    krc = sb1("krc", [128, 4, 128], BF16); R_krc = Res()
                    Vcc = sb1("Vcc", [128, 256], BF16); R_Vcc = Res()
                    P.dma("pool", kcc[:], cc_k, writes=[R_kcc])
                    P.dma("pool", Vcc[:], cc_v, writes=[R_Vcc])
                    for a in range(4):
                        for d2 in range(2):
                            P.op("pool", lambda e, a=a, d2=d2: e.tensor_copy(
                                out=kcd[:, a, d2 * 64:(d2 + 1) * 64], in_=kcc[:, a * 64:(a + 1) * 64]),
                                reads=[R_kcc], writes=[R_kcd])
                    for a in range(4):
                        P.op("pe", lambda e, a=a: e.transpose(out=ptp[:, a * 128:(a + 1) * 128], in_=kcd[:, a, :],
                                                              identity=ident[:, :]),
                             reads=[R_kcd, R_const], writes=[R_tp])
                    for a in range(4):
                        P.op("dve", lambda e, a=a: e.tensor_copy(out=krc[:, a, :], in_=ptp[:, a * 128:(a + 1) * 128]),
                             reads=[R_tp], writes=[R_krc])
                lt0 = pos0 + T - ts
                P.dma("sp", ctm[:ts, :], c_cosTM[lt0:lt0 + ts, :], writes=[R_ctm])
                P.dma("sp", stm[:ts, :], c_sinTM[lt0:lt0 + ts, :], writes=[R_ctm])

                yc = {"n": 0, "bx": 0, "t": 0}
                LAG_A = 1
                XB = [0, 1, 2]
                YB = [3, 4, 5]

                def nbx():
                    b_ = XB[yc["bx"] % 3]; yc["bx"] += 1
                    return b_

                def post_norm_residual(bk0, bk1, gain, res_ap, res_r, out_ap, out_r):
                    k = yc["t"]; yc["t"] += 1
                    st = statY[k % 2]; Rst = R_statY[k % 2]
                    jk = junk2[k % 2]; Rjk = R_junk2[k % 2]
                    for half, bk in enumerate((bk0, bk1)):
                        P.op("act", lambda e, half=half, bk=bk: e.activation(
                            out=jk[:ts, half * 512:(half + 1) * 512], in_=bank(bk)[:ts, :], func=AF.Square,
                            accum_out=st[:ts, half:half + 1]), reads=[R_bank[bk]], writes=[Rjk, Rst])
                    P.op("dve", lambda e: e.tensor_tensor(out=st[:ts, 2:3], in0=st[:ts, 0:1], in1=st[:ts, 1:2],
                                                          op=ALU.add), reads=[Rst], writes=[Rst])
                    rstd_from(st[:ts, 2:3], st[:ts, 3:4], ts, [Rst], [Rst])
                    for half, bk in enumerate((bk0, bk1)):
                        P.op("dve", lambda e, half=half, bk=bk: e.scalar_tensor_tensor(
                            out=out_ap[:, half * 512:(half + 1) * 512], in0=bank(bk)[:ts, :], scalar=st[:ts, 3:4],
                            in1=gain[:ts, half * 512:(half + 1) * 512], op0=ALU.mult, op1=ALU.mult),
                            reads=[R_bank[bk], Rst, R_gn], writes=[out_r])
                    P.op("pool", lambda e: e.tensor_tensor(out=out_ap, in0=out_ap, in1=res_ap, op=ALU.add),
                         reads=[res_r, out_r], writes=[out_r])

                def gen_a(g):
                    gb = g % 2
                    t0 = g * gs1
                    g0 = t0 // gs
                    P.dma("sp", cosg[gb][:, :], c_cosT[:, pos0 + t0:pos0 + t0 + gs1], writes=[R_cos[gb]])
                    P.dma("sp", sing[gb][:, :], c_sinT[:, pos0 + t0:pos0 + t0 + gs1], writes=[R_sin[gb]])
                    for tt in range(tpg1):
                        ti = g * tpg1 + tt
                        k = cnt["x"]
                        xt = xst[k % 2]; Rxt = R_xst[k % 2]
                        P.dma("sp", xt[:ts, :], x_src(kind, si, ti * ts, ts), writes=[Rxt])
                        bks = (nbx(), nbx())
                        for half in range(2):
                            for c in range(8):
                                P.op("pe", lambda e, c=c, half=half: e.matmul(
                                    bank(bks[half])[:ts, :], lhsT=oT[:, c, ti * ts:(ti + 1) * ts],
                                    rhs=woab[:, c, half * 512:(half + 1) * 512], start=(c == 0), stop=(c == 7)),
                                    reads=[R_oT[c][g0], R_woab], writes=[R_bank[bks[half]]])
                        yield
                        post_norm_residual(bks[0], bks[1], gpostab, xt[:ts, :], Rxt, Y0[gb][:ts, tt, :], R_Y0[gb])
                        kk = norm_part1(Y0[gb][:ts, tt, :], R_Y0[gb], ts, gprec, gain_res=R_gn)
                        for _ in range(LAG_A):
                            yield
                        norm_part2(kk, ts, xn1T[gb][:, :, tt * ts:(tt + 1) * ts], R_xn1[gb])
                        yield

                def gen_b(g):
                    gb = g % 2
                    xn = xn1T[gb]; Rxn = R_xn1[gb]
                    cs, sn = cosg[gb], sing[gb]
                    if g > 0:
                        P.op("pool", lambda e: e.tensor_copy(out=kr[gb][:, :, 0:128], in_=kr[1 - gb][:, :, gs1:gs1 + 128]),
                             reads=[R_kr[1 - gb]], writes=[R_kr[gb]])
                        P.op("pool", lambda e: e.tensor_copy(out=V1[gb][:, 0, :], in_=V1[1 - gb][:, tpg1, :]),
                             reads=[R_V1[1 - gb]], writes=[R_V1[gb]])
                    for fc in range(8):
                        b1 = nbx()
                        for kc in range(8):
                            P.op("pe", lambda e, kc=kc: e.matmul(
                                bank(b1)[:, 0:gs1], lhsT=wc[:, kc, fc * 128:(fc + 1) * 128], rhs=xn[:, kc, :],
                                start=(kc == 0), stop=(kc == 7)), reads=[R_wc, Rxn], writes=[R_bank[b1]])
                        s2 = fc % 2
                        P.op("act", lambda e: e.activation(out=qbf[s2][:, :], in_=bank(b1)[:, 0:gs1], func=AF.Copy, scale=0.125),
                             reads=[R_bank[b1]], writes=[R_qbf[s2]])
                        P.op("dve", lambda e: e.scalar_tensor_tensor(
                            out=ta[s2][:, 0:gs1], in0=bank(b1)[:, 0:gs1], scalar=0.125, in1=cs[:, :], op0=ALU.mult,
                            op1=ALU.mult), reads=[R_bank[b1], R_cos[gb]], writes=[R_ta[s2]])
                        b2 = nbx()
                        P.op("pe", lambda e: e.matmul(bank(b2)[:, 0:gs1], lhsT=rot[:, :], rhs=qbf[s2][:, :], start=True, stop=True),
                             reads=[R_qbf[s2], R_const], writes=[R_bank[b2]])
                        P.op("dve", lambda e: e.tensor_tensor(out=tb[s2][:, 0:gs1], in0=bank(b2)[:, 0:gs1], in1=sn[:, :],
                                                              op=ALU.mult), reads=[R_bank[b2], R_sin[gb]], writes=[R_tb[s2]])
                        P.op("pool", lambda e: e.tensor_tensor(out=qr[gb][:, fc, :], in0=ta[s2][:, 0:gs1], in1=tb[s2][:, 0:gs1],
                                                               op=ALU.add), reads=[R_ta[s2], R_tb[s2]], writes=[R_qr[gb]])
                        yield
                def gen_b2(g):
                    gb = g % 2
                    xn = xn1T[gb]; Rxn = R_xn1[gb]
                    cs, sn = cosg[gb], sing[gb]
                    for kc2 in range(2):
                        b1 = nbx()
                        for kc in range(8):
                            P.op("pe", lambda e, kc=kc: e.matmul(
                                bank(b1)[:, 0:gs1], lhsT=wc[:, kc, 1024 + kc2 * 128:1024 + (kc2 + 1) * 128],
                                rhs=xn[:, kc, :], start=(kc == 0), stop=(kc == 7)),
                                reads=[R_wc, Rxn], writes=[R_bank[b1]])
                        P.op("act", lambda e: e.activation(out=kbf[0][:, kc2, :], in_=bank(b1)[:, 0:gs1], func=AF.Copy),
                             reads=[R_bank[b1]], writes=[R_kbf[0]])
                    yield
                    for a in range(4):
                        s2 = a % 2
                        b1 = nbx()
                        P.op("pe", lambda e: e.matmul(bank(b1)[:, 0:gs1], lhsT=dsel[:, a % 2, :], rhs=kbf[0][:, a // 2, :],
                                                      start=True, stop=True),
                             reads=[R_kbf[0], R_const], writes=[R_bank[b1]])
                        b2 = nbx()
                        P.op("pe", lambda e: e.matmul(bank(b2)[:, 0:gs1], lhsT=dselrot[:, a % 2, :], rhs=kbf[0][:, a // 2, :],
                                                      start=True, stop=True),
                             reads=[R_kbf[0], R_const], writes=[R_bank[b2]])
                        P.op("dve", lambda e: e.tensor_tensor(out=ta[s2][:, 0:gs1], in0=bank(b1)[:, 0:gs1], in1=cs[:, :],
                                                              op=ALU.mult), reads=[R_bank[b1], R_cos[gb]], writes=[R_ta[s2]])
                        P.op("dve", lambda e: e.tensor_tensor(out=tb[s2][:, 0:gs1], in0=bank(b2)[:, 0:gs1], in1=sn[:, :],
                                                              op=ALU.mult), reads=[R_bank[b2], R_sin[gb]], writes=[R_tb[s2]])
                        P.op("pool", lambda e: e.tensor_tensor(
                            out=kr[gb][:, a, 128:128 + gs1], in0=ta[s2][:, 0:gs1], in1=tb[s2][:, 0:gs1], op=ALU.add),
                            reads=[R_ta[s2], R_tb[s2]], writes=[R_kr[gb]])
                        yield
                    for fc in range(8):
                        b1 = nbx()
                        for kc in range(8):
                            P.op("pe", lambda e, kc=kc: e.matmul(
                                bank(b1)[:, 0:gs1], lhsT=wc[:, kc, 1536 + fc * 128:1536 + (fc + 1) * 128],
                                rhs=xn[:, kc, :], start=(kc == 0), stop=(kc == 7)),
                                reads=[R_wc, Rxn], writes=[R_bank[b1]])
                        s2 = fc % 2
                        P.op("act", lambda e: e.activation(out=tcb[s2][:, 0:gs1], in_=bank(b1)[:, 0:gs1], func=AF.Exp,
                                                           scale=-1.0), reads=[R_bank[b1]], writes=[R_tcb[s2]])
                        P.op("act", lambda e: e.activation(out=tcb[s2][:, 0:gs1], in_=tcb[s2][:, 0:gs1], func=AF.Ln, bias=1.0),
                             reads=[R_tcb[s2]], writes=[R_tcb[s2]])
                        P.op("act", lambda e: e.activation(out=tcb[s2][:, 0:gs1], in_=tcb[s2][:, 0:gs1], func=AF.Exp,
                                                           scale=-1.0), reads=[R_tcb[s2]], writes=[R_tcb[s2]])
                        P.op("dve", lambda e: e.tensor_tensor(out=g1[gb][:, fc, :], in0=bank(b1)[:, 0:gs1],
                                                              in1=tcb[s2][:, 0:gs1], op=ALU.mult),
                             reads=[R_bank[b1], R_tcb[s2]], writes=[R_g1[gb]])
                        yield
                    for tt in range(tpg1):
                        ti = g * tpg1 + tt
                        b1 = nbx()
                        for kc in range(8):
                            P.op("pe", lambda e, kc=kc: e.matmul(
                                bank(b1)[:ts, :], lhsT=xn[:, kc, tt * ts:(tt + 1) * ts], rhs=wc[:, kc, 1024:1536],
                                start=(kc == 0), stop=(kc == 7)), reads=[R_wc, Rxn], writes=[R_bank[b1]])
                        P.op("dve", lambda e: e.tensor_copy(out=V1[gb][:ts, 1 + tt, :], in_=bank(b1)[:ts, 256:512]),
                             reads=[R_bank[b1]], writes=[R_V1[gb]])
                        if ti == nt - 1:
                            ysk = kvst; Rysk = R_kvst
                            P.op("dve", lambda e: e.tensor_copy(out=ysk[:ts, 0:256], in_=bank(b1)[:ts, 256:512]),
                                 reads=[R_bank[b1]], writes=[Rysk])
                            dv_ap = o_cvp[si, :, :] if kind == "p" else o_cvs[:, :]
                            dk_ap = o_ckp[si, :, :] if kind == "p" else o_cks[:, :]
                            P.dma("pool", dv_ap, ysk[:ts, 0:256], reads=[Rysk], is_output=True)
                            P.op("dve", lambda e: e.tensor_copy(out=kst[:ts, :], in_=bank(b1)[:ts, 0:256]),
                                 reads=[R_bank[b1]], writes=[R_kst])
                            for hk in range(4):
                                for b2_ in range(2):
                                    P.op("dve", lambda e, hk=hk, b2_=b2_: e.tensor_copy(
                                        out=ksw[:ts, hk * 64 + b2_ * 32:hk * 64 + b2_ * 32 + 32],
                                        in_=kst[:ts, hk * 64 + (1 - b2_) * 32:hk * 64 + (1 - b2_) * 32 + 32]),
                                        reads=[R_kst], writes=[R_ksw])
                            P.op("dve", lambda e: e.tensor_tensor(out=kst[:ts, :], in0=kst[:ts, :], in1=ctm[:ts, :],
                                                                  op=ALU.mult), reads=[R_kst, R_ctm], writes=[R_kst])
                            P.op("dve", lambda e: e.tensor_tensor(out=ksw[:ts, :], in0=ksw[:ts, :], in1=stm[:ts, :],
                                                                  op=ALU.mult), reads=[R_ksw, R_ctm], writes=[R_ksw])
                            P.op("dve", lambda e: e.tensor_tensor(out=ysk[:ts, 256:512], in0=kst[:ts, :],
                                                                  in1=ksw[:ts, :], op=ALU.add),
                                 reads=[R_kst, R_ksw], writes=[Rysk])
                            P.dma("pool", dk_ap, ysk[:ts, 256:512], reads=[Rysk], is_output=True)
                        yield

                def c_blocks(g, j, a):
                    gb = g % 2
                    J = g * tpg1 + j
                    blocks = []
                    if kind == "p":
                        if J > 0:
                            blocks.append((kr[gb][:, a, j * 128:(j + 1) * 128], 128,
                                           V1[gb][:, j, a * 64:(a + 1) * 64], "prev", [R_kr[gb]], [R_V1[gb]]))
                        blocks.append((kr[gb][:, a, (j + 1) * 128:(j + 2) * 128], 128,
                                       V1[gb][:, j + 1, a * 64:(a + 1) * 64], "diag", [R_kr[gb]], [R_V1[gb]]))
                    else:
                        blocks.append((krc[:, a, :], 128, Vcc[:, a * 64:(a + 1) * 64], "c", [R_krc], [R_Vcc]))
                        blocks.append((kr[gb][:, a, 128:128 + TS], TS, V1[gb][0:TS, 1, a * 64:(a + 1) * 64], "n",
                                       [R_kr[gb]], [R_V1[gb]]))
                    return blocks

                def c_stage1(g, n):
                    gb = g % 2
                    j, a = n // 4, n % 4
                    blocks = c_blocks(g, j, a)
                    for par in range(2):
                        sbk = YB[par]
                        Sps = bank(sbk)
                        po = par * 64
                        pt = PTc[(n % 2) * 2 + par]; Rpt = R_PTc[(n % 2) * 2 + par]
                        for bi_, (k_ap, nk, v_ap, tag, rk, rv) in enumerate(blocks):
                            if qw == 128:
                                col = bi_ * 2 * qw
                                P.op("pe", lambda e, k_ap=k_ap, nk=nk, col=col: e.matmul(
                                    Sps[:nk, col:col + 2 * qw].rearrange("p (h q) -> p h q", h=2),
                                    lhsT=k_ap[po:po + 64, :],
                                    rhs=qr[gb][po:po + 64, 2 * a:2 * a + 2, j * qw:(j + 1) * qw],
                                    start=True, stop=True), reads=rk + [R_qr[gb]], writes=[R_bank[sbk]])
                                continue
                            for hi in range(2):
                                fc = 2 * a + hi
                                col = (bi_ * 2 + hi) * qw
                                P.op("pe", lambda e, k_ap=k_ap, nk=nk, fc=fc, col=col: e.matmul(
                                    Sps[:nk, col:col + qw], lhsT=k_ap[po:po + 64, :],
                                    rhs=qr[gb][po:po + 64, fc, j * qw:(j + 1) * qw],
                                    start=True, stop=True), reads=rk + [R_qr[gb]], writes=[R_bank[sbk]])
                        for bi_, (k_ap, nk, v_ap, tag, rk, rv) in enumerate(blocks):
                            c0 = bi_ * 2 * qw
                            P.op("act", lambda e, nk=nk, c0=c0: e.activation(
                                out=pt[:nk, c0:c0 + 2 * qw], in_=Sps[:nk, c0:c0 + 2 * qw], func=AF.Exp),
                                reads=[R_bank[sbk]], writes=[Rpt])
                            if tag == "prev":
                                P.op("pool", lambda e, c0=c0: e.memset(
                                    pt[0:64, c0:c0 + 2 * qw].rearrange("p (h q) -> p h q", h=2)[:, :, 64:128], 0.0),
                                    writes=[Rpt])
                            if tag == "diag":
                                P.op("pool", lambda e, c0=c0: e.memset(
                                    pt[64:128, c0:c0 + 2 * qw].rearrange("p (h q) -> p h q", h=2)[:, :, 0:64], 0.0),
                                    writes=[Rpt])

                def c_stage2(g, n):
                    gb = g % 2
                    j, a = n // 4, n % 4
                    blocks = c_blocks(g, j, a)
                    nb = len(blocks)
                    ocb = YB[2]
                    OC = bank(ocb)
                    for par in range(2):
                        po = par * 64
                        pt = PTc[(n % 2) * 2 + par]; Rpt = R_PTc[(n % 2) * 2 + par]
                        if qw == 128:
                            for bi_, (k_ap, nk, v_ap, tag, rk, rv) in enumerate(blocks):
                                col = bi_ * 2 * qw
                                P.op("pe", lambda e, v_ap=v_ap, nk=nk, col=col, bi_=bi_: e.matmul(
                                    OC[po:po + 64, 0:256], lhsT=v_ap, rhs=pt[:nk, col:col + 256],
                                    start=(bi_ == 0), stop=(bi_ == nb - 1)),
                                    reads=rv + [Rpt], writes=[R_bank[ocb]])
                            for bi_, (k_ap, nk, v_ap, tag, rk, rv) in enumerate(blocks):
                                col = bi_ * 2 * qw
                                P.op("pe", lambda e, nk=nk, col=col, bi_=bi_: e.matmul(
                                    OC[po:po + 64, 256:512], lhsT=ones[:nk, 0:64],
                                    rhs=pt[:nk, col:col + 256], start=(bi_ == 0), stop=(bi_ == nb - 1)),
                                    reads=[Rpt, R_const], writes=[R_bank[ocb]])
                            continue
                        for hi in range(2):
                            for bi_, (k_ap, nk, v_ap, tag, rk, rv) in enumerate(blocks):
                                col = (bi_ * 2 + hi) * qw
                                P.op("pe", lambda e, v_ap=v_ap, nk=nk, col=col, bi_=bi_: e.matmul(
                                    OC[po:po + 64, hi * 128:hi * 128 + qw], lhsT=v_ap, rhs=pt[:nk, col:col + qw],
                                    start=(bi_ == 0), stop=(bi_ == nb - 1)),
                                    reads=rv + [Rpt], writes=[R_bank[ocb]])
                            for bi_, (k_ap, nk, v_ap, tag, rk, rv) in enumerate(blocks):
                                col = (bi_ * 2 + hi) * qw
                                P.op("pe", lambda e, nk=nk, col=col, bi_=bi_: e.matmul(
                                    OC[po:po + 64, 256 + hi * 128:256 + hi * 128 + qw], lhsT=ones[:nk, 0:64],
                                    rhs=pt[:nk, col:col + qw], start=(bi_ == 0), stop=(bi_ == nb - 1)),
                                    reads=[Rpt, R_const], writes=[R_bank[ocb]])
                    s2 = n % 2
                    rc = recc[s2]; Rrc = R_recc[s2]
                    tm = tmpc[s2]; Rtm = R_tmpc[s2]
                    for hi in range(2):
                        fc = 2 * a + hi
                        P.op("act", lambda e, hi=hi, fc=fc: e.activation(
                            out=rc[:, hi * 128:hi * 128 + qw], in_=OC[:, 256 + hi * 128:256 + hi * 128 + qw],
                            func=AF.Ln, bias=esink[:, fc:fc + 1]),
                            reads=[R_bank[ocb], R_const], writes=[Rrc])
                    if qw == 128:
                        P.op("act", lambda e: e.activation(out=rc[:, 0:256], in_=rc[:, 0:256], func=AF.Exp, scale=-1.0),
                             reads=[Rrc], writes=[Rrc])
                        P.op("dve", lambda e: e.tensor_tensor(out=tm[:, 0:256], in0=OC[:, 0:256], in1=rc[:, 0:256],
                                                              op=ALU.mult), reads=[R_bank[ocb], Rrc], writes=[Rtm])
                        P.op("pool", lambda e: e.tensor_tensor(
                            out=qr[gb][:, 2 * a:2 * a + 2, j * qw:(j + 1) * qw],
                            in0=tm[:, 0:256].rearrange("p (h q) -> p h q", h=2),
                            in1=g1[gb][:, 2 * a:2 * a + 2, j * qw:(j + 1) * qw], op=ALU.mult),
                            reads=[Rtm, R_g1[gb]], writes=[R_qr[gb]])
                    else:
                        for hi in range(2):
                            fc = 2 * a + hi
                            P.op("act", lambda e, hi=hi: e.activation(out=rc[:, hi * 128:hi * 128 + qw],
                                                                      in_=rc[:, hi * 128:hi * 128 + qw], func=AF.Exp,
                                                                      scale=-1.0),
                                 reads=[Rrc], writes=[Rrc])
                            P.op("dve", lambda e, hi=hi: e.tensor_tensor(
                                out=tm[:, hi * 128:hi * 128 + qw], in0=OC[:, hi * 128:hi * 128 + qw],
                                in1=rc[:, hi * 128:hi * 128 + qw], op=ALU.mult),
                                reads=[R_bank[ocb], Rrc], writes=[Rtm])
                            P.op("pool", lambda e, hi=hi, fc=fc: e.tensor_tensor(
                                out=qr[gb][:, fc, j * qw:(j + 1) * qw], in0=tm[:, hi * 128:hi * 128 + qw],
                                in1=g1[gb][:, fc, j * qw:(j + 1) * qw], op=ALU.mult),
                                reads=[Rtm, R_g1[gb]], writes=[R_qr[gb]])

                def gen_c(g):
                    nn = tpg1 * 4
                    c_stage1(g, 0)
                    yield
                    for n in range(nn):
                        if n + 1 < nn:
                            c_stage1(g, n + 1)
                            yield
                        c_stage2(g, n)
                        yield

                def gen_d(g):
                    gb = g % 2
                    for tt in range(tpg1):
                        ti = g * tpg1 + tt
                        bks = (nbx(), nbx())
                        for half in range(2):
                            for c in range(8):
                                P.op("pe", lambda e, c=c, half=half: e.matmul(
                                    bank(bks[half])[:ts, :], lhsT=qr[gb][:, c, tt * ts:(tt + 1) * ts],
                                    rhs=woc[:, c, half * 512:(half + 1) * 512], start=(c == 0), stop=(c == 7)),
                                    reads=[R_qr[gb], R_woc], writes=[R_bank[bks[half]]])
                        yield
                        sk = yc["n"] % 2; yc["n"] += 1
                        ysk = t1b[sk]; Rysk = R_t1[sk]
                        post_norm_residual(bks[0], bks[1], gpostc, Y0[gb][:ts, tt, :], R_Y0[gb], ysk[:ts, :], Rysk)
                        dst = yp[si, ti * ts:(ti + 1) * ts, :] if kind == "p" else ys[0:ts, :]
                        P.dma("pool", dst, ysk[:ts, :], reads=[Rysk], is_output=True)
                        yield

                def chain(*gens):
                    for g_ in gens:
                        yield from g_

                def run_streams(streams):
                    alive = list(streams)
                    while alive:
                        for s_ in list(alive):
                            try:
                                next(s_)
                            except StopIteration:
                                alive.remove(s_)

                flags = {}

                def wait_for(*keys):
                    while not all(flags.get(k) for k in keys):
                        yield

                def SA():
                    for g in range(ng1):
                        if g >= 2:
                            yield from wait_for(("d", g - 2))
                        if g >= 1:
                            yield from wait_for(("b2", g - 1))
                        yield from gen_a(g)
                        flags[("a", g)] = True
                        first = True
                        for _ in gen_b(g):
                            if first:
                                flags[("halo", g)] = True
                                first = False
                            yield
                        flags[("halo", g)] = True
                        flags[("bq", g)] = True

                def SB():
                    for g in range(ng1):
                        yield from wait_for(("a", g), ("halo", g))
                        yield from gen_b2(g)
                        flags[("b2", g)] = True

                def SC():
                    for g in range(ng1):
                        yield from wait_for(("bq", g), ("b2", g))
                        yield from gen_c(g)
                        flags[("c", g)] = True

                def SD():
                    for g in range(ng1):
                        yield from wait_for(("c", g))
                        yield from gen_d(g)
                        flags[("d", g)] = True

                run_streams([SA(), SB(), SC(), SD()])
            P.barrier()


        with nc.Block() as block:
            P.finalize(block, sems, dma_sems)
    return nc


_NC_CACHE = {}


def _prep(x_prompt, x_sample, cache_a_k, cache_a_v, cache_b_k, cache_b_v, cache_c_k, cache_c_v,
          ab_norm_pre, ab_w_in, ab_w_out, ab_norm_post, a_rel_bias,
          c_norm_pre, c_w_in, c_sinks, c_w_out, c_norm_post):
    f32 = np.float32
    A = lambda a: np.ascontiguousarray(np.asarray(a, dtype=f32))
    x_prompt, x_sample = A(x_prompt), A(x_sample)
    ncore = 8
    cst = _consts()
    w = A(ab_w_in)[0]
    w_ab = np.zeros((8, D, 512), f32)
    for pi in range(8):
        base = 0 if pi < 4 else 2048
        hp = pi % 4
        sl = lambda blk: w[:, base + blk * 512 + hp * 128: base + blk * 512 + (hp + 1) * 128]
        w_ab[pi, :, 0:128] = sl(0)
        w_ab[pi, :, 128:256] = sl(3)
        w_ab[pi, :, 256:384] = sl(1)
        w_ab[pi, :, 384:512] = sl(2)
    rep = lambda v: np.ascontiguousarray(np.broadcast_to(A(v).reshape(1, D), (128, D)))
    bp, bs = _bias_tiles(A(a_rel_bias)[0])
    sk = A(c_sinks)[0]
    sinks_l = np.zeros((128, 8), f32)
    for fc in range(8):
        sinks_l[0:64, fc] = sk[2 * fc]
        sinks_l[64:128, fc] = sk[2 * fc + 1]
    common = {
        "w_ab": w_ab, "w_oab": A(ab_w_out)[0], "w_c": A(c_w_in)[0], "w_oc": A(c_w_out)[0],
        "gpre_ab": rep(ab_norm_pre[0]), "gpost_ab": rep(ab_norm_post[0]),
        "gpre_c": rep(c_norm_pre[0]), "gpost_c": rep(c_norm_post[0]),
        "biasP": bp, "biasS": bs, "sinks": sinks_l,
        "c_ident": cst["ident"], "c_tri": cst["tri"], "c_ones": cst["ones"], "c_lmask": cst["lmask"],
        "c_rot": cst["rot"], "c_dsel": cst["dsel"], "c_dselrot": cst["dselrot"],
        "c_cosT": cst["cosT"], "c_sinT": cst["sinT"], "c_cosTM": cst["cosTM"], "c_sinTM": cst["sinTM"],
    }
    cak, cav = A(cache_a_k)[0], A(cache_a_v)[0]
    cbk, cbv = A(cache_b_k)[0], A(cache_b_v)[0]
    cck, ccv = A(cache_c_k)[0], A(cache_c_v)[0]
    in_maps = []
    for i in range(ncore):
        m = dict(common)
        m["xp"] = np.ascontiguousarray(x_prompt[2 * i:2 * i + 2])
        m["xs"] = np.ascontiguousarray(x_sample[i])
        m["ca_k"] = cak[i].reshape(512, 512); m["ca_v"] = cav[i].reshape(512, 512)
        m["cb_k"] = cbk[i].reshape(PAST, 512); m["cb_v"] = cbv[i].reshape(PAST, 512)
        m["cc_k"] = cck[i].reshape(128, 256); m["cc_v"] = ccv[i].reshape(128, 256)
        in_maps.append(m)
    return in_maps


def kernel(**inputs):
    ncore = 8
    in_maps = _prep(**inputs)
    if "nc" not in _NC_CACHE:
        _NC_CACHE["nc"] = build()
    nc = _NC_CACHE["nc"]
    res = run_bass_kernel_spmd(nc, in_maps, core_ids=list(range(ncore)))
    return _gather(res.results)


def _gather(R):
    ncore = len(R)
    cat = lambda name: np.concatenate([R[i][name] for i in range(ncore)], axis=0)
    stk = lambda name: np.stack([R[i][name] for i in range(ncore)], axis=0)
    y_prompt = cat("yp")
    y_sample = stk("ys")
    out = (
        y_prompt, y_sample,
        cat("o_akp").reshape(1, 16, 512, 8, 64), cat("o_avp").reshape(1, 16, 512, 8, 64),
        cat("o_bkp").reshape(1, 16, S, 8, 64), cat("o_bvp").reshape(1, 16, S, 8, 64),
        cat("o_ckp").reshape(1, 16, 128, 4, 64), cat("o_cvp").reshape(1, 16, 128, 4, 64),
        stk("o_aks").reshape(1, 8, TS, 8, 64), stk("o_avs").reshape(1, 8, TS, 8, 64),
        stk("o_bks").reshape(1, 8, TS, 8, 64), stk("o_bvs").reshape(1, 8, TS, 8, 64),
        stk("o_cks").reshape(1, 8, TS, 4, 64), stk("o_cvs").reshape(1, 8, TS, 4, 64),
    )
    return tuple(np.ascontiguousarray(o.astype(np.float32)) for o in out)
```

```python
import numpy as np
import concourse.bass as bass
import concourse.mybir as mybir
from concourse.bass_utils import run_bass_kernel_spmd

F32 = mybir.dt.float32
BF16 = mybir.dt.bfloat16
AF = mybir.ActivationFunctionType
ALU = mybir.AluOpType

D = 1024
S = 2048
TS = 32
PAST = 2048
EPS = 1e-6
NEG = -30000.0
SEM_LIM = 20000


class Res:
    __slots__ = ("w", "r", "name", "excl")

    def __init__(self, name="", excl=False):
        self.w = None
        self.r = {}
        self.name = name
        self.excl = excl


class _Rec:
    def __init__(self):
        self.call = None

    def __getattr__(self, name):
        def f(*args, **kwargs):
            self.call = (name, args, kwargs)
            return None
        return f


class Prog:
    ENG = ("pe", "act", "dve", "pool", "sp")

    def __init__(self, nc):
        self.nc = nc
        self.ops = {e: [] for e in self.ENG}
        self.waited = {e: {} for e in self.ENG}
        self.ndma_sems = 12
        self.ndma_q = {"sp": 12, "pool": 6}
        self.dma_cnt = {q: [0] * self.ndma_sems for q in ("sp", "pool")}
        self.dma_next = {"sp": 0, "pool": 0}
        self.dma_last = {q: [None] * self.ndma_sems for q in ("sp", "pool")}
        self.out_dma_refs = []
        self.last_pe = None

    def _need(self, eng, ref, waits):
        if ref is None:
            return
        if ref[0] == "op":
            _, e2, idx = ref
            if e2 == eng and eng == "pe":
                return
            if self.waited[eng].get(e2, -1) >= idx:
                return
            if e2 == eng and idx >= len(self.ops[eng]):
                return
            self.waited[eng][e2] = idx
            self.ops[e2][idx]["inc"] = True
            waits.append(ref)
        else:
            _, q, slot, val = ref
            key = ("dma", q, slot)
            if self.waited[eng].get(key, -1) >= val:
                return
            self.waited[eng][key] = val
            waits.append(ref)

    def _deps(self, eng, reads, writes, same_engine_war=False):
        waits = []
        for r in reads:
            self._need(eng, r.w, waits)
        for w in writes:
            self._need(eng, w.w, waits)
            for e2, ref in w.r.items():
                self._need(eng, ref, waits)
        return waits

    def _commit(self, ref, reads, writes):
        for r in reads:
            r.r[ref[1] if ref[0] == "op" else ("dma", ref[1], ref[2])] = ref
        for w in writes:
            w.w = ref
            w.r = {}

    def op(self, eng, fn, reads=(), writes=()):
        ex = [r for r in reads if r.excl]
        if ex:
            reads = [r for r in reads if not r.excl]
            writes = list(writes) + ex
        waits = self._deps(eng, reads, writes)
        idx = len(self.ops[eng])
        rec = _Rec()
        fn(rec)
        assert rec.call is not None
        self.ops[eng].append({"fn": rec.call, "waits": waits, "inc": False, "dma": None})
        self._commit(("op", eng, idx), reads, writes)
        return ("op", eng, idx)

    def dma(self, q, out, in_, reads=(), writes=(), is_output=False):
        waits = self._deps(q, reads, writes)
        slot = self.dma_next[q]
        self.dma_next[q] = (slot + 1) % self.ndma_q[q]
        prev = self.dma_last[q][slot]
        if prev is not None:
            self._need(q, prev, waits)
        self.dma_cnt[q][slot] += 1
        val = self.dma_cnt[q][slot] * 16
        ref = ("dma", q, slot, val)
        self.dma_last[q][slot] = ref
        self.ops[q].append({"fn": ("dma_start", (), {"out": out, "in_": in_}), "waits": waits,
                            "inc": False, "dma": (q, slot)})
        self._commit(ref, reads, writes)
        if is_output:
            self.out_dma_refs.append(ref)
        return ref

    def barrier(self):
        refs = []
        for e in self.ENG:
            for idx in range(len(self.ops[e]) - 1, -1, -1):
                o = self.ops[e][idx]
                if o["fn"] is not None and o["dma"] is None:
                    refs.append(("op", e, idx))
                    break
        for q in ("sp", "pool"):
            for slot in range(self.ndma_sems):
                if self.dma_last[q][slot] is not None:
                    refs.append(self.dma_last[q][slot])
        for e in self.ENG:
            waits = []
            for ref in refs:
                if ref[0] == "op" and ref[1] == e:
                    continue
                self._need(e, ref, waits)
            self.ops[e].append({"fn": None, "waits": waits, "inc": False, "dma": None})

    def finalize(self, block, sems, dma_sems):
        nc = self.nc
        marks = {}
        for e in self.ENG:
            c = 0
            m = []
            for o in self.ops[e]:
                if o["inc"] and o["fn"] is not None and o["dma"] is None:
                    c += 1
                m.append(c)
            marks[e] = m
            assert c <= SEM_LIM * len(sems[e]), (e, c)

        def sem_of(e, idx):
            m = marks[e][idx]
            assert m >= 1
            k = (m - 1) // SEM_LIM
            return sems[e][k], (m - 1) % SEM_LIM + 1

        engs = {"pe": nc.tensor, "act": nc.scalar, "dve": nc.vector, "pool": nc.gpsimd, "sp": nc.sync}

        def _pinfo(ap):
            fs = 1
            for s_ in list(ap.tensor.shape)[1:]:
                fs *= int(s_)
            p0 = int(ap.offset) // fs
            col = int(ap.offset) % fs
            return p0, int(ap.ap[0][1]), col, fs
        prev = None
        nviol = 0
        for o in self.ops["pe"]:
            if o["fn"] is None:
                continue
            name_, args_, kw_ = o["fn"]
            out_ap = args_[0] if args_ else kw_["out"]
            l_ap = kw_.get("lhsT", kw_.get("in_"))
            p0, kk, _, _ = _pinfo(l_ap)
            _, _, col, fs = _pinfo(out_ap)
            esz = 4 if fs in (512, 1024) and out_ap.tensor.name.startswith("pb") else 2
            bank_id = (out_ap.tensor.name, (col * esz) // 2048)
            rows = (p0, p0 + kk)
            cur = (rows, bank_id)
            if prev is not None and kk < 128 and (prev[0][1] - prev[0][0]) < 128:
                disjoint = rows[0] >= prev[0][1] or prev[0][0] >= rows[1]
                if disjoint and prev[1] == bank_id:
                    nviol += 1
            prev = cur
        assert nviol == 0, f"row-tile bank violations: {nviol}"

        def run(e, eng):
            for idx, o in enumerate(self.ops[e]):
                for ref in o["waits"]:
                    if ref[0] == "op":
                        s_, v_ = sem_of(ref[1], ref[2])
                        eng.wait_ge(s_, v_)
                    else:
                        eng.wait_ge(dma_sems[ref[1]][ref[2]], ref[3])
                if o["fn"] is None:
                    continue
                name_, args_, kw_ = o["fn"]
                ins = getattr(eng, name_)(*args_, **kw_)
                if o["dma"] is not None:
                    ins.then_inc(dma_sems[o["dma"][0]][o["dma"][1]], 16)
                elif o["inc"]:
                    s_, _ = sem_of(e, idx)
                    ins.then_inc(s_, 1)

        @block.tensor
        def _(eng):
            run("pe", eng)

        @block.scalar
        def _(eng):
            run("act", eng)

        @block.vector
        def _(eng):
            run("dve", eng)

        @block.gpsimd
        def _(eng):
            run("pool", eng)

        @block.sync
        def _(eng):
            run("sp", eng)


def _consts():
    c = {}
    i = np.arange(128)
    c["ident"] = np.eye(128, dtype=np.float32)
    c["tri"] = (i[:, None] >= i[None, :]).astype(np.float32)
    c["ones"] = np.ones((128, 128), np.float32)
    c["lmask"] = (i[:, None] < i[None, :]).astype(np.float32)
    rot = np.zeros((128, 128), np.float32)
    for p in range(128):
        if p % 64 < 32:
            rot[p + 32, p] = 1.0
        else:
            rot[p - 32, p] = 1.0
    c["rot"] = rot
    dsel = np.zeros((2, 128, 128), np.float32)
    for a in range(2):
        for p in range(128):
            dsel[a, a * 64 + (p % 64), p] = 1.0
    c["dsel"] = dsel
    c["dselrot"] = np.stack([dsel[a] @ rot for a in range(2)])
    half = 32
    inv = (10000.0 ** (-np.arange(half, dtype=np.float32) * np.float32(2.0 / 64))).astype(np.float32)
    pos = np.arange(S + TS, dtype=np.float32)
    ang = (pos[:, None] * inv[None, :]).astype(np.float32)
    cos, sin = np.cos(ang).astype(np.float32), np.sin(ang).astype(np.float32)
    pidx = np.arange(128) % 32
    sign = np.where((np.arange(128) % 64) < 32, -1.0, 1.0).astype(np.float32)
    c["cosT"] = np.ascontiguousarray(cos[:, pidx].T)
    c["sinT"] = np.ascontiguousarray((sin[:, pidx] * sign[None, :]).T)
    fidx = np.arange(256) % 32
    fsign = np.where((np.arange(256) % 64) < 32, -1.0, 1.0).astype(np.float32)
    c["cosTM"] = np.ascontiguousarray(cos[:, fidx])
    c["sinTM"] = np.ascontiguousarray(sin[:, fidx] * fsign[None, :])
    return c


def _bias_tiles(table):
    k = np.arange(128)[:, None]
    q = np.arange(128)[None, :]
    bp = np.zeros((8, 128, 640), np.float32)
    for slot in range(5):
        rel = (4 - slot) * 128 + (q - k)
        idx = np.clip(rel, -128, 128) + 128
        bp[:, :, slot * 128:(slot + 1) * 128] = table[:, idx]
    qs = PAST + np.arange(TS)[None, :]
    bs = np.zeros((8, 128, 160), np.float32)
    for slot in range(5):
        kpos = PAST - 512 + slot * 128 + np.arange(128)[:, None]
        idx = np.clip(qs - kpos, -128, 128) + 128
        bs[:, :, slot * 32:(slot + 1) * 32] = table[:, idx]
    return bp, bs


def build():
    nc = bass.Bass("TRN2", target_bir_lowering=False)
    P = Prog(nc)

    def din(name, shape):
        return nc.dram_tensor(name, list(shape), F32, kind="ExternalInput").ap()

    def dout(name, shape):
        return nc.dram_tensor(name, list(shape), F32, kind="ExternalOutput").ap()

    xp = din("xp", [2, S, D])
    xs = din("xs", [TS, D])
    ca_k = din("ca_k", [512, 512]); ca_v = din("ca_v", [512, 512])
    cb_k = din("cb_k", [PAST, 512]); cb_v = din("cb_v", [PAST, 512])
    cc_k = din("cc_k", [128, 256]); cc_v = din("cc_v", [128, 256])
    w_ab = din("w_ab", [8, D, 512])
    w_oab = din("w_oab", [D, D])
    w_c = din("w_c", [D, 2560])
    w_oc = din("w_oc", [D, D])
    gpre_ab = din("gpre_ab", [128, D]); gpost_ab = din("gpost_ab", [128, D])
    gpre_c = din("gpre_c", [128, D]); gpost_c = din("gpost_c", [128, D])
    biasP = din("biasP", [8, 128, 640]); biasS = din("biasS", [8, 128, 160])
    sinks = din("sinks", [128, 8])
    c_ident = din("c_ident", [128, 128]); c_tri = din("c_tri", [128, 128]); c_ones = din("c_ones", [128, 128])
    c_lmask = din("c_lmask", [128, 128]); c_rot = din("c_rot", [128, 128])
    c_dsel = din("c_dsel", [2, 128, 128]); c_dselrot = din("c_dselrot", [2, 128, 128])
    c_cosT = din("c_cosT", [128, S + TS]); c_sinT = din("c_sinT", [128, S + TS])
    c_cosTM = din("c_cosTM", [S + TS, 256]); c_sinTM = din("c_sinTM", [S + TS, 256])

    yp = dout("yp", [2, S, D]); ys = dout("ys", [TS, D])
    o_akp = dout("o_akp", [2, 512, 512]); o_avp = dout("o_avp", [2, 512, 512])
    o_bkp = dout("o_bkp", [2, S, 512]); o_bvp = dout("o_bvp", [2, S, 512])
    o_ckp = dout("o_ckp", [2, 128, 256]); o_cvp = dout("o_cvp", [2, 128, 256])
    o_aks = dout("o_aks", [TS, 512]); o_avs = dout("o_avs", [TS, 512])
    o_bks = dout("o_bks", [TS, 512]); o_bvs = dout("o_bvs", [TS, 512])
    o_cks = dout("o_cks", [TS, 256]); o_cvs = dout("o_cvs", [TS, 256])

    from contextlib import ExitStack
    es = ExitStack()

    def sb(name, shape, dt):
        return es.enter_context(nc.sbuf_tensor(name, list(shape), dt))

    def ps(name, shape, dt):
        return es.enter_context(nc.psum_tensor(name, list(shape), dt))

    with es:
        sems = {e: [es.enter_context(nc.semaphore(f"s_{e}{k}")) for k in range(2)] for e in Prog.ENG}
        dma_sems = {q: [es.enter_context(nc.semaphore(f"d_{q}{k}")) for k in range(P.ndma_sems)]
                    for q in ("sp", "pool")}

        ident = sb("ident", [128, 128], BF16); tri = sb("tri", [128, 128], BF16)
        ones = sb("ones", [128, 128], BF16); lmask = sb("lmask", [128, 128], BF16)
        rot = sb("rot", [128, 128], BF16)
        dsel = sb("dsel", [128, 2, 128], BF16); dselrot = sb("dselrot", [128, 2, 128], BF16)
        esink = sb("esink", [128, 8], F32)
        R_const = Res("const")
        for t_, src in ((ident, c_ident), (tri, c_tri), (ones, c_ones), (lmask, c_lmask), (rot, c_rot)):
            P.dma("pool", t_[:], src, writes=[R_const])
        for a in range(2):
            P.dma("pool", dsel[:, a, :], c_dsel[a], writes=[R_const])
            P.dma("pool", dselrot[:, a, :], c_dselrot[a], writes=[R_const])
        for t_, src in ((esink, sinks),):
            P.dma("sp", t_[:], src, writes=[R_const])
        P.op("act", lambda e: e.activation(out=esink[:], in_=esink[:], func=AF.Exp), reads=[R_const], writes=[R_const])

        pbank = [ps(f"pb{i}", [128, 1024], F32) for i in range(3)]
        ptps = [ps(f"ptp{i}", [128, 1024], BF16) for i in range(2)]
        ptp = ptps[0]
        R_bank = [Res(f"bank{i}", excl=True) for i in range(6)]
        R_tps = [Res(f"tp{i}", excl=True) for i in range(2)]
        R_tp = R_tps[0]

        def bank(i):
            return pbank[i // 2][:, (i % 2) * 512:(i % 2 + 1) * 512]

        oT = sb("oT", [128, 8, S], BF16)
        R_oT = [[Res(f"oT{c}_{g}") for g in range(4)] for c in range(8)]
        xst = [sb(f"xst{i}", [128, D], F32) for i in range(2)]
        R_xst = [Res(f"xst{i}") for i in range(2)]
        junk2 = [sb("junk2_0", [128, D], BF16)] * 2; R_junk2 = [Res()] * 2
        stat2 = [sb(f"stat2_{i}", [128, 2], F32) for i in range(2)]; R_stat2 = [Res() for _ in range(2)]
        xsb = [sb(f"xsb{i}", [128, D], BF16) for i in range(2)]
        R_xsb = [Res(f"xsb{i}") for i in range(2)]
        cnt = {"x": 0, "stg": 0, "pj": 0}

        seqs = [("p", 0), ("p", 1), ("s", 0)]

        def x_src(kind, si, t0, n):
            return xp[si, t0:t0 + n, :] if kind == "p" else xs[t0:t0 + n, :]

        def rstd_from(ss_ap, out_ap, n, reads, writes):
            P.op("act", lambda e: e.activation(out=out_ap, in_=ss_ap, func=AF.Ln, scale=1.0 / D, bias=EPS),
                 reads=reads, writes=writes)
            P.op("act", lambda e: e.activation(out=out_ap, in_=out_ap, func=AF.Exp, scale=-0.5),
                 reads=writes, writes=writes)

        def norm_part1(src_ap, src_res, ts, gain, gain_res=None):
            gain_res = gain_res or R_const
            k = cnt["x"]; cnt["x"] += 1
            xb = xsb[k % 2]; Rxb = R_xsb[k % 2]
            st = stat2[k % 2]; Rst = R_stat2[k % 2]
            jk = junk2[k % 2]; Rjk = R_junk2[k % 2]
            P.op("act", lambda e: e.activation(out=jk[:ts, :], in_=src_ap, func=AF.Square,
                                               accum_out=st[:ts, 0:1]),
                 reads=[src_res], writes=[Rjk, Rst])
            rstd_from(st[:ts, 0:1], st[:ts, 1:2], ts, [Rst], [Rst])
            P.op("dve", lambda e: e.scalar_tensor_tensor(out=xb[:ts, :], in0=src_ap, scalar=st[:ts, 1:2],
                                                         in1=gain[:ts, :], op0=ALU.mult, op1=ALU.mult),
                 reads=[src_res, Rst, gain_res], writes=[Rxb])
            return k

        def norm_part2(k, ts, dst_ap3, dst_res):
            xb = xsb[k % 2]; Rxb = R_xsb[k % 2]
            tp = ptps[k % 2]; Rtp = R_tps[k % 2]
            for c in range(8):
                P.op("pe", lambda e, c=c: e.transpose(out=tp[:, c * ts:(c + 1) * ts],
                                                      in_=xb[:ts, c * 128:(c + 1) * 128], identity=ident[:ts, :ts]),
                     reads=[Rxb, R_const], writes=[Rtp])
            P.op("dve", lambda e: e.tensor_copy(out=dst_ap3,
                                                in_=tp[:, 0:8 * ts].rearrange("p (c t) -> p c t", c=8)),
                 reads=[Rtp], writes=[dst_res])

        def norm_transpose(src_ap, src_res, ts, gain, dst_ap3, dst_res, gain_res=None):
            k = norm_part1(src_ap, src_res, ts, gain, gain_res)
            norm_part2(k, ts, dst_ap3, dst_res)

        for (kind, si) in seqs:
            T = S if kind == "p" else TS
            ts = min(T, 128)
            nt = T // ts
            gs = min(T, 512)
            ng = T // gs
            tpg = gs // ts
            pos0 = 0 if kind == "p" else PAST

            with ExitStack() as l0:
                def sb0(name, shape, dt):
                    return l0.enter_context(nc.sbuf_tensor(f"{name}_{kind}{si}", list(shape), dt))

                def mk0(name, shape, dt, n):
                    return [sb0(f"{name}{i}", shape, dt) for i in range(n)]

                gpreab = sb0("gpreab", [128, D], F32); R_gab = Res()
                P.dma("sp", gpreab[:], gpre_ab, writes=[R_gab])
                xnT = sb0("xnT", [128, 8, T], BF16)
                R_xnT = [Res(f"xnT{g}") for g in range(ng)]
                wring = mk0("wr", [128, 8, 512], BF16, 2)
                R_wring = [Res(f"wr{i}") for i in range(2)]
                qTs = mk0("qT", [128, T], BF16, 2); kTs = mk0("kT", [128, T], BF16, 2); gTs = mk0("gT", [128, T], BF16, 2)
                merged = (kind == "p")
                if merged:
                    Vs = mk0("V", [128, nt, 2, 128], BF16, 2)
                else:
                    Vs = mk0("V", [128, nt, 128], BF16, 2)
                R_qs = [[Res() for _ in range(ng)] for _ in range(2)]
                R_ks = [[Res() for _ in range(ng)] for _ in range(2)]
                R_gs = [[Res() for _ in range(ng)] for _ in range(2)]
                R_Vs = [[Res() for _ in range(nt)] for _ in range(2)]
                stg = mk0("stg", [128, 256], F32, 2); R_stg = [Res() for _ in range(2)]
                biasT = sb0("biasT", [128, 8, 640], F32); R_bias = Res("bias")
                Ssb = mk0("Ssb", [128, 640], F32, 2); R_Ssb = [Res() for _ in range(2)]
                PT = mk0("PT", [128, 640], BF16, 2); R_PT = [Res() for _ in range(2)]
                rec2 = mk0("rec", [128, 128], F32, 2); R_rec2 = [Res() for _ in range(2)]
                tmpo2 = mk0("tmpo", [128, 128], F32, 2); R_tmpo2 = [Res() for _ in range(2)]
                EH = [mk0(f"E{h}_", [128, 512], F32, 2) for h in range(2)]; R_EH = [[Res() for _ in range(2)] for _ in range(2)]
                SPH = [mk0(f"SP{h}_", [128, 512], BF16, 2) for h in range(2)]; R_SPH = [[Res() for _ in range(2)] for _ in range(2)]
                ATH = [mk0(f"AT{h}_", [128, 512], BF16, 2) for h in range(2)]; R_ATH = [[Res() for _ in range(2)] for _ in range(2)]
                sgt = sb0("sgt", [128, 512], F32); R_sgt = Res()
                SaccH = mk0("Sacc", [128, 512], F32, 2); R_SaccH = [Res() for _ in range(2)]
                SaccBH = [mk0(f"SaccB{h}_", [128, 512], BF16, 2) for h in range(2)]; R_SaccBH = [[Res() for _ in range(2)] for _ in range(2)]
                nqT = mk0("nq", [128, 512], BF16, 2); R_nq = [Res() for _ in range(2)]
                if kind == "s":
                    kcache = sb0("kcache", [128, 16, 128], BF16); R_kc = Res()
                    kTcs = mk0("kTc", [128, PAST], BF16, 2); R_kTcs = [Res() for _ in range(2)]
                    Vcs = mk0("Vc", [128, 16, 128], BF16, 2); R_Vcs = [Res() for _ in range(2)]

                def load_w(pi):
                    P.dma("pool", wring[pi % 2][:], w_ab[pi].rearrange("(c p) n -> p c n", p=128),
                          writes=[R_wring[pi % 2]])

                load_w(0)
                load_w(1)
                if merged:
                    for s_ in range(2):
                        for ti_ in range(nt):
                            P.op("pool", lambda e, s_=s_, ti_=ti_: e.memset(Vs[s_][:, ti_, :, :], 1.0), writes=[R_Vs[s_][ti_]])
                if kind == "p":
                    for h in range(8):
                        P.dma("sp", biasT[:, h, :], biasP[h], writes=[R_bias])
                    for h in range(8):
                        P.op("pool", lambda e, h=h: e.memset(biasT[0:64, h, 64:128], NEG), writes=[R_bias])
                        P.op("pool", lambda e, h=h: e.memset(biasT[64:128, h, 512:576], NEG), writes=[R_bias])
                else:
                    for h in range(8):
                        P.dma("sp", biasT[:, h, 0:160], biasS[h], writes=[R_bias])

                p0flags = {}

                def gen_phase0():
                    for ti in range(nt):
                        k = cnt["x"]
                        xt = xst[k % 2]; Rxt = R_xst[k % 2]
                        P.dma("sp", xt[:ts, :], x_src(kind, si, ti * ts, ts), writes=[Rxt])
                        g = ti // tpg
                        norm_transpose(xt[:ts, :], Rxt, ts, gpreab, xnT[:, :, ti * ts:(ti + 1) * ts], R_xnT[g],
                                       gain_res=R_gab)
                        if (ti + 1) % tpg == 0:
                            p0flags[g] = True
                        yield


                PJB = 5

                def gen_proj(pi):
                    isA = pi < 4
                    hp = pi % 4
                    st = pi % 2
                    W = wring[st]; RW = R_wring[st]
                    qT, kT, gT, V = qTs[st], kTs[st], gTs[st], Vs[st]
                    R_q, R_k, R_g, R_V = R_qs[st], R_ks[st], R_gs[st], R_Vs[st]
                    if kind == "s":
                        kTc, R_kTc, Vc, R_Vc = kTcs[st], R_kTcs[st], Vcs[st], R_Vcs[st]
                        csrc_k, csrc_v, nck = (ca_k, ca_v, 4) if isA else (cb_k, cb_v, 16)
                        if pi == 0:
                            P.dma("pool", kcache[:, 0:nck, :],
                                  csrc_k[:, hp * 128:(hp + 1) * 128].rearrange("(t p) f -> p t f", p=128), writes=[R_kc])
                        P.dma("pool", Vc[:, 0:nck, :],
                              csrc_v[:, hp * 128:(hp + 1) * 128].rearrange("(t p) f -> p t f", p=128), writes=[R_Vc])
                        for t0 in range(0, nck, 8):
                            nb_ = min(8, nck - t0)
                            for t_ in range(nb_):
                                P.op("pe", lambda e, t_=t_: e.transpose(
                                    out=ptp[:, t_ * 128:(t_ + 1) * 128], in_=kcache[:, t0 + t_, :], identity=ident[:, :]),
                                    reads=[R_kc, R_const], writes=[R_tp])
                            P.op("dve", lambda e: e.tensor_copy(
                                out=kTc[:, t0 * 128:(t0 + nb_) * 128], in_=ptp[:, 0:nb_ * 128]),
                                reads=[R_tp], writes=[R_kTc])
                            yield
                        if pi + 1 < 8:
                            pn = pi + 1
                            nsrc, nn = (ca_k, 4) if pn < 4 else (cb_k, 16)
                            P.dma("pool", kcache[:, 0:nn, :],
                                  nsrc[:, (pn % 4) * 128:(pn % 4 + 1) * 128].rearrange("(t p) f -> p t f", p=128),
                                  writes=[R_kc])
                    pjbanks = [5, 4] if pi == 0 else [5]
                    pjc = [0]

                    def nextpj():
                        b_ = pjbanks[pjc[0] % len(pjbanks)]; pjc[0] += 1
                        return b_
                    for g in range(ng):
                        t0 = g * gs
                        while pi == 0 and not p0flags.get(g):
                            yield
                        for (fc, kindf) in ((0, "q"), (2, "k"), (1, "g")):
                            bi = nextpj()
                            for kc in range(8):
                                P.op("pe", lambda e, kc=kc: e.matmul(
                                    bank(bi)[:, 0:gs], lhsT=W[:, kc, fc * 128:(fc + 1) * 128],
                                    rhs=xnT[:, kc, t0:t0 + gs], start=(kc == 0), stop=(kc == 7)),
                                    reads=[RW, R_xnT[g]], writes=[R_bank[bi]])
                            if kindf == "q":
                                P.op("dve", lambda e: e.tensor_scalar(
                                    out=qT[:, t0:t0 + gs], in0=bank(bi)[:, 0:gs], scalar1=0.125, scalar2=None,
                                    op0=ALU.mult), reads=[R_bank[bi]], writes=[R_q[g]])
                            elif kindf == "k":
                                P.op("dve", lambda e: e.tensor_copy(
                                    out=kT[:, t0:t0 + gs], in_=bank(bi)[:, 0:gs]), reads=[R_bank[bi]], writes=[R_k[g]])
                            else:
                                P.op("act", lambda e: e.activation(out=sgt[:, 0:gs], in_=bank(bi)[:, 0:gs], func=AF.Exp,
                                                                   scale=-1.0), reads=[R_bank[bi]], writes=[R_sgt])
                                P.op("act", lambda e: e.activation(out=sgt[:, 0:gs], in_=sgt[:, 0:gs], func=AF.Ln, bias=1.0),
                                     reads=[R_sgt], writes=[R_sgt])
                                P.op("act", lambda e: e.activation(out=sgt[:, 0:gs], in_=sgt[:, 0:gs], func=AF.Exp,
                                                                   scale=-1.0), reads=[R_sgt], writes=[R_sgt])
                                P.op("dve", lambda e: e.tensor_tensor(out=gT[:, t0:t0 + gs], in0=bank(bi)[:, 0:gs],
                                                                      in1=sgt[:, 0:gs], op=ALU.mult),
                                     reads=[R_bank[bi], R_sgt], writes=[R_g[g]])
                            yield
                        for tt in range(tpg):
                            ti = g * tpg + tt
                            bi = nextpj()
                            for kc in range(8):
                                P.op("pe", lambda e, kc=kc: e.matmul(
                                    bank(bi)[:ts, 0:256], lhsT=xnT[:, kc, ti * ts:(ti + 1) * ts],
                                    rhs=W[:, kc, 256:512], start=(kc == 0), stop=(kc == 7)),
                                    reads=[RW, R_xnT[g]], writes=[R_bank[bi]])
                            if merged:
                                P.op("dve", lambda e: e.tensor_copy(
                                    out=V[:ts, ti, 0, 0:64], in_=bank(bi)[:ts, 128:192]), reads=[R_bank[bi]], writes=[R_V[ti]])
                                P.op("dve", lambda e: e.tensor_copy(
                                    out=V[:ts, ti, 1, 64:128], in_=bank(bi)[:ts, 192:256]), reads=[R_bank[bi]], writes=[R_V[ti]])
                            else:
                                P.op("dve", lambda e: e.tensor_copy(
                                    out=V[:ts, ti, :], in_=bank(bi)[:ts, 128:256]), reads=[R_bank[bi]], writes=[R_V[ti]])
                            if kind == "p":
                                if isA:
                                    need = ti * ts >= S - 512
                                    dk, dv, r0 = o_akp, o_avp, ti * ts - (S - 512)
                                else:
                                    need = True
                                    dk, dv, r0 = o_bkp, o_bvp, ti * ts
                                dk_ap = dk[si, r0:r0 + ts, hp * 128:(hp + 1) * 128] if need else None
                                dv_ap = dv[si, r0:r0 + ts, hp * 128:(hp + 1) * 128] if need else None
                            else:
                                need = True
                                dk, dv = (o_aks, o_avs) if isA else (o_bks, o_bvs)
                                dk_ap = dk[0:ts, hp * 128:(hp + 1) * 128]
                                dv_ap = dv[0:ts, hp * 128:(hp + 1) * 128]
                            if need:
                                sk = cnt["stg"] % 2; cnt["stg"] += 1
                                P.op("dve", lambda e: e.tensor_copy(
                                    out=stg[sk][:ts, :], in_=bank(bi)[:ts, 0:256]),
                                    reads=[R_bank[bi]], writes=[R_stg[sk]])
                                P.dma("pool", dk_ap, stg[sk][:ts, 0:128], reads=[R_stg[sk]], is_output=True)
                                P.dma("pool", dv_ap, stg[sk][:ts, 128:256], reads=[R_stg[sk]], is_output=True)
                            yield

                def gen_attnA(pi):
                    hp = pi % 4
                    st = pi % 2
                    qT, kT, gT, V = qTs[st], kTs[st], gTs[st], Vs[st]
                    R_q, R_k, R_g, R_V = R_qs[st], R_ks[st], R_gs[st], R_Vs[st]
                    if kind == "s":
                        kTc, R_kTc, Vc, R_Vc = kTcs[st], R_kTcs[st], Vcs[st], R_Vcs[st]
                    nqb = T // ts
                    qw = ts
                    its = [(j, hh) for j in range(nqb) for hh in range(2)]

                    def a_blocks(j, hh):
                        po = hh * 64
                        blocks = []
                        if kind == "p":
                            for slot in range(5):
                                kb = j - 4 + slot
                                if kb < 0:
                                    continue
                                blocks.append((kT[po:po + 64, kb * 128:(kb + 1) * 128], 128,
                                               V[:, kb, hh, :], slot,
                                               [R_k[kb // 4]], [R_V[kb]]))
                        else:
                            for slot in range(4):
                                blocks.append((kTc[po:po + 64, slot * 128:(slot + 1) * 128], 128,
                                               Vc[:, slot, hh * 64:(hh + 1) * 64], slot, [R_kTc], [R_Vc]))
                            blocks.append((kT[po:po + 64, 0:TS], TS, V[0:TS, 0, hh * 64:(hh + 1) * 64], 4,
                                           [R_k[0]], [R_V[0]]))
                        return blocks

                    def a_stage1(n):
                        j, hh = its[n]
                        h = hp * 2 + hh
                        po = hh * 64
                        sbk = n % 2
                        RS = [R_bank[2 * sbk], R_bank[2 * sbk + 1]]
                        Sps = pbank[sbk]
                        blocks = a_blocks(j, hh)
                        q_ap = qT[po:po + 64, j * qw:(j + 1) * qw]
                        gq = (j * qw) // gs
                        for (k_ap, nk, v_ap, slot, rk, rv) in blocks:
                            P.op("pe", lambda e, k_ap=k_ap, nk=nk, slot=slot: e.matmul(
                                Sps[:nk, slot * qw:(slot + 1) * qw], lhsT=k_ap, rhs=q_ap, start=True, stop=True),
                                reads=rk + [R_q[gq]], writes=RS)
                        s0 = blocks[0][3]
                        full = [b for b in blocks if b[1] == 128]
                        part = [b for b in blocks if b[1] != 128]
                        sk = n % 2
                        lo, hi = s0 * qw, (full[-1][3] + 1) * qw
                        P.op("dve", lambda e: e.tensor_tensor(
                            out=Ssb[sk][:, lo:hi], in0=Sps[:, lo:hi], in1=biasT[:, h, lo:hi], op=ALU.add),
                            reads=RS + [R_bias], writes=[R_Ssb[sk]])
                        P.op("act", lambda e: e.activation(
                            out=PT[sk][:, lo:hi], in_=Ssb[sk][:, lo:hi], func=AF.Exp),
                            reads=[R_Ssb[sk]], writes=[R_PT[sk]])
                        for (k_ap, nk, v_ap, slot, rk, rv) in part:
                            lo2, hi2 = slot * qw, (slot + 1) * qw
                            P.op("dve", lambda e, lo2=lo2, hi2=hi2, nk=nk: e.tensor_tensor(
                                out=Ssb[sk][:nk, lo2:hi2], in0=Sps[:nk, lo2:hi2], in1=biasT[:nk, h, lo2:hi2],
                                op=ALU.add), reads=RS + [R_bias], writes=[R_Ssb[sk]])
                            P.op("act", lambda e, lo2=lo2, hi2=hi2, nk=nk: e.activation(
                                out=PT[sk][:nk, lo2:hi2], in_=Ssb[sk][:nk, lo2:hi2], func=AF.Exp),
                                reads=[R_Ssb[sk]], writes=[R_PT[sk]])

                    def a_stage2m(n):
                        j, hh = its[n]
                        po = hh * 64
                        pd = 64 - po
                        sk = n % 2
                        odb = 4
                        OD = bank(4)[:, (n % 2) * 128:(n % 2) * 128 + 128]
                        blocks = a_blocks(j, hh)
                        nb = len(blocks)
                        gq = (j * qw) // gs
                        for bi_, (k_ap, nk, v_ap, slot, rk, rv) in enumerate(blocks):
                            P.op("pe", lambda e, v_ap=v_ap, nk=nk, slot=slot, bi_=bi_: e.matmul(
                                OD[:, 0:qw], lhsT=v_ap, rhs=PT[sk][:nk, slot * qw:(slot + 1) * qw],
                                start=(bi_ == 0), stop=(bi_ == nb - 1)),
                                reads=rv + [R_PT[sk]], writes=[R_bank[odb]])
                        rc = rec2[hh]; Rrc = R_rec2[hh]
                        tm = tmpo2[hh]; Rtm = R_tmpo2[hh]
                        P.op("act", lambda e: e.activation(
                            out=rc[po:po + 64, 0:qw], in_=OD[pd:pd + 64, 0:qw], func=AF.Ln),
                            reads=[R_bank[odb]], writes=[Rrc])
                        P.op("act", lambda e: e.activation(
                            out=rc[po:po + 64, 0:qw], in_=rc[po:po + 64, 0:qw], func=AF.Exp, scale=-1.0),
                            reads=[Rrc], writes=[Rrc])
                        P.op("dve", lambda e: e.tensor_tensor(
                            out=tm[po:po + 64, 0:qw], in0=OD[po:po + 64, 0:qw], in1=rc[po:po + 64, 0:qw],
                            op=ALU.mult), reads=[R_bank[odb], Rrc], writes=[Rtm])
                        P.op("pool", lambda e: e.tensor_tensor(
                            out=oT[po:po + 64, pi, j * qw:(j + 1) * qw], in0=tm[po:po + 64, 0:qw],
                            in1=gT[po:po + 64, j * qw:(j + 1) * qw], op=ALU.mult),
                            reads=[Rtm, R_g[gq]], writes=[R_oT[pi][gq]])

                    def a_stage2(n):
                        if merged:
                            return a_stage2m(n)
                        j, hh = its[n]
                        po = hh * 64
                        sk = n % 2
                        odb = 4
                        OD = bank(odb)
                        blocks = a_blocks(j, hh)
                        nb = len(blocks)
                        gq = (j * qw) // gs
                        for bi_, (k_ap, nk, v_ap, slot, rk, rv) in enumerate(blocks):
                            P.op("pe", lambda e, v_ap=v_ap, nk=nk, slot=slot, bi_=bi_: e.matmul(
                                OD[po:po + 64, 0:qw], lhsT=v_ap, rhs=PT[sk][:nk, slot * qw:(slot + 1) * qw],
                                start=(bi_ == 0), stop=(bi_ == nb - 1)),
                                reads=rv + [R_PT[sk]], writes=[R_bank[odb]])
                        for bi_, (k_ap, nk, v_ap, slot, rk, rv) in enumerate(blocks):
                            P.op("pe", lambda e, nk=nk, slot=slot, bi_=bi_: e.matmul(
                                OD[po:po + 64, 128:128 + qw], lhsT=ones[:nk, 0:64],
                                rhs=PT[sk][:nk, slot * qw:(slot + 1) * qw],
                                start=(bi_ == 0), stop=(bi_ == nb - 1)),
                                reads=[R_PT[sk], R_const], writes=[R_bank[odb]])
                        rc = rec2[hh]; Rrc = R_rec2[hh]
                        tm = tmpo2[hh]; Rtm = R_tmpo2[hh]
                        P.op("act", lambda e: e.activation(
                            out=rc[po:po + 64, 0:qw], in_=OD[po:po + 64, 128:128 + qw], func=AF.Ln),
                            reads=[R_bank[odb]], writes=[Rrc])
                        P.op("act", lambda e: e.activation(
                            out=rc[po:po + 64, 0:qw], in_=rc[po:po + 64, 0:qw], func=AF.Exp, scale=-1.0),
                            reads=[Rrc], writes=[Rrc])
                        P.op("dve", lambda e: e.tensor_tensor(
                            out=tm[po:po + 64, 0:qw], in0=OD[po:po + 64, 0:qw], in1=rc[po:po + 64, 0:qw],
                            op=ALU.mult), reads=[R_bank[odb], Rrc], writes=[Rtm])
                        P.op("pool", lambda e: e.tensor_tensor(
                            out=oT[po:po + 64, pi, j * qw:(j + 1) * qw], in0=tm[po:po + 64, 0:qw],
                            in1=gT[po:po + 64, j * qw:(j + 1) * qw], op=ALU.mult),
                            reads=[Rtm, R_g[gq]], writes=[R_oT[pi][gq]])

                    a_stage1(0)
                    yield
                    for n in range(len(its)):
                        if n + 1 < len(its):
                            a_stage1(n + 1)
                            yield
                        a_stage2(n)
                        yield

                def gen_attnB(pi):
                    st = pi % 2
                    qT, kT, gT, V = qTs[st], kTs[st], gTs[st], Vs[st]
                    R_q, R_k, R_g, R_V = R_qs[st], R_ks[st], R_gs[st], R_Vs[st]
                    if kind == "s":
                        kTc, R_kTc, Vc, R_Vc = kTcs[st], R_kTcs[st], Vcs[st], R_Vcs[st]
                    cw = gs
                    ob = 4
                    OB = bank(ob)
                    dq = min(128, cw)
                    for c in range(ng):
                        steps = [[], []]
                        for hh in range(2):
                            po = hh * 64
                            P.op("dve", lambda e: e.tensor_scalar(
                                out=nqT[hh][po:po + 64, 0:cw], in0=qT[po:po + 64, c * cw:(c + 1) * cw],
                                scalar1=-1.0, scalar2=None, op0=ALU.mult), reads=[R_q[c]], writes=[R_nq[hh]])
                            P.op("pool", lambda e: e.memset(SaccH[hh][:, 0:cw], 0.0), writes=[R_SaccH[hh]])
                            P.op("pool", lambda e: e.memset(SaccBH[hh][0][:, 0:cw], 0.0), writes=[R_SaccBH[hh][0]])
                            if kind == "p":
                                for kb in range(4 * c + 3, -1, -1):
                                    q0 = max(0, kb * 128 - c * 512)
                                    steps[hh].append((kT[po:po + 64, kb * 128:(kb + 1) * 128], 128,
                                                      V[:, kb, hh, hh * 64:(hh + 1) * 64], q0, kb >= 4 * c,
                                                      [R_k[kb // 4]], [R_V[kb]]))
                            else:
                                steps[hh].append((kT[po:po + 64, 0:TS], TS, V[0:TS, 0, hh * 64:(hh + 1) * 64], 0, True,
                                                  [R_k[0]], [R_V[0]]))
                                for kb in range(15, -1, -1):
                                    steps[hh].append((kTc[po:po + 64, kb * 128:(kb + 1) * 128], 128,
                                                      Vc[:, kb, hh * 64:(hh + 1) * 64], 0, False, [R_kTc], [R_Vc]))
                        ns = len(steps[0])

                        def stage1(i):
                            for hh in range(2):
                                po = hh * 64
                                k_ap, nk, v_ap, q0, diag, rk, rv = steps[hh][i]
                                z = bank(hh); Rz = R_bank[hh]
                                P.op("pe", lambda e: e.matmul(z[:nk, q0:cw], lhsT=k_ap,
                                                              rhs=qT[po:po + 64, c * cw + q0:(c + 1) * cw],
                                                              start=True, stop=True),
                                     reads=rk + [R_q[c]], writes=[Rz])
                            for hh in range(2):
                                k_ap, nk, v_ap, q0, diag, rk, rv = steps[hh][i]
                                z = bank(hh); Rz = R_bank[hh]
                                E = EH[hh][i % 2]; RE = R_EH[hh][i % 2]
                                P.op("act", lambda e: e.activation(out=E[:nk, q0:cw], in_=z[:nk, q0:cw], func=AF.Exp),
                                     reads=[Rz], writes=[RE])
                                if diag:
                                    P.op("dve", lambda e: e.tensor_tensor(
                                        out=E[:nk, q0:q0 + dq], in0=E[:nk, q0:q0 + dq], in1=lmask[:nk, 0:dq],
                                        op=ALU.mult), reads=[RE, R_const], writes=[RE])
                            for hh in range(2):
                                k_ap, nk, v_ap, q0, diag, rk, rv = steps[hh][i]
                                E = EH[hh][i % 2]; RE = R_EH[hh][i % 2]
                                SP = SPH[hh][i % 2]; RSP = R_SPH[hh][i % 2]
                                P.op("act", lambda e: e.activation(out=SP[:nk, q0:cw], in_=E[:nk, q0:cw], func=AF.Ln,
                                                                   bias=1.0), reads=[RE], writes=[RSP])

                        def stage2(i):
                            for hh in range(2):
                                k_ap, nk, v_ap, q0, diag, rk, rv = steps[hh][i]
                                cps = bank(2 + hh); Rc = R_bank[2 + hh]
                                SP = SPH[hh][i % 2]; RSP = R_SPH[hh][i % 2]
                                P.op("pe", lambda e: e.matmul(cps[:nk, q0:cw], lhsT=tri[:nk, :nk], rhs=SP[:nk, q0:cw],
                                                              start=True, stop=False),
                                     reads=[RSP, R_const], writes=[Rc])
                                P.op("pe", lambda e: e.matmul(cps[:nk, q0:cw], lhsT=ones[:, :nk],
                                                              rhs=SaccBH[hh][i % 2][:, q0:cw], start=False, stop=False),
                                     reads=[R_SaccBH[hh][i % 2], R_const], writes=[Rc])
                            for hh in range(2):
                                po = hh * 64
                                k_ap, nk, v_ap, q0, diag, rk, rv = steps[hh][i]
                                cps = bank(2 + hh); Rc = R_bank[2 + hh]
                                P.op("pe", lambda e: e.matmul(cps[:nk, q0:cw], lhsT=k_ap,
                                                              rhs=nqT[hh][po:po + 64, q0:cw], start=False, stop=True),
                                     reads=rk + [R_nq[hh]], writes=[Rc])
                            if i + 1 < ns:
                                for hh in range(2):
                                    k_ap, nk, v_ap, q0, diag, rk, rv = steps[hh][i]
                                    SP = SPH[hh][i % 2]; RSP = R_SPH[hh][i % 2]
                                    P.op("dve", lambda e: e.tensor_tensor(
                                        out=SaccH[hh][:nk, q0:cw], in0=SaccH[hh][:nk, q0:cw], in1=SP[:nk, q0:cw], op=ALU.add),
                                        reads=[RSP, R_SaccH[hh]], writes=[R_SaccH[hh]])
                                    P.op("dve", lambda e: e.tensor_copy(out=SaccBH[hh][(i + 1) % 2][:, 0:cw],
                                                                        in_=SaccH[hh][:, 0:cw]),
                                         reads=[R_SaccH[hh]], writes=[R_SaccBH[hh][(i + 1) % 2]])
                            for hh in range(2):
                                k_ap, nk, v_ap, q0, diag, rk, rv = steps[hh][i]
                                cps = bank(2 + hh); Rc = R_bank[2 + hh]
                                AT = ATH[hh][i % 2]; RAT = R_ATH[hh][i % 2]
                                P.op("act", lambda e: e.activation(out=AT[:nk, q0:cw], in_=cps[:nk, q0:cw], func=AF.Exp,
                                                                   scale=-1.0), reads=[Rc], writes=[RAT])
                                if q0 > 0:
                                    P.op("pool", lambda e: e.memset(AT[:nk, 0:q0], 0.0), writes=[RAT])
                                if diag:
                                    P.op("dve", lambda e: e.tensor_tensor(
                                        out=AT[:nk, q0:q0 + dq], in0=AT[:nk, q0:q0 + dq], in1=lmask[:nk, 0:dq],
                                        op=ALU.mult), reads=[RAT, R_const], writes=[RAT])

                        def stage3(i):
                            for hh in range(2):
                                po = hh * 64
                                k_ap, nk, v_ap, q0, diag, rk, rv = steps[hh][i]
                                AT = ATH[hh][i % 2]; RAT = R_ATH[hh][i % 2]
                                P.op("pe", lambda e: e.matmul(OB[po:po + 64, 0:cw], lhsT=v_ap, rhs=AT[:nk, 0:cw],
                                                              start=(i == 0), stop=(i == ns - 1)),
                                     reads=rv + [RAT], writes=[R_bank[ob]])

                        stage1(0)
                        yield
                        for i in range(ns):
                            if i + 1 < ns:
                                stage1(i + 1)
                            stage2(i)
                            if i > 0:
                                stage3(i - 1)
                            yield
                        stage3(ns - 1)
                        P.op("dve", lambda e: e.tensor_tensor(
                            out=oT[:, pi, c * cw:(c + 1) * cw], in0=OB[:, 0:cw],
                            in1=gT[:, c * cw:(c + 1) * cw], op=ALU.mult),
                            reads=[R_bank[ob], R_g[c]], writes=[R_oT[pi][c]])
                        yield

                def run_weighted(ga, na, gb, nb_):
                    da = db = 0
                    a_alive, b_alive = True, gb is not None
                    while a_alive or b_alive:
                        pick_b = b_alive and (not a_alive or (db + 1) * na <= (da + 1) * nb_)
                        if pick_b:
                            try:
                                next(gb); db += 1
                            except StopIteration:
                                b_alive = False
                        else:
                            try:
                                next(ga); da += 1
                            except StopIteration:
                                a_alive = False

                n_proj = ng * (3 + tpg) + (3 if kind == "s" else 0)
                gp0, gj0 = gen_phase0(), gen_proj(0)
                alive = [gp0, gj0]
                while alive:
                    for s_ in list(alive):
                        try:
                            next(s_)
                        except StopIteration:
                            alive.remove(s_)
                for pi in range(8):
                    if pi + 2 < 8:
                        load_w(pi + 2)
                    isA = pi < 4
                    if isA:
                        ga = gen_attnA(pi); na = 2 * (T // ts) * 2
                    else:
                        ga = gen_attnB(pi)
                        na = sum((4 * c + 4 + 2) for c in range(ng)) if kind == "p" else 19
                    gb = gen_proj(pi + 1) if pi + 1 < 8 else None
                    run_weighted(ga, na, gb, n_proj)
            P.barrier()

            with ExitStack() as l1:
                def sb1(name, shape, dt):
                    return l1.enter_context(nc.sbuf_tensor(f"{name}_{kind}{si}", list(shape), dt))

                gs1 = min(T, 256)
                ng1 = T // gs1
                tpg1 = gs1 // ts
                qw = ts
                woab = sb1("woab", [128, 8, D], BF16); R_woab = Res()
                wc = sb1("wc", [128, 8, 2560], BF16); R_wcq = [Res() for _ in range(4)]
                woc = sb1("woc", [128, 8, D], BF16); R_woc = Res()
                gpostab = sb1("gpostab", [128, D], F32); gprec = sb1("gprec", [128, D], F32)
                gpostc = sb1("gpostc", [128, D], F32); R_gn = Res()
                for t_, src in ((gpostab, gpost_ab), (gprec, gpre_c), (gpostc, gpost_c)):
                    P.dma("sp", t_[:], src, writes=[R_gn])
                P.dma("pool", woab[:], w_oab.rearrange("(c p) n -> p c n", p=128), writes=[R_woab])
                for q4 in range(4):
                    P.dma("pool", wc[:, :, q4 * 640:(q4 + 1) * 640],
                          w_c[:, q4 * 640:(q4 + 1) * 640].rearrange("(c p) n -> p c n", p=128), writes=[R_wcq[q4]])
                P.dma("pool", woc[:], w_oc.rearrange("(c p) n -> p c n", p=128), writes=[R_woc])

                def mk(name, shape, dt, n):
                    return [sb1(f"{name}{i}", shape, dt) for i in range(n)], [Res() for _ in range(n)]

                Y0, R_Y0 = mk("Y0", [128, tpg1, D], F32, 2)
                t1b, R_t1 = mk("t1b", [128, D], F32, 2)
                statY, R_statY = mk("statY", [128, 4], F32, 2)
                xn1T, R_xn1 = mk("xn1T", [128, 8, gs1], BF16, 1)
                xn1T, R_xn1 = xn1T * 2, R_xn1 * 2
                qbf, R_qbf = mk("qbf", [128, gs1], BF16, 2)
                qr, R_qr = mk("qr", [128, 8, gs1], BF16, 2)
                kbf, R_kbf = mk("kbf", [128, 2, gs1], BF16, 1)
                kr, R_kr = mk("kr", [128, 4, 128 + gs1], BF16, 2)
                g1, R_g1 = mk("g1", [128, 8, gs1], BF16, 2)
                V1, R_V1 = mk("V1", [128, 1 + tpg1, 256], BF16, 2)
                cosg, R_cos = mk("cosg", [128, gs1], F32, 1)
                sing, R_sin = mk("sing", [128, gs1], F32, 1)
                cosg, R_cos, sing, R_sin = cosg * 2, R_cos * 2, sing * 2, R_sin * 2
                ta, R_ta = mk("ta", [128, 256], F32, 2)
                tb, R_tb = mk("tb", [128, 256], F32, 2)
                tcb, R_tcb = mk("tcb", [128, 256], F32, 2)
                PTc, R_PTc = mk("PTc", [128, 512], BF16, 4)
                recc, R_recc = mk("recc", [128, 256], F32, 1)
                tmpc, R_tmpc = mk("tmpc", [128, 256], F32, 1)
                recc, R_recc, tmpc, R_tmpc = recc * 2, R_recc * 2, tmpc * 2, R_tmpc * 2
                kst = sb1("kst", [128, 256], F32); R_kst = Res()
                ksw = sb1("ksw", [128, 256], F32); R_ksw = Res()
                ctm = sb1("ctm", [128, 256], F32); stm = sb1("stm", [128, 256], F32); R_ctm = Res()
                kvst = sb1("kvst", [128, 512], F32); R_kvst = Res()
                if kind == "s":
                    kcc = sb1("kcc", [128, 256], BF16); R_kcc = Res()
                    kcd = sb1("kcd", [128, 4, 128], BF16); R_kcd = Res()
                    krc = sb1("krc", [128, 4, 128], BF16); R_krc = Res()
                    Vcc = sb1("Vcc", [128, 256], BF16); R_Vcc = Res()
                    P.dma("pool", kcc[:], cc_k, writes=[R_kcc])
                    P.dma("pool", Vcc[:], cc_v, writes=[R_Vcc])
                    for a in range(4):
                        for d2 in range(2):
                            P.op("pool", lambda e, a=a, d2=d2: e.tensor_copy(
                                out=kcd[:, a, d2 * 64:(d2 + 1) * 64], in_=kcc[:, a * 64:(a + 1) * 64]),
                                reads=[R_kcc], writes=[R_kcd])
                    for a in range(4):
                        P.op("pe", lambda e, a=a: e.transpose(out=ptp[:, a * 128:(a + 1) * 128], in_=kcd[:, a, :],
                                                              identity=ident[:, :]),
                             reads=[R_kcd, R_const], writes=[R_tp])
                    for a in range(4):
                        P.op("dve", lambda e, a=a: e.tensor_copy(out=krc[:, a, :], in_=ptp[:, a * 128:(a + 1) * 128]),
                             reads=[R_tp], writes=[R_krc])
                lt0 = pos0 + T - ts
                P.dma("sp", ctm[:ts, :], c_cosTM[lt0:lt0 + ts, :], writes=[R_ctm])
                P.dma("sp", stm[:ts, :], c_sinTM[lt0:lt0 + ts, :], writes=[R_ctm])

                yc = {"n": 0, "bx": 0, "t": 0}
                LAG_A = 1
                XB = [0, 1, 2]
                YB = [3, 4, 5]

                def nbx():
                    b_ = XB[yc["bx"] % 3]; yc["bx"] += 1
                    return b_

                def post_norm_residual(bk0, bk1, gain, res_ap, res_r, out_ap, out_r):
                    k = yc["t"]; yc["t"] += 1
                    st = statY[k % 2]; Rst = R_statY[k % 2]
                    jk = junk2[k % 2]; Rjk = R_junk2[k % 2]
                    for half, bk in enumerate((bk0, bk1)):
                        P.op("act", lambda e, half=half, bk=bk: e.activation(
                            out=jk[:ts, half * 512:(half + 1) * 512], in_=bank(bk)[:ts, :], func=AF.Square,
                            accum_out=st[:ts, half:half + 1]), reads=[R_bank[bk]], writes=[Rjk, Rst])
                    P.op("dve", lambda e: e.tensor_tensor(out=st[:ts, 2:3], in0=st[:ts, 0:1], in1=st[:ts, 1:2],
                                                          op=ALU.add), reads=[Rst], writes=[Rst])
                    rstd_from(st[:ts, 2:3], st[:ts, 3:4], ts, [Rst], [Rst])
                    for half, bk in enumerate((bk0, bk1)):
                        P.op("dve", lambda e, half=half, bk=bk: e.scalar_tensor_tensor(
                            out=out_ap[:, half * 512:(half + 1) * 512], in0=bank(bk)[:ts, :], scalar=st[:ts, 3:4],
                            in1=gain[:ts, half * 512:(half + 1) * 512], op0=ALU.mult, op1=ALU.mult),
                            reads=[R_bank[bk], Rst, R_gn], writes=[out_r])
                    P.op("pool", lambda e: e.tensor_tensor(out=out_ap, in0=out_ap, in1=res_ap, op=ALU.add),
                         reads=[res_r, out_r], writes=[out_r])

                def gen_a(g):
                    gb = g % 2
                    t0 = g * gs1
                    g0 = t0 // gs
                    P.dma("sp", cosg[gb][:, :], c_cosT[:, pos0 + t0:pos0 + t0 + gs1], writes=[R_cos[gb]])
                    P.dma("sp", sing[gb][:, :], c_sinT[:, pos0 + t0:pos0 + t0 + gs1], writes=[R_sin[gb]])
                    for tt in range(tpg1):
                        ti = g * tpg1 + tt
                        k = cnt["x"]
                        xt = xst[k % 2]; Rxt = R_xst[k % 2]
                        P.dma("sp", xt[:ts, :], x_src(kind, si, ti * ts, ts), writes=[Rxt])
                        bks = (nbx(), nbx())
                        for half in range(2):
                            for c in range(8):
                                P.op("pe", lambda e, c=c, half=half: e.matmul(
                                    bank(bks[half])[:ts, :], lhsT=oT[:, c, ti * ts:(ti + 1) * ts],
                                    rhs=woab[:, c, half * 512:(half + 1) * 512], start=(c == 0), stop=(c == 7)),
                                    reads=[R_oT[c][g0], R_woab], writes=[R_bank[bks[half]]])
                        yield
                        post_norm_residual(bks[0], bks[1], gpostab, xt[:ts, :], Rxt, Y0[gb][:ts, tt, :], R_Y0[gb])
                        kk = norm_part1(Y0[gb][:ts, tt, :], R_Y0[gb], ts, gprec, gain_res=R_gn)
                        for _ in range(LAG_A):
                            yield
                        norm_part2(kk, ts, xn1T[gb][:, :, tt * ts:(tt + 1) * ts], R_xn1[gb])
                        yield

                def gen_b(g):
                    gb = g % 2
                    xn = xn1T[gb]; Rxn = R_xn1[gb]
                    cs, sn = cosg[gb], sing[gb]
                    if g > 0:
                        P.op("pool", lambda e: e.tensor_copy(out=kr[gb][:, :, 0:128], in_=kr[1 - gb][:, :, gs1:gs1 + 128]),
                             reads=[R_kr[1 - gb]], writes=[R_kr[gb]])
                        P.op("pool", lambda e: e.tensor_copy(out=V1[gb][:, 0, :], in_=V1[1 - gb][:, tpg1, :]),
                             reads=[R_V1[1 - gb]], writes=[R_V1[gb]])
                    for fc in range(8):
                        b1 = nbx()
                        for kc in range(8):
                            P.op("pe", lambda e, kc=kc: e.matmul(
                                bank(b1)[:, 0:gs1], lhsT=wc[:, kc, fc * 128:(fc + 1) * 128], rhs=xn[:, kc, :],
                                start=(kc == 0), stop=(kc == 7)), reads=[R_wcq[(fc * 128) // 640], Rxn], writes=[R_bank[b1]])
                        s2 = fc % 2
                        P.op("act", lambda e: e.activation(out=qbf[s2][:, :], in_=bank(b1)[:, 0:gs1], func=AF.Copy, scale=0.125),
                             reads=[R_bank[b1]], writes=[R_qbf[s2]])
                        P.op("dve", lambda e: e.scalar_tensor_tensor(
                            out=ta[s2][:, 0:gs1], in0=bank(b1)[:, 0:gs1], scalar=0.125, in1=cs[:, :], op0=ALU.mult,
                            op1=ALU.mult), reads=[R_bank[b1], R_cos[gb]], writes=[R_ta[s2]])
                        b2 = nbx()
                        P.op("pe", lambda e: e.matmul(bank(b2)[:, 0:gs1], lhsT=rot[:, :], rhs=qbf[s2][:, :], start=True, stop=True),
                             reads=[R_qbf[s2], R_const], writes=[R_bank[b2]])
                        P.op("dve", lambda e: e.tensor_tensor(out=tb[s2][:, 0:gs1], in0=bank(b2)[:, 0:gs1], in1=sn[:, :],
                                                              op=ALU.mult), reads=[R_bank[b2], R_sin[gb]], writes=[R_tb[s2]])
                        P.op("pool", lambda e: e.tensor_tensor(out=qr[gb][:, fc, :], in0=ta[s2][:, 0:gs1], in1=tb[s2][:, 0:gs1],
                                                               op=ALU.add), reads=[R_ta[s2], R_tb[s2]], writes=[R_qr[gb]])
                        yield
                def gen_b2(g):
                    gb = g % 2
                    xn = xn1T[gb]; Rxn = R_xn1[gb]
                    cs, sn = cosg[gb], sing[gb]
                    for kc2 in range(2):
                        b1 = nbx()
                        for kc in range(8):
                            P.op("pe", lambda e, kc=kc: e.matmul(
                                bank(b1)[:, 0:gs1], lhsT=wc[:, kc, 1024 + kc2 * 128:1024 + (kc2 + 1) * 128],
                                rhs=xn[:, kc, :], start=(kc == 0), stop=(kc == 7)),
                                reads=[R_wcq[1], Rxn], writes=[R_bank[b1]])
                        P.op("act", lambda e: e.activation(out=kbf[0][:, kc2, :], in_=bank(b1)[:, 0:gs1], func=AF.Copy),
                             reads=[R_bank[b1]], writes=[R_kbf[0]])
                    yield
                    for a in range(4):
                        s2 = a % 2
                        b1 = nbx()
                        P.op("pe", lambda e: e.matmul(bank(b1)[:, 0:gs1], lhsT=dsel[:, a % 2, :], rhs=kbf[0][:, a // 2, :],
                                                      start=True, stop=True),
                             reads=[R_kbf[0], R_const], writes=[R_bank[b1]])
                        b2 = nbx()
                        P.op("pe", lambda e: e.matmul(bank(b2)[:, 0:gs1], lhsT=dselrot[:, a % 2, :], rhs=kbf[0][:, a // 2, :],
                                                      start=True, stop=True),
                             reads=[R_kbf[0], R_const], writes=[R_bank[b2]])
                        P.op("dve", lambda e: e.tensor_tensor(out=ta[s2][:, 0:gs1], in0=bank(b1)[:, 0:gs1], in1=cs[:, :],
                                                              op=ALU.mult), reads=[R_bank[b1], R_cos[gb]], writes=[R_ta[s2]])
                        P.op("dve", lambda e: e.tensor_tensor(out=tb[s2][:, 0:gs1], in0=bank(b2)[:, 0:gs1], in1=sn[:, :],
                                                              op=ALU.mult), reads=[R_bank[b2], R_sin[gb]], writes=[R_tb[s2]])
                        P.op("pool", lambda e: e.tensor_tensor(
                            out=kr[gb][:, a, 128:128 + gs1], in0=ta[s2][:, 0:gs1], in1=tb[s2][:, 0:gs1], op=ALU.add),
                            reads=[R_ta[s2], R_tb[s2]], writes=[R_kr[gb]])
                        yield
                    for fc in range(8):
                        b1 = nbx()
                        for kc in range(8):
                            P.op("pe", lambda e, kc=kc: e.matmul(
                                bank(b1)[:, 0:gs1], lhsT=wc[:, kc, 1536 + fc * 128:1536 + (fc + 1) * 128],
                                rhs=xn[:, kc, :], start=(kc == 0), stop=(kc == 7)),
                                reads=[R_wcq[(1536 + fc * 128) // 640], Rxn], writes=[R_bank[b1]])
                        s2 = fc % 2
                        P.op("act", lambda e: e.activation(out=tcb[s2][:, 0:gs1], in_=bank(b1)[:, 0:gs1], func=AF.Exp,
                                                           scale=-1.0), reads=[R_bank[b1]], writes=[R_tcb[s2]])
                        P.op("act", lambda e: e.activation(out=tcb[s2][:, 0:gs1], in_=tcb[s2][:, 0:gs1], func=AF.Ln, bias=1.0),
                             reads=[R_tcb[s2]], writes=[R_tcb[s2]])
                        P.op("act", lambda e: e.activation(out=tcb[s2][:, 0:gs1], in_=tcb[s2][:, 0:gs1], func=AF.Exp,
                                                           scale=-1.0), reads=[R_tcb[s2]], writes=[R_tcb[s2]])
                        P.op("dve", lambda e: e.tensor_tensor(out=g1[gb][:, fc, :], in0=bank(b1)[:, 0:gs1],
                                                              in1=tcb[s2][:, 0:gs1], op=ALU.mult),
                             reads=[R_bank[b1], R_tcb[s2]], writes=[R_g1[gb]])
                        yield
                    for tt in range(tpg1):
                        ti = g * tpg1 + tt
                        b1 = nbx()
                        for kc in range(8):
                            P.op("pe", lambda e, kc=kc: e.matmul(
                                bank(b1)[:ts, :], lhsT=xn[:, kc, tt * ts:(tt + 1) * ts], rhs=wc[:, kc, 1024:1536],
                                start=(kc == 0), stop=(kc == 7)), reads=[R_wcq[1], R_wcq[2], Rxn], writes=[R_bank[b1]])
                        P.op("dve", lambda e: e.tensor_copy(out=V1[gb][:ts, 1 + tt, :], in_=bank(b1)[:ts, 256:512]),
                             reads=[R_bank[b1]], writes=[R_V1[gb]])
                        if ti == nt - 1:
                            ysk = kvst; Rysk = R_kvst
                            P.op("dve", lambda e: e.tensor_copy(out=ysk[:ts, 0:256], in_=bank(b1)[:ts, 256:512]),
                                 reads=[R_bank[b1]], writes=[Rysk])
                            dv_ap = o_cvp[si, :, :] if kind == "p" else o_cvs[:, :]
                            dk_ap = o_ckp[si, :, :] if kind == "p" else o_cks[:, :]
                            P.dma("pool", dv_ap, ysk[:ts, 0:256], reads=[Rysk], is_output=True)
                            P.op("dve", lambda e: e.tensor_copy(out=kst[:ts, :], in_=bank(b1)[:ts, 0:256]),
                                 reads=[R_bank[b1]], writes=[R_kst])
                            for hk in range(4):
                                for b2_ in range(2):
                                    P.op("dve", lambda e, hk=hk, b2_=b2_: e.tensor_copy(
                                        out=ksw[:ts, hk * 64 + b2_ * 32:hk * 64 + b2_ * 32 + 32],
                                        in_=kst[:ts, hk * 64 + (1 - b2_) * 32:hk * 64 + (1 - b2_) * 32 + 32]),
                                        reads=[R_kst], writes=[R_ksw])
                            P.op("dve", lambda e: e.tensor_tensor(out=kst[:ts, :], in0=kst[:ts, :], in1=ctm[:ts, :],
                                                                  op=ALU.mult), reads=[R_kst, R_ctm], writes=[R_kst])
                            P.op("dve", lambda e: e.tensor_tensor(out=ksw[:ts, :], in0=ksw[:ts, :], in1=stm[:ts, :],
                                                                  op=ALU.mult), reads=[R_ksw, R_ctm], writes=[R_ksw])
                            P.op("dve", lambda e: e.tensor_tensor(out=ysk[:ts, 256:512], in0=kst[:ts, :],
                                                                  in1=ksw[:ts, :], op=ALU.add),
                                 reads=[R_kst, R_ksw], writes=[Rysk])
                            P.dma("pool", dk_ap, ysk[:ts, 256:512], reads=[Rysk], is_output=True)
                        yield

                def c_blocks(g, j, a):
                    gb = g % 2
                    J = g * tpg1 + j
                    blocks = []
                    if kind == "p":
                        if J > 0:
                            blocks.append((kr[gb][:, a, j * 128:(j + 1) * 128], 128,
                                           V1[gb][:, j, a * 64:(a + 1) * 64], "prev", [R_kr[gb]], [R_V1[gb]]))
                        blocks.append((kr[gb][:, a, (j + 1) * 128:(j + 2) * 128], 128,
                                       V1[gb][:, j + 1, a * 64:(a + 1) * 64], "diag", [R_kr[gb]], [R_V1[gb]]))
                    else:
                        blocks.append((krc[:, a, :], 128, Vcc[:, a * 64:(a + 1) * 64], "c", [R_krc], [R_Vcc]))
                        blocks.append((kr[gb][:, a, 128:128 + TS], TS, V1[gb][0:TS, 1, a * 64:(a + 1) * 64], "n",
                                       [R_kr[gb]], [R_V1[gb]]))
                    return blocks

                def c_stage1(g, n):
                    gb = g % 2
                    j, a = n // 4, n % 4
                    blocks = c_blocks(g, j, a)
                    for par in range(2):
                        sbk = YB[par]
                        Sps = bank(sbk)
                        po = par * 64
                        pt = PTc[(n % 2) * 2 + par]; Rpt = R_PTc[(n % 2) * 2 + par]
                        for bi_, (k_ap, nk, v_ap, tag, rk, rv) in enumerate(blocks):
                            if qw == 128:
                                col = bi_ * 2 * qw
                                P.op("pe", lambda e, k_ap=k_ap, nk=nk, col=col: e.matmul(
                                    Sps[:nk, col:col + 2 * qw].rearrange("p (h q) -> p h q", h=2),
                                    lhsT=k_ap[po:po + 64, :],
                                    rhs=qr[gb][po:po + 64, 2 * a:2 * a + 2, j * qw:(j + 1) * qw],
                                    start=True, stop=True), reads=rk + [R_qr[gb]], writes=[R_bank[sbk]])
                                continue
                            for hi in range(2):
                                fc = 2 * a + hi
                                col = (bi_ * 2 + hi) * qw
                                P.op("pe", lambda e, k_ap=k_ap, nk=nk, fc=fc, col=col: e.matmul(
                                    Sps[:nk, col:col + qw], lhsT=k_ap[po:po + 64, :],
                                    rhs=qr[gb][po:po + 64, fc, j * qw:(j + 1) * qw],
                                    start=True, stop=True), reads=rk + [R_qr[gb]], writes=[R_bank[sbk]])
                        for bi_, (k_ap, nk, v_ap, tag, rk, rv) in enumerate(blocks):
                            c0 = bi_ * 2 * qw
                            P.op("act", lambda e, nk=nk, c0=c0: e.activation(
                                out=pt[:nk, c0:c0 + 2 * qw], in_=Sps[:nk, c0:c0 + 2 * qw], func=AF.Exp),
                                reads=[R_bank[sbk]], writes=[Rpt])
                            if tag == "prev":
                                P.op("pool", lambda e, c0=c0: e.memset(
                                    pt[0:64, c0:c0 + 2 * qw].rearrange("p (h q) -> p h q", h=2)[:, :, 64:128], 0.0),
                                    writes=[Rpt])
                            if tag == "diag":
                                P.op("pool", lambda e, c0=c0: e.memset(
                                    pt[64:128, c0:c0 + 2 * qw].rearrange("p (h q) -> p h q", h=2)[:, :, 0:64], 0.0),
                                    writes=[Rpt])

                def c_stage2(g, n):
                    gb = g % 2
                    j, a = n // 4, n % 4
                    blocks = c_blocks(g, j, a)
                    nb = len(blocks)
                    ocb = YB[2]
                    OC = bank(ocb)
                    for par in range(2):
                        po = par * 64
                        pt = PTc[(n % 2) * 2 + par]; Rpt = R_PTc[(n % 2) * 2 + par]
                        if qw == 128:
                            for bi_, (k_ap, nk, v_ap, tag, rk, rv) in enumerate(blocks):
                                col = bi_ * 2 * qw
                                P.op("pe", lambda e, v_ap=v_ap, nk=nk, col=col, bi_=bi_: e.matmul(
                                    OC[po:po + 64, 0:256], lhsT=v_ap, rhs=pt[:nk, col:col + 256],
                                    start=(bi_ == 0), stop=(bi_ == nb - 1)),
                                    reads=rv + [Rpt], writes=[R_bank[ocb]])
                            for bi_, (k_ap, nk, v_ap, tag, rk, rv) in enumerate(blocks):
                                col = bi_ * 2 * qw
                                P.op("pe", lambda e, nk=nk, col=col, bi_=bi_: e.matmul(
                                    OC[po:po + 64, 256:512], lhsT=ones[:nk, 0:64],
                                    rhs=pt[:nk, col:col + 256], start=(bi_ == 0), stop=(bi_ == nb - 1)),
                                    reads=[Rpt, R_const], writes=[R_bank[ocb]])
                            continue
                        for hi in range(2):
                            for bi_, (k_ap, nk, v_ap, tag, rk, rv) in enumerate(blocks):
                                col = (bi_ * 2 + hi) * qw
                                P.op("pe", lambda e, v_ap=v_ap, nk=nk, col=col, bi_=bi_: e.matmul(
                                    OC[po:po + 64, hi * 128:hi * 128 + qw], lhsT=v_ap, rhs=pt[:nk, col:col + qw],
                                    start=(bi_ == 0), stop=(bi_ == nb - 1)),
                                    reads=rv + [Rpt], writes=[R_bank[ocb]])
                            for bi_, (k_ap, nk, v_ap, tag, rk, rv) in enumerate(blocks):
                                col = (bi_ * 2 + hi) * qw
                                P.op("pe", lambda e, nk=nk, col=col, bi_=bi_: e.matmul(
                                    OC[po:po + 64, 256 + hi * 128:256 + hi * 128 + qw], lhsT=ones[:nk, 0:64],
                                    rhs=pt[:nk, col:col + qw], start=(bi_ == 0), stop=(bi_ == nb - 1)),
                                    reads=[Rpt, R_const], writes=[R_bank[ocb]])
                    s2 = n % 2
                    rc = recc[s2]; Rrc = R_recc[s2]
                    tm = tmpc[s2]; Rtm = R_tmpc[s2]
                    for hi in range(2):
                        fc = 2 * a + hi
                        P.op("act", lambda e, hi=hi, fc=fc: e.activation(
                            out=rc[:, hi * 128:hi * 128 + qw], in_=OC[:, 256 + hi * 128:256 + hi * 128 + qw],
                            func=AF.Ln, bias=esink[:, fc:fc + 1]),
                            reads=[R_bank[ocb], R_const], writes=[Rrc])
                    if qw == 128:
                        P.op("act", lambda e: e.activation(out=rc[:, 0:256], in_=rc[:, 0:256], func=AF.Exp, scale=-1.0),
                             reads=[Rrc], writes=[Rrc])
                        P.op("dve", lambda e: e.tensor_tensor(out=tm[:, 0:256], in0=OC[:, 0:256], in1=rc[:, 0:256],
                                                              op=ALU.mult), reads=[R_bank[ocb], Rrc], writes=[Rtm])
                        P.op("pool", lambda e: e.tensor_tensor(
                            out=qr[gb][:, 2 * a:2 * a + 2, j * qw:(j + 1) * qw],
                            in0=tm[:, 0:256].rearrange("p (h q) -> p h q", h=2),
                            in1=g1[gb][:, 2 * a:2 * a + 2, j * qw:(j + 1) * qw], op=ALU.mult),
                            reads=[Rtm, R_g1[gb]], writes=[R_qr[gb]])
                    else:
                        for hi in range(2):
                            fc = 2 * a + hi
                            P.op("act", lambda e, hi=hi: e.activation(out=rc[:, hi * 128:hi * 128 + qw],
                                                                      in_=rc[:, hi * 128:hi * 128 + qw], func=AF.Exp,
                                                                      scale=-1.0),
                                 reads=[Rrc], writes=[Rrc])
                            P.op("dve", lambda e, hi=hi: e.tensor_tensor(
                                out=tm[:, hi * 128:hi * 128 + qw], in0=OC[:, hi * 128:hi * 128 + qw],
                                in1=rc[:, hi * 128:hi * 128 + qw], op=ALU.mult),
                                reads=[R_bank[ocb], Rrc], writes=[Rtm])
                            P.op("pool", lambda e, hi=hi, fc=fc: e.tensor_tensor(
                                out=qr[gb][:, fc, j * qw:(j + 1) * qw], in0=tm[:, hi * 128:hi * 128 + qw],
                                in1=g1[gb][:, fc, j * qw:(j + 1) * qw], op=ALU.mult),
                                reads=[Rtm, R_g1[gb]], writes=[R_qr[gb]])

                def gen_c(g):
                    nn = tpg1 * 4
                    c_stage1(g, 0)
                    yield
                    for n in range(nn):
                        if n + 1 < nn:
                            c_stage1(g, n + 1)
                            yield
                        c_stage2(g, n)
                        yield

                def gen_d(g):
                    gb = g % 2
                    for tt in range(tpg1):
                        ti = g * tpg1 + tt
                        bks = (nbx(), nbx())
                        for half in range(2):
                            for c in range(8):
                                P.op("pe", lambda e, c=c, half=half: e.matmul(
                                    bank(bks[half])[:ts, :], lhsT=qr[gb][:, c, tt * ts:(tt + 1) * ts],
                                    rhs=woc[:, c, half * 512:(half + 1) * 512], start=(c == 0), stop=(c == 7)),
                                    reads=[R_qr[gb], R_woc], writes=[R_bank[bks[half]]])
                        yield
                        sk = yc["n"] % 2; yc["n"] += 1
                        ysk = t1b[sk]; Rysk = R_t1[sk]
                        post_norm_residual(bks[0], bks[1], gpostc, Y0[gb][:ts, tt, :], R_Y0[gb], ysk[:ts, :], Rysk)
                        dst = yp[si, ti * ts:(ti + 1) * ts, :] if kind == "p" else ys[0:ts, :]
                        P.dma("pool", dst, ysk[:ts, :], reads=[Rysk], is_output=True)
                        yield

                def chain(*gens):
                    for g_ in gens:
                        yield from g_

                def run_streams(streams):
                    alive = list(streams)
                    while alive:
                        for s_ in list(alive):
                            try:
                                next(s_)
                            except StopIteration:
                                alive.remove(s_)

                flags = {}

                def wait_for(*keys):
                    while not all(flags.get(k) for k in keys):
                        yield

                def SA():
                    for g in range(ng1):
                        if g >= 2:
                            yield from wait_for(("d", g - 2))
                        if g >= 1:
                            yield from wait_for(("b2", g - 1))
                        yield from gen_a(g)
                        flags[("a", g)] = True
                        first = True
                        for _ in gen_b(g):
                            if first:
                                flags[("halo", g)] = True
                                first = False
                            yield
                        flags[("halo", g)] = True
                        flags[("bq", g)] = True

                def SB():
                    for g in range(ng1):
                        yield from wait_for(("a", g), ("halo", g))
                        yield from gen_b2(g)
                        flags[("b2", g)] = True

                def SC():
                    for g in range(ng1):
                        yield from wait_for(("bq", g), ("b2", g))
                        yield from gen_c(g)
                        flags[("c", g)] = True

                def SD():
                    for g in range(ng1):
                        yield from wait_for(("c", g))
                        yield from gen_d(g)
                        flags[("d", g)] = True

                run_streams([SA(), SB(), SC(), SD()])
            P.barrier()


        with nc.Block() as block:
            P.finalize(block, sems, dma_sems)
    return nc


_NC_CACHE = {}


def _prep(x_prompt, x_sample, cache_a_k, cache_a_v, cache_b_k, cache_b_v, cache_c_k, cache_c_v,
          ab_norm_pre, ab_w_in, ab_w_out, ab_norm_post, a_rel_bias,
          c_norm_pre, c_w_in, c_sinks, c_w_out, c_norm_post):
    f32 = np.float32
    A = lambda a: np.ascontiguousarray(np.asarray(a, dtype=f32))
    x_prompt, x_sample = A(x_prompt), A(x_sample)
    ncore = 8
    cst = _consts()
    w = A(ab_w_in)[0]
    w_ab = np.zeros((8, D, 512), f32)
    for pi in range(8):
        base = 0 if pi < 4 else 2048
        hp = pi % 4
        sl = lambda blk: w[:, base + blk * 512 + hp * 128: base + blk * 512 + (hp + 1) * 128]
        w_ab[pi, :, 0:128] = sl(0)
        w_ab[pi, :, 128:256] = sl(3)
        w_ab[pi, :, 256:384] = sl(1)
        w_ab[pi, :, 384:512] = sl(2)
    rep = lambda v: np.ascontiguousarray(np.broadcast_to(A(v).reshape(1, D), (128, D)))
    bp, bs = _bias_tiles(A(a_rel_bias)[0])
    sk = A(c_sinks)[0]
    sinks_l = np.zeros((128, 8), f32)
    for fc in range(8):
        sinks_l[0:64, fc] = sk[2 * fc]
        sinks_l[64:128, fc] = sk[2 * fc + 1]
    common = {
        "w_ab": w_ab, "w_oab": A(ab_w_out)[0], "w_c": A(c_w_in)[0], "w_oc": A(c_w_out)[0],
        "gpre_ab": rep(ab_norm_pre[0]), "gpost_ab": rep(ab_norm_post[0]),
        "gpre_c": rep(c_norm_pre[0]), "gpost_c": rep(c_norm_post[0]),
        "biasP": bp, "biasS": bs, "sinks": sinks_l,
        "c_ident": cst["ident"], "c_tri": cst["tri"], "c_ones": cst["ones"], "c_lmask": cst["lmask"],
        "c_rot": cst["rot"], "c_dsel": cst["dsel"], "c_dselrot": cst["dselrot"],
        "c_cosT": cst["cosT"], "c_sinT": cst["sinT"], "c_cosTM": cst["cosTM"], "c_sinTM": cst["sinTM"],
    }
    cak, cav = A(cache_a_k)[0], A(cache_a_v)[0]
    cbk, cbv = A(cache_b_k)[0], A(cache_b_v)[0]
    cck, ccv = A(cache_c_k)[0], A(cache_c_v)[0]
    in_maps = []
    for i in range(ncore):
        m = dict(common)
        m["xp"] = np.ascontiguousarray(x_prompt[2 * i:2 * i + 2])
        m["xs"] = np.ascontiguousarray(x_sample[i])
        m["ca_k"] = cak[i].reshape(512, 512); m["ca_v"] = cav[i].reshape(512, 512)
        m["cb_k"] = cbk[i].reshape(PAST, 512); m["cb_v"] = cbv[i].reshape(PAST, 512)
        m["cc_k"] = cck[i].reshape(128, 256); m["cc_v"] = ccv[i].reshape(128, 256)
        in_maps.append(m)
    return in_maps


def kernel(**inputs):
    ncore = 8
    in_maps = _prep(**inputs)
    if "nc" not in _NC_CACHE:
        _NC_CACHE["nc"] = build()
    nc = _NC_CACHE["nc"]
    res = run_bass_kernel_spmd(nc, in_maps, core_ids=list(range(ncore)))
    return _gather(res.results)


def _gather(R):
    ncore = len(R)
    cat = lambda name: np.concatenate([R[i][name] for i in range(ncore)], axis=0)
    stk = lambda name: np.stack([R[i][name] for i in range(ncore)], axis=0)
    y_prompt = cat("yp")
    y_sample = stk("ys")
    out = (
        y_prompt, y_sample,
        cat("o_akp").reshape(1, 16, 512, 8, 64), cat("o_avp").reshape(1, 16, 512, 8, 64),
        cat("o_bkp").reshape(1, 16, S, 8, 64), cat("o_bvp").reshape(1, 16, S, 8, 64),
        cat("o_ckp").reshape(1, 16, 128, 4, 64), cat("o_cvp").reshape(1, 16, 128, 4, 64),
        stk("o_aks").reshape(1, 8, TS, 8, 64), stk("o_avs").reshape(1, 8, TS, 8, 64),
        stk("o_bks").reshape(1, 8, TS, 8, 64), stk("o_bvs").reshape(1, 8, TS, 8, 64),
        stk("o_cks").reshape(1, 8, TS, 4, 64), stk("o_cvs").reshape(1, 8, TS, 4, 64),
    )
    return tuple(np.ascontiguousarray(o.astype(np.float32)) for o in out)
```

```python
import numpy as np
import concourse.bass as bass
import concourse.mybir as mybir
from concourse.bass_utils import run_bass_kernel_spmd

F32 = mybir.dt.float32
BF16 = mybir.dt.bfloat16
AF = mybir.ActivationFunctionType
ALU = mybir.AluOpType

D = 1024
S = 2048
TS = 32
PAST = 2048
EPS = 1e-6
NEG = -30000.0
SEM_LIM = 20000


class Res:
    __slots__ = ("w", "r", "name", "excl")

    def __init__(self, name="", excl=False):
        self.w = None
        self.r = {}
        self.name = name
        self.excl = excl


class _Rec:
    def __init__(self):
        self.call = None

    def __getattr__(self, name):
        def f(*args, **kwargs):
            self.call = (name, args, kwargs)
            return None
        return f


class Prog:
    ENG = ("pe", "act", "dve", "pool", "sp")

    def __init__(self, nc):
        self.nc = nc
        self.ops = {e: [] for e in self.ENG}
        self.waited = {e: {} for e in self.ENG}
        self.ndma_sems = 12
        self.ndma_q = {"sp": 12, "pool": 6}
        self.dma_cnt = {q: [0] * self.ndma_sems for q in ("sp", "pool")}
        self.dma_next = {"sp": 0, "pool": 0}
        self.dma_last = {q: [None] * self.ndma_sems for q in ("sp", "pool")}
        self.out_dma_refs = []
        self.last_pe = None

    def _need(self, eng, ref, waits):
        if ref is None:
            return
        if ref[0] == "op":
            _, e2, idx = ref
            if e2 == eng and eng == "pe":
                return
            if self.waited[eng].get(e2, -1) >= idx:
                return
            if e2 == eng and idx >= len(self.ops[eng]):
                return
            self.waited[eng][e2] = idx
            self.ops[e2][idx]["inc"] = True
            waits.append(ref)
        else:
            _, q, slot, val = ref
            key = ("dma", q, slot)
            if self.waited[eng].get(key, -1) >= val:
                return
            self.waited[eng][key] = val
            waits.append(ref)

    def _deps(self, eng, reads, writes, same_engine_war=False):
        waits = []
        for r in reads:
            self._need(eng, r.w, waits)
        for w in writes:
            self._need(eng, w.w, waits)
            for e2, ref in w.r.items():
                self._need(eng, ref, waits)
        return waits

    def _commit(self, ref, reads, writes):
        for r in reads:
            r.r[ref[1] if ref[0] == "op" else ("dma", ref[1], ref[2])] = ref
        for w in writes:
            w.w = ref
            w.r = {}

    def op(self, eng, fn, reads=(), writes=()):
        ex = [r for r in reads if r.excl]
        if ex:
            reads = [r for r in reads if not r.excl]
            writes = list(writes) + ex
        waits = self._deps(eng, reads, writes)
        idx = len(self.ops[eng])
        rec = _Rec()
        fn(rec)
        assert rec.call is not None
        self.ops[eng].append({"fn": rec.call, "waits": waits, "inc": False, "dma": None})
        self._commit(("op", eng, idx), reads, writes)
        return ("op", eng, idx)

    def dma(self, q, out, in_, reads=(), writes=(), is_output=False):
        waits = self._deps(q, reads, writes)
        slot = self.dma_next[q]
        self.dma_next[q] = (slot + 1) % self.ndma_q[q]
        prev = self.dma_last[q][slot]
        if prev is not None:
            self._need(q, prev, waits)
        self.dma_cnt[q][slot] += 1
        val = self.dma_cnt[q][slot] * 16
        ref = ("dma", q, slot, val)
        self.dma_last[q][slot] = ref
        self.ops[q].append({"fn": ("dma_start", (), {"out": out, "in_": in_}), "waits": waits,
                            "inc": False, "dma": (q, slot)})
        self._commit(ref, reads, writes)
        if is_output:
            self.out_dma_refs.append(ref)
        return ref

    def barrier(self):
        refs = []
        for e in self.ENG:
            for idx in range(len(self.ops[e]) - 1, -1, -1):
                o = self.ops[e][idx]
                if o["fn"] is not None and o["dma"] is None:
                    refs.append(("op", e, idx))
                    break
        for q in ("sp", "pool"):
            for slot in range(self.ndma_sems):
                if self.dma_last[q][slot] is not None:
                    refs.append(self.dma_last[q][slot])
        for e in self.ENG:
            waits = []
            for ref in refs:
                if ref[0] == "op" and ref[1] == e:
                    continue
                self._need(e, ref, waits)
            self.ops[e].append({"fn": None, "waits": waits, "inc": False, "dma": None})

    def finalize(self, block, sems, dma_sems):
        nc = self.nc
        marks = {}
        for e in self.ENG:
            c = 0
            m = []
            for o in self.ops[e]:
                if o["inc"] and o["fn"] is not None and o["dma"] is None:
                    c += 1
                m.append(c)
            marks[e] = m
            assert c <= SEM_LIM * len(sems[e]), (e, c)

        def sem_of(e, idx):
            m = marks[e][idx]
            assert m >= 1
            k = (m - 1) // SEM_LIM
            return sems[e][k], (m - 1) % SEM_LIM + 1

        engs = {"pe": nc.tensor, "act": nc.scalar, "dve": nc.vector, "pool": nc.gpsimd, "sp": nc.sync}

        def _pinfo(ap):
            fs = 1
            for s_ in list(ap.tensor.shape)[1:]:
                fs *= int(s_)
            p0 = int(ap.offset) // fs
            col = int(ap.offset) % fs
            return p0, int(ap.ap[0][1]), col, fs
        prev = None
        nviol = 0
        for o in self.ops["pe"]:
            if o["fn"] is None:
                continue
            name_, args_, kw_ = o["fn"]
            out_ap = args_[0] if args_ else kw_["out"]
            l_ap = kw_.get("lhsT", kw_.get("in_"))
            p0, kk, _, _ = _pinfo(l_ap)
            _, _, col, fs = _pinfo(out_ap)
            esz = 4 if fs in (512, 1024) and out_ap.tensor.name.startswith("pb") else 2
            bank_id = (out_ap.tensor.name, (col * esz) // 2048)
            rows = (p0, p0 + kk)
            cur = (rows, bank_id)
            if prev is not None and kk < 128 and (prev[0][1] - prev[0][0]) < 128:
                disjoint = rows[0] >= prev[0][1] or prev[0][0] >= rows[1]
                if disjoint and prev[1] == bank_id:
                    nviol += 1
            prev = cur
        assert nviol == 0, f"row-tile bank violations: {nviol}"

        def run(e, eng):
            for idx, o in enumerate(self.ops[e]):
                for ref in o["waits"]:
                    if ref[0] == "op":
                        s_, v_ = sem_of(ref[1], ref[2])
                        eng.wait_ge(s_, v_)
                    else:
                        eng.wait_ge(dma_sems[ref[1]][ref[2]], ref[3])
                if o["fn"] is None:
                    continue
                name_, args_, kw_ = o["fn"]
                ins = getattr(eng, name_)(*args_, **kw_)
                if o["dma"] is not None:
                    ins.then_inc(dma_sems[o["dma"][0]][o["dma"][1]], 16)
                elif o["inc"]:
                    s_, _ = sem_of(e, idx)
                    ins.then_inc(s_, 1)

        @block.tensor
        def _(eng):
            run("pe", eng)

        @block.scalar
        def _(eng):
            run("act", eng)

        @block.vector
        def _(eng):
            run("dve", eng)

        @block.gpsimd
        def _(eng):
            run("pool", eng)

        @block.sync
        def _(eng):
            run("sp", eng)


def _consts():
    c = {}
    i = np.arange(128)
    c["ident"] = np.eye(128, dtype=np.float32)
    c["tri"] = (i[:, None] >= i[None, :]).astype(np.float32)
    c["ones"] = np.ones((128, 128), np.float32)
    c["lmask"] = (i[:, None] < i[None, :]).astype(np.float32)
    rot = np.zeros((128, 128), np.float32)
    for p in range(128):
        if p % 64 < 32:
            rot[p + 32, p] = 1.0
        else:
            rot[p - 32, p] = 1.0
    c["rot"] = rot
    dsel = np.zeros((2, 128, 128), np.float32)
    for a in range(2):
        for p in range(128):
            dsel[a, a * 64 + (p % 64), p] = 1.0
    c["dsel"] = dsel
    c["dselrot"] = np.stack([dsel[a] @ rot for a in range(2)])
    half = 32
    inv = (10000.0 ** (-np.arange(half, dtype=np.float32) * np.float32(2.0 / 64))).astype(np.float32)
    pos = np.arange(S + TS, dtype=np.float32)
    ang = (pos[:, None] * inv[None, :]).astype(np.float32)
    cos, sin = np.cos(ang).astype(np.float32), np.sin(ang).astype(np.float32)
    pidx = np.arange(128) % 32
    sign = np.where((np.arange(128) % 64) < 32, -1.0, 1.0).astype(np.float32)
    c["cosT"] = np.ascontiguousarray(cos[:, pidx].T)
    c["sinT"] = np.ascontiguousarray((sin[:, pidx] * sign[None, :]).T)
    fidx = np.arange(256) % 32
    fsign = np.where((np.arange(256) % 64) < 32, -1.0, 1.0).astype(np.float32)
    c["cosTM"] = np.ascontiguousarray(cos[:, fidx])
    c["sinTM"] = np.ascontiguousarray(sin[:, fidx] * fsign[None, :])
    return c


def _bias_tiles(table):
    k = np.arange(128)[:, None]
    q = np.arange(128)[None, :]
    bp = np.zeros((8, 128, 640), np.float32)
    for slot in range(5):
        rel = (4 - slot) * 128 + (q - k)
        idx = np.clip(rel, -128, 128) + 128
        bp[:, :, slot * 128:(slot + 1) * 128] = table[:, idx]
    qs = PAST + np.arange(TS)[None, :]
    bs = np.zeros((8, 128, 160), np.float32)
    for slot in range(5):
        kpos = PAST - 512 + slot * 128 + np.arange(128)[:, None]
        idx = np.clip(qs - kpos, -128, 128) + 128
        bs[:, :, slot * 32:(slot + 1) * 32] = table[:, idx]
    return bp, bs


def build():
    nc = bass.Bass("TRN2", target_bir_lowering=False)
    P = Prog(nc)

    def din(name, shape):
        return nc.dram_tensor(name, list(shape), F32, kind="ExternalInput").ap()

    def dout(name, shape):
        return nc.dram_tensor(name, list(shape), F32, kind="ExternalOutput").ap()

    xp = din("xp", [2, S, D])
    xs = din("xs", [TS, D])
    ca_k = din("ca_k", [512, 512]); ca_v = din("ca_v", [512, 512])
    cb_k = din("cb_k", [PAST, 512]); cb_v = din("cb_v", [PAST, 512])
    cc_k = din("cc_k", [128, 256]); cc_v = din("cc_v", [128, 256])
    w_ab = din("w_ab", [8, D, 512])
    w_oab = din("w_oab", [D, D])
    w_c = din("w_c", [D, 2560])
    w_oc = din("w_oc", [D, D])
    gpre_ab = din("gpre_ab", [128, D]); gpost_ab = din("gpost_ab", [128, D])
    gpre_c = din("gpre_c", [128, D]); gpost_c = din("gpost_c", [128, D])
    biasP = din("biasP", [8, 128, 640]); biasS = din("biasS", [8, 128, 160])
    sinks = din("sinks", [128, 8])
    c_ident = din("c_ident", [128, 128]); c_tri = din("c_tri", [128, 128]); c_ones = din("c_ones", [128, 128])
    c_lmask = din("c_lmask", [128, 128]); c_rot = din("c_rot", [128, 128])
    c_dsel = din("c_dsel", [2, 128, 128]); c_dselrot = din("c_dselrot", [2, 128, 128])
    c_cosT = din("c_cosT", [128, S + TS]); c_sinT = din("c_sinT", [128, S + TS])
    c_cosTM = din("c_cosTM", [S + TS, 256]); c_sinTM = din("c_sinTM", [S + TS, 256])

    yp = dout("yp", [2, S, D]); ys = dout("ys", [TS, D])
    o_akp = dout("o_akp", [2, 512, 512]); o_avp = dout("o_avp", [2, 512, 512])
    o_bkp = dout("o_bkp", [2, S, 512]); o_bvp = dout("o_bvp", [2, S, 512])
    o_ckp = dout("o_ckp", [2, 128, 256]); o_cvp = dout("o_cvp", [2, 128, 256])
    o_aks = dout("o_aks", [TS, 512]); o_avs = dout("o_avs", [TS, 512])
    o_bks = dout("o_bks", [TS, 512]); o_bvs = dout("o_bvs", [TS, 512])
    o_cks = dout("o_cks", [TS, 256]); o_cvs = dout("o_cvs", [TS, 256])

    from contextlib import ExitStack
    es = ExitStack()

    def sb(name, shape, dt):
        return es.enter_context(nc.sbuf_tensor(name, list(shape), dt))

    def ps(name, shape, dt):
        return es.enter_context(nc.psum_tensor(name, list(shape), dt))

    with es:
        sems = {e: [es.enter_context(nc.semaphore(f"s_{e}{k}")) for k in range(2)] for e in Prog.ENG}
        dma_sems = {q: [es.enter_context(nc.semaphore(f"d_{q}{k}")) for k in range(P.ndma_sems)]
                    for q in ("sp", "pool")}

        ident = sb("ident", [128, 128], BF16); tri = sb("tri", [128, 128], BF16)
        ones = sb("ones", [128, 128], BF16); lmask = sb("lmask", [128, 128], BF16)
        rot = sb("rot", [128, 128], BF16)
        dsel = sb("dsel", [128, 2, 128], BF16); dselrot = sb("dselrot", [128, 2, 128], BF16)
        esink = sb("esink", [128, 8], F32)
        R_const = Res("const")
        for t_, src in ((ident, c_ident), (tri, c_tri), (ones, c_ones), (lmask, c_lmask), (rot, c_rot)):
            P.dma("pool", t_[:], src, writes=[R_const])
        for a in range(2):
            P.dma("pool", dsel[:, a, :], c_dsel[a], writes=[R_const])
            P.dma("pool", dselrot[:, a, :], c_dselrot[a], writes=[R_const])
        for t_, src in ((esink, sinks),):
            P.dma("sp", t_[:], src, writes=[R_const])
        P.op("act", lambda e: e.activation(out=esink[:], in_=esink[:], func=AF.Exp), reads=[R_const], writes=[R_const])

        pbank = [ps(f"pb{i}", [128, 1024], F32) for i in range(3)]
        ptps = [ps(f"ptp{i}", [128, 1024], BF16) for i in range(2)]
        ptp = ptps[0]
        R_bank = [Res(f"bank{i}", excl=True) for i in range(6)]
        R_tps = [Res(f"tp{i}", excl=True) for i in range(2)]
        R_tp = R_tps[0]

        def bank(i):
            return pbank[i // 2][:, (i % 2) * 512:(i % 2 + 1) * 512]

        oT = sb("oT", [128, 8, S], BF16)
        R_oT = [[Res(f"oT{c}_{g}") for g in range(4)] for c in range(8)]
        xst = [sb(f"xst{i}", [128, D], F32) for i in range(2)]
        R_xst = [Res(f"xst{i}") for i in range(2)]
        junk2 = [sb("junk2_0", [128, D], BF16)] * 2; R_junk2 = [Res()] * 2
        stat2 = [sb(f"stat2_{i}", [128, 2], F32) for i in range(2)]; R_stat2 = [Res() for _ in range(2)]
        xsb = [sb(f"xsb{i}", [128, D], BF16) for i in range(2)]
        R_xsb = [Res(f"xsb{i}") for i in range(2)]
        cnt = {"x": 0, "stg": 0, "pj": 0}

        seqs = [("p", 0), ("p", 1), ("s", 0)]

        def x_src(kind, si, t0, n):
            return xp[si, t0:t0 + n, :] if kind == "p" else xs[t0:t0 + n, :]

        def rstd_from(ss_ap, out_ap, n, reads, writes):
            P.op("act", lambda e: e.activation(out=out_ap, in_=ss_ap, func=AF.Ln, scale=1.0 / D, bias=EPS),
                 reads=reads, writes=writes)
            P.op("act", lambda e: e.activation(out=out_ap, in_=out_ap, func=AF.Exp, scale=-0.5),
                 reads=writes, writes=writes)

        def norm_part1(src_ap, src_res, ts, gain, gain_res=None):
            gain_res = gain_res or R_const
            k = cnt["x"]; cnt["x"] += 1
            xb = xsb[k % 2]; Rxb = R_xsb[k % 2]
            st = stat2[k % 2]; Rst = R_stat2[k % 2]
            jk = junk2[k % 2]; Rjk = R_junk2[k % 2]
            P.op("act", lambda e: e.activation(out=jk[:ts, :], in_=src_ap, func=AF.Square,
                                               accum_out=st[:ts, 0:1]),
                 reads=[src_res], writes=[Rjk, Rst])
            rstd_from(st[:ts, 0:1], st[:ts, 1:2], ts, [Rst], [Rst])
            P.op("dve", lambda e: e.scalar_tensor_tensor(out=xb[:ts, :], in0=src_ap, scalar=st[:ts, 1:2],
                                                         in1=gain[:ts, :], op0=ALU.mult, op1=ALU.mult),
                 reads=[src_res, Rst, gain_res], writes=[Rxb])
            return k

        def norm_part2(k, ts, dst_ap3, dst_res):
            xb = xsb[k % 2]; Rxb = R_xsb[k % 2]
            tp = ptps[k % 2]; Rtp = R_tps[k % 2]
            for c in range(8):
                P.op("pe", lambda e, c=c: e.transpose(out=tp[:, c * ts:(c + 1) * ts],
                                                      in_=xb[:ts, c * 128:(c + 1) * 128], identity=ident[:ts, :ts]),
                     reads=[Rxb, R_const], writes=[Rtp])
            P.op("dve", lambda e: e.tensor_copy(out=dst_ap3,
                                                in_=tp[:, 0:8 * ts].rearrange("p (c t) -> p c t", c=8)),
                 reads=[Rtp], writes=[dst_res])

        def norm_transpose(src_ap, src_res, ts, gain, dst_ap3, dst_res, gain_res=None):
            k = norm_part1(src_ap, src_res, ts, gain, gain_res)
            norm_part2(k, ts, dst_ap3, dst_res)

        for (kind, si) in seqs:
            T = S if kind == "p" else TS
            ts = min(T, 128)
            nt = T // ts
            gs = min(T, 512)
            ng = T // gs
            tpg = gs // ts
            pos0 = 0 if kind == "p" else PAST

            with ExitStack() as l0:
                def sb0(name, shape, dt):
                    return l0.enter_context(nc.sbuf_tensor(f"{name}_{kind}{si}", list(shape), dt))

                def mk0(name, shape, dt, n):
                    return [sb0(f"{name}{i}", shape, dt) for i in range(n)]

                gpreab = sb0("gpreab", [128, D], F32); R_gab = Res()
                P.dma("sp", gpreab[:], gpre_ab, writes=[R_gab])
                xnT = sb0("xnT", [128, 8, T], BF16)
                R_xnT = [Res(f"xnT{g}") for g in range(ng)]
                wring = mk0("wr", [128, 8, 512], BF16, 2)
                R_wring = [Res(f"wr{i}") for i in range(2)]
                qTs = mk0("qT", [128, T], BF16, 2); kTs = mk0("kT", [128, T], BF16, 2); gTs = mk0("gT", [128, T], BF16, 2)
                merged = (kind == "p")
                if merged:
                    Vs = mk0("V", [128, nt, 2, 128], BF16, 2)
                else:
                    Vs = mk0("V", [128, nt, 128], BF16, 2)
                R_qs = [[Res() for _ in range(ng)] for _ in range(2)]
                R_ks = [[Res() for _ in range(ng)] for _ in range(2)]
                R_gs = [[Res() for _ in range(ng)] for _ in range(2)]
                R_Vs = [[Res() for _ in range(nt)] for _ in range(2)]
                stg = mk0("stg", [128, 256], F32, 2); R_stg = [Res() for _ in range(2)]
                biasT = sb0("biasT", [128, 8, 640], F32); R_bias = Res("bias")
                Ssb = mk0("Ssb", [128, 640], F32, 2); R_Ssb = [Res() for _ in range(2)]
                PT = mk0("PT", [128, 640], BF16, 2); R_PT = [Res() for _ in range(2)]
                rec2 = mk0("rec", [128, 128], F32, 2); R_rec2 = [Res() for _ in range(2)]
                tmpo2 = mk0("tmpo", [128, 128], F32, 2); R_tmpo2 = [Res() for _ in range(2)]
                EH = [mk0(f"E{h}_", [128, 512], F32, 2) for h in range(2)]; R_EH = [[Res() for _ in range(2)] for _ in range(2)]
                SPH = [mk0(f"SP{h}_", [128, 512], BF16, 2) for h in range(2)]; R_SPH = [[Res() for _ in range(2)] for _ in range(2)]
                ATH = [mk0(f"AT{h}_", [128, 512], BF16, 2) for h in range(2)]; R_ATH = [[Res() for _ in range(2)] for _ in range(2)]
                sgt = sb0("sgt", [128, 512], F32); R_sgt = Res()
                SaccH = mk0("Sacc", [128, 512], F32, 2); R_SaccH = [Res() for _ in range(2)]
                SaccBH = [mk0(f"SaccB{h}_", [128, 512], BF16, 2) for h in range(2)]; R_SaccBH = [[Res() for _ in range(2)] for _ in range(2)]
                nqT = mk0("nq", [128, 512], BF16, 2); R_nq = [Res() for _ in range(2)]
                if kind == "s":
                    kcache = sb0("kcache", [128, 16, 128], BF16); R_kc = Res()
                    kTcs = mk0("kTc", [128, PAST], BF16, 2); R_kTcs = [Res() for _ in range(2)]
                    Vcs = mk0("Vc", [128, 16, 128], BF16, 2); R_Vcs = [Res() for _ in range(2)]

                def load_w(pi):
                    P.dma("pool", wring[pi % 2][:], w_ab[pi].rearrange("(c p) n -> p c n", p=128),
                          writes=[R_wring[pi % 2]])

                load_w(0)
                load_w(1)
                if merged:
                    for s_ in range(2):
                        for ti_ in range(nt):
                            P.op("pool", lambda e, s_=s_, ti_=ti_: e.memset(Vs[s_][:, ti_, :, :], 1.0), writes=[R_Vs[s_][ti_]])
                if kind == "p":
                    for h in range(8):
                        P.dma("sp", biasT[:, h, :], biasP[h], writes=[R_bias])
                    for h in range(8):
                        P.op("pool", lambda e, h=h: e.memset(biasT[0:64, h, 64:128], NEG), writes=[R_bias])
                        P.op("pool", lambda e, h=h: e.memset(biasT[64:128, h, 512:576], NEG), writes=[R_bias])
                else:
                    for h in range(8):
                        P.dma("sp", biasT[:, h, 0:160], biasS[h], writes=[R_bias])

                p0flags = {}

                def gen_phase0():
                    for ti in range(nt):
                        k = cnt["x"]
                        xt = xst[k % 2]; Rxt = R_xst[k % 2]
                        P.dma("sp", xt[:ts, :], x_src(kind, si, ti * ts, ts), writes=[Rxt])
                        g = ti // tpg
                        norm_transpose(xt[:ts, :], Rxt, ts, gpreab, xnT[:, :, ti * ts:(ti + 1) * ts], R_xnT[g],
                                       gain_res=R_gab)
                        if (ti + 1) % tpg == 0:
                            p0flags[g] = True
                        yield


                PJB = 5

                def gen_proj(pi):
                    isA = pi < 4
                    hp = pi % 4
                    st = pi % 2
                    W = wring[st]; RW = R_wring[st]
                    qT, kT, gT, V = qTs[st], kTs[st], gTs[st], Vs[st]
                    R_q, R_k, R_g, R_V = R_qs[st], R_ks[st], R_gs[st], R_Vs[st]
                    if kind == "s":
                        kTc, R_kTc, Vc, R_Vc = kTcs[st], R_kTcs[st], Vcs[st], R_Vcs[st]
                        csrc_k, csrc_v, nck = (ca_k, ca_v, 4) if isA else (cb_k, cb_v, 16)
                        if pi == 0:
                            P.dma("pool", kcache[:, 0:nck, :],
                                  csrc_k[:, hp * 128:(hp + 1) * 128].rearrange("(t p) f -> p t f", p=128), writes=[R_kc])
                        P.dma("pool", Vc[:, 0:nck, :],
                              csrc_v[:, hp * 128:(hp + 1) * 128].rearrange("(t p) f -> p t f", p=128), writes=[R_Vc])
                        for t0 in range(0, nck, 8):
                            nb_ = min(8, nck - t0)
                            for t_ in range(nb_):
                                P.op("pe", lambda e, t_=t_: e.transpose(
                                    out=ptp[:, t_ * 128:(t_ + 1) * 128], in_=kcache[:, t0 + t_, :], identity=ident[:, :]),
                                    reads=[R_kc, R_const], writes=[R_tp])
                            P.op("dve", lambda e: e.tensor_copy(
                                out=kTc[:, t0 * 128:(t0 + nb_) * 128], in_=ptp[:, 0:nb_ * 128]),
                                reads=[R_tp], writes=[R_kTc])
                            yield
                        if pi + 1 < 8:
                            pn = pi + 1
                            nsrc, nn = (ca_k, 4) if pn < 4 else (cb_k, 16)
                            P.dma("pool", kcache[:, 0:nn, :],
                                  nsrc[:, (pn % 4) * 128:(pn % 4 + 1) * 128].rearrange("(t p) f -> p t f", p=128),
                                  writes=[R_kc])
                    pjbanks = [5, 4] if pi == 0 else [5]
                    pjc = [0]

                    def nextpj():
                        b_ = pjbanks[pjc[0] % len(pjbanks)]; pjc[0] += 1
                        return b_
                    for g in range(ng):
                        t0 = g * gs
                        while pi == 0 and not p0flags.get(g):
                            yield
                        for (fc, kindf) in ((0, "q"), (2, "k"), (1, "g")):
                            bi = nextpj()
                            for kc in range(8):
                                P.op("pe", lambda e, kc=kc: e.matmul(
                                    bank(bi)[:, 0:gs], lhsT=W[:, kc, fc * 128:(fc + 1) * 128],
                                    rhs=xnT[:, kc, t0:t0 + gs], start=(kc == 0), stop=(kc == 7)),
                                    reads=[RW, R_xnT[g]], writes=[R_bank[bi]])
                            if kindf == "q":
                                P.op("dve", lambda e: e.tensor_scalar(
                                    out=qT[:, t0:t0 + gs], in0=bank(bi)[:, 0:gs], scalar1=0.125, scalar2=None,
                                    op0=ALU.mult), reads=[R_bank[bi]], writes=[R_q[g]])
                            elif kindf == "k":
                                P.op("dve", lambda e: e.tensor_copy(
                                    out=kT[:, t0:t0 + gs], in_=bank(bi)[:, 0:gs]), reads=[R_bank[bi]], writes=[R_k[g]])
                            else:
                                P.op("act", lambda e: e.activation(out=sgt[:, 0:gs], in_=bank(bi)[:, 0:gs], func=AF.Exp,
                                                                   scale=-1.0), reads=[R_bank[bi]], writes=[R_sgt])
                                P.op("act", lambda e: e.activation(out=sgt[:, 0:gs], in_=sgt[:, 0:gs], func=AF.Ln, bias=1.0),
                                     reads=[R_sgt], writes=[R_sgt])
                                P.op("act", lambda e: e.activation(out=sgt[:, 0:gs], in_=sgt[:, 0:gs], func=AF.Exp,
                                                                   scale=-1.0), reads=[R_sgt], writes=[R_sgt])
                                P.op("dve", lambda e: e.tensor_tensor(out=gT[:, t0:t0 + gs], in0=bank(bi)[:, 0:gs],
                                                                      in1=sgt[:, 0:gs], op=ALU.mult),
                                     reads=[R_bank[bi], R_sgt], writes=[R_g[g]])
                            yield
                        for tt in range(tpg):
                            ti = g * tpg + tt
                            bi = nextpj()
                            for kc in range(8):
                                P.op("pe", lambda e, kc=kc: e.matmul(
                                    bank(bi)[:ts, 0:256], lhsT=xnT[:, kc, ti * ts:(ti + 1) * ts],
                                    rhs=W[:, kc, 256:512], start=(kc == 0), stop=(kc == 7)),
                                    reads=[RW, R_xnT[g]], writes=[R_bank[bi]])
                            if merged:
                                P.op("dve", lambda e: e.tensor_copy(
                                    out=V[:ts, ti, 0, 0:64], in_=bank(bi)[:ts, 128:192]), reads=[R_bank[bi]], writes=[R_V[ti]])
                                P.op("dve", lambda e: e.tensor_copy(
                                    out=V[:ts, ti, 1, 64:128], in_=bank(bi)[:ts, 192:256]), reads=[R_bank[bi]], writes=[R_V[ti]])
                            else:
                                P.op("dve", lambda e: e.tensor_copy(
                                    out=V[:ts, ti, :], in_=bank(bi)[:ts, 128:256]), reads=[R_bank[bi]], writes=[R_V[ti]])
                            if kind == "p":
                                if isA:
                                    need = ti * ts >= S - 512
                                    dk, dv, r0 = o_akp, o_avp, ti * ts - (S - 512)
                                else:
                                    need = True
                                    dk, dv, r0 = o_bkp, o_bvp, ti * ts
                                dk_ap = dk[si, r0:r0 + ts, hp * 128:(hp + 1) * 128] if need else None
                                dv_ap = dv[si, r0:r0 + ts, hp * 128:(hp + 1) * 128] if need else None
                            else:
                                need = True
                                dk, dv = (o_aks, o_avs) if isA else (o_bks, o_bvs)
                                dk_ap = dk[0:ts, hp * 128:(hp + 1) * 128]
                                dv_ap = dv[0:ts, hp * 128:(hp + 1) * 128]
                            if need:
                                sk = cnt["stg"] % 2; cnt["stg"] += 1
                                P.op("dve", lambda e: e.tensor_copy(
                                    out=stg[sk][:ts, :], in_=bank(bi)[:ts, 0:256]),
                                    reads=[R_bank[bi]], writes=[R_stg[sk]])
                                P.dma("pool", dk_ap, stg[sk][:ts, 0:128], reads=[R_stg[sk]], is_output=True)
                                P.dma("pool", dv_ap, stg[sk][:ts, 128:256], reads=[R_stg[sk]], is_output=True)
                            yield

                def gen_attnA(pi):
                    hp = pi % 4
                    st = pi % 2
                    qT, kT, gT, V = qTs[st], kTs[st], gTs[st], Vs[st]
                    R_q, R_k, R_g, R_V = R_qs[st], R_ks[st], R_gs[st], R_Vs[st]
                    if kind == "s":
                        kTc, R_kTc, Vc, R_Vc = kTcs[st], R_kTcs[st], Vcs[st], R_Vcs[st]
                    nqb = T // ts
                    qw = ts
                    its = [(j, hh) for j in range(nqb) for hh in range(2)]

                    def a_blocks(j, hh):
                        po = hh * 64
                        blocks = []
                        if kind == "p":
                            for slot in range(5):
                                kb = j - 4 + slot
                                if kb < 0:
                                    continue
                                blocks.append((kT[po:po + 64, kb * 128:(kb + 1) * 128], 128,
                                               V[:, kb, hh, :], slot,
                                               [R_k[kb // 4]], [R_V[kb]]))
                        else:
                            for slot in range(4):
                                blocks.append((kTc[po:po + 64, slot * 128:(slot + 1) * 128], 128,
                                               Vc[:, slot, hh * 64:(hh + 1) * 64], slot, [R_kTc], [R_Vc]))
                            blocks.append((kT[po:po + 64, 0:TS], TS, V[0:TS, 0, hh * 64:(hh + 1) * 64], 4,
                                           [R_k[0]], [R_V[0]]))
                        return blocks

                    def a_stage1(n):
                        j, hh = its[n]
                        h = hp * 2 + hh
                        po = hh * 64
                        sbk = n % 2
                        RS = [R_bank[2 * sbk], R_bank[2 * sbk + 1]]
                        Sps = pbank[sbk]
                        blocks = a_blocks(j, hh)
                        q_ap = qT[po:po + 64, j * qw:(j + 1) * qw]
                        gq = (j * qw) // gs
                        for (k_ap, nk, v_ap, slot, rk, rv) in blocks:
                            P.op("pe", lambda e, k_ap=k_ap, nk=nk, slot=slot: e.matmul(
                                Sps[:nk, slot * qw:(slot + 1) * qw], lhsT=k_ap, rhs=q_ap, start=True, stop=True),
                                reads=rk + [R_q[gq]], writes=RS)
                        s0 = blocks[0][3]
                        full = [b for b in blocks if b[1] == 128]
                        part = [b for b in blocks if b[1] != 128]
                        sk = n % 2
                        lo, hi = s0 * qw, (full[-1][3] + 1) * qw
                        P.op("dve", lambda e: e.tensor_tensor(
                            out=Ssb[sk][:, lo:hi], in0=Sps[:, lo:hi], in1=biasT[:, h, lo:hi], op=ALU.add),
                            reads=RS + [R_bias], writes=[R_Ssb[sk]])
                        P.op("act", lambda e: e.activation(
                            out=PT[sk][:, lo:hi], in_=Ssb[sk][:, lo:hi], func=AF.Exp),
                            reads=[R_Ssb[sk]], writes=[R_PT[sk]])
                        for (k_ap, nk, v_ap, slot, rk, rv) in part:
                            lo2, hi2 = slot * qw, (slot + 1) * qw
                            P.op("dve", lambda e, lo2=lo2, hi2=hi2, nk=nk: e.tensor_tensor(
                                out=Ssb[sk][:nk, lo2:hi2], in0=Sps[:nk, lo2:hi2], in1=biasT[:nk, h, lo2:hi2],
                                op=ALU.add), reads=RS + [R_bias], writes=[R_Ssb[sk]])
                            P.op("act", lambda e, lo2=lo2, hi2=hi2, nk=nk: e.activation(
                                out=PT[sk][:nk, lo2:hi2], in_=Ssb[sk][:nk, lo2:hi2], func=AF.Exp),
                                reads=[R_Ssb[sk]], writes=[R_PT[sk]])

                    def a_stage2m(n):
                        j, hh = its[n]
                        po = hh * 64
                        pd = 64 - po
                        sk = n % 2
                        odb = 4
                        OD = bank(4)[:, (n % 2) * 128:(n % 2) * 128 + 128]
                        blocks = a_blocks(j, hh)
                        nb = len(blocks)
                        gq = (j * qw) // gs
                        for bi_, (k_ap, nk, v_ap, slot, rk, rv) in enumerate(blocks):
                            P.op("pe", lambda e, v_ap=v_ap, nk=nk, slot=slot, bi_=bi_: e.matmul(
                                OD[:, 0:qw], lhsT=v_ap, rhs=PT[sk][:nk, slot * qw:(slot + 1) * qw],
                                start=(bi_ == 0), stop=(bi_ == nb - 1)),
                                reads=rv + [R_PT[sk]], writes=[R_bank[odb]])
                        rc = rec2[hh]; Rrc = R_rec2[hh]
                        tm = tmpo2[hh]; Rtm = R_tmpo2[hh]
                        P.op("act", lambda e: e.activation(
                            out=rc[po:po + 64, 0:qw], in_=OD[pd:pd + 64, 0:qw], func=AF.Ln),
                            reads=[R_bank[odb]], writes=[Rrc])
                        P.op("act", lambda e: e.activation(
                            out=rc[po:po + 64, 0:qw], in_=rc[po:po + 64, 0:qw], func=AF.Exp, scale=-1.0),
                            reads=[Rrc], writes=[Rrc])
                        P.op("dve", lambda e: e.tensor_tensor(
                            out=tm[po:po + 64, 0:qw], in0=OD[po:po + 64, 0:qw], in1=rc[po:po + 64, 0:qw],
                            op=ALU.mult), reads=[R_bank[odb], Rrc], writes=[Rtm])
                        P.op("pool", lambda e: e.tensor_tensor(
                            out=oT[po:po + 64, pi, j * qw:(j + 1) * qw], in0=tm[po:po + 64, 0:qw],
                            in1=gT[po:po + 64, j * qw:(j + 1) * qw], op=ALU.mult),
                            reads=[Rtm, R_g[gq]], writes=[R_oT[pi][gq]])

                    def a_stage2(n):
                        if merged:
                            return a_stage2m(n)
                        j, hh = its[n]
                        po = hh * 64
                        sk = n % 2
                        odb = 4
                        OD = bank(odb)
                        blocks = a_blocks(j, hh)
                        nb = len(blocks)
                        gq = (j * qw) // gs
                        for bi_, (k_ap, nk, v_ap, slot, rk, rv) in enumerate(blocks):
                            P.op("pe", lambda e, v_ap=v_ap, nk=nk, slot=slot, bi_=bi_: e.matmul(
                                OD[po:po + 64, 0:qw], lhsT=v_ap, rhs=PT[sk][:nk, slot * qw:(slot + 1) * qw],
                                start=(bi_ == 0), stop=(bi_ == nb - 1)),
                                reads=rv + [R_PT[sk]], writes=[R_bank[odb]])
                        for bi_, (k_ap, nk, v_ap, slot, rk, rv) in enumerate(blocks):
                            P.op("pe", lambda e, nk=nk, slot=slot, bi_=bi_: e.matmul(
                                OD[po:po + 64, 128:128 + qw], lhsT=ones[:nk, 0:64],
                                rhs=PT[sk][:nk, slot * qw:(slot + 1) * qw],
                                start=(bi_ == 0), stop=(bi_ == nb - 1)),
                                reads=[R_PT[sk], R_const], writes=[R_bank[odb]])
                        rc = rec2[hh]; Rrc = R_rec2[hh]
                        tm = tmpo2[hh]; Rtm = R_tmpo2[hh]
                        P.op("act", lambda e: e.activation(
                            out=rc[po:po + 64, 0:qw], in_=OD[po:po + 64, 128:128 + qw], func=AF.Ln),
                            reads=[R_bank[odb]], writes=[Rrc])
                        P.op("act", lambda e: e.activation(
                            out=rc[po:po + 64, 0:qw], in_=rc[po:po + 64, 0:qw], func=AF.Exp, scale=-1.0),
                            reads=[Rrc], writes=[Rrc])
                        P.op("dve", lambda e: e.tensor_tensor(
                            out=tm[po:po + 64, 0:qw], in0=OD[po:po + 64, 0:qw], in1=rc[po:po + 64, 0:qw],
                            op=ALU.mult), reads=[R_bank[odb], Rrc], writes=[Rtm])
                        P.op("pool", lambda e: e.tensor_tensor(
                            out=oT[po:po + 64, pi, j * qw:(j + 1) * qw], in0=tm[po:po + 64, 0:qw],
                            in1=gT[po:po + 64, j * qw:(j + 1) * qw], op=ALU.mult),
                            reads=[Rtm, R_g[gq]], writes=[R_oT[pi][gq]])

                    a_stage1(0)
                    yield
                    for n in range(len(its)):
                        if n + 1 < len(its):
                            a_stage1(n + 1)
                            yield
                        a_stage2(n)
                        yield

                def gen_attnB(pi):
                    st = pi % 2
                    qT, kT, gT, V = qTs[st], kTs[st], gTs[st], Vs[st]
                    R_q, R_k, R_g, R_V = R_qs[st], R_ks[st], R_gs[st], R_Vs[st]
                    if kind == "s":
                        kTc, R_kTc, Vc, R_Vc = kTcs[st], R_kTcs[st], Vcs[st], R_Vcs[st]
                    cw = gs
                    ob = 4
                    OB = bank(ob)
                    dq = min(128, cw)
                    for c in range(ng):
                        steps = [[], []]
                        for hh in range(2):
                            po = hh * 64
                            P.op("dve", lambda e: e.tensor_scalar(
                                out=nqT[hh][po:po + 64, 0:cw], in0=qT[po:po + 64, c * cw:(c + 1) * cw],
                                scalar1=-1.0, scalar2=None, op0=ALU.mult), reads=[R_q[c]], writes=[R_nq[hh]])
                            P.op("pool", lambda e: e.memset(SaccH[hh][:, 0:cw], 0.0), writes=[R_SaccH[hh]])
                            P.op("pool", lambda e: e.memset(SaccBH[hh][0][:, 0:cw], 0.0), writes=[R_SaccBH[hh][0]])
                            if kind == "p":
                                for kb in range(4 * c + 3, -1, -1):
                                    q0 = max(0, kb * 128 - c * 512)
                                    steps[hh].append((kT[po:po + 64, kb * 128:(kb + 1) * 128], 128,
                                                      V[:, kb, hh, hh * 64:(hh + 1) * 64], q0, kb >= 4 * c,
                                                      [R_k[kb // 4]], [R_V[kb]]))
                            else:
                                steps[hh].append((kT[po:po + 64, 0:TS], TS, V[0:TS, 0, hh * 64:(hh + 1) * 64], 0, True,
                                                  [R_k[0]], [R_V[0]]))
                                for kb in range(15, -1, -1):
                                    steps[hh].append((kTc[po:po + 64, kb * 128:(kb + 1) * 128], 128,
                                                      Vc[:, kb, hh * 64:(hh + 1) * 64], 0, False, [R_kTc], [R_Vc]))
                        ns = len(steps[0])

                        def stage1(i):
                            for hh in range(2):
                                po = hh * 64
                                k_ap, nk, v_ap, q0, diag, rk, rv = steps[hh][i]
                                z = bank(hh); Rz = R_bank[hh]
                                P.op("pe", lambda e: e.matmul(z[:nk, q0:cw], lhsT=k_ap,
                                                              rhs=qT[po:po + 64, c * cw + q0:(c + 1) * cw],
                                                              start=True, stop=True),
                                     reads=rk + [R_q[c]], writes=[Rz])
                            for hh in range(2):
                                k_ap, nk, v_ap, q0, diag, rk, rv = steps[hh][i]
                                z = bank(hh); Rz = R_bank[hh]
                                E = EH[hh][i % 2]; RE = R_EH[hh][i % 2]
                                P.op("act", lambda e: e.activation(out=E[:nk, q0:cw], in_=z[:nk, q0:cw], func=AF.Exp),
                                     reads=[Rz], writes=[RE])
                                if diag:
                                    P.op("dve", lambda e: e.tensor_tensor(
                                        out=E[:nk, q0:q0 + dq], in0=E[:nk, q0:q0 + dq], in1=lmask[:nk, 0:dq],
                                        op=ALU.mult), reads=[RE, R_const], writes=[RE])
                            for hh in range(2):
                                k_ap, nk, v_ap, q0, diag, rk, rv = steps[hh][i]
                                E = EH[hh][i % 2]; RE = R_EH[hh][i % 2]
                                SP = SPH[hh][i % 2]; RSP = R_SPH[hh][i % 2]
                                P.op("act", lambda e: e.activation(out=SP[:nk, q0:cw], in_=E[:nk, q0:cw], func=AF.Ln,
                                                                   bias=1.0), reads=[RE], writes=[RSP])

                        def stage2(i):
                            for hh in range(2):
                                k_ap, nk, v_ap, q0, diag, rk, rv = steps[hh][i]
                                cps = bank(2 + hh); Rc = R_bank[2 + hh]
                                SP = SPH[hh][i % 2]; RSP = R_SPH[hh][i % 2]
                                P.op("pe", lambda e: e.matmul(cps[:nk, q0:cw], lhsT=tri[:nk, :nk], rhs=SP[:nk, q0:cw],
                                                              start=True, stop=False),
                                     reads=[RSP, R_const], writes=[Rc])
                                P.op("pe", lambda e: e.matmul(cps[:nk, q0:cw], lhsT=ones[:, :nk],
                                                              rhs=SaccBH[hh][i % 2][:, q0:cw], start=False, stop=False),
                                     reads=[R_SaccBH[hh][i % 2], R_const], writes=[Rc])
                            for hh in range(2):
                                po = hh * 64
                                k_ap, nk, v_ap, q0, diag, rk, rv = steps[hh][i]
                                cps = bank(2 + hh); Rc = R_bank[2 + hh]
                                P.op("pe", lambda e: e.matmul(cps[:nk, q0:cw], lhsT=k_ap,
                                                              rhs=nqT[hh][po:po + 64, q0:cw], start=False, stop=True),
                                     reads=rk + [R_nq[hh]], writes=[Rc])
                            if i + 1 < ns:
                                for hh in range(2):
                                    k_ap, nk, v_ap, q0, diag, rk, rv = steps[hh][i]
                                    SP = SPH[hh][i % 2]; RSP = R_SPH[hh][i % 2]
                                    P.op("dve", lambda e: e.tensor_tensor(
                                        out=SaccH[hh][:nk, q0:cw], in0=SaccH[hh][:nk, q0:cw], in1=SP[:nk, q0:cw], op=ALU.add),
                                        reads=[RSP, R_SaccH[hh]], writes=[R_SaccH[hh]])
                                    P.op("dve", lambda e: e.tensor_copy(out=SaccBH[hh][(i + 1) % 2][:, 0:cw],
                                                                        in_=SaccH[hh][:, 0:cw]),
                                         reads=[R_SaccH[hh]], writes=[R_SaccBH[hh][(i + 1) % 2]])
                            for hh in range(2):
                                k_ap, nk, v_ap, q0, diag, rk, rv = steps[hh][i]
                                cps = bank(2 + hh); Rc = R_bank[2 + hh]
                                AT = ATH[hh][i % 2]; RAT = R_ATH[hh][i % 2]
                                P.op("act", lambda e: e.activation(out=AT[:nk, q0:cw], in_=cps[:nk, q0:cw], func=AF.Exp,
                                                                   scale=-1.0), reads=[Rc], writes=[RAT])
                                if q0 > 0:
                                    P.op("pool", lambda e: e.memset(AT[:nk, 0:q0], 0.0), writes=[RAT])
                                if diag:
                                    P.op("dve", lambda e: e.tensor_tensor(
                                        out=AT[:nk, q0:q0 + dq], in0=AT[:nk, q0:q0 + dq], in1=lmask[:nk, 0:dq],
                                        op=ALU.mult), reads=[RAT, R_const], writes=[RAT])

                        def stage3(i):
                            for hh in range(2):
                                po = hh * 64
                                k_ap, nk, v_ap, q0, diag, rk, rv = steps[hh][i]
                                AT = ATH[hh][i % 2]; RAT = R_ATH[hh][i % 2]
                                P.op("pe", lambda e: e.matmul(OB[po:po + 64, 0:cw], lhsT=v_ap, rhs=AT[:nk, 0:cw],
                                                              start=(i == 0), stop=(i == ns - 1)),
                                     reads=rv + [RAT], writes=[R_bank[ob]])

                        stage1(0)
                        yield
                        for i in range(ns):
                            if i + 1 < ns:
                                stage1(i + 1)
                            stage2(i)
                            if i > 0:
                                stage3(i - 1)
                            yield
                        stage3(ns - 1)
                        P.op("dve", lambda e: e.tensor_tensor(
                            out=oT[:, pi, c * cw:(c + 1) * cw], in0=OB[:, 0:cw],
                            in1=gT[:, c * cw:(c + 1) * cw], op=ALU.mult),
                            reads=[R_bank[ob], R_g[c]], writes=[R_oT[pi][c]])
                        yield

                def run_weighted(ga, na, gb, nb_):
                    da = db = 0
                    a_alive, b_alive = True, gb is not None
                    while a_alive or b_alive:
                        pick_b = b_alive and (not a_alive or (db + 1) * na <= (da + 1) * nb_)
                        if pick_b:
                            try:
                                next(gb); db += 1
                            except StopIteration:
                                b_alive = False
                        else:
                            try:
                                next(ga); da += 1
                            except StopIteration:
                                a_alive = False

                n_proj = ng * (3 + tpg) + (3 if kind == "s" else 0)
                gp0, gj0 = gen_phase0(), gen_proj(0)
                alive = [gp0, gj0]
                while alive:
                    for s_ in list(alive):
                        try:
                            next(s_)
                        except StopIteration:
                            alive.remove(s_)
                for pi in range(8):
                    if pi + 2 < 8:
                        load_w(pi + 2)
                    isA = pi < 4
                    if isA:
                        ga = gen_attnA(pi); na = 2 * (T // ts) * 2
                    else:
                        ga = gen_attnB(pi)
                        na = sum((4 * c + 4 + 2) for c in range(ng)) if kind == "p" else 19
                    gb = gen_proj(pi + 1) if pi + 1 < 8 else None
                    run_weighted(ga, na, gb, n_proj)
            P.barrier()

            with ExitStack() as l1:
                def sb1(name, shape, dt):
                    return l1.enter_context(nc.sbuf_tensor(f"{name}_{kind}{si}", list(shape), dt))

                gs1 = min(T, 256)
                ng1 = T // gs1
                tpg1 = gs1 // ts
                qw = ts
                woab = sb1("woab", [128, 8, D], BF16); R_woab = Res()
                wc = sb1("wc", [128, 8, 2560], BF16); R_wcq = [Res() for _ in range(4)]
                woc = sb1("woc", [128, 8, D], BF16); R_woc = Res()
                gpostab = sb1("gpostab", [128, D], F32); gprec = sb1("gprec", [128, D], F32)
                gpostc = sb1("gpostc", [128, D], F32); R_gn = Res()
                for t_, src in ((gpostab, gpost_ab), (gprec, gpre_c), (gpostc, gpost_c)):
                    P.dma("sp", t_[:], src, writes=[R_gn])
                P.dma("pool", woab[:], w_oab.rearrange("(c p) n -> p c n", p=128), writes=[R_woab])
                for q4 in range(4):
                    P.dma("pool", wc[:, :, q4 * 640:(q4 + 1) * 640],
                          w_c[:, q4 * 640:(q4 + 1) * 640].rearrange("(c p) n -> p c n", p=128), writes=[R_wcq[q4]])
                P.dma("pool", woc[:], w_oc.rearrange("(c p) n -> p c n", p=128), writes=[R_woc])

                def mk(name, shape, dt, n):
                    return [sb1(f"{name}{i}", shape, dt) for i in range(n)], [Res() for _ in range(n)]

                Y0, R_Y0 = mk("Y0", [128, tpg1, D], F32, 2)
                t1b, R_t1 = mk("t1b", [128, D], F32, 2)
                statY, R_statY = mk("statY", [128, 4], F32, 2)
                xn1T, R_xn1 = mk("xn1T", [128, 8, gs1], BF16, 1)
                xn1T, R_xn1 = xn1T * 2, R_xn1 * 2
                qbf, R_qbf = mk("qbf", [128, gs1], BF16, 2)
                qr, R_qr = mk("qr", [128, 8, gs1], BF16, 2)
                kbf, R_kbf = mk("kbf", [128, 2, gs1], BF16, 1)
                kr, R_kr = mk("kr", [128, 4, 128 + gs1], BF16, 2)
                g1, R_g1 = mk("g1", [128, 8, gs1], BF16, 2)
                V1, R_V1 = mk("V1", [128, 1 + tpg1, 256], BF16, 2)
                cosg, R_cos = mk("cosg", [128, gs1], F32, 1)
                sing, R_sin = mk("sing", [128, gs1], F32, 1)
                cosg, R_cos, sing, R_sin = cosg * 2, R_cos * 2, sing * 2, R_sin * 2
                ta, R_ta = mk("ta", [128, 256], F32, 2)
                tb, R_tb = mk("tb", [128, 256], F32, 2)
                tcb, R_tcb = mk("tcb", [128, 256], F32, 2)
                PTc, R_PTc = mk("PTc", [128, 512], BF16, 4)
                recc, R_recc = mk("recc", [128, 256], F32, 1)
                tmpc, R_tmpc = mk("tmpc", [128, 256], F32, 1)
                recc, R_recc, tmpc, R_tmpc = recc * 2, R_recc * 2, tmpc * 2, R_tmpc * 2
                kst = sb1("kst", [128, 256], F32); R_kst = Res()
                ksw = sb1("ksw", [128, 256], F32); R_ksw = Res()
                ctm = sb1("ctm", [128, 256], F32); stm = sb1("stm", [128, 256], F32); R_ctm = Res()
                kvst = sb1("kvst", [128, 512], F32); R_kvst = Res()
                if kind == "s":
                    kcc = sb1("kcc", [128, 256], BF16); R_kcc = Res()
                    kcd = sb1("kcd", [128, 4, 128], BF16); R_kcd = Res()
                    krc = sb1("krc", [128, 4, 128], BF16); R_krc = Res()
                    Vcc = sb1("Vcc", [128, 256], BF16); R_Vcc = Res()
                    P.dma("pool", kcc[:], cc_k, writes=[R_kcc])
                    P.dma("pool", Vcc[:], cc_v, writes=[R_Vcc])
                    for a in range(4):
                        for d2 in range(2):
                            P.op("pool", lambda e, a=a, d2=d2: e.tensor_copy(
                                out=kcd[:, a, d2 * 64:(d2 + 1) * 64], in_=kcc[:, a * 64:(a + 1) * 64]),
                                reads=[R_kcc], writes=[R_kcd])
                    for a in range(4):
                        P.op("pe", lambda e, a=a: e.transpose(out=ptp[:, a * 128:(a + 1) * 128], in_=kcd[:, a, :],
                                                              identity=ident[:, :]),
                             reads=[R_kcd, R_const], writes=[R_tp])
                    for a in range(4):
                        P.op("dve", lambda e, a=a: e.tensor_copy(out=krc[:, a, :], in_=ptp[:, a * 128:(a + 1) * 128]),
                             reads=[R_tp], writes=[R_krc])
                lt0 = pos0 + T - ts
                P.dma("sp", ctm[:ts, :], c_cosTM[lt0:lt0 + ts, :], writes=[R_ctm])
                P.dma("sp", stm[:ts, :], c_sinTM[lt0:lt0 + ts, :], writes=[R_ctm])

                yc = {"n": 0, "bx": 0, "t": 0}
                LAG_A = 1
                XB = [0, 1, 2]
                YB = [3, 4, 5]

                def nbx():
                    b_ = XB[yc["bx"] % 3]; yc["bx"] += 1
                    return b_

                def post_norm_residual(bk0, bk1, gain, res_ap, res_r, out_ap, out_r):
                    k = yc["t"]; yc["t"] += 1
                    st = statY[k % 2]; Rst = R_statY[k % 2]
                    jk = junk2[k % 2]; Rjk = R_junk2[k % 2]
                    for half, bk in enumerate((bk0, bk1)):
                        P.op("act", lambda e, half=half, bk=bk: e.activation(
                            out=jk[:ts, half * 512:(half + 1) * 512], in_=bank(bk)[:ts, :], func=AF.Square,
                            accum_out=st[:ts, half:half + 1]), reads=[R_bank[bk]], writes=[Rjk, Rst])
                    P.op("dve", lambda e: e.tensor_tensor(out=st[:ts, 2:3], in0=st[:ts, 0:1], in1=st[:ts, 1:2],
                                                          op=ALU.add), reads=[Rst], writes=[Rst])
                    rstd_from(st[:ts, 2:3], st[:ts, 3:4], ts, [Rst], [Rst])
                    for half, bk in enumerate((bk0, bk1)):
                        P.op("dve", lambda e, half=half, bk=bk: e.scalar_tensor_tensor(
                            out=out_ap[:, half * 512:(half + 1) * 512], in0=bank(bk)[:ts, :], scalar=st[:ts, 3:4],
                            in1=gain[:ts, half * 512:(half + 1) * 512], op0=ALU.mult, op1=ALU.mult),
                            reads=[R_bank[bk], Rst, R_gn], writes=[out_r])
                    P.op("pool", lambda e: e.tensor_tensor(out=out_ap, in0=out_ap, in1=res_ap, op=ALU.add),
                         reads=[res_r, out_r], writes=[out_r])

                def gen_a(g):
                    gb = g % 2
                    t0 = g * gs1
                    g0 = t0 // gs
                    P.dma("sp", cosg[gb][:, :], c_cosT[:, pos0 + t0:pos0 + t0 + gs1], writes=[R_cos[gb]])
                    P.dma("sp", sing[gb][:, :], c_sinT[:, pos0 + t0:pos0 + t0 + gs1], writes=[R_sin[gb]])
                    pend = []
                    for tt in range(tpg1):
                        ti = g * tpg1 + tt
                        k = cnt["x"]
                        xt = xst[k % 2]; Rxt = R_xst[k % 2]
                        P.dma("sp", xt[:ts, :], x_src(kind, si, ti * ts, ts), writes=[Rxt])
                        bks = (nbx(), nbx())
                        for half in range(2):
                            for c in range(8):
                                P.op("pe", lambda e, c=c, half=half: e.matmul(
                                    bank(bks[half])[:ts, :], lhsT=oT[:, c, ti * ts:(ti + 1) * ts],
                                    rhs=woab[:, c, half * 512:(half + 1) * 512], start=(c == 0), stop=(c == 7)),
                                    reads=[R_oT[c][g0], R_woab], writes=[R_bank[bks[half]]])
                        yield
                        post_norm_residual(bks[0], bks[1], gpostab, xt[:ts, :], Rxt, Y0[gb][:ts, tt, :], R_Y0[gb])
                        kk = norm_part1(Y0[gb][:ts, tt, :], R_Y0[gb], ts, gprec, gain_res=R_gn)
                        pend.append((kk, tt))
                        yield
                    for (kk, tt) in pend:
                        norm_part2(kk, ts, xn1T[gb][:, :, tt * ts:(tt + 1) * ts], R_xn1[gb])
                        yield

                def gen_b(g):
                    gb = g % 2
                    xn = xn1T[gb]; Rxn = R_xn1[gb]
                    cs, sn = cosg[gb], sing[gb]
                    if g > 0:
                        P.op("pool", lambda e: e.tensor_copy(out=kr[gb][:, :, 0:128], in_=kr[1 - gb][:, :, gs1:gs1 + 128]),
                             reads=[R_kr[1 - gb]], writes=[R_kr[gb]])
                        P.op("pool", lambda e: e.tensor_copy(out=V1[gb][:, 0, :], in_=V1[1 - gb][:, tpg1, :]),
                             reads=[R_V1[1 - gb]], writes=[R_V1[gb]])
                    for fc in range(8):
                        b1 = nbx()
                        for kc in range(8):
                            P.op("pe", lambda e, kc=kc: e.matmul(
                                bank(b1)[:, 0:gs1], lhsT=wc[:, kc, fc * 128:(fc + 1) * 128], rhs=xn[:, kc, :],
                                start=(kc == 0), stop=(kc == 7)), reads=[R_wcq[(fc * 128) // 640], Rxn], writes=[R_bank[b1]])
                        s2 = fc % 2
                        P.op("act", lambda e: e.activation(out=qbf[s2][:, :], in_=bank(b1)[:, 0:gs1], func=AF.Copy, scale=0.125),
                             reads=[R_bank[b1]], writes=[R_qbf[s2]])
                        P.op("dve", lambda e: e.scalar_tensor_tensor(
                            out=ta[s2][:, 0:gs1], in0=bank(b1)[:, 0:gs1], scalar=0.125, in1=cs[:, :], op0=ALU.mult,
                            op1=ALU.mult), reads=[R_bank[b1], R_cos[gb]], writes=[R_ta[s2]])
                        b2 = nbx()
                        P.op("pe", lambda e: e.matmul(bank(b2)[:, 0:gs1], lhsT=rot[:, :], rhs=qbf[s2][:, :], start=True, stop=True),
                             reads=[R_qbf[s2], R_const], writes=[R_bank[b2]])
                        P.op("dve", lambda e: e.tensor_tensor(out=tb[s2][:, 0:gs1], in0=bank(b2)[:, 0:gs1], in1=sn[:, :],
                                                              op=ALU.mult), reads=[R_bank[b2], R_sin[gb]], writes=[R_tb[s2]])
                        P.op("pool", lambda e: e.tensor_tensor(out=qr[gb][:, fc, :], in0=ta[s2][:, 0:gs1], in1=tb[s2][:, 0:gs1],
                                                               op=ALU.add), reads=[R_ta[s2], R_tb[s2]], writes=[R_qr[gb]])
                        yield
                def gen_b2(g):
                    gb = g % 2
                    xn = xn1T[gb]; Rxn = R_xn1[gb]
                    cs, sn = cosg[gb], sing[gb]
                    for kc2 in range(2):
                        b1 = nbx()
                        for kc in range(8):
                            P.op("pe", lambda e, kc=kc: e.matmul(
                                bank(b1)[:, 0:gs1], lhsT=wc[:, kc, 1024 + kc2 * 128:1024 + (kc2 + 1) * 128],
                                rhs=xn[:, kc, :], start=(kc == 0), stop=(kc == 7)),
                                reads=[R_wcq[1], Rxn], writes=[R_bank[b1]])
                        P.op("act", lambda e: e.activation(out=kbf[0][:, kc2, :], in_=bank(b1)[:, 0:gs1], func=AF.Copy),
                             reads=[R_bank[b1]], writes=[R_kbf[0]])
                    yield
                    for a in range(4):
                        s2 = a % 2
                        b1 = nbx()
                        P.op("pe", lambda e: e.matmul(bank(b1)[:, 0:gs1], lhsT=dsel[:, a % 2, :], rhs=kbf[0][:, a // 2, :],
                                                      start=True, stop=True),
                             reads=[R_kbf[0], R_const], writes=[R_bank[b1]])
                        b2 = nbx()
                        P.op("pe", lambda e: e.matmul(bank(b2)[:, 0:gs1], lhsT=dselrot[:, a % 2, :], rhs=kbf[0][:, a // 2, :],
                                                      start=True, stop=True),
                             reads=[R_kbf[0], R_const], writes=[R_bank[b2]])
                        P.op("dve", lambda e: e.tensor_tensor(out=ta[s2][:, 0:gs1], in0=bank(b1)[:, 0:gs1], in1=cs[:, :],
                                                              op=ALU.mult), reads=[R_bank[b1], R_cos[gb]], writes=[R_ta[s2]])
                        P.op("dve", lambda e: e.tensor_tensor(out=tb[s2][:, 0:gs1], in0=bank(b2)[:, 0:gs1], in1=sn[:, :],
                                                              op=ALU.mult), reads=[R_bank[b2], R_sin[gb]], writes=[R_tb[s2]])
                        P.op("pool", lambda e: e.tensor_tensor(
                            out=kr[gb][:, a, 128:128 + gs1], in0=ta[s2][:, 0:gs1], in1=tb[s2][:, 0:gs1], op=ALU.add),
                            reads=[R_ta[s2], R_tb[s2]], writes=[R_kr[gb]])
                        yield
                    for fc in range(8):
                        b1 = nbx()
                        for kc in range(8):
                            P.op("pe", lambda e, kc=kc: e.matmul(
                                bank(b1)[:, 0:gs1], lhsT=wc[:, kc, 1536 + fc * 128:1536 + (fc + 1) * 128],
                                rhs=xn[:, kc, :], start=(kc == 0), stop=(kc == 7)),
                                reads=[R_wcq[(1536 + fc * 128) // 640], Rxn], writes=[R_bank[b1]])
                        s2 = fc % 2
                        P.op("act", lambda e: e.activation(out=tcb[s2][:, 0:gs1], in_=bank(b1)[:, 0:gs1], func=AF.Exp,
                                                           scale=-1.0), reads=[R_bank[b1]], writes=[R_tcb[s2]])
                        P.op("act", lambda e: e.activation(out=tcb[s2][:, 0:gs1], in_=tcb[s2][:, 0:gs1], func=AF.Ln, bias=1.0),
                             reads=[R_tcb[s2]], writes=[R_tcb[s2]])
                        P.op("act", lambda e: e.activation(out=tcb[s2][:, 0:gs1], in_=tcb[s2][:, 0:gs1], func=AF.Exp,
                                                           scale=-1.0), reads=[R_tcb[s2]], writes=[R_tcb[s2]])
                        P.op("dve", lambda e: e.tensor_tensor(out=g1[gb][:, fc, :], in0=bank(b1)[:, 0:gs1],
                                                              in1=tcb[s2][:, 0:gs1], op=ALU.mult),
                             reads=[R_bank[b1], R_tcb[s2]], writes=[R_g1[gb]])
                        yield
                    for tt in range(tpg1):
                        ti = g * tpg1 + tt
                        b1 = nbx()
                        for kc in range(8):
                            P.op("pe", lambda e, kc=kc: e.matmul(
                                bank(b1)[:ts, :], lhsT=xn[:, kc, tt * ts:(tt + 1) * ts], rhs=wc[:, kc, 1024:1536],
                                start=(kc == 0), stop=(kc == 7)), reads=[R_wcq[1], R_wcq[2], Rxn], writes=[R_bank[b1]])
                        P.op("dve", lambda e: e.tensor_copy(out=V1[gb][:ts, 1 + tt, :], in_=bank(b1)[:ts, 256:512]),
                             reads=[R_bank[b1]], writes=[R_V1[gb]])
                        if ti == nt - 1:
                            ysk = kvst; Rysk = R_kvst
                            P.op("dve", lambda e: e.tensor_copy(out=ysk[:ts, 0:256], in_=bank(b1)[:ts, 256:512]),
                                 reads=[R_bank[b1]], writes=[Rysk])
                            dv_ap = o_cvp[si, :, :] if kind == "p" else o_cvs[:, :]
                            dk_ap = o_ckp[si, :, :] if kind == "p" else o_cks[:, :]
                            P.dma("pool", dv_ap, ysk[:ts, 0:256], reads=[Rysk], is_output=True)
                            P.op("dve", lambda e: e.tensor_copy(out=kst[:ts, :], in_=bank(b1)[:ts, 0:256]),
                                 reads=[R_bank[b1]], writes=[R_kst])
                            for hk in range(4):
                                for b2_ in range(2):
                                    P.op("dve", lambda e, hk=hk, b2_=b2_: e.tensor_copy(
                                        out=ksw[:ts, hk * 64 + b2_ * 32:hk * 64 + b2_ * 32 + 32],
                                        in_=kst[:ts, hk * 64 + (1 - b2_) * 32:hk * 64 + (1 - b2_) * 32 + 32]),
                                        reads=[R_kst], writes=[R_ksw])
                            P.op("dve", lambda e: e.tensor_tensor(out=kst[:ts, :], in0=kst[:ts, :], in1=ctm[:ts, :],
                                                                  op=ALU.mult), reads=[R_kst, R_ctm], writes=[R_kst])
                            P.op("dve", lambda e: e.tensor_tensor(out=ksw[:ts, :], in0=ksw[:ts, :], in1=stm[:ts, :],
                                                                  op=ALU.mult), reads=[R_ksw, R_ctm], writes=[R_ksw])
                            P.op("dve", lambda e: e.tensor_tensor(out=ysk[:ts, 256:512], in0=kst[:ts, :],
                                                                  in1=ksw[:ts, :], op=ALU.add),
                                 reads=[R_kst, R_ksw], writes=[Rysk])
                            P.dma("pool", dk_ap, ysk[:ts, 256:512], reads=[Rysk], is_output=True)
                        yield

                def c_blocks(g, j, a):
                    gb = g % 2
                    J = g * tpg1 + j
                    blocks = []
                    if kind == "p":
                        if J > 0:
                            blocks.append((kr[gb][:, a, j * 128:(j + 1) * 128], 128,
                                           V1[gb][:, j, a * 64:(a + 1) * 64], "prev", [R_kr[gb]], [R_V1[gb]]))
                        blocks.append((kr[gb][:, a, (j + 1) * 128:(j + 2) * 128], 128,
                                       V1[gb][:, j + 1, a * 64:(a + 1) * 64], "diag", [R_kr[gb]], [R_V1[gb]]))
                    else:
                        blocks.append((krc[:, a, :], 128, Vcc[:, a * 64:(a + 1) * 64], "c", [R_krc], [R_Vcc]))
                        blocks.append((kr[gb][:, a, 128:128 + TS], TS, V1[gb][0:TS, 1, a * 64:(a + 1) * 64], "n",
                                       [R_kr[gb]], [R_V1[gb]]))
                    return blocks

                def c_stage1(g, n):
                    gb = g % 2
                    j, a = n // 4, n % 4
                    blocks = c_blocks(g, j, a)
                    for par in range(2):
                        sbk = YB[par]
                        Sps = bank(sbk)
                        po = par * 64
                        pt = PTc[(n % 2) * 2 + par]; Rpt = R_PTc[(n % 2) * 2 + par]
                        for bi_, (k_ap, nk, v_ap, tag, rk, rv) in enumerate(blocks):
                            if qw == 128:
                                col = bi_ * 2 * qw
                                P.op("pe", lambda e, k_ap=k_ap, nk=nk, col=col: e.matmul(
                                    Sps[:nk, col:col + 2 * qw].rearrange("p (h q) -> p h q", h=2),
                                    lhsT=k_ap[po:po + 64, :],
                                    rhs=qr[gb][po:po + 64, 2 * a:2 * a + 2, j * qw:(j + 1) * qw],
                                    start=True, stop=True), reads=rk + [R_qr[gb]], writes=[R_bank[sbk]])
                                continue
                            for hi in range(2):
                                fc = 2 * a + hi
                                col = (bi_ * 2 + hi) * qw
                                P.op("pe", lambda e, k_ap=k_ap, nk=nk, fc=fc, col=col: e.matmul(
                                    Sps[:nk, col:col + qw], lhsT=k_ap[po:po + 64, :],
                                    rhs=qr[gb][po:po + 64, fc, j * qw:(j + 1) * qw],
                                    start=True, stop=True), reads=rk + [R_qr[gb]], writes=[R_bank[sbk]])
                        for bi_, (k_ap, nk, v_ap, tag, rk, rv) in enumerate(blocks):
                            c0 = bi_ * 2 * qw
                            P.op("act", lambda e, nk=nk, c0=c0: e.activation(
                                out=pt[:nk, c0:c0 + 2 * qw], in_=Sps[:nk, c0:c0 + 2 * qw], func=AF.Exp),
                                reads=[R_bank[sbk]], writes=[Rpt])
                            if tag == "prev":
                                P.op("pool", lambda e, c0=c0: e.memset(
                                    pt[0:64, c0:c0 + 2 * qw].rearrange("p (h q) -> p h q", h=2)[:, :, 64:128], 0.0),
                                    writes=[Rpt])
                            if tag == "diag":
                                P.op("pool", lambda e, c0=c0: e.memset(
                                    pt[64:128, c0:c0 + 2 * qw].rearrange("p (h q) -> p h q", h=2)[:, :, 0:64], 0.0),
                                    writes=[Rpt])

                def c_stage2(g, n):
                    gb = g % 2
                    j, a = n // 4, n % 4
                    blocks = c_blocks(g, j, a)
                    nb = len(blocks)
                    ocb = YB[2]
                    OC = bank(ocb)
                    for par in range(2):
                        po = par * 64
                        pt = PTc[(n % 2) * 2 + par]; Rpt = R_PTc[(n % 2) * 2 + par]
                        if qw == 128:
                            for bi_, (k_ap, nk, v_ap, tag, rk, rv) in enumerate(blocks):
                                col = bi_ * 2 * qw
                                P.op("pe", lambda e, v_ap=v_ap, nk=nk, col=col, bi_=bi_: e.matmul(
                                    OC[po:po + 64, 0:256], lhsT=v_ap, rhs=pt[:nk, col:col + 256],
                                    start=(bi_ == 0), stop=(bi_ == nb - 1)),
                                    reads=rv + [Rpt], writes=[R_bank[ocb]])
                            for bi_, (k_ap, nk, v_ap, tag, rk, rv) in enumerate(blocks):
                                col = bi_ * 2 * qw
                                P.op("pe", lambda e, nk=nk, col=col, bi_=bi_: e.matmul(
                                    OC[po:po + 64, 256:512], lhsT=ones[:nk, 0:64],
                                    rhs=pt[:nk, col:col + 256], start=(bi_ == 0), stop=(bi_ == nb - 1)),
                                    reads=[Rpt, R_const], writes=[R_bank[ocb]])
                            continue
                        for hi in range(2):
                            for bi_, (k_ap, nk, v_ap, tag, rk, rv) in enumerate(blocks):
                                col = (bi_ * 2 + hi) * qw
                                P.op("pe", lambda e, v_ap=v_ap, nk=nk, col=col, bi_=bi_: e.matmul(
                                    OC[po:po + 64, hi * 128:hi * 128 + qw], lhsT=v_ap, rhs=pt[:nk, col:col + qw],
                                    start=(bi_ == 0), stop=(bi_ == nb - 1)),
                                    reads=rv + [Rpt], writes=[R_bank[ocb]])
                            for bi_, (k_ap, nk, v_ap, tag, rk, rv) in enumerate(blocks):
                                col = (bi_ * 2 + hi) * qw
                                P.op("pe", lambda e, nk=nk, col=col, bi_=bi_: e.matmul(
                                    OC[po:po + 64, 256 + hi * 128:256 + hi * 128 + qw], lhsT=ones[:nk, 0:64],
                                    rhs=pt[:nk, col:col + qw], start=(bi_ == 0), stop=(bi_ == nb - 1)),
                                    reads=[Rpt, R_const], writes=[R_bank[ocb]])
                    s2 = n % 2
                    rc = recc[s2]; Rrc = R_recc[s2]
                    tm = tmpc[s2]; Rtm = R_tmpc[s2]
                    for hi in range(2):
                        fc = 2 * a + hi
                        P.op("act", lambda e, hi=hi, fc=fc: e.activation(
                            out=rc[:, hi * 128:hi * 128 + qw], in_=OC[:, 256 + hi * 128:256 + hi * 128 + qw],
                            func=AF.Ln, bias=esink[:, fc:fc + 1]),
                            reads=[R_bank[ocb], R_const], writes=[Rrc])
                    if qw == 128:
                        P.op("act", lambda e: e.activation(out=rc[:, 0:256], in_=rc[:, 0:256], func=AF.Exp, scale=-1.0),
                             reads=[Rrc], writes=[Rrc])
                        P.op("dve", lambda e: e.tensor_tensor(out=tm[:, 0:256], in0=OC[:, 0:256], in1=rc[:, 0:256],
                                                              op=ALU.mult), reads=[R_bank[ocb], Rrc], writes=[Rtm])
                        P.op("pool", lambda e: e.tensor_tensor(
                            out=qr[gb][:, 2 * a:2 * a + 2, j * qw:(j + 1) * qw],
                            in0=tm[:, 0:256].rearrange("p (h q) -> p h q", h=2),
                            in1=g1[gb][:, 2 * a:2 * a + 2, j * qw:(j + 1) * qw], op=ALU.mult),
                            reads=[Rtm, R_g1[gb]], writes=[R_qr[gb]])
                    else:
                        for hi in range(2):
                            fc = 2 * a + hi
                            P.op("act", lambda e, hi=hi: e.activation(out=rc[:, hi * 128:hi * 128 + qw],
                                                                      in_=rc[:, hi * 128:hi * 128 + qw], func=AF.Exp,
                                                                      scale=-1.0),
                                 reads=[Rrc], writes=[Rrc])
                            P.op("dve", lambda e, hi=hi: e.tensor_tensor(
                                out=tm[:, hi * 128:hi * 128 + qw], in0=OC[:, hi * 128:hi * 128 + qw],
                                in1=rc[:, hi * 128:hi * 128 + qw], op=ALU.mult),
                                reads=[R_bank[ocb], Rrc], writes=[Rtm])
                            P.op("pool", lambda e, hi=hi, fc=fc: e.tensor_tensor(
                                out=qr[gb][:, fc, j * qw:(j + 1) * qw], in0=tm[:, hi * 128:hi * 128 + qw],
                                in1=g1[gb][:, fc, j * qw:(j + 1) * qw], op=ALU.mult),
                                reads=[Rtm, R_g1[gb]], writes=[R_qr[gb]])

                def gen_c(g):
                    nn = tpg1 * 4
                    c_stage1(g, 0)
                    yield
                    for n in range(nn):
                        if n + 1 < nn:
                            c_stage1(g, n + 1)
                            yield
                        c_stage2(g, n)
                        yield

                def gen_d(g):
                    gb = g % 2
                    for tt in range(tpg1):
                        ti = g * tpg1 + tt
                        bks = (nbx(), nbx())
                        for half in range(2):
                            for c in range(8):
                                P.op("pe", lambda e, c=c, half=half: e.matmul(
                                    bank(bks[half])[:ts, :], lhsT=qr[gb][:, c, tt * ts:(tt + 1) * ts],
                                    rhs=woc[:, c, half * 512:(half + 1) * 512], start=(c == 0), stop=(c == 7)),
                                    reads=[R_qr[gb], R_woc], writes=[R_bank[bks[half]]])
                        yield
                        sk = yc["n"] % 2; yc["n"] += 1
                        ysk = t1b[sk]; Rysk = R_t1[sk]
                        post_norm_residual(bks[0], bks[1], gpostc, Y0[gb][:ts, tt, :], R_Y0[gb], ysk[:ts, :], Rysk)
                        dst = yp[si, ti * ts:(ti + 1) * ts, :] if kind == "p" else ys[0:ts, :]
                        P.dma("pool", dst, ysk[:ts, :], reads=[Rysk], is_output=True)
                        yield

                def chain(*gens):
                    for g_ in gens:
                        yield from g_

                def run_streams(streams):
                    alive = list(streams)
                    while alive:
                        for s_ in list(alive):
                            try:
                                next(s_)
                            except StopIteration:
                                alive.remove(s_)

                flags = {}

                def wait_for(*keys):
                    while not all(flags.get(k) for k in keys):
                        yield

                def SA():
                    for g in range(ng1):
                        if g >= 2:
                            yield from wait_for(("d", g - 2))
                        if g >= 1:
                            yield from wait_for(("b2", g - 1))
                        yield from gen_a(g)
                        flags[("a", g)] = True
                        first = True
                        for _ in gen_b(g):
                            if first:
                                flags[("halo", g)] = True
                                first = False
                            yield
                        flags[("halo", g)] = True
                        flags[("bq", g)] = True

                def SB():
                    for g in range(ng1):
                        yield from wait_for(("a", g), ("halo", g))
                        yield from gen_b2(g)
                        flags[("b2", g)] = True

                def SC():
                    for g in range(ng1):
                        yield from wait_for(("bq", g), ("b2", g))
                        yield from gen_c(g)
                        flags[("c", g)] = True

                def SD():
                    for g in range(ng1):
                        yield from wait_for(("c", g))
                        yield from gen_d(g)
                        flags[("d", g)] = True

                run_streams([SA(), SB(), SC(), SD()])
            P.barrier()


        with nc.Block() as block:
            P.finalize(block, sems, dma_sems)
    return nc


_NC_CACHE = {}


def _prep(x_prompt, x_sample, cache_a_k, cache_a_v, cache_b_k, cache_b_v, cache_c_k, cache_c_v,
          ab_norm_pre, ab_w_in, ab_w_out, ab_norm_post, a_rel_bias,
          c_norm_pre, c_w_in, c_sinks, c_w_out, c_norm_post):
    f32 = np.float32
    A = lambda a: np.ascontiguousarray(np.asarray(a, dtype=f32))
    x_prompt, x_sample = A(x_prompt), A(x_sample)
    ncore = 8
    cst = _consts()
    w = A(ab_w_in)[0]
    w_ab = np.zeros((8, D, 512), f32)
    for pi in range(8):
        base = 0 if pi < 4 else 2048
        hp = pi % 4
        sl = lambda blk: w[:, base + blk * 512 + hp * 128: base + blk * 512 + (hp + 1) * 128]
        w_ab[pi, :, 0:128] = sl(0)
        w_ab[pi, :, 128:256] = sl(3)
        w_ab[pi, :, 256:384] = sl(1)
        w_ab[pi, :, 384:512] = sl(2)
    rep = lambda v: np.ascontiguousarray(np.broadcast_to(A(v).reshape(1, D), (128, D)))
    bp, bs = _bias_tiles(A(a_rel_bias)[0])
    sk = A(c_sinks)[0]
    sinks_l = np.zeros((128, 8), f32)
    for fc in range(8):
        sinks_l[0:64, fc] = sk[2 * fc]
        sinks_l[64:128, fc] = sk[2 * fc + 1]
    common = {
        "w_ab": w_ab, "w_oab": A(ab_w_out)[0], "w_c": A(c_w_in)[0], "w_oc": A(c_w_out)[0],
        "gpre_ab": rep(ab_norm_pre[0]), "gpost_ab": rep(ab_norm_post[0]),
        "gpre_c": rep(c_norm_pre[0]), "gpost_c": rep(c_norm_post[0]),
        "biasP": bp, "biasS": bs, "sinks": sinks_l,
        "c_ident": cst["ident"], "c_tri": cst["tri"], "c_ones": cst["ones"], "c_lmask": cst["lmask"],
        "c_rot": cst["rot"], "c_dsel": cst["dsel"], "c_dselrot": cst["dselrot"],
        "c_cosT": cst["cosT"], "c_sinT": cst["sinT"], "c_cosTM": cst["cosTM"], "c_sinTM": cst["sinTM"],
    }
    cak, cav = A(cache_a_k)[0], A(cache_a_v)[0]
    cbk, cbv = A(cache_b_k)[0], A(cache_b_v)[0]
    cck, ccv = A(cache_c_k)[0], A(cache_c_v)[0]
    in_maps = []
    for i in range(ncore):
        m = dict(common)
        m["xp"] = np.ascontiguousarray(x_prompt[2 * i:2 * i + 2])
        m["xs"] = np.ascontiguousarray(x_sample[i])
        m["ca_k"] = cak[i].reshape(512, 512); m["ca_v"] = cav[i].reshape(512, 512)
        m["cb_k"] = cbk[i].reshape(PAST, 512); m["cb_v"] = cbv[i].reshape(PAST, 512)
        m["cc_k"] = cck[i].reshape(128, 256); m["cc_v"] = ccv[i].reshape(128, 256)
        in_maps.append(m)
    return in_maps


def kernel(**inputs):
    ncore = 8
    in_maps = _prep(**inputs)
    if "nc" not in _NC_CACHE:
        _NC_CACHE["nc"] = build()
    nc = _NC_CACHE["nc"]
    res = run_bass_kernel_spmd(nc, in_maps, core_ids=list(range(ncore)))
    return _gather(res.results)


def _gather(R):
    ncore = len(R)
    cat = lambda name: np.concatenate([R[i][name] for i in range(ncore)], axis=0)
    stk = lambda name: np.stack([R[i][name] for i in range(ncore)], axis=0)
    y_prompt = cat("yp")
    y_sample = stk("ys")
    out = (
        y_prompt, y_sample,
        cat("o_akp").reshape(1, 16, 512, 8, 64), cat("o_avp").reshape(1, 16, 512, 8, 64),
        cat("o_bkp").reshape(1, 16, S, 8, 64), cat("o_bvp").reshape(1, 16, S, 8, 64),
        cat("o_ckp").reshape(1, 16, 128, 4, 64), cat("o_cvp").reshape(1, 16, 128, 4, 64),
        stk("o_aks").reshape(1, 8, TS, 8, 64), stk("o_avs").reshape(1, 8, TS, 8, 64),
        stk("o_bks").reshape(1, 8, TS, 8, 64), stk("o_bvs").reshape(1, 8, TS, 8, 64),
        stk("o_cks").reshape(1, 8, TS, 4, 64), stk("o_cvs").reshape(1, 8, TS, 4, 64),
    )
    return tuple(np.ascontiguousarray(o.astype(np.float32)) for o in out)
```

```python
import numpy as np
import concourse.bass as bass
import concourse.mybir as mybir
from concourse.bass_utils import run_bass_kernel_spmd

F32 = mybir.dt.float32
BF16 = mybir.dt.bfloat16
AF = mybir.ActivationFunctionType
ALU = mybir.AluOpType

D = 1024
S = 2048
TS = 32
PAST = 2048
EPS = 1e-6
NEG = -30000.0
SEM_LIM = 20000


class Res:
    __slots__ = ("w", "r", "name", "excl")

    def __init__(self, name="", excl=False):
        self.w = None
        self.r = {}
        self.name = name
        self.excl = excl


class _Rec:
    def __init__(self):
        self.call = None

    def __getattr__(self, name):
        def f(*args, **kwargs):
            self.call = (name, args, kwargs)
            return None
        return f


class Prog:
    ENG = ("pe", "act", "dve", "pool", "sp")

    def __init__(self, nc):
        self.nc = nc
        self.ops = {e: [] for e in self.ENG}
        self.waited = {e: {} for e in self.ENG}
        self.ndma_sems = 12
        self.ndma_q = {"sp": 12, "pool": 6}
        self.dma_cnt = {q: [0] * self.ndma_sems for q in ("sp", "pool")}
        self.dma_next = {"sp": 0, "pool": 0}
        self.dma_last = {q: [None] * self.ndma_sems for q in ("sp", "pool")}
        self.out_dma_refs = []
        self.last_pe = None

    def _need(self, eng, ref, waits):
        if ref is None:
            return
        if ref[0] == "op":
            _, e2, idx = ref
            if e2 == eng and eng == "pe":
                return
            if self.waited[eng].get(e2, -1) >= idx:
                return
            if e2 == eng and idx >= len(self.ops[eng]):
                return
            self.waited[eng][e2] = idx
            self.ops[e2][idx]["inc"] = True
            waits.append(ref)
        else:
            _, q, slot, val = ref
            key = ("dma", q, slot)
            if self.waited[eng].get(key, -1) >= val:
                return
            self.waited[eng][key] = val
            waits.append(ref)

    def _deps(self, eng, reads, writes, same_engine_war=False):
        waits = []
        for r in reads:
            self._need(eng, r.w, waits)
        for w in writes:
            self._need(eng, w.w, waits)
            for e2, ref in w.r.items():
                self._need(eng, ref, waits)
        return waits

    def _commit(self, ref, reads, writes):
        for r in reads:
            r.r[ref[1] if ref[0] == "op" else ("dma", ref[1], ref[2])] = ref
        for w in writes:
            w.w = ref
            w.r = {}

    def op(self, eng, fn, reads=(), writes=()):
        ex = [r for r in reads if r.excl]
        if ex:
            reads = [r for r in reads if not r.excl]
            writes = list(writes) + ex
        waits = self._deps(eng, reads, writes)
        idx = len(self.ops[eng])
        rec = _Rec()
        fn(rec)
        assert rec.call is not None
        self.ops[eng].append({"fn": rec.call, "waits": waits, "inc": False, "dma": None})
        self._commit(("op", eng, idx), reads, writes)
        return ("op", eng, idx)

    def dma(self, q, out, in_, reads=(), writes=(), is_output=False):
        waits = self._deps(q, reads, writes)
        slot = self.dma_next[q]
        self.dma_next[q] = (slot + 1) % self.ndma_q[q]
        prev = self.dma_last[q][slot]
        if prev is not None:
            self._need(q, prev, waits)
        self.dma_cnt[q][slot] += 1
        val = self.dma_cnt[q][slot] * 16
        ref = ("dma", q, slot, val)
        self.dma_last[q][slot] = ref
        self.ops[q].append({"fn": ("dma_start", (), {"out": out, "in_": in_}), "waits": waits,
                            "inc": False, "dma": (q, slot)})
        self._commit(ref, reads, writes)
        if is_output:
            self.out_dma_refs.append(ref)
        return ref

    def barrier(self):
        refs = []
        for e in self.ENG:
            for idx in range(len(self.ops[e]) - 1, -1, -1):
                o = self.ops[e][idx]
                if o["fn"] is not None and o["dma"] is None:
                    refs.append(("op", e, idx))
                    break
        for q in ("sp", "pool"):
            for slot in range(self.ndma_sems):
                if self.dma_last[q][slot] is not None:
                    refs.append(self.dma_last[q][slot])
        for e in self.ENG:
            waits = []
            for ref in refs:
                if ref[0] == "op" and ref[1] == e:
                    continue
                self._need(e, ref, waits)
            self.ops[e].append({"fn": None, "waits": waits, "inc": False, "dma": None})

    def finalize(self, block, sems, dma_sems):
        nc = self.nc
        marks = {}
        for e in self.ENG:
            c = 0
            m = []
            for o in self.ops[e]:
                if o["inc"] and o["fn"] is not None and o["dma"] is None:
                    c += 1
                m.append(c)
            marks[e] = m
            assert c <= SEM_LIM * len(sems[e]), (e, c)

        def sem_of(e, idx):
            m = marks[e][idx]
            assert m >= 1
            k = (m - 1) // SEM_LIM
            return sems[e][k], (m - 1) % SEM_LIM + 1

        engs = {"pe": nc.tensor, "act": nc.scalar, "dve": nc.vector, "pool": nc.gpsimd, "sp": nc.sync}

        def _pinfo(ap):
            fs = 1
            for s_ in list(ap.tensor.shape)[1:]:
                fs *= int(s_)
            p0 = int(ap.offset) // fs
            col = int(ap.offset) % fs
            return p0, int(ap.ap[0][1]), col, fs
        prev = None
        nviol = 0
        for o in self.ops["pe"]:
            if o["fn"] is None:
                continue
            name_, args_, kw_ = o["fn"]
            out_ap = args_[0] if args_ else kw_["out"]
            l_ap = kw_.get("lhsT", kw_.get("in_"))
            p0, kk, _, _ = _pinfo(l_ap)
            _, _, col, fs = _pinfo(out_ap)
            esz = 4 if fs in (512, 1024) and out_ap.tensor.name.startswith("pb") else 2
            bank_id = (out_ap.tensor.name, (col * esz) // 2048)
            rows = (p0, p0 + kk)
            cur = (rows, bank_id)
            if prev is not None and kk < 128 and (prev[0][1] - prev[0][0]) < 128:
                disjoint = rows[0] >= prev[0][1] or prev[0][0] >= rows[1]
                if disjoint and prev[1] == bank_id:
                    nviol += 1
            prev = cur
        assert nviol == 0, f"row-tile bank violations: {nviol}"

        def run(e, eng):
            for idx, o in enumerate(self.ops[e]):
                for ref in o["waits"]:
                    if ref[0] == "op":
                        s_, v_ = sem_of(ref[1], ref[2])
                        eng.wait_ge(s_, v_)
                    else:
                        eng.wait_ge(dma_sems[ref[1]][ref[2]], ref[3])
                if o["fn"] is None:
                    continue
                name_, args_, kw_ = o["fn"]
                ins = getattr(eng, name_)(*args_, **kw_)
                if o["dma"] is not None:
                    ins.then_inc(dma_sems[o["dma"][0]][o["dma"][1]], 16)
                elif o["inc"]:
                    s_, _ = sem_of(e, idx)
                    ins.then_inc(s_, 1)

        @block.tensor
        def _(eng):
            run("pe", eng)

        @block.scalar
        def _(eng):
            run("act", eng)

        @block.vector
        def _(eng):
            run("dve", eng)

        @block.gpsimd
        def _(eng):
            run("pool", eng)

        @block.sync
        def _(eng):
            run("sp", eng)


def _consts():
    c = {}
    i = np.arange(128)
    c["ident"] = np.eye(128, dtype=np.float32)
    c["tri"] = (i[:, None] >= i[None, :]).astype(np.float32)
    c["ones"] = np.ones((128, 128), np.float32)
    c["lmask"] = (i[:, None] < i[None, :]).astype(np.float32)
    rot = np.zeros((128, 128), np.float32)
    for p in range(128):
        if p % 64 < 32:
            rot[p + 32, p] = 1.0
        else:
            rot[p - 32, p] = 1.0
    c["rot"] = rot
    dsel = np.zeros((2, 128, 128), np.float32)
    for a in range(2):
        for p in range(128):
            dsel[a, a * 64 + (p % 64), p] = 1.0
    c["dsel"] = dsel
    c["dselrot"] = np.stack([dsel[a] @ rot for a in range(2)])
    half = 32
    inv = (10000.0 ** (-np.arange(half, dtype=np.float32) * np.float32(2.0 / 64))).astype(np.float32)
    pos = np.arange(S + TS, dtype=np.float32)
    ang = (pos[:, None] * inv[None, :]).astype(np.float32)
    cos, sin = np.cos(ang).astype(np.float32), np.sin(ang).astype(np.float32)
    pidx = np.arange(128) % 32
    sign = np.where((np.arange(128) % 64) < 32, -1.0, 1.0).astype(np.float32)
    c["cosT"] = np.ascontiguousarray(cos[:, pidx].T)
    c["sinT"] = np.ascontiguousarray((sin[:, pidx] * sign[None, :]).T)
    fidx = np.arange(256) % 32
    fsign = np.where((np.arange(256) % 64) < 32, -1.0, 1.0).astype(np.float32)
    c["cosTM"] = np.ascontiguousarray(cos[:, fidx])
    c["sinTM"] = np.ascontiguousarray(sin[:, fidx] * fsign[None, :])
    return c


def _bias_tiles(table):
    k = np.arange(128)[:, None]
    q = np.arange(128)[None, :]
    bp = np.zeros((8, 128, 640), np.float32)
    for slot in range(5):
        rel = (4 - slot) * 128 + (q - k)
        idx = np.clip(rel, -128, 128) + 128
        bp[:, :, slot * 128:(slot + 1) * 128] = table[:, idx]
    qs = PAST + np.arange(TS)[None, :]
    bs = np.zeros((8, 128, 160), np.float32)
    for slot in range(5):
        kpos = PAST - 512 + slot * 128 + np.arange(128)[:, None]
        idx = np.clip(qs - kpos, -128, 128) + 128
        bs[:, :, slot * 32:(slot + 1) * 32] = table[:, idx]
    return bp, bs


def build():
    nc = bass.Bass("TRN2", target_bir_lowering=False)
    P = Prog(nc)

    def din(name, shape):
        return nc.dram_tensor(name, list(shape), F32, kind="ExternalInput").ap()

    def dout(name, shape):
        return nc.dram_tensor(name, list(shape), F32, kind="ExternalOutput").ap()

    xp = din("xp", [2, S, D])
    xs = din("xs", [TS, D])
    ca_k = din("ca_k", [512, 512]); ca_v = din("ca_v", [512, 512])
    cb_k = din("cb_k", [PAST, 512]); cb_v = din("cb_v", [PAST, 512])
    cc_k = din("cc_k", [128, 256]); cc_v = din("cc_v", [128, 256])
    w_ab = din("w_ab", [8, D, 512])
    w_oab = din("w_oab", [D, D])
    w_c = din("w_c", [D, 2560])
    w_oc = din("w_oc", [D, D])
    gpre_ab = din("gpre_ab", [128, D]); gpost_ab = din("gpost_ab", [128, D])
    gpre_c = din("gpre_c", [128, D]); gpost_c = din("gpost_c", [128, D])
    biasP = din("biasP", [8, 128, 640]); biasS = din("biasS", [8, 128, 160])
    sinks = din("sinks", [128, 8])
    c_ident = din("c_ident", [128, 128]); c_tri = din("c_tri", [128, 128]); c_ones = din("c_ones", [128, 128])
    c_lmask = din("c_lmask", [128, 128]); c_rot = din("c_rot", [128, 128])
    c_dsel = din("c_dsel", [2, 128, 128]); c_dselrot = din("c_dselrot", [2, 128, 128])
    c_cosT = din("c_cosT", [128, S + TS]); c_sinT = din("c_sinT", [128, S + TS])
    c_cosTM = din("c_cosTM", [S + TS, 256]); c_sinTM = din("c_sinTM", [S + TS, 256])

    yp = dout("yp", [2, S, D]); ys = dout("ys", [TS, D])
    o_akp = dout("o_akp", [2, 512, 512]); o_avp = dout("o_avp", [2, 512, 512])
    o_bkp = dout("o_bkp", [2, S, 512]); o_bvp = dout("o_bvp", [2, S, 512])
    o_ckp = dout("o_ckp", [2, 128, 256]); o_cvp = dout("o_cvp", [2, 128, 256])
    o_aks = dout("o_aks", [TS, 512]); o_avs = dout("o_avs", [TS, 512])
    o_bks = dout("o_bks", [TS, 512]); o_bvs = dout("o_bvs", [TS, 512])
    o_cks = dout("o_cks", [TS, 256]); o_cvs = dout("o_cvs", [TS, 256])

    from contextlib import ExitStack
    es = ExitStack()

    def sb(name, shape, dt):
        return es.enter_context(nc.sbuf_tensor(name, list(shape), dt))

    def ps(name, shape, dt):
        return es.enter_context(nc.psum_tensor(name, list(shape), dt))

    with es:
        sems = {e: [es.enter_context(nc.semaphore(f"s_{e}{k}")) for k in range(2)] for e in Prog.ENG}
        dma_sems = {q: [es.enter_context(nc.semaphore(f"d_{q}{k}")) for k in range(P.ndma_sems)]
                    for q in ("sp", "pool")}

        ident = sb("ident", [128, 128], BF16); tri = sb("tri", [128, 128], BF16)
        ones = sb("ones", [128, 128], BF16); lmask = sb("lmask", [128, 128], BF16)
        rot = sb("rot", [128, 128], BF16)
        dsel = sb("dsel", [128, 2, 128], BF16); dselrot = sb("dselrot", [128, 2, 128], BF16)
        esink = sb("esink", [128, 8], F32)
        R_const = Res("const")
        for t_, src in ((ident, c_ident), (tri, c_tri), (ones, c_ones), (lmask, c_lmask), (rot, c_rot)):
            P.dma("pool", t_[:], src, writes=[R_const])
        for a in range(2):
            P.dma("pool", dsel[:, a, :], c_dsel[a], writes=[R_const])
            P.dma("pool", dselrot[:, a, :], c_dselrot[a], writes=[R_const])
        for t_, src in ((esink, sinks),):
            P.dma("sp", t_[:], src, writes=[R_const])
        P.op("act", lambda e: e.activation(out=esink[:], in_=esink[:], func=AF.Exp), reads=[R_const], writes=[R_const])

        pbank = [ps(f"pb{i}", [128, 1024], F32) for i in range(3)]
        pb6 = ps("pb6", [128, 512], F32)
        ptp0 = ps("ptp0", [128, 1024], BF16)
        ptps = [ptp0, ptp0]
        ptp = ptps[0]
        R_bank = [Res(f"bank{i}", excl=True) for i in range(7)]
        R_tp = Res("tp0", excl=True)
        R_tps = [R_tp, R_tp]

        def bank(i):
            if i == 6:
                return pb6[:, :]
            return pbank[i // 2][:, (i % 2) * 512:(i % 2 + 1) * 512]

        oT = sb("oT", [128, 8, S], BF16)
        R_oT = [[Res(f"oT{c}_{g}") for g in range(4)] for c in range(8)]
        xst = [sb(f"xst{i}", [128, D], F32) for i in range(2)]
        R_xst = [Res(f"xst{i}") for i in range(2)]
        junk2 = [sb("junk2_0", [128, D], BF16)] * 2; R_junk2 = [Res()] * 2
        stat2 = [sb(f"stat2_{i}", [128, 2], F32) for i in range(2)]; R_stat2 = [Res() for _ in range(2)]
        xsb = [sb(f"xsb{i}", [128, D], BF16) for i in range(2)]
        R_xsb = [Res(f"xsb{i}") for i in range(2)]
        cnt = {"x": 0, "stg": 0, "pj": 0}

        seqs = [("p", 0), ("p", 1), ("s", 0)]

        def x_src(kind, si, t0, n):
            return xp[si, t0:t0 + n, :] if kind == "p" else xs[t0:t0 + n, :]

        def rstd_from(ss_ap, out_ap, n, reads, writes):
            P.op("act", lambda e: e.activation(out=out_ap, in_=ss_ap, func=AF.Ln, scale=1.0 / D, bias=EPS),
                 reads=reads, writes=writes)
            P.op("act", lambda e: e.activation(out=out_ap, in_=out_ap, func=AF.Exp, scale=-0.5),
                 reads=writes, writes=writes)

        def norm_part1(src_ap, src_res, ts, gain, gain_res=None):
            gain_res = gain_res or R_const
            k = cnt["x"]; cnt["x"] += 1
            xb = xsb[k % 2]; Rxb = R_xsb[k % 2]
            st = stat2[k % 2]; Rst = R_stat2[k % 2]
            jk = junk2[k % 2]; Rjk = R_junk2[k % 2]
            P.op("act", lambda e: e.activation(out=jk[:ts, :], in_=src_ap, func=AF.Square,
                                               accum_out=st[:ts, 0:1]),
                 reads=[src_res], writes=[Rjk, Rst])
            rstd_from(st[:ts, 0:1], st[:ts, 1:2], ts, [Rst], [Rst])
            P.op("dve", lambda e: e.scalar_tensor_tensor(out=xb[:ts, :], in0=src_ap, scalar=st[:ts, 1:2],
                                                         in1=gain[:ts, :], op0=ALU.mult, op1=ALU.mult),
                 reads=[src_res, Rst, gain_res], writes=[Rxb])
            return k

        def norm_part2(k, ts, dst_ap3, dst_res):
            xb = xsb[k % 2]; Rxb = R_xsb[k % 2]
            tp = ptps[k % 2]; Rtp = R_tps[k % 2]
            for c in range(8):
                P.op("pe", lambda e, c=c: e.transpose(out=tp[:, c * ts:(c + 1) * ts],
                                                      in_=xb[:ts, c * 128:(c + 1) * 128], identity=ident[:ts, :ts]),
                     reads=[Rxb, R_const], writes=[Rtp])
            P.op("dve", lambda e: e.tensor_copy(out=dst_ap3,
                                                in_=tp[:, 0:8 * ts].rearrange("p (c t) -> p c t", c=8)),
                 reads=[Rtp], writes=[dst_res])

        def norm_transpose(src_ap, src_res, ts, gain, dst_ap3, dst_res, gain_res=None):
            k = norm_part1(src_ap, src_res, ts, gain, gain_res)
            norm_part2(k, ts, dst_ap3, dst_res)

        for (kind, si) in seqs:
            T = S if kind == "p" else TS
            ts = min(T, 128)
            nt = T // ts
            gs = min(T, 512)
            ng = T // gs
            tpg = gs // ts
            pos0 = 0 if kind == "p" else PAST

            with ExitStack() as l0:
                def sb0(name, shape, dt):
                    return l0.enter_context(nc.sbuf_tensor(f"{name}_{kind}{si}", list(shape), dt))

                def mk0(name, shape, dt, n):
                    return [sb0(f"{name}{i}", shape, dt) for i in range(n)]

                gpreab = sb0("gpreab", [128, D], F32); R_gab = Res()
                P.dma("sp", gpreab[:], gpre_ab, writes=[R_gab])
                xnT = sb0("xnT", [128, 8, T], BF16)
                R_xnT = [Res(f"xnT{g}") for g in range(ng)]
                wring = mk0("wr", [128, 8, 512], BF16, 2)
                R_wring = [Res(f"wr{i}") for i in range(2)]
                qTs = mk0("qT", [128, T], BF16, 2); kTs = mk0("kT", [128, T], BF16, 2); gTs = mk0("gT", [128, T], BF16, 2)
                merged = (kind == "p")
                if merged:
                    Vs = mk0("V", [128, nt, 2, 128], BF16, 2)
                else:
                    Vs = mk0("V", [128, nt, 128], BF16, 2)
                R_qs = [[Res() for _ in range(ng)] for _ in range(2)]
                R_ks = [[Res() for _ in range(ng)] for _ in range(2)]
                R_gs = [[Res() for _ in range(ng)] for _ in range(2)]
                R_Vs = [[Res() for _ in range(nt)] for _ in range(2)]
                stg = mk0("stg", [128, 256], F32, 2); R_stg = [Res() for _ in range(2)]
                biasT = sb0("biasT", [128, 8, 640], F32); R_bias = Res("bias")
                Ssb = mk0("Ssb", [128, 640], F32, 2); R_Ssb = [Res() for _ in range(2)]
                PT = mk0("PT", [128, 640], BF16, 2); R_PT = [Res() for _ in range(2)]
                rec2 = mk0("rec", [128, 128], F32, 2); R_rec2 = [Res() for _ in range(2)]
                tmpo2 = mk0("tmpo", [128, 128], F32, 2); R_tmpo2 = [Res() for _ in range(2)]
                EH = [mk0(f"E{h}_", [128, 512], F32, 2) for h in range(2)]; R_EH = [[Res() for _ in range(2)] for _ in range(2)]
                SPH = [mk0(f"SP{h}_", [128, 512], BF16, 2) for h in range(2)]; R_SPH = [[Res() for _ in range(2)] for _ in range(2)]
                ATH = [mk0(f"AT{h}_", [128, 512], BF16, 2) for h in range(2)]; R_ATH = [[Res() for _ in range(2)] for _ in range(2)]
                sgt = sb0("sgt", [128, 512], F32); R_sgt = Res()
                SaccH = mk0("Sacc", [128, 512], F32, 2); R_SaccH = [Res() for _ in range(2)]
                SaccBH = [mk0(f"SaccB{h}_", [128, 512], BF16, 2) for h in range(2)]; R_SaccBH = [[Res() for _ in range(2)] for _ in range(2)]
                nqT = mk0("nq", [128, 512], BF16, 2); R_nq = [Res() for _ in range(2)]
                if kind == "s":
                    kcache = sb0("kcache", [128, 16, 128], BF16); R_kc = Res()
                    kTcs = mk0("kTc", [128, PAST], BF16, 2); R_kTcs = [Res() for _ in range(2)]
                    Vcs = mk0("Vc", [128, 16, 128], BF16, 2); R_Vcs = [Res() for _ in range(2)]

                def load_w(pi):
                    P.dma("pool", wring[pi % 2][:], w_ab[pi].rearrange("(c p) n -> p c n", p=128),
                          writes=[R_wring[pi % 2]])

                load_w(0)
                load_w(1)
                if merged:
                    for s_ in range(2):
                        for ti_ in range(nt):
                            P.op("pool", lambda e, s_=s_, ti_=ti_: e.memset(Vs[s_][:, ti_, :, :], 1.0), writes=[R_Vs[s_][ti_]])
                if kind == "p":
                    for h in range(8):
                        P.dma("sp", biasT[:, h, :], biasP[h], writes=[R_bias])
                    for h in range(8):
                        P.op("pool", lambda e, h=h: e.memset(biasT[0:64, h, 64:128], NEG), writes=[R_bias])
                        P.op("pool", lambda e, h=h: e.memset(biasT[64:128, h, 512:576], NEG), writes=[R_bias])
                else:
                    for h in range(8):
                        P.dma("sp", biasT[:, h, 0:160], biasS[h], writes=[R_bias])

                p0flags = {}

                def gen_phase0():
                    for ti in range(nt):
                        k = cnt["x"]
                        xt = xst[k % 2]; Rxt = R_xst[k % 2]
                        P.dma("sp", xt[:ts, :], x_src(kind, si, ti * ts, ts), writes=[Rxt])
                        g = ti // tpg
                        norm_transpose(xt[:ts, :], Rxt, ts, gpreab, xnT[:, :, ti * ts:(ti + 1) * ts], R_xnT[g],
                                       gain_res=R_gab)
                        if (ti + 1) % tpg == 0:
                            p0flags[g] = True
                        yield


                PJB = 5

                def gen_proj(pi):
                    isA = pi < 4
                    hp = pi % 4
                    st = pi % 2
                    W = wring[st]; RW = R_wring[st]
                    qT, kT, gT, V = qTs[st], kTs[st], gTs[st], Vs[st]
                    R_q, R_k, R_g, R_V = R_qs[st], R_ks[st], R_gs[st], R_Vs[st]
                    if kind == "s":
                        kTc, R_kTc, Vc, R_Vc = kTcs[st], R_kTcs[st], Vcs[st], R_Vcs[st]
                        csrc_k, csrc_v, nck = (ca_k, ca_v, 4) if isA else (cb_k, cb_v, 16)
                        if pi == 0:
                            P.dma("pool", kcache[:, 0:nck, :],
                                  csrc_k[:, hp * 128:(hp + 1) * 128].rearrange("(t p) f -> p t f", p=128), writes=[R_kc])
                        P.dma("pool", Vc[:, 0:nck, :],
                              csrc_v[:, hp * 128:(hp + 1) * 128].rearrange("(t p) f -> p t f", p=128), writes=[R_Vc])
                        for t0 in range(0, nck, 8):
                            nb_ = min(8, nck - t0)
                            for t_ in range(nb_):
                                P.op("pe", lambda e, t_=t_: e.transpose(
                                    out=ptp[:, t_ * 128:(t_ + 1) * 128], in_=kcache[:, t0 + t_, :], identity=ident[:, :]),
                                    reads=[R_kc, R_const], writes=[R_tp])
                            P.op("dve", lambda e: e.tensor_copy(
                                out=kTc[:, t0 * 128:(t0 + nb_) * 128], in_=ptp[:, 0:nb_ * 128]),
                                reads=[R_tp], writes=[R_kTc])
                            yield
                        if pi + 1 < 8:
                            pn = pi + 1
                            nsrc, nn = (ca_k, 4) if pn < 4 else (cb_k, 16)
                            P.dma("pool", kcache[:, 0:nn, :],
                                  nsrc[:, (pn % 4) * 128:(pn % 4 + 1) * 128].rearrange("(t p) f -> p t f", p=128),
                                  writes=[R_kc])
                    pjbanks = [5, 6]
                    pjc = [0]

                    def nextpj():
                        b_ = pjbanks[pjc[0] % len(pjbanks)]; pjc[0] += 1
                        return b_
                    for g in range(ng):
                        t0 = g * gs
                        while pi == 0 and not p0flags.get(g):
                            yield
                        for (fc, kindf) in ((0, "q"), (2, "k"), (1, "g")):
                            bi = nextpj()
                            for kc in range(8):
                                P.op("pe", lambda e, kc=kc: e.matmul(
                                    bank(bi)[:, 0:gs], lhsT=W[:, kc, fc * 128:(fc + 1) * 128],
                                    rhs=xnT[:, kc, t0:t0 + gs], start=(kc == 0), stop=(kc == 7)),
                                    reads=[RW, R_xnT[g]], writes=[R_bank[bi]])
                            if kindf == "q":
                                P.op("dve", lambda e: e.tensor_scalar(
                                    out=qT[:, t0:t0 + gs], in0=bank(bi)[:, 0:gs], scalar1=0.125, scalar2=None,
                                    op0=ALU.mult), reads=[R_bank[bi]], writes=[R_q[g]])
                            elif kindf == "k":
                                P.op("dve", lambda e: e.tensor_copy(
                                    out=kT[:, t0:t0 + gs], in_=bank(bi)[:, 0:gs]), reads=[R_bank[bi]], writes=[R_k[g]])
                            else:
                                P.op("act", lambda e: e.activation(out=sgt[:, 0:gs], in_=bank(bi)[:, 0:gs], func=AF.Exp,
                                                                   scale=-1.0), reads=[R_bank[bi]], writes=[R_sgt])
                                P.op("act", lambda e: e.activation(out=sgt[:, 0:gs], in_=sgt[:, 0:gs], func=AF.Ln, bias=1.0),
                                     reads=[R_sgt], writes=[R_sgt])
                                P.op("act", lambda e: e.activation(out=sgt[:, 0:gs], in_=sgt[:, 0:gs], func=AF.Exp,
                                                                   scale=-1.0), reads=[R_sgt], writes=[R_sgt])
                                P.op("dve", lambda e: e.tensor_tensor(out=gT[:, t0:t0 + gs], in0=bank(bi)[:, 0:gs],
                                                                      in1=sgt[:, 0:gs], op=ALU.mult),
                                     reads=[R_bank[bi], R_sgt], writes=[R_g[g]])
                            yield
                        for tt in range(tpg):
                            ti = g * tpg + tt
                            bi = nextpj()
                            for kc in range(8):
                                P.op("pe", lambda e, kc=kc: e.matmul(
                                    bank(bi)[:ts, 0:256], lhsT=xnT[:, kc, ti * ts:(ti + 1) * ts],
                                    rhs=W[:, kc, 256:512], start=(kc == 0), stop=(kc == 7)),
                                    reads=[RW, R_xnT[g]], writes=[R_bank[bi]])
                            if merged:
                                P.op("dve", lambda e: e.tensor_copy(
                                    out=V[:ts, ti, 0, 0:64], in_=bank(bi)[:ts, 128:192]), reads=[R_bank[bi]], writes=[R_V[ti]])
                                P.op("dve", lambda e: e.tensor_copy(
                                    out=V[:ts, ti, 1, 64:128], in_=bank(bi)[:ts, 192:256]), reads=[R_bank[bi]], writes=[R_V[ti]])
                            else:
                                P.op("dve", lambda e: e.tensor_copy(
                                    out=V[:ts, ti, :], in_=bank(bi)[:ts, 128:256]), reads=[R_bank[bi]], writes=[R_V[ti]])
                            if kind == "p":
                                if isA:
                                    need = ti * ts >= S - 512
                                    dk, dv, r0 = o_akp, o_avp, ti * ts - (S - 512)
                                else:
                                    need = True
                                    dk, dv, r0 = o_bkp, o_bvp, ti * ts
                                dk_ap = dk[si, r0:r0 + ts, hp * 128:(hp + 1) * 128] if need else None
                                dv_ap = dv[si, r0:r0 + ts, hp * 128:(hp + 1) * 128] if need else None
                            else:
                                need = True
                                dk, dv = (o_aks, o_avs) if isA else (o_bks, o_bvs)
                                dk_ap = dk[0:ts, hp * 128:(hp + 1) * 128]
                                dv_ap = dv[0:ts, hp * 128:(hp + 1) * 128]
                            if need:
                                sk = cnt["stg"] % 2; cnt["stg"] += 1
                                P.op("dve", lambda e: e.tensor_copy(
                                    out=stg[sk][:ts, :], in_=bank(bi)[:ts, 0:256]),
                                    reads=[R_bank[bi]], writes=[R_stg[sk]])
                                P.dma("pool", dk_ap, stg[sk][:ts, 0:128], reads=[R_stg[sk]], is_output=True)
                                P.dma("pool", dv_ap, stg[sk][:ts, 128:256], reads=[R_stg[sk]], is_output=True)
                            yield

                def gen_attnA(pi):
                    hp = pi % 4
                    st = pi % 2
                    qT, kT, gT, V = qTs[st], kTs[st], gTs[st], Vs[st]
                    R_q, R_k, R_g, R_V = R_qs[st], R_ks[st], R_gs[st], R_Vs[st]
                    if kind == "s":
                        kTc, R_kTc, Vc, R_Vc = kTcs[st], R_kTcs[st], Vcs[st], R_Vcs[st]
                    nqb = T // ts
                    qw = ts
                    its = [(j, hh) for j in range(nqb) for hh in range(2)]

                    def a_blocks(j, hh):
                        po = hh * 64
                        blocks = []
                        if kind == "p":
                            for slot in range(5):
                                kb = j - 4 + slot
                                if kb < 0:
                                    continue
                                blocks.append((kT[po:po + 64, kb * 128:(kb + 1) * 128], 128,
                                               V[:, kb, hh, :], slot,
                                               [R_k[kb // 4]], [R_V[kb]]))
                        else:
                            for slot in range(4):
                                blocks.append((kTc[po:po + 64, slot * 128:(slot + 1) * 128], 128,
                                               Vc[:, slot, hh * 64:(hh + 1) * 64], slot, [R_kTc], [R_Vc]))
                            blocks.append((kT[po:po + 64, 0:TS], TS, V[0:TS, 0, hh * 64:(hh + 1) * 64], 4,
                                           [R_k[0]], [R_V[0]]))
                        return blocks

                    def a_stage1(n):
                        j, hh = its[n]
                        h = hp * 2 + hh
                        po = hh * 64
                        sbk = n % 2
                        RS = [R_bank[2 * sbk], R_bank[2 * sbk + 1]]
                        Sps = pbank[sbk]
                        blocks = a_blocks(j, hh)
                        q_ap = qT[po:po + 64, j * qw:(j + 1) * qw]
                        gq = (j * qw) // gs
                        for (k_ap, nk, v_ap, slot, rk, rv) in blocks:
                            P.op("pe", lambda e, k_ap=k_ap, nk=nk, slot=slot: e.matmul(
                                Sps[:nk, slot * qw:(slot + 1) * qw], lhsT=k_ap, rhs=q_ap, start=True, stop=True),
                                reads=rk + [R_q[gq]], writes=RS)
                        s0 = blocks[0][3]
                        full = [b for b in blocks if b[1] == 128]
                        part = [b for b in blocks if b[1] != 128]
                        sk = n % 2
                        lo, hi = s0 * qw, (full[-1][3] + 1) * qw
                        P.op("dve", lambda e: e.tensor_tensor(
                            out=Ssb[sk][:, lo:hi], in0=Sps[:, lo:hi], in1=biasT[:, h, lo:hi], op=ALU.add),
                            reads=RS + [R_bias], writes=[R_Ssb[sk]])
                        P.op("act", lambda e: e.activation(
                            out=PT[sk][:, lo:hi], in_=Ssb[sk][:, lo:hi], func=AF.Exp),
                            reads=[R_Ssb[sk]], writes=[R_PT[sk]])
                        for (k_ap, nk, v_ap, slot, rk, rv) in part:
                            lo2, hi2 = slot * qw, (slot + 1) * qw
                            P.op("dve", lambda e, lo2=lo2, hi2=hi2, nk=nk: e.tensor_tensor(
                                out=Ssb[sk][:nk, lo2:hi2], in0=Sps[:nk, lo2:hi2], in1=biasT[:nk, h, lo2:hi2],
                                op=ALU.add), reads=RS + [R_bias], writes=[R_Ssb[sk]])
                            P.op("act", lambda e, lo2=lo2, hi2=hi2, nk=nk: e.activation(
                                out=PT[sk][:nk, lo2:hi2], in_=Ssb[sk][:nk, lo2:hi2], func=AF.Exp),
                                reads=[R_Ssb[sk]], writes=[R_PT[sk]])

                    def a_stage2m(n):
                        j, hh = its[n]
                        po = hh * 64
                        pd = 64 - po
                        sk = n % 2
                        odb = 4
                        OD = bank(4)[:, (n % 2) * 128:(n % 2) * 128 + 128]
                        blocks = a_blocks(j, hh)
                        nb = len(blocks)
                        gq = (j * qw) // gs
                        for bi_, (k_ap, nk, v_ap, slot, rk, rv) in enumerate(blocks):
                            P.op("pe", lambda e, v_ap=v_ap, nk=nk, slot=slot, bi_=bi_: e.matmul(
                                OD[:, 0:qw], lhsT=v_ap, rhs=PT[sk][:nk, slot * qw:(slot + 1) * qw],
                                start=(bi_ == 0), stop=(bi_ == nb - 1)),
                                reads=rv + [R_PT[sk]], writes=[R_bank[odb]])
                        rc = rec2[hh]; Rrc = R_rec2[hh]
                        tm = tmpo2[hh]; Rtm = R_tmpo2[hh]
                        P.op("act", lambda e: e.activation(
                            out=rc[po:po + 64, 0:qw], in_=OD[pd:pd + 64, 0:qw], func=AF.Ln),
                            reads=[R_bank[odb]], writes=[Rrc])
                        P.op("act", lambda e: e.activation(
                            out=rc[po:po + 64, 0:qw], in_=rc[po:po + 64, 0:qw], func=AF.Exp, scale=-1.0),
                            reads=[Rrc], writes=[Rrc])
                        P.op("dve", lambda e: e.tensor_tensor(
                            out=tm[po:po + 64, 0:qw], in0=OD[po:po + 64, 0:qw], in1=rc[po:po + 64, 0:qw],
                            op=ALU.mult), reads=[R_bank[odb], Rrc], writes=[Rtm])
                        P.op("pool", lambda e: e.tensor_tensor(
                            out=oT[po:po + 64, pi, j * qw:(j + 1) * qw], in0=tm[po:po + 64, 0:qw],
                            in1=gT[po:po + 64, j * qw:(j + 1) * qw], op=ALU.mult),
                            reads=[Rtm, R_g[gq]], writes=[R_oT[pi][gq]])

                    def a_stage2(n):
                        if merged:
                            return a_stage2m(n)
                        j, hh = its[n]
                        po = hh * 64
                        sk = n % 2
                        odb = 4
                        OD = bank(odb)
                        blocks = a_blocks(j, hh)
                        nb = len(blocks)
                        gq = (j * qw) // gs
                        for bi_, (k_ap, nk, v_ap, slot, rk, rv) in enumerate(blocks):
                            P.op("pe", lambda e, v_ap=v_ap, nk=nk, slot=slot, bi_=bi_: e.matmul(
                                OD[po:po + 64, 0:qw], lhsT=v_ap, rhs=PT[sk][:nk, slot * qw:(slot + 1) * qw],
                                start=(bi_ == 0), stop=(bi_ == nb - 1)),
                                reads=rv + [R_PT[sk]], writes=[R_bank[odb]])
                        for bi_, (k_ap, nk, v_ap, slot, rk, rv) in enumerate(blocks):
                            P.op("pe", lambda e, nk=nk, slot=slot, bi_=bi_: e.matmul(
                                OD[po:po + 64, 128:128 + qw], lhsT=ones[:nk, 0:64],
                                rhs=PT[sk][:nk, slot * qw:(slot + 1) * qw],
                                start=(bi_ == 0), stop=(bi_ == nb - 1)),
                                reads=[R_PT[sk], R_const], writes=[R_bank[odb]])
                        rc = rec2[hh]; Rrc = R_rec2[hh]
                        tm = tmpo2[hh]; Rtm = R_tmpo2[hh]
                        P.op("act", lambda e: e.activation(
                            out=rc[po:po + 64, 0:qw], in_=OD[po:po + 64, 128:128 + qw], func=AF.Ln),
                            reads=[R_bank[odb]], writes=[Rrc])
                        P.op("act", lambda e: e.activation(
                            out=rc[po:po + 64, 0:qw], in_=rc[po:po + 64, 0:qw], func=AF.Exp, scale=-1.0),
                            reads=[Rrc], writes=[Rrc])
                        P.op("dve", lambda e: e.tensor_tensor(
                            out=tm[po:po + 64, 0:qw], in0=OD[po:po + 64, 0:qw], in1=rc[po:po + 64, 0:qw],
                            op=ALU.mult), reads=[R_bank[odb], Rrc], writes=[Rtm])
                        P.op("pool", lambda e: e.tensor_tensor(
                            out=oT[po:po + 64, pi, j * qw:(j + 1) * qw], in0=tm[po:po + 64, 0:qw],
                            in1=gT[po:po + 64, j * qw:(j + 1) * qw], op=ALU.mult),
                            reads=[Rtm, R_g[gq]], writes=[R_oT[pi][gq]])

                    a_stage1(0)
                    yield
                    for n in range(len(its)):
                        if n + 1 < len(its):
                            a_stage1(n + 1)
                            yield
                        a_stage2(n)
                        yield

                def gen_attnB(pi):
                    st = pi % 2
                    qT, kT, gT, V = qTs[st], kTs[st], gTs[st], Vs[st]
                    R_q, R_k, R_g, R_V = R_qs[st], R_ks[st], R_gs[st], R_Vs[st]
                    if kind == "s":
                        kTc, R_kTc, Vc, R_Vc = kTcs[st], R_kTcs[st], Vcs[st], R_Vcs[st]
                    cw = gs
                    ob = 4
                    OB = bank(ob)
                    dq = min(128, cw)
                    for c in range(ng):
                        steps = [[], []]
                        for hh in range(2):
                            po = hh * 64
                            P.op("dve", lambda e: e.tensor_scalar(
                                out=nqT[hh][po:po + 64, 0:cw], in0=qT[po:po + 64, c * cw:(c + 1) * cw],
                                scalar1=-1.0, scalar2=None, op0=ALU.mult), reads=[R_q[c]], writes=[R_nq[hh]])
                            P.op("pool", lambda e: e.memset(SaccH[hh][:, 0:cw], 0.0), writes=[R_SaccH[hh]])
                            P.op("pool", lambda e: e.memset(SaccBH[hh][0][:, 0:cw], 0.0), writes=[R_SaccBH[hh][0]])
                            if kind == "p":
                                for kb in range(4 * c + 3, -1, -1):
                                    q0 = max(0, kb * 128 - c * 512)
                                    steps[hh].append((kT[po:po + 64, kb * 128:(kb + 1) * 128], 128,
                                                      V[:, kb, hh, hh * 64:(hh + 1) * 64], q0, kb >= 4 * c,
                                                      [R_k[kb // 4]], [R_V[kb]]))
                            else:
                                steps[hh].append((kT[po:po + 64, 0:TS], TS, V[0:TS, 0, hh * 64:(hh + 1) * 64], 0, True,
                                                  [R_k[0]], [R_V[0]]))
                                for kb in range(15, -1, -1):
                                    steps[hh].append((kTc[po:po + 64, kb * 128:(kb + 1) * 128], 128,
                                                      Vc[:, kb, hh * 64:(hh + 1) * 64], 0, False, [R_kTc], [R_Vc]))
                        ns = len(steps[0])

                        def stage1(i):
                            for hh in range(2):
                                po = hh * 64
                                k_ap, nk, v_ap, q0, diag, rk, rv = steps[hh][i]
                                z = bank(hh); Rz = R_bank[hh]
                                P.op("pe", lambda e: e.matmul(z[:nk, q0:cw], lhsT=k_ap,
                                                              rhs=qT[po:po + 64, c * cw + q0:(c + 1) * cw],
                                                              start=True, stop=True),
                                     reads=rk + [R_q[c]], writes=[Rz])
                            for hh in range(2):
                                k_ap, nk, v_ap, q0, diag, rk, rv = steps[hh][i]
                                z = bank(hh); Rz = R_bank[hh]
                                E = EH[hh][i % 2]; RE = R_EH[hh][i % 2]
                                P.op("act", lambda e: e.activation(out=E[:nk, q0:cw], in_=z[:nk, q0:cw], func=AF.Exp),
                                     reads=[Rz], writes=[RE])
                                if diag:
                                    P.op("dve", lambda e: e.tensor_tensor(
                                        out=E[:nk, q0:q0 + dq], in0=E[:nk, q0:q0 + dq], in1=lmask[:nk, 0:dq],
                                        op=ALU.mult), reads=[RE, R_const], writes=[RE])
                            for hh in range(2):
                                k_ap, nk, v_ap, q0, diag, rk, rv = steps[hh][i]
                                E = EH[hh][i % 2]; RE = R_EH[hh][i % 2]
                                SP = SPH[hh][i % 2]; RSP = R_SPH[hh][i % 2]
                                P.op("act", lambda e: e.activation(out=SP[:nk, q0:cw], in_=E[:nk, q0:cw], func=AF.Ln,
                                                                   bias=1.0), reads=[RE], writes=[RSP])

                        def stage2(i):
                            for hh in range(2):
                                k_ap, nk, v_ap, q0, diag, rk, rv = steps[hh][i]
                                cps = bank(2 + hh); Rc = R_bank[2 + hh]
                                SP = SPH[hh][i % 2]; RSP = R_SPH[hh][i % 2]
                                P.op("pe", lambda e: e.matmul(cps[:nk, q0:cw], lhsT=tri[:nk, :nk], rhs=SP[:nk, q0:cw],
                                                              start=True, stop=False),
                                     reads=[RSP, R_const], writes=[Rc])
                                P.op("pe", lambda e: e.matmul(cps[:nk, q0:cw], lhsT=ones[:, :nk],
                                                              rhs=SaccBH[hh][i % 2][:, q0:cw], start=False, stop=False),
                                     reads=[R_SaccBH[hh][i % 2], R_const], writes=[Rc])
                            for hh in range(2):
                                po = hh * 64
                                k_ap, nk, v_ap, q0, diag, rk, rv = steps[hh][i]
                                cps = bank(2 + hh); Rc = R_bank[2 + hh]
                                P.op("pe", lambda e: e.matmul(cps[:nk, q0:cw], lhsT=k_ap,
                                                              rhs=nqT[hh][po:po + 64, q0:cw], start=False, stop=True),
                                     reads=rk + [R_nq[hh]], writes=[Rc])
                            if i + 1 < ns:
                                for hh in range(2):
                                    k_ap, nk, v_ap, q0, diag, rk, rv = steps[hh][i]
                                    SP = SPH[hh][i % 2]; RSP = R_SPH[hh][i % 2]
                                    P.op("dve", lambda e: e.tensor_tensor(
                                        out=SaccH[hh][:nk, q0:cw], in0=SaccH[hh][:nk, q0:cw], in1=SP[:nk, q0:cw], op=ALU.add),
                                        reads=[RSP, R_SaccH[hh]], writes=[R_SaccH[hh]])
                                    P.op("dve", lambda e: e.tensor_copy(out=SaccBH[hh][(i + 1) % 2][:, 0:cw],
                                                                        in_=SaccH[hh][:, 0:cw]),
                                         reads=[R_SaccH[hh]], writes=[R_SaccBH[hh][(i + 1) % 2]])
                            for hh in range(2):
                                k_ap, nk, v_ap, q0, diag, rk, rv = steps[hh][i]
                                cps = bank(2 + hh); Rc = R_bank[2 + hh]
                                AT = ATH[hh][i % 2]; RAT = R_ATH[hh][i % 2]
                                P.op("act", lambda e: e.activation(out=AT[:nk, q0:cw], in_=cps[:nk, q0:cw], func=AF.Exp,
                                                                   scale=-1.0), reads=[Rc], writes=[RAT])
                                if q0 > 0:
                                    P.op("pool", lambda e: e.memset(AT[:nk, 0:q0], 0.0), writes=[RAT])
                                if diag:
                                    P.op("dve", lambda e: e.tensor_tensor(
                                        out=AT[:nk, q0:q0 + dq], in0=AT[:nk, q0:q0 + dq], in1=lmask[:nk, 0:dq],
                                        op=ALU.mult), reads=[RAT, R_const], writes=[RAT])

                        def stage3(i):
                            for hh in range(2):
                                po = hh * 64
                                k_ap, nk, v_ap, q0, diag, rk, rv = steps[hh][i]
                                AT = ATH[hh][i % 2]; RAT = R_ATH[hh][i % 2]
                                P.op("pe", lambda e: e.matmul(OB[po:po + 64, 0:cw], lhsT=v_ap, rhs=AT[:nk, 0:cw],
                                                              start=(i == 0), stop=(i == ns - 1)),
                                     reads=rv + [RAT], writes=[R_bank[ob]])

                        stage1(0)
                        yield
                        for i in range(ns):
                            if i + 1 < ns:
                                stage1(i + 1)
                            stage2(i)
                            if i > 0:
                                stage3(i - 1)
                            yield
                        stage3(ns - 1)
                        P.op("dve", lambda e: e.tensor_tensor(
                            out=oT[:, pi, c * cw:(c + 1) * cw], in0=OB[:, 0:cw],
                            in1=gT[:, c * cw:(c + 1) * cw], op=ALU.mult),
                            reads=[R_bank[ob], R_g[c]], writes=[R_oT[pi][c]])
                        yield

                def run_weighted(ga, na, gb, nb_):
                    da = db = 0
                    a_alive, b_alive = True, gb is not None
                    while a_alive or b_alive:
                        pick_b = b_alive and (not a_alive or (db + 1) * na <= (da + 1) * nb_)
                        if pick_b:
                            try:
                                next(gb); db += 1
                            except StopIteration:
                                b_alive = False
                        else:
                            try:
                                next(ga); da += 1
                            except StopIteration:
                                a_alive = False

                n_proj = ng * (3 + tpg) + (3 if kind == "s" else 0)
                gp0, gj0 = gen_phase0(), gen_proj(0)
                alive = [gp0, gj0]
                while alive:
                    for s_ in list(alive):
                        try:
                            next(s_)
                        except StopIteration:
                            alive.remove(s_)
                for pi in range(8):
                    if pi + 2 < 8:
                        load_w(pi + 2)
                    isA = pi < 4
                    if isA:
                        ga = gen_attnA(pi); na = 2 * (T // ts) * 2
                    else:
                        ga = gen_attnB(pi)
                        na = sum((4 * c + 4 + 2) for c in range(ng)) if kind == "p" else 19
                    gb = gen_proj(pi + 1) if pi + 1 < 8 else None
                    run_weighted(ga, na, gb, n_proj)
            P.barrier()

            with ExitStack() as l1:
                def sb1(name, shape, dt):
                    return l1.enter_context(nc.sbuf_tensor(f"{name}_{kind}{si}", list(shape), dt))

                gs1 = min(T, 256)
                ng1 = T // gs1
                tpg1 = gs1 // ts
                qw = ts
                woab = sb1("woab", [128, 8, D], BF16); R_woab = Res()
                wc = sb1("wc", [128, 8, 2560], BF16); R_wcq = [Res() for _ in range(4)]
                woc = sb1("woc", [128, 8, D], BF16); R_woc = Res()
                gpostab = sb1("gpostab", [128, D], F32); gprec = sb1("gprec", [128, D], F32)
                gpostc = sb1("gpostc", [128, D], F32); R_gn = Res()
                for t_, src in ((gpostab, gpost_ab), (gprec, gpre_c), (gpostc, gpost_c)):
                    P.dma("sp", t_[:], src, writes=[R_gn])
                P.dma("pool", woab[:], w_oab.rearrange("(c p) n -> p c n", p=128), writes=[R_woab])
                for q4 in range(4):
                    P.dma("pool", wc[:, :, q4 * 640:(q4 + 1) * 640],
                          w_c[:, q4 * 640:(q4 + 1) * 640].rearrange("(c p) n -> p c n", p=128), writes=[R_wcq[q4]])
                P.dma("pool", woc[:], w_oc.rearrange("(c p) n -> p c n", p=128), writes=[R_woc])

                def mk(name, shape, dt, n):
                    return [sb1(f"{name}{i}", shape, dt) for i in range(n)], [Res() for _ in range(n)]

                Y0, R_Y0 = mk("Y0", [128, tpg1, D], F32, 2)
                t1b, R_t1 = mk("t1b", [128, D], F32, 2)
                statY, R_statY = mk("statY", [128, 4], F32, 2)
                xn1T, R_xn1 = mk("xn1T", [128, 8, gs1], BF16, 1)
                xn1T, R_xn1 = xn1T * 2, R_xn1 * 2
                qbf, R_qbf = mk("qbf", [128, gs1], BF16, 2)
                qr, R_qr = mk("qr", [128, 8, gs1], BF16, 2)
                kbf, R_kbf = mk("kbf", [128, 2, gs1], BF16, 1)
                kr, R_kr = mk("kr", [128, 4, 128 + gs1], BF16, 2)
                g1, R_g1 = mk("g1", [128, 8, gs1], BF16, 2)
                V1, R_V1 = mk("V1", [128, 1 + tpg1, 256], BF16, 2)
                cosg, R_cos = mk("cosg", [128, gs1], F32, 1)
                sing, R_sin = mk("sing", [128, gs1], F32, 1)
                cosg, R_cos, sing, R_sin = cosg * 2, R_cos * 2, sing * 2, R_sin * 2
                ta, R_ta = mk("ta", [128, 256], F32, 2)
                tb, R_tb = mk("tb", [128, 256], F32, 2)
                tcb, R_tcb = mk("tcb", [128, 256], F32, 2)
                PTc, R_PTc = mk("PTc", [128, 512], BF16, 4)
                recc, R_recc = mk("recc", [128, 256], F32, 1)
                tmpc, R_tmpc = mk("tmpc", [128, 256], F32, 1)
                recc, R_recc, tmpc, R_tmpc = recc * 2, R_recc * 2, tmpc * 2, R_tmpc * 2
                kst = sb1("kst", [128, 256], F32); R_kst = Res()
                ksw = sb1("ksw", [128, 256], F32); R_ksw = Res()
                ctm = sb1("ctm", [128, 256], F32); stm = sb1("stm", [128, 256], F32); R_ctm = Res()
                kvst = sb1("kvst", [128, 512], F32); R_kvst = Res()
                if kind == "s":
                    kcc = sb1("kcc", [128, 256], BF16); R_kcc = Res()
                    kcd = sb1("kcd", [128, 4, 128], BF16); R_kcd = Res()
                    krc = sb1("krc", [128, 4, 128], BF16); R_krc = Res()
                    Vcc = sb1("Vcc", [128, 256], BF16); R_Vcc = Res()
                    P.dma("pool", kcc[:], cc_k, writes=[R_kcc])
                    P.dma("pool", Vcc[:], cc_v, writes=[R_Vcc])
                    for a in range(4):
                        for d2 in range(2):
                            P.op("pool", lambda e, a=a, d2=d2: e.tensor_copy(
                                out=kcd[:, a, d2 * 64:(d2 + 1) * 64], in_=kcc[:, a * 64:(a + 1) * 64]),
                                reads=[R_kcc], writes=[R_kcd])
                    for a in range(4):
                        P.op("pe", lambda e, a=a: e.transpose(out=ptp[:, a * 128:(a + 1) * 128], in_=kcd[:, a, :],
                                                              identity=ident[:, :]),
                             reads=[R_kcd, R_const], writes=[R_tp])
                    for a in range(4):
                        P.op("dve", lambda e, a=a: e.tensor_copy(out=krc[:, a, :], in_=ptp[:, a * 128:(a + 1) * 128]),
                             reads=[R_tp], writes=[R_krc])
                lt0 = pos0 + T - ts
                P.dma("sp", ctm[:ts, :], c_cosTM[lt0:lt0 + ts, :], writes=[R_ctm])
                P.dma("sp", stm[:ts, :], c_sinTM[lt0:lt0 + ts, :], writes=[R_ctm])

                yc = {"n": 0, "bx": 0, "t": 0}
                LAG_A = 1
                XB = [0, 1, 2, 6]
                YB = [3, 4, 5]

                def nbx():
                    b_ = XB[yc["bx"] % 4]; yc["bx"] += 1
                    return b_

                def post_norm_residual(bk0, bk1, gain, res_ap, res_r, out_ap, out_r):
                    k = yc["t"]; yc["t"] += 1
                    st = statY[k % 2]; Rst = R_statY[k % 2]
                    jk = junk2[k % 2]; Rjk = R_junk2[k % 2]
                    for half, bk in enumerate((bk0, bk1)):
                        P.op("act", lambda e, half=half, bk=bk: e.activation(
                            out=jk[:ts, half * 512:(half + 1) * 512], in_=bank(bk)[:ts, :], func=AF.Square,
                            accum_out=st[:ts, half:half + 1]), reads=[R_bank[bk]], writes=[Rjk, Rst])
                    P.op("dve", lambda e: e.tensor_tensor(out=st[:ts, 2:3], in0=st[:ts, 0:1], in1=st[:ts, 1:2],
                                                          op=ALU.add), reads=[Rst], writes=[Rst])
                    rstd_from(st[:ts, 2:3], st[:ts, 3:4], ts, [Rst], [Rst])
                    for half, bk in enumerate((bk0, bk1)):
                        P.op("dve", lambda e, half=half, bk=bk: e.scalar_tensor_tensor(
                            out=out_ap[:, half * 512:(half + 1) * 512], in0=bank(bk)[:ts, :], scalar=st[:ts, 3:4],
                            in1=gain[:ts, half * 512:(half + 1) * 512], op0=ALU.mult, op1=ALU.mult),
                            reads=[R_bank[bk], Rst, R_gn], writes=[out_r])
                    P.op("pool", lambda e: e.tensor_tensor(out=out_ap, in0=out_ap, in1=res_ap, op=ALU.add),
                         reads=[res_r, out_r], writes=[out_r])

                def gen_a(g):
                    gb = g % 2
                    t0 = g * gs1
                    g0 = t0 // gs
                    P.dma("sp", cosg[gb][:, :], c_cosT[:, pos0 + t0:pos0 + t0 + gs1], writes=[R_cos[gb]])
                    P.dma("sp", sing[gb][:, :], c_sinT[:, pos0 + t0:pos0 + t0 + gs1], writes=[R_sin[gb]])
                    pend = []
                    for tt in range(tpg1):
                        ti = g * tpg1 + tt
                        k = cnt["x"]
                        xt = xst[k % 2]; Rxt = R_xst[k % 2]
                        P.dma("sp", xt[:ts, :], x_src(kind, si, ti * ts, ts), writes=[Rxt])
                        bks = (nbx(), nbx())
                        for half in range(2):
                            for c in range(8):
                                P.op("pe", lambda e, c=c, half=half: e.matmul(
                                    bank(bks[half])[:ts, :], lhsT=oT[:, c, ti * ts:(ti + 1) * ts],
                                    rhs=woab[:, c, half * 512:(half + 1) * 512], start=(c == 0), stop=(c == 7)),
                                    reads=[R_oT[c][g0], R_woab], writes=[R_bank[bks[half]]])
                        yield
                        post_norm_residual(bks[0], bks[1], gpostab, xt[:ts, :], Rxt, Y0[gb][:ts, tt, :], R_Y0[gb])
                        kk = norm_part1(Y0[gb][:ts, tt, :], R_Y0[gb], ts, gprec, gain_res=R_gn)
                        pend.append((kk, tt))
                        yield
                    for (kk, tt) in pend:
                        norm_part2(kk, ts, xn1T[gb][:, :, tt * ts:(tt + 1) * ts], R_xn1[gb])
                        yield

                def gen_b(g):
                    gb = g % 2
                    xn = xn1T[gb]; Rxn = R_xn1[gb]
                    cs, sn = cosg[gb], sing[gb]
                    if g > 0:
                        P.op("pool", lambda e: e.tensor_copy(out=kr[gb][:, :, 0:128], in_=kr[1 - gb][:, :, gs1:gs1 + 128]),
                             reads=[R_kr[1 - gb]], writes=[R_kr[gb]])
                        P.op("pool", lambda e: e.tensor_copy(out=V1[gb][:, 0, :], in_=V1[1 - gb][:, tpg1, :]),
                             reads=[R_V1[1 - gb]], writes=[R_V1[gb]])
                    for fc in range(8):
                        b1 = nbx()
                        for kc in range(8):
                            P.op("pe", lambda e, kc=kc: e.matmul(
                                bank(b1)[:, 0:gs1], lhsT=wc[:, kc, fc * 128:(fc + 1) * 128], rhs=xn[:, kc, :],
                                start=(kc == 0), stop=(kc == 7)), reads=[R_wcq[(fc * 128) // 640], Rxn], writes=[R_bank[b1]])
                        s2 = fc % 2
                        P.op("act", lambda e: e.activation(out=qbf[s2][:, :], in_=bank(b1)[:, 0:gs1], func=AF.Copy, scale=0.125),
                             reads=[R_bank[b1]], writes=[R_qbf[s2]])
                        P.op("dve", lambda e: e.scalar_tensor_tensor(
                            out=ta[s2][:, 0:gs1], in0=bank(b1)[:, 0:gs1], scalar=0.125, in1=cs[:, :], op0=ALU.mult,
                            op1=ALU.mult), reads=[R_bank[b1], R_cos[gb]], writes=[R_ta[s2]])
                        b2 = nbx()
                        P.op("pe", lambda e: e.matmul(bank(b2)[:, 0:gs1], lhsT=rot[:, :], rhs=qbf[s2][:, :], start=True, stop=True),
                             reads=[R_qbf[s2], R_const], writes=[R_bank[b2]])
                        P.op("dve", lambda e: e.tensor_tensor(out=tb[s2][:, 0:gs1], in0=bank(b2)[:, 0:gs1], in1=sn[:, :],
                                                              op=ALU.mult), reads=[R_bank[b2], R_sin[gb]], writes=[R_tb[s2]])
                        P.op("pool", lambda e: e.tensor_tensor(out=qr[gb][:, fc, :], in0=ta[s2][:, 0:gs1], in1=tb[s2][:, 0:gs1],
                                                               op=ALU.add), reads=[R_ta[s2], R_tb[s2]], writes=[R_qr[gb]])
                        yield
                def gen_b2(g):
                    gb = g % 2
                    xn = xn1T[gb]; Rxn = R_xn1[gb]
                    cs, sn = cosg[gb], sing[gb]
                    for kc2 in range(2):
                        b1 = nbx()
                        for kc in range(8):
                            P.op("pe", lambda e, kc=kc: e.matmul(
                                bank(b1)[:, 0:gs1], lhsT=wc[:, kc, 1024 + kc2 * 128:1024 + (kc2 + 1) * 128],
                                rhs=xn[:, kc, :], start=(kc == 0), stop=(kc == 7)),
                                reads=[R_wcq[1], Rxn], writes=[R_bank[b1]])
                        P.op("act", lambda e: e.activation(out=kbf[0][:, kc2, :], in_=bank(b1)[:, 0:gs1], func=AF.Copy),
                             reads=[R_bank[b1]], writes=[R_kbf[0]])
                    yield
                    for a in range(4):
                        s2 = a % 2
                        b1 = nbx()
                        P.op("pe", lambda e: e.matmul(bank(b1)[:, 0:gs1], lhsT=dsel[:, a % 2, :], rhs=kbf[0][:, a // 2, :],
                                                      start=True, stop=True),
                             reads=[R_kbf[0], R_const], writes=[R_bank[b1]])
                        b2 = nbx()
                        P.op("pe", lambda e: e.matmul(bank(b2)[:, 0:gs1], lhsT=dselrot[:, a % 2, :], rhs=kbf[0][:, a // 2, :],
                                                      start=True, stop=True),
                             reads=[R_kbf[0], R_const], writes=[R_bank[b2]])
                        P.op("dve", lambda e: e.tensor_tensor(out=ta[s2][:, 0:gs1], in0=bank(b1)[:, 0:gs1], in1=cs[:, :],
                                                              op=ALU.mult), reads=[R_bank[b1], R_cos[gb]], writes=[R_ta[s2]])
                        P.op("dve", lambda e: e.tensor_tensor(out=tb[s2][:, 0:gs1], in0=bank(b2)[:, 0:gs1], in1=sn[:, :],
                                                              op=ALU.mult), reads=[R_bank[b2], R_sin[gb]], writes=[R_tb[s2]])
                        P.op("pool", lambda e: e.tensor_tensor(
                            out=kr[gb][:, a, 128:128 + gs1], in0=ta[s2][:, 0:gs1], in1=tb[s2][:, 0:gs1], op=ALU.add),
                            reads=[R_ta[s2], R_tb[s2]], writes=[R_kr[gb]])
                        yield
                    for fc in range(8):
                        b1 = nbx()
                        for kc in range(8):
                            P.op("pe", lambda e, kc=kc: e.matmul(
                                bank(b1)[:, 0:gs1], lhsT=wc[:, kc, 1536 + fc * 128:1536 + (fc + 1) * 128],
                                rhs=xn[:, kc, :], start=(kc == 0), stop=(kc == 7)),
                                reads=[R_wcq[(1536 + fc * 128) // 640], Rxn], writes=[R_bank[b1]])
                        s2 = fc % 2
                        P.op("act", lambda e: e.activation(out=tcb[s2][:, 0:gs1], in_=bank(b1)[:, 0:gs1], func=AF.Exp,
                                                           scale=-1.0), reads=[R_bank[b1]], writes=[R_tcb[s2]])
                        P.op("act", lambda e: e.activation(out=tcb[s2][:, 0:gs1], in_=tcb[s2][:, 0:gs1], func=AF.Ln, bias=1.0),
                             reads=[R_tcb[s2]], writes=[R_tcb[s2]])
                        P.op("act", lambda e: e.activation(out=tcb[s2][:, 0:gs1], in_=tcb[s2][:, 0:gs1], func=AF.Exp,
                                                           scale=-1.0), reads=[R_tcb[s2]], writes=[R_tcb[s2]])
                        P.op("dve", lambda e: e.tensor_tensor(out=g1[gb][:, fc, :], in0=bank(b1)[:, 0:gs1],
                                                              in1=tcb[s2][:, 0:gs1], op=ALU.mult),
                             reads=[R_bank[b1], R_tcb[s2]], writes=[R_g1[gb]])
                        yield
                    for tt in range(tpg1):
                        ti = g * tpg1 + tt
                        b1 = nbx()
                        for kc in range(8):
                            P.op("pe", lambda e, kc=kc: e.matmul(
                                bank(b1)[:ts, :], lhsT=xn[:, kc, tt * ts:(tt + 1) * ts], rhs=wc[:, kc, 1024:1536],
                                start=(kc == 0), stop=(kc == 7)), reads=[R_wcq[1], R_wcq[2], Rxn], writes=[R_bank[b1]])
                        P.op("dve", lambda e: e.tensor_copy(out=V1[gb][:ts, 1 + tt, :], in_=bank(b1)[:ts, 256:512]),
                             reads=[R_bank[b1]], writes=[R_V1[gb]])
                        if ti == nt - 1:
                            ysk = kvst; Rysk = R_kvst
                            P.op("dve", lambda e: e.tensor_copy(out=ysk[:ts, 0:256], in_=bank(b1)[:ts, 256:512]),
                                 reads=[R_bank[b1]], writes=[Rysk])
                            dv_ap = o_cvp[si, :, :] if kind == "p" else o_cvs[:, :]
                            dk_ap = o_ckp[si, :, :] if kind == "p" else o_cks[:, :]
                            P.dma("pool", dv_ap, ysk[:ts, 0:256], reads=[Rysk], is_output=True)
                            P.op("dve", lambda e: e.tensor_copy(out=kst[:ts, :], in_=bank(b1)[:ts, 0:256]),
                                 reads=[R_bank[b1]], writes=[R_kst])
                            for hk in range(4):
                                for b2_ in range(2):
                                    P.op("dve", lambda e, hk=hk, b2_=b2_: e.tensor_copy(
                                        out=ksw[:ts, hk * 64 + b2_ * 32:hk * 64 + b2_ * 32 + 32],
                                        in_=kst[:ts, hk * 64 + (1 - b2_) * 32:hk * 64 + (1 - b2_) * 32 + 32]),
                                        reads=[R_kst], writes=[R_ksw])
                            P.op("dve", lambda e: e.tensor_tensor(out=kst[:ts, :], in0=kst[:ts, :], in1=ctm[:ts, :],
                                                                  op=ALU.mult), reads=[R_kst, R_ctm], writes=[R_kst])
                            P.op("dve", lambda e: e.tensor_tensor(out=ksw[:ts, :], in0=ksw[:ts, :], in1=stm[:ts, :],
                                                                  op=ALU.mult), reads=[R_ksw, R_ctm], writes=[R_ksw])
                            P.op("dve", lambda e: e.tensor_tensor(out=ysk[:ts, 256:512], in0=kst[:ts, :],
                                                                  in1=ksw[:ts, :], op=ALU.add),
                                 reads=[R_kst, R_ksw], writes=[Rysk])
                            P.dma("pool", dk_ap, ysk[:ts, 256:512], reads=[Rysk], is_output=True)
                        yield

                def c_blocks(g, j, a):
                    gb = g % 2
                    J = g * tpg1 + j
                    blocks = []
                    if kind == "p":
                        if J > 0:
                            blocks.append((kr[gb][:, a, j * 128:(j + 1) * 128], 128,
                                           V1[gb][:, j, a * 64:(a + 1) * 64], "prev", [R_kr[gb]], [R_V1[gb]]))
                        blocks.append((kr[gb][:, a, (j + 1) * 128:(j + 2) * 128], 128,
                                       V1[gb][:, j + 1, a * 64:(a + 1) * 64], "diag", [R_kr[gb]], [R_V1[gb]]))
                    else:
                        blocks.append((krc[:, a, :], 128, Vcc[:, a * 64:(a + 1) * 64], "c", [R_krc], [R_Vcc]))
                        blocks.append((kr[gb][:, a, 128:128 + TS], TS, V1[gb][0:TS, 1, a * 64:(a + 1) * 64], "n",
                                       [R_kr[gb]], [R_V1[gb]]))
                    return blocks

                def c_stage1(g, n):
                    gb = g % 2
                    j, a = n // 4, n % 4
                    blocks = c_blocks(g, j, a)
                    for par in range(2):
                        sbk = YB[par]
                        Sps = bank(sbk)
                        po = par * 64
                        pt = PTc[(n % 2) * 2 + par]; Rpt = R_PTc[(n % 2) * 2 + par]
                        for bi_, (k_ap, nk, v_ap, tag, rk, rv) in enumerate(blocks):
                            if qw == 128:
                                col = bi_ * 2 * qw
                                P.op("pe", lambda e, k_ap=k_ap, nk=nk, col=col: e.matmul(
                                    Sps[:nk, col:col + 2 * qw].rearrange("p (h q) -> p h q", h=2),
                                    lhsT=k_ap[po:po + 64, :],
                                    rhs=qr[gb][po:po + 64, 2 * a:2 * a + 2, j * qw:(j + 1) * qw],
                                    start=True, stop=True), reads=rk + [R_qr[gb]], writes=[R_bank[sbk]])
                                continue
                            for hi in range(2):
                                fc = 2 * a + hi
                                col = (bi_ * 2 + hi) * qw
                                P.op("pe", lambda e, k_ap=k_ap, nk=nk, fc=fc, col=col: e.matmul(
                                    Sps[:nk, col:col + qw], lhsT=k_ap[po:po + 64, :],
                                    rhs=qr[gb][po:po + 64, fc, j * qw:(j + 1) * qw],
                                    start=True, stop=True), reads=rk + [R_qr[gb]], writes=[R_bank[sbk]])
                        for bi_, (k_ap, nk, v_ap, tag, rk, rv) in enumerate(blocks):
                            c0 = bi_ * 2 * qw
                            P.op("act", lambda e, nk=nk, c0=c0: e.activation(
                                out=pt[:nk, c0:c0 + 2 * qw], in_=Sps[:nk, c0:c0 + 2 * qw], func=AF.Exp),
                                reads=[R_bank[sbk]], writes=[Rpt])
                            if tag == "prev":
                                P.op("pool", lambda e, c0=c0: e.memset(
                                    pt[0:64, c0:c0 + 2 * qw].rearrange("p (h q) -> p h q", h=2)[:, :, 64:128], 0.0),
                                    writes=[Rpt])
                            if tag == "diag":
                                P.op("pool", lambda e, c0=c0: e.memset(
                                    pt[64:128, c0:c0 + 2 * qw].rearrange("p (h q) -> p h q", h=2)[:, :, 0:64], 0.0),
                                    writes=[Rpt])

                def c_stage2(g, n):
                    gb = g % 2
                    j, a = n // 4, n % 4
                    blocks = c_blocks(g, j, a)
                    nb = len(blocks)
                    ocb = YB[2]
                    OC = bank(ocb)
                    for par in range(2):
                        po = par * 64
                        pt = PTc[(n % 2) * 2 + par]; Rpt = R_PTc[(n % 2) * 2 + par]
                        if qw == 128:
                            for bi_, (k_ap, nk, v_ap, tag, rk, rv) in enumerate(blocks):
                                col = bi_ * 2 * qw
                                P.op("pe", lambda e, v_ap=v_ap, nk=nk, col=col, bi_=bi_: e.matmul(
                                    OC[po:po + 64, 0:256], lhsT=v_ap, rhs=pt[:nk, col:col + 256],
                                    start=(bi_ == 0), stop=(bi_ == nb - 1)),
                                    reads=rv + [Rpt], writes=[R_bank[ocb]])
                            for bi_, (k_ap, nk, v_ap, tag, rk, rv) in enumerate(blocks):
                                col = bi_ * 2 * qw
                                P.op("pe", lambda e, nk=nk, col=col, bi_=bi_: e.matmul(
                                    OC[po:po + 64, 256:512], lhsT=ones[:nk, 0:64],
                                    rhs=pt[:nk, col:col + 256], start=(bi_ == 0), stop=(bi_ == nb - 1)),
                                    reads=[Rpt, R_const], writes=[R_bank[ocb]])
                            continue
                        for hi in range(2):
                            for bi_, (k_ap, nk, v_ap, tag, rk, rv) in enumerate(blocks):
                                col = (bi_ * 2 + hi) * qw
                                P.op("pe", lambda e, v_ap=v_ap, nk=nk, col=col, bi_=bi_: e.matmul(
                                    OC[po:po + 64, hi * 128:hi * 128 + qw], lhsT=v_ap, rhs=pt[:nk, col:col + qw],
                                    start=(bi_ == 0), stop=(bi_ == nb - 1)),
                                    reads=rv + [Rpt], writes=[R_bank[ocb]])
                            for bi_, (k_ap, nk, v_ap, tag, rk, rv) in enumerate(blocks):
                                col = (bi_ * 2 + hi) * qw
                                P.op("pe", lambda e, nk=nk, col=col, bi_=bi_: e.matmul(
                                    OC[po:po + 64, 256 + hi * 128:256 + hi * 128 + qw], lhsT=ones[:nk, 0:64],
                                    rhs=pt[:nk, col:col + qw], start=(bi_ == 0), stop=(bi_ == nb - 1)),
                                    reads=[Rpt, R_const], writes=[R_bank[ocb]])
                    s2 = n % 2
                    rc = recc[s2]; Rrc = R_recc[s2]
                    tm = tmpc[s2]; Rtm = R_tmpc[s2]
                    for hi in range(2):
                        fc = 2 * a + hi
                        P.op("act", lambda e, hi=hi, fc=fc: e.activation(
                            out=rc[:, hi * 128:hi * 128 + qw], in_=OC[:, 256 + hi * 128:256 + hi * 128 + qw],
                            func=AF.Ln, bias=esink[:, fc:fc + 1]),
                            reads=[R_bank[ocb], R_const], writes=[Rrc])
                    if qw == 128:
                        P.op("act", lambda e: e.activation(out=rc[:, 0:256], in_=rc[:, 0:256], func=AF.Exp, scale=-1.0),
                             reads=[Rrc], writes=[Rrc])
                        P.op("dve", lambda e: e.tensor_tensor(out=tm[:, 0:256], in0=OC[:, 0:256], in1=rc[:, 0:256],
                                                              op=ALU.mult), reads=[R_bank[ocb], Rrc], writes=[Rtm])
                        P.op("pool", lambda e: e.tensor_tensor(
                            out=qr[gb][:, 2 * a:2 * a + 2, j * qw:(j + 1) * qw],
                            in0=tm[:, 0:256].rearrange("p (h q) -> p h q", h=2),
                            in1=g1[gb][:, 2 * a:2 * a + 2, j * qw:(j + 1) * qw], op=ALU.mult),
                            reads=[Rtm, R_g1[gb]], writes=[R_qr[gb]])
                    else:
                        for hi in range(2):
                            fc = 2 * a + hi
                            P.op("act", lambda e, hi=hi: e.activation(out=rc[:, hi * 128:hi * 128 + qw],
                                                                      in_=rc[:, hi * 128:hi * 128 + qw], func=AF.Exp,
                                                                      scale=-1.0),
                                 reads=[Rrc], writes=[Rrc])
                            P.op("dve", lambda e, hi=hi: e.tensor_tensor(
                                out=tm[:, hi * 128:hi * 128 + qw], in0=OC[:, hi * 128:hi * 128 + qw],
                                in1=rc[:, hi * 128:hi * 128 + qw], op=ALU.mult),
                                reads=[R_bank[ocb], Rrc], writes=[Rtm])
                            P.op("pool", lambda e, hi=hi, fc=fc: e.tensor_tensor(
                                out=qr[gb][:, fc, j * qw:(j + 1) * qw], in0=tm[:, hi * 128:hi * 128 + qw],
                                in1=g1[gb][:, fc, j * qw:(j + 1) * qw], op=ALU.mult),
                                reads=[Rtm, R_g1[gb]], writes=[R_qr[gb]])

                def gen_c(g):
                    nn = tpg1 * 4
                    c_stage1(g, 0)
                    yield
                    for n in range(nn):
                        if n + 1 < nn:
                            c_stage1(g, n + 1)
                            yield
                        c_stage2(g, n)
                        yield

                def gen_d(g):
                    gb = g % 2
                    for tt in range(tpg1):
                        ti = g * tpg1 + tt
                        bks = (nbx(), nbx())
                        for half in range(2):
                            for c in range(8):
                                P.op("pe", lambda e, c=c, half=half: e.matmul(
                                    bank(bks[half])[:ts, :], lhsT=qr[gb][:, c, tt * ts:(tt + 1) * ts],
                                    rhs=woc[:, c, half * 512:(half + 1) * 512], start=(c == 0), stop=(c == 7)),
                                    reads=[R_qr[gb], R_woc], writes=[R_bank[bks[half]]])
                        yield
                        sk = yc["n"] % 2; yc["n"] += 1
                        ysk = t1b[sk]; Rysk = R_t1[sk]
                        post_norm_residual(bks[0], bks[1], gpostc, Y0[gb][:ts, tt, :], R_Y0[gb], ysk[:ts, :], Rysk)
                        dst = yp[si, ti * ts:(ti + 1) * ts, :] if kind == "p" else ys[0:ts, :]
                        P.dma("pool", dst, ysk[:ts, :], reads=[Rysk], is_output=True)
                        yield

                def chain(*gens):
                    for g_ in gens:
                        yield from g_

                def run_streams(streams):
                    alive = list(streams)
                    while alive:
                        for s_ in list(alive):
                            try:
                                next(s_)
                            except StopIteration:
                                alive.remove(s_)

                flags = {}

                def wait_for(*keys):
                    while not all(flags.get(k) for k in keys):
                        yield

                def SA():
                    for g in range(ng1):
                        if g >= 2:
                            yield from wait_for(("d", g - 2))
                        if g >= 1:
                            yield from wait_for(("b2", g - 1))
                        yield from gen_a(g)
                        flags[("a", g)] = True
                        first = True
                        for _ in gen_b(g):
                            if first:
                                flags[("halo", g)] = True
                                first = False
                            yield
                        flags[("halo", g)] = True
                        flags[("bq", g)] = True

                def SB():
                    for g in range(ng1):
                        yield from wait_for(("a", g), ("halo", g))
                        yield from gen_b2(g)
                        flags[("b2", g)] = True

                def SC():
                    for g in range(ng1):
                        yield from wait_for(("bq", g), ("b2", g))
                        yield from gen_c(g)
                        flags[("c", g)] = True

                def SD():
                    for g in range(ng1):
                        yield from wait_for(("c", g))
                        yield from gen_d(g)
                        flags[("d", g)] = True

                run_streams([SA(), SB(), SC(), SD()])
            P.barrier()


        with nc.Block() as block:
            P.finalize(block, sems, dma_sems)
    return nc


_NC_CACHE = {}


def _prep(x_prompt, x_sample, cache_a_k, cache_a_v, cache_b_k, cache_b_v, cache_c_k, cache_c_v,
          ab_norm_pre, ab_w_in, ab_w_out, ab_norm_post, a_rel_bias,
          c_norm_pre, c_w_in, c_sinks, c_w_out, c_norm_post):
    f32 = np.float32
    A = lambda a: np.ascontiguousarray(np.asarray(a, dtype=f32))
    x_prompt, x_sample = A(x_prompt), A(x_sample)
    ncore = 8
    cst = _consts()
    w = A(ab_w_in)[0]
    w_ab = np.zeros((8, D, 512), f32)
    for pi in range(8):
        base = 0 if pi < 4 else 2048
        hp = pi % 4
        sl = lambda blk: w[:, base + blk * 512 + hp * 128: base + blk * 512 + (hp + 1) * 128]
        w_ab[pi, :, 0:128] = sl(0)
        w_ab[pi, :, 128:256] = sl(3)
        w_ab[pi, :, 256:384] = sl(1)
        w_ab[pi, :, 384:512] = sl(2)
    rep = lambda v: np.ascontiguousarray(np.broadcast_to(A(v).reshape(1, D), (128, D)))
    bp, bs = _bias_tiles(A(a_rel_bias)[0])
    sk = A(c_sinks)[0]
    sinks_l = np.zeros((128, 8), f32)
    for fc in range(8):
        sinks_l[0:64, fc] = sk[2 * fc]
        sinks_l[64:128, fc] = sk[2 * fc + 1]
    common = {
        "w_ab": w_ab, "w_oab": A(ab_w_out)[0], "w_c": A(c_w_in)[0], "w_oc": A(c_w_out)[0],
        "gpre_ab": rep(ab_norm_pre[0]), "gpost_ab": rep(ab_norm_post[0]),
        "gpre_c": rep(c_norm_pre[0]), "gpost_c": rep(c_norm_post[0]),
        "biasP": bp, "biasS": bs, "sinks": sinks_l,
        "c_ident": cst["ident"], "c_tri": cst["tri"], "c_ones": cst["ones"], "c_lmask": cst["lmask"],
        "c_rot": cst["rot"], "c_dsel": cst["dsel"], "c_dselrot": cst["dselrot"],
        "c_cosT": cst["cosT"], "c_sinT": cst["sinT"], "c_cosTM": cst["cosTM"], "c_sinTM": cst["sinTM"],
    }
    cak, cav = A(cache_a_k)[0], A(cache_a_v)[0]
    cbk, cbv = A(cache_b_k)[0], A(cache_b_v)[0]
    cck, ccv = A(cache_c_k)[0], A(cache_c_v)[0]
    in_maps = []
    for i in range(ncore):
        m = dict(common)
        m["xp"] = np.ascontiguousarray(x_prompt[2 * i:2 * i + 2])
        m["xs"] = np.ascontiguousarray(x_sample[i])
        m["ca_k"] = cak[i].reshape(512, 512); m["ca_v"] = cav[i].reshape(512, 512)
        m["cb_k"] = cbk[i].reshape(PAST, 512); m["cb_v"] = cbv[i].reshape(PAST, 512)
        m["cc_k"] = cck[i].reshape(128, 256); m["cc_v"] = ccv[i].reshape(128, 256)
        in_maps.append(m)
    return in_maps


def kernel(**inputs):
    ncore = 8
    in_maps = _prep(**inputs)
    if "nc" not in _NC_CACHE:
        _NC_CACHE["nc"] = build()
    nc = _NC_CACHE["nc"]
    res = run_bass_kernel_spmd(nc, in_maps, core_ids=list(range(ncore)))
    return _gather(res.results)


def _gather(R):
    ncore = len(R)
    cat = lambda name: np.concatenate([R[i][name] for i in range(ncore)], axis=0)
    stk = lambda name: np.stack([R[i][name] for i in range(ncore)], axis=0)
    y_prompt = cat("yp")
    y_sample = stk("ys")
    out = (
        y_prompt, y_sample,
        cat("o_akp").reshape(1, 16, 512, 8, 64), cat("o_avp").reshape(1, 16, 512, 8, 64),
        cat("o_bkp").reshape(1, 16, S, 8, 64), cat("o_bvp").reshape(1, 16, S, 8, 64),
        cat("o_ckp").reshape(1, 16, 128, 4, 64), cat("o_cvp").reshape(1, 16, 128, 4, 64),
        stk("o_aks").reshape(1, 8, TS, 8, 64), stk("o_avs").reshape(1, 8, TS, 8, 64),
        stk("o_bks").reshape(1, 8, TS, 8, 64), stk("o_bvs").reshape(1, 8, TS, 8, 64),
        stk("o_cks").reshape(1, 8, TS, 4, 64), stk("o_cvs").reshape(1, 8, TS, 4, 64),
    )
    return tuple(np.ascontiguousarray(o.astype(np.float32)) for o in out)
```

```python
import numpy as np
import concourse.bass as bass
import concourse.mybir as mybir
from concourse.bass_utils import run_bass_kernel_spmd

F32 = mybir.dt.float32
BF16 = mybir.dt.bfloat16
AF = mybir.ActivationFunctionType
ALU = mybir.AluOpType

D = 1024
S = 2048
TS = 32
PAST = 2048
EPS = 1e-6
NEG = -30000.0
SEM_LIM = 20000


class Res:
    __slots__ = ("w", "r", "name", "excl")

    def __init__(self, name="", excl=False):
        self.w = None
        self.r = {}
        self.name = name
        self.excl = excl


class _Rec:
    def __init__(self):
        self.call = None

    def __getattr__(self, name):
        def f(*args, **kwargs):
            self.call = (name, args, kwargs)
            return None
        return f


class Prog:
    ENG = ("pe", "act", "dve", "pool", "sp")

    def __init__(self, nc):
        self.nc = nc
        self.ops = {e: [] for e in self.ENG}
        self.waited = {e: {} for e in self.ENG}
        self.ndma_sems = 12
        self.ndma_q = {"sp": 12, "pool": 6}
        self.dma_cnt = {q: [0] * self.ndma_sems for q in ("sp", "pool")}
        self.dma_next = {"sp": 0, "pool": 0}
        self.dma_last = {q: [None] * self.ndma_sems for q in ("sp", "pool")}
        self.out_dma_refs = []
        self.last_pe = None

    def _need(self, eng, ref, waits):
        if ref is None:
            return
        if ref[0] == "op":
            _, e2, idx = ref
            if e2 == eng and eng == "pe":
                return
            if self.waited[eng].get(e2, -1) >= idx:
                return
            if e2 == eng and idx >= len(self.ops[eng]):
                return
            self.waited[eng][e2] = idx
            self.ops[e2][idx]["inc"] = True
            waits.append(ref)
        else:
            _, q, slot, val = ref
            key = ("dma", q, slot)
            if self.waited[eng].get(key, -1) >= val:
                return
            self.waited[eng][key] = val
            waits.append(ref)

    def _deps(self, eng, reads, writes, same_engine_war=False):
        waits = []
        for r in reads:
            self._need(eng, r.w, waits)
        for w in writes:
            self._need(eng, w.w, waits)
            for e2, ref in w.r.items():
                self._need(eng, ref, waits)
        return waits

    def _commit(self, ref, reads, writes):
        for r in reads:
            r.r[ref[1] if ref[0] == "op" else ("dma", ref[1], ref[2])] = ref
        for w in writes:
            w.w = ref
            w.r = {}

    def op(self, eng, fn, reads=(), writes=()):
        ex = [r for r in reads if r.excl]
        if ex:
            reads = [r for r in reads if not r.excl]
            writes = list(writes) + ex
        waits = self._deps(eng, reads, writes)
        idx = len(self.ops[eng])
        rec = _Rec()
        fn(rec)
        assert rec.call is not None
        self.ops[eng].append({"fn": rec.call, "waits": waits, "inc": False, "dma": None})
        self._commit(("op", eng, idx), reads, writes)
        return ("op", eng, idx)

    def dma(self, q, out, in_, reads=(), writes=(), is_output=False):
        waits = self._deps(q, reads, writes)
        slot = self.dma_next[q]
        self.dma_next[q] = (slot + 1) % self.ndma_q[q]
        prev = self.dma_last[q][slot]
        if prev is not None:
            self._need(q, prev, waits)
        self.dma_cnt[q][slot] += 1
        val = self.dma_cnt[q][slot] * 16
        ref = ("dma", q, slot, val)
        self.dma_last[q][slot] = ref
        self.ops[q].append({"fn": ("dma_start", (), {"out": out, "in_": in_}), "waits": waits,
                            "inc": False, "dma": (q, slot)})
        self._commit(ref, reads, writes)
        if is_output:
            self.out_dma_refs.append(ref)
        return ref

    def barrier(self):
        refs = []
        for e in self.ENG:
            for idx in range(len(self.ops[e]) - 1, -1, -1):
                o = self.ops[e][idx]
                if o["fn"] is not None and o["dma"] is None:
                    refs.append(("op", e, idx))
                    break
        for q in ("sp", "pool"):
            for slot in range(self.ndma_sems):
                if self.dma_last[q][slot] is not None:
                    refs.append(self.dma_last[q][slot])
        for e in self.ENG:
            waits = []
            for ref in refs:
                if ref[0] == "op" and ref[1] == e:
                    continue
                self._need(e, ref, waits)
            self.ops[e].append({"fn": None, "waits": waits, "inc": False, "dma": None})

    def finalize(self, block, sems, dma_sems):
        nc = self.nc
        marks = {}
        for e in self.ENG:
            c = 0
            m = []
            for o in self.ops[e]:
                if o["inc"] and o["fn"] is not None and o["dma"] is None:
                    c += 1
                m.append(c)
            marks[e] = m
            assert c <= SEM_LIM * len(sems[e]), (e, c)

        def sem_of(e, idx):
            m = marks[e][idx]
            assert m >= 1
            k = (m - 1) // SEM_LIM
            return sems[e][k], (m - 1) % SEM_LIM + 1

        engs = {"pe": nc.tensor, "act": nc.scalar, "dve": nc.vector, "pool": nc.gpsimd, "sp": nc.sync}

        def _pinfo(ap):
            fs = 1
            for s_ in list(ap.tensor.shape)[1:]:
                fs *= int(s_)
            p0 = int(ap.offset) // fs
            col = int(ap.offset) % fs
            return p0, int(ap.ap[0][1]), col, fs
        prev = None
        nviol = 0
        for o in self.ops["pe"]:
            if o["fn"] is None:
                continue
            name_, args_, kw_ = o["fn"]
            out_ap = args_[0] if args_ else kw_["out"]
            l_ap = kw_.get("lhsT", kw_.get("in_"))
            p0, kk, _, _ = _pinfo(l_ap)
            _, _, col, fs = _pinfo(out_ap)
            esz = 4 if fs in (512, 1024) and out_ap.tensor.name.startswith("pb") else 2
            bank_id = (out_ap.tensor.name, (col * esz) // 2048)
            rows = (p0, p0 + kk)
            cur = (rows, bank_id)
            if prev is not None and kk < 128 and (prev[0][1] - prev[0][0]) < 128:
                disjoint = rows[0] >= prev[0][1] or prev[0][0] >= rows[1]
                if disjoint and prev[1] == bank_id:
                    nviol += 1
            prev = cur
        assert nviol == 0, f"row-tile bank violations: {nviol}"

        def run(e, eng):
            for idx, o in enumerate(self.ops[e]):
                for ref in o["waits"]:
                    if ref[0] == "op":
                        s_, v_ = sem_of(ref[1], ref[2])
                        eng.wait_ge(s_, v_)
                    else:
                        eng.wait_ge(dma_sems[ref[1]][ref[2]], ref[3])
                if o["fn"] is None:
                    continue
                name_, args_, kw_ = o["fn"]
                ins = getattr(eng, name_)(*args_, **kw_)
                if o["dma"] is not None:
                    ins.then_inc(dma_sems[o["dma"][0]][o["dma"][1]], 16)
                elif o["inc"]:
                    s_, _ = sem_of(e, idx)
                    ins.then_inc(s_, 1)

        @block.tensor
        def _(eng):
            run("pe", eng)

        @block.scalar
        def _(eng):
            run("act", eng)

        @block.vector
        def _(eng):
            run("dve", eng)

        @block.gpsimd
        def _(eng):
            run("pool", eng)

        @block.sync
        def _(eng):
            run("sp", eng)


def _consts():
    c = {}
    i = np.arange(128)
    c["ident"] = np.eye(128, dtype=np.float32)
    c["tri"] = (i[:, None] >= i[None, :]).astype(np.float32)
    c["ones"] = np.ones((128, 128), np.float32)
    c["lmask"] = (i[:, None] < i[None, :]).astype(np.float32)
    rot = np.zeros((128, 128), np.float32)
    for p in range(128):
        if p % 64 < 32:
            rot[p + 32, p] = 1.0
        else:
            rot[p - 32, p] = 1.0
    c["rot"] = rot
    dsel = np.zeros((2, 128, 128), np.float32)
    for a in range(2):
        for p in range(128):
            dsel[a, a * 64 + (p % 64), p] = 1.0
    c["dsel"] = dsel
    c["dselrot"] = np.stack([dsel[a] @ rot for a in range(2)])
    half = 32
    inv = (10000.0 ** (-np.arange(half, dtype=np.float32) * np.float32(2.0 / 64))).astype(np.float32)
    pos = np.arange(S + TS, dtype=np.float32)
    ang = (pos[:, None] * inv[None, :]).astype(np.float32)
    cos, sin = np.cos(ang).astype(np.float32), np.sin(ang).astype(np.float32)
    pidx = np.arange(128) % 32
    sign = np.where((np.arange(128) % 64) < 32, -1.0, 1.0).astype(np.float32)
    c["cosT"] = np.ascontiguousarray(cos[:, pidx].T)
    c["sinT"] = np.ascontiguousarray((sin[:, pidx] * sign[None, :]).T)
    fidx = np.arange(256) % 32
    fsign = np.where((np.arange(256) % 64) < 32, -1.0, 1.0).astype(np.float32)
    c["cosTM"] = np.ascontiguousarray(cos[:, fidx])
    c["sinTM"] = np.ascontiguousarray(sin[:, fidx] * fsign[None, :])
    return c


def _bias_tiles(table):
    k = np.arange(128)[:, None]
    q = np.arange(128)[None, :]
    bp = np.zeros((8, 128, 640), np.float32)
    for slot in range(5):
        rel = (4 - slot) * 128 + (q - k)
        idx = np.clip(rel, -128, 128) + 128
        bp[:, :, slot * 128:(slot + 1) * 128] = table[:, idx]
    qs = PAST + np.arange(TS)[None, :]
    bs = np.zeros((8, 128, 160), np.float32)
    for slot in range(5):
        kpos = PAST - 512 + slot * 128 + np.arange(128)[:, None]
        idx = np.clip(qs - kpos, -128, 128) + 128
        bs[:, :, slot * 32:(slot + 1) * 32] = table[:, idx]
    return bp, bs


def build():
    nc = bass.Bass("TRN2", target_bir_lowering=False)
    P = Prog(nc)

    def din(name, shape):
        return nc.dram_tensor(name, list(shape), F32, kind="ExternalInput").ap()

    def dout(name, shape):
        return nc.dram_tensor(name, list(shape), F32, kind="ExternalOutput").ap()

    xp = din("xp", [2, S, D])
    xs = din("xs", [TS, D])
    ca_k = din("ca_k", [512, 512]); ca_v = din("ca_v", [512, 512])
    cb_k = din("cb_k", [PAST, 512]); cb_v = din("cb_v", [PAST, 512])
    cc_k = din("cc_k", [128, 256]); cc_v = din("cc_v", [128, 256])
    w_ab = din("w_ab", [8, D, 512])
    w_oab = din("w_oab", [D, D])
    w_c = din("w_c", [D, 2560])
    w_oc = din("w_oc", [D, D])
    gpre_ab = din("gpre_ab", [128, D]); gpost_ab = din("gpost_ab", [128, D])
    gpre_c = din("gpre_c", [128, D]); gpost_c = din("gpost_c", [128, D])
    biasP = din("biasP", [8, 128, 640]); biasS = din("biasS", [8, 128, 160])
    sinks = din("sinks", [128, 8])
    c_ident = din("c_ident", [128, 128]); c_tri = din("c_tri", [128, 128]); c_ones = din("c_ones", [128, 128])
    c_lmask = din("c_lmask", [128, 128]); c_rot = din("c_rot", [128, 128])
    c_dsel = din("c_dsel", [2, 128, 128]); c_dselrot = din("c_dselrot", [2, 128, 128])
    c_cosT = din("c_cosT", [128, S + TS]); c_sinT = din("c_sinT", [128, S + TS])
    c_cosTM = din("c_cosTM", [S + TS, 256]); c_sinTM = din("c_sinTM", [S + TS, 256])

    yp = dout("yp", [2, S, D]); ys = dout("ys", [TS, D])
    o_akp = dout("o_akp", [2, 512, 512]); o_avp = dout("o_avp", [2, 512, 512])
    o_bkp = dout("o_bkp", [2, S, 512]); o_bvp = dout("o_bvp", [2, S, 512])
    o_ckp = dout("o_ckp", [2, 128, 256]); o_cvp = dout("o_cvp", [2, 128, 256])
    o_aks = dout("o_aks", [TS, 512]); o_avs = dout("o_avs", [TS, 512])
    o_bks = dout("o_bks", [TS, 512]); o_bvs = dout("o_bvs", [TS, 512])
    o_cks = dout("o_cks", [TS, 256]); o_cvs = dout("o_cvs", [TS, 256])

    from contextlib import ExitStack
    es = ExitStack()

    def sb(name, shape, dt):
        return es.enter_context(nc.sbuf_tensor(name, list(shape), dt))

    def ps(name, shape, dt):
        return es.enter_context(nc.psum_tensor(name, list(shape), dt))

    with es:
        sems = {e: [es.enter_context(nc.semaphore(f"s_{e}{k}")) for k in range(2)] for e in Prog.ENG}
        dma_sems = {q: [es.enter_context(nc.semaphore(f"d_{q}{k}")) for k in range(P.ndma_sems)]
                    for q in ("sp", "pool")}

        ident = sb("ident", [128, 128], BF16); tri = sb("tri", [128, 128], BF16)
        ones = sb("ones", [128, 128], BF16); lmask = sb("lmask", [128, 128], BF16)
        rot = sb("rot", [128, 128], BF16)
        dsel = sb("dsel", [128, 2, 128], BF16); dselrot = sb("dselrot", [128, 2, 128], BF16)
        esink = sb("esink", [128, 8], F32)
        R_const = Res("const")
        for t_, src in ((ident, c_ident), (tri, c_tri), (ones, c_ones), (lmask, c_lmask), (rot, c_rot)):
            P.dma("pool", t_[:], src, writes=[R_const])
        for a in range(2):
            P.dma("pool", dsel[:, a, :], c_dsel[a], writes=[R_const])
            P.dma("pool", dselrot[:, a, :], c_dselrot[a], writes=[R_const])
        for t_, src in ((esink, sinks),):
            P.dma("sp", t_[:], src, writes=[R_const])
        P.op("act", lambda e: e.activation(out=esink[:], in_=esink[:], func=AF.Exp), reads=[R_const], writes=[R_const])

        pbank = [ps(f"pb{i}", [128, 1024], F32) for i in range(3)]
        pb6 = ps("pb6", [128, 512], F32)
        ptp0 = ps("ptp0", [128, 1024], BF16)
        ptps = [ptp0, ptp0]
        ptp = ptps[0]
        R_bank = [Res(f"bank{i}", excl=True) for i in range(7)]
        R_tp = Res("tp0", excl=True)
        R_tps = [R_tp, R_tp]

        def bank(i):
            if i == 6:
                return pb6[:, :]
            return pbank[i // 2][:, (i % 2) * 512:(i % 2 + 1) * 512]

        oT = sb("oT", [128, 8, S], BF16)
        R_oT = [[Res(f"oT{c}_{g}") for g in range(4)] for c in range(8)]
        xst = [sb(f"xst{i}", [128, D], F32) for i in range(2)]
        R_xst = [Res(f"xst{i}") for i in range(2)]
        junk2 = [sb("junk2_0", [128, D], BF16)] * 2; R_junk2 = [Res()] * 2
        stat2 = [sb(f"stat2_{i}", [128, 2], F32) for i in range(2)]; R_stat2 = [Res() for _ in range(2)]
        xsb = [sb(f"xsb{i}", [128, D], BF16) for i in range(2)]
        R_xsb = [Res(f"xsb{i}") for i in range(2)]
        cnt = {"x": 0, "stg": 0, "pj": 0}

        seqs = [("p", 0), ("p", 1), ("s", 0)]

        def x_src(kind, si, t0, n):
            return xp[si, t0:t0 + n, :] if kind == "p" else xs[t0:t0 + n, :]

        def rstd_from(ss_ap, out_ap, n, reads, writes):
            P.op("act", lambda e: e.activation(out=out_ap, in_=ss_ap, func=AF.Ln, scale=1.0 / D, bias=EPS),
                 reads=reads, writes=writes)
            P.op("act", lambda e: e.activation(out=out_ap, in_=out_ap, func=AF.Exp, scale=-0.5),
                 reads=writes, writes=writes)

        def norm_part1(src_ap, src_res, ts, gain, gain_res=None):
            gain_res = gain_res or R_const
            k = cnt["x"]; cnt["x"] += 1
            xb = xsb[k % 2]; Rxb = R_xsb[k % 2]
            st = stat2[k % 2]; Rst = R_stat2[k % 2]
            jk = junk2[k % 2]; Rjk = R_junk2[k % 2]
            P.op("act", lambda e: e.activation(out=jk[:ts, :], in_=src_ap, func=AF.Square,
                                               accum_out=st[:ts, 0:1]),
                 reads=[src_res], writes=[Rjk, Rst])
            rstd_from(st[:ts, 0:1], st[:ts, 1:2], ts, [Rst], [Rst])
            P.op("dve", lambda e: e.scalar_tensor_tensor(out=xb[:ts, :], in0=src_ap, scalar=st[:ts, 1:2],
                                                         in1=gain[:ts, :], op0=ALU.mult, op1=ALU.mult),
                 reads=[src_res, Rst, gain_res], writes=[Rxb])
            return k

        def norm_part2(k, ts, dst_ap3, dst_res):
            xb = xsb[k % 2]; Rxb = R_xsb[k % 2]
            tp = ptps[k % 2]; Rtp = R_tps[k % 2]
            for c in range(8):
                P.op("pe", lambda e, c=c: e.transpose(out=tp[:, c * ts:(c + 1) * ts],
                                                      in_=xb[:ts, c * 128:(c + 1) * 128], identity=ident[:ts, :ts]),
                     reads=[Rxb, R_const], writes=[Rtp])
            P.op("dve", lambda e: e.tensor_copy(out=dst_ap3,
                                                in_=tp[:, 0:8 * ts].rearrange("p (c t) -> p c t", c=8)),
                 reads=[Rtp], writes=[dst_res])

        def norm_transpose(src_ap, src_res, ts, gain, dst_ap3, dst_res, gain_res=None):
            k = norm_part1(src_ap, src_res, ts, gain, gain_res)
            norm_part2(k, ts, dst_ap3, dst_res)

        for (kind, si) in seqs:
            T = S if kind == "p" else TS
            ts = min(T, 128)
            nt = T // ts
            gs = min(T, 512)
            ng = T // gs
            tpg = gs // ts
            pos0 = 0 if kind == "p" else PAST

            with ExitStack() as l0:
                def sb0(name, shape, dt):
                    return l0.enter_context(nc.sbuf_tensor(f"{name}_{kind}{si}", list(shape), dt))

                def mk0(name, shape, dt, n):
                    return [sb0(f"{name}{i}", shape, dt) for i in range(n)]

                gpreab = sb0("gpreab", [128, D], F32); R_gab = Res()
                P.dma("sp", gpreab[:], gpre_ab, writes=[R_gab])
                xnT = sb0("xnT", [128, 8, T], BF16)
                R_xnT = [Res(f"xnT{g}") for g in range(ng)]
                wring = mk0("wr", [128, 8, 512], BF16, 2)
                R_wring = [Res(f"wr{i}") for i in range(2)]
                qTs = mk0("qT", [128, T], BF16, 2); kTs = mk0("kT", [128, T], BF16, 2); gTs = mk0("gT", [128, T], BF16, 2)
                merged = (kind == "p")
                if merged:
                    Vs = mk0("V", [128, nt, 2, 128], BF16, 2)
                else:
                    Vs = mk0("V", [128, nt, 128], BF16, 2)
                R_qs = [[Res() for _ in range(ng)] for _ in range(2)]
                R_ks = [[Res() for _ in range(ng)] for _ in range(2)]
                R_gs = [[Res() for _ in range(ng)] for _ in range(2)]
                R_Vs = [[Res() for _ in range(nt)] for _ in range(2)]
                stg = mk0("stg", [128, 256], F32, 2); R_stg = [Res() for _ in range(2)]
                biasT = sb0("biasT", [128, 8, 640], F32); R_bias = Res("bias")
                Ssb = mk0("Ssb", [128, 640], F32, 2); R_Ssb = [Res() for _ in range(2)]
                PT = mk0("PT", [128, 640], BF16, 2); R_PT = [Res() for _ in range(2)]
                rec2 = mk0("rec", [128, 128], F32, 2); R_rec2 = [Res() for _ in range(2)]
                tmpo2 = mk0("tmpo", [128, 128], F32, 2); R_tmpo2 = [Res() for _ in range(2)]
                EH = [mk0(f"E{h}_", [128, 512], F32, 2) for h in range(2)]; R_EH = [[Res() for _ in range(2)] for _ in range(2)]
                SPH = [mk0(f"SP{h}_", [128, 512], BF16, 2) for h in range(2)]; R_SPH = [[Res() for _ in range(2)] for _ in range(2)]
                ATH = [mk0(f"AT{h}_", [128, 512], BF16, 2) for h in range(2)]; R_ATH = [[Res() for _ in range(2)] for _ in range(2)]
                sgt = sb0("sgt", [128, 512], F32); R_sgt = Res()
                SaccH = mk0("Sacc", [128, 512], F32, 2); R_SaccH = [Res() for _ in range(2)]
                SaccBH = [mk0(f"SaccB{h}_", [128, 512], BF16, 2) for h in range(2)]; R_SaccBH = [[Res() for _ in range(2)] for _ in range(2)]
                nqT = mk0("nq", [128, 512], BF16, 2); R_nq = [Res() for _ in range(2)]
                if kind == "s":
                    kcache = sb0("kcache", [128, 16, 128], BF16); R_kc = Res()
                    kTcs = mk0("kTc", [128, PAST], BF16, 2); R_kTcs = [Res() for _ in range(2)]
                    Vcs = mk0("Vc", [128, 16, 128], BF16, 2); R_Vcs = [Res() for _ in range(2)]

                def load_w(pi):
                    P.dma("pool", wring[pi % 2][:], w_ab[pi].rearrange("(c p) n -> p c n", p=128),
                          writes=[R_wring[pi % 2]])

                load_w(0)
                load_w(1)
                if merged:
                    for s_ in range(2):
                        for ti_ in range(nt):
                            P.op("pool", lambda e, s_=s_, ti_=ti_: e.memset(Vs[s_][:, ti_, :, :], 1.0), writes=[R_Vs[s_][ti_]])
                if kind == "p":
                    for h in range(8):
                        P.dma("sp", biasT[:, h, :], biasP[h], writes=[R_bias])
                    for h in range(8):
                        P.op("pool", lambda e, h=h: e.memset(biasT[0:64, h, 64:128], NEG), writes=[R_bias])
                        P.op("pool", lambda e, h=h: e.memset(biasT[64:128, h, 512:576], NEG), writes=[R_bias])
                else:
                    for h in range(8):
                        P.dma("sp", biasT[:, h, 0:160], biasS[h], writes=[R_bias])

                p0flags = {}

                def gen_phase0():
                    for ti in range(nt):
                        k = cnt["x"]
                        xt = xst[k % 2]; Rxt = R_xst[k % 2]
                        P.dma("sp", xt[:ts, :], x_src(kind, si, ti * ts, ts), writes=[Rxt])
                        g = ti // tpg
                        norm_transpose(xt[:ts, :], Rxt, ts, gpreab, xnT[:, :, ti * ts:(ti + 1) * ts], R_xnT[g],
                                       gain_res=R_gab)
                        if (ti + 1) % tpg == 0:
                            p0flags[g] = True
                        yield


                PJB = 5

                def gen_proj(pi):
                    isA = pi < 4
                    hp = pi % 4
                    st = pi % 2
                    W = wring[st]; RW = R_wring[st]
                    qT, kT, gT, V = qTs[st], kTs[st], gTs[st], Vs[st]
                    R_q, R_k, R_g, R_V = R_qs[st], R_ks[st], R_gs[st], R_Vs[st]
                    if kind == "s":
                        kTc, R_kTc, Vc, R_Vc = kTcs[st], R_kTcs[st], Vcs[st], R_Vcs[st]
                        csrc_k, csrc_v, nck = (ca_k, ca_v, 4) if isA else (cb_k, cb_v, 16)
                        if pi == 0:
                            P.dma("pool", kcache[:, 0:nck, :],
                                  csrc_k[:, hp * 128:(hp + 1) * 128].rearrange("(t p) f -> p t f", p=128), writes=[R_kc])
                        P.dma("pool", Vc[:, 0:nck, :],
                              csrc_v[:, hp * 128:(hp + 1) * 128].rearrange("(t p) f -> p t f", p=128), writes=[R_Vc])
                        for t0 in range(0, nck, 8):
                            nb_ = min(8, nck - t0)
                            for t_ in range(nb_):
                                P.op("pe", lambda e, t_=t_: e.transpose(
                                    out=ptp[:, t_ * 128:(t_ + 1) * 128], in_=kcache[:, t0 + t_, :], identity=ident[:, :]),
                                    reads=[R_kc, R_const], writes=[R_tp])
                            P.op("dve", lambda e: e.tensor_copy(
                                out=kTc[:, t0 * 128:(t0 + nb_) * 128], in_=ptp[:, 0:nb_ * 128]),
                                reads=[R_tp], writes=[R_kTc])
                            yield
                        if pi + 1 < 8:
                            pn = pi + 1
                            nsrc, nn = (ca_k, 4) if pn < 4 else (cb_k, 16)
                            P.dma("pool", kcache[:, 0:nn, :],
                                  nsrc[:, (pn % 4) * 128:(pn % 4 + 1) * 128].rearrange("(t p) f -> p t f", p=128),
                                  writes=[R_kc])
                    pjbanks = [5, 6]
                    pjc = [0]

                    def nextpj():
                        b_ = pjbanks[pjc[0] % len(pjbanks)]; pjc[0] += 1
                        return b_
                    for g in range(ng):
                        t0 = g * gs
                        while pi == 0 and not p0flags.get(g):
                            yield
                        for (fc, kindf) in ((0, "q"), (2, "k"), (1, "g")):
                            bi = nextpj()
                            for kc in range(8):
                                P.op("pe", lambda e, kc=kc: e.matmul(
                                    bank(bi)[:, 0:gs], lhsT=W[:, kc, fc * 128:(fc + 1) * 128],
                                    rhs=xnT[:, kc, t0:t0 + gs], start=(kc == 0), stop=(kc == 7)),
                                    reads=[RW, R_xnT[g]], writes=[R_bank[bi]])
                            if kindf == "q":
                                P.op("dve", lambda e: e.tensor_scalar(
                                    out=qT[:, t0:t0 + gs], in0=bank(bi)[:, 0:gs], scalar1=0.125, scalar2=None,
                                    op0=ALU.mult), reads=[R_bank[bi]], writes=[R_q[g]])
                            elif kindf == "k":
                                P.op("dve", lambda e: e.tensor_copy(
                                    out=kT[:, t0:t0 + gs], in_=bank(bi)[:, 0:gs]), reads=[R_bank[bi]], writes=[R_k[g]])
                            else:
                                P.op("act", lambda e: e.activation(out=sgt[:, 0:gs], in_=bank(bi)[:, 0:gs], func=AF.Exp,
                                                                   scale=-1.0), reads=[R_bank[bi]], writes=[R_sgt])
                                P.op("act", lambda e: e.activation(out=sgt[:, 0:gs], in_=sgt[:, 0:gs], func=AF.Ln, bias=1.0),
                                     reads=[R_sgt], writes=[R_sgt])
                                P.op("act", lambda e: e.activation(out=sgt[:, 0:gs], in_=sgt[:, 0:gs], func=AF.Exp,
                                                                   scale=-1.0), reads=[R_sgt], writes=[R_sgt])
                                P.op("dve", lambda e: e.tensor_tensor(out=gT[:, t0:t0 + gs], in0=bank(bi)[:, 0:gs],
                                                                      in1=sgt[:, 0:gs], op=ALU.mult),
                                     reads=[R_bank[bi], R_sgt], writes=[R_g[g]])
                            yield
                        for tt in range(tpg):
                            ti = g * tpg + tt
                            bi = nextpj()
                            for kc in range(8):
                                P.op("pe", lambda e, kc=kc: e.matmul(
                                    bank(bi)[:ts, 0:256], lhsT=xnT[:, kc, ti * ts:(ti + 1) * ts],
                                    rhs=W[:, kc, 256:512], start=(kc == 0), stop=(kc == 7)),
                                    reads=[RW, R_xnT[g]], writes=[R_bank[bi]])
                            if merged:
                                P.op("dve", lambda e: e.tensor_copy(
                                    out=V[:ts, ti, 0, 0:64], in_=bank(bi)[:ts, 128:192]), reads=[R_bank[bi]], writes=[R_V[ti]])
                                P.op("dve", lambda e: e.tensor_copy(
                                    out=V[:ts, ti, 1, 64:128], in_=bank(bi)[:ts, 192:256]), reads=[R_bank[bi]], writes=[R_V[ti]])
                            else:
                                P.op("dve", lambda e: e.tensor_copy(
                                    out=V[:ts, ti, :], in_=bank(bi)[:ts, 128:256]), reads=[R_bank[bi]], writes=[R_V[ti]])
                            if kind == "p":
                                if isA:
                                    need = ti * ts >= S - 512
                                    dk, dv, r0 = o_akp, o_avp, ti * ts - (S - 512)
                                else:
                                    need = True
                                    dk, dv, r0 = o_bkp, o_bvp, ti * ts
                                dk_ap = dk[si, r0:r0 + ts, hp * 128:(hp + 1) * 128] if need else None
                                dv_ap = dv[si, r0:r0 + ts, hp * 128:(hp + 1) * 128] if need else None
                            else:
                                need = True
                                dk, dv = (o_aks, o_avs) if isA else (o_bks, o_bvs)
                                dk_ap = dk[0:ts, hp * 128:(hp + 1) * 128]
                                dv_ap = dv[0:ts, hp * 128:(hp + 1) * 128]
                            if need:
                                sk = cnt["stg"] % 2; cnt["stg"] += 1
                                P.op("dve", lambda e: e.tensor_copy(
                                    out=stg[sk][:ts, :], in_=bank(bi)[:ts, 0:256]),
                                    reads=[R_bank[bi]], writes=[R_stg[sk]])
                                P.dma("pool", dk_ap, stg[sk][:ts, 0:128], reads=[R_stg[sk]], is_output=True)
                                P.dma("pool", dv_ap, stg[sk][:ts, 128:256], reads=[R_stg[sk]], is_output=True)
                            yield

                def gen_attnA(pi):
                    hp = pi % 4
                    st = pi % 2
                    qT, kT, gT, V = qTs[st], kTs[st], gTs[st], Vs[st]
                    R_q, R_k, R_g, R_V = R_qs[st], R_ks[st], R_gs[st], R_Vs[st]
                    if kind == "s":
                        kTc, R_kTc, Vc, R_Vc = kTcs[st], R_kTcs[st], Vcs[st], R_Vcs[st]
                    nqb = T // ts
                    qw = ts
                    its = [(j, hh) for j in range(nqb) for hh in range(2)]

                    def a_blocks(j, hh):
                        po = hh * 64
                        blocks = []
                        if kind == "p":
                            for slot in range(5):
                                kb = j - 4 + slot
                                if kb < 0:
                                    continue
                                blocks.append((kT[po:po + 64, kb * 128:(kb + 1) * 128], 128,
                                               V[:, kb, hh, :], slot,
                                               [R_k[kb // 4]], [R_V[kb]]))
                        else:
                            for slot in range(4):
                                blocks.append((kTc[po:po + 64, slot * 128:(slot + 1) * 128], 128,
                                               Vc[:, slot, hh * 64:(hh + 1) * 64], slot, [R_kTc], [R_Vc]))
                            blocks.append((kT[po:po + 64, 0:TS], TS, V[0:TS, 0, hh * 64:(hh + 1) * 64], 4,
                                           [R_k[0]], [R_V[0]]))
                        return blocks

                    def a_stage1(n):
                        j, hh = its[n]
                        h = hp * 2 + hh
                        po = hh * 64
                        sbk = n % 2
                        RS = [R_bank[2 * sbk], R_bank[2 * sbk + 1]]
                        Sps = pbank[sbk]
                        blocks = a_blocks(j, hh)
                        q_ap = qT[po:po + 64, j * qw:(j + 1) * qw]
                        gq = (j * qw) // gs
                        for (k_ap, nk, v_ap, slot, rk, rv) in blocks:
                            P.op("pe", lambda e, k_ap=k_ap, nk=nk, slot=slot: e.matmul(
                                Sps[:nk, slot * qw:(slot + 1) * qw], lhsT=k_ap, rhs=q_ap, start=True, stop=True),
                                reads=rk + [R_q[gq]], writes=RS)
                        s0 = blocks[0][3]
                        full = [b for b in blocks if b[1] == 128]
                        part = [b for b in blocks if b[1] != 128]
                        sk = n % 2
                        lo, hi = s0 * qw, (full[-1][3] + 1) * qw
                        P.op("dve", lambda e: e.tensor_tensor(
                            out=Ssb[sk][:, lo:hi], in0=Sps[:, lo:hi], in1=biasT[:, h, lo:hi], op=ALU.add),
                            reads=RS + [R_bias], writes=[R_Ssb[sk]])
                        P.op("act", lambda e: e.activation(
                            out=PT[sk][:, lo:hi], in_=Ssb[sk][:, lo:hi], func=AF.Exp),
                            reads=[R_Ssb[sk]], writes=[R_PT[sk]])
                        for (k_ap, nk, v_ap, slot, rk, rv) in part:
                            lo2, hi2 = slot * qw, (slot + 1) * qw
                            P.op("dve", lambda e, lo2=lo2, hi2=hi2, nk=nk: e.tensor_tensor(
                                out=Ssb[sk][:nk, lo2:hi2], in0=Sps[:nk, lo2:hi2], in1=biasT[:nk, h, lo2:hi2],
                                op=ALU.add), reads=RS + [R_bias], writes=[R_Ssb[sk]])
                            P.op("act", lambda e, lo2=lo2, hi2=hi2, nk=nk: e.activation(
                                out=PT[sk][:nk, lo2:hi2], in_=Ssb[sk][:nk, lo2:hi2], func=AF.Exp),
                                reads=[R_Ssb[sk]], writes=[R_PT[sk]])

                    def a_stage2m(n):
                        j, hh = its[n]
                        po = hh * 64
                        pd = 64 - po
                        sk = n % 2
                        odb = 4
                        OD = bank(4)[:, (n % 2) * 128:(n % 2) * 128 + 128]
                        blocks = a_blocks(j, hh)
                        nb = len(blocks)
                        gq = (j * qw) // gs
                        for bi_, (k_ap, nk, v_ap, slot, rk, rv) in enumerate(blocks):
                            P.op("pe", lambda e, v_ap=v_ap, nk=nk, slot=slot, bi_=bi_: e.matmul(
                                OD[:, 0:qw], lhsT=v_ap, rhs=PT[sk][:nk, slot * qw:(slot + 1) * qw],
                                start=(bi_ == 0), stop=(bi_ == nb - 1)),
                                reads=rv + [R_PT[sk]], writes=[R_bank[odb]])
                        rc = rec2[hh]; Rrc = R_rec2[hh]
                        tm = tmpo2[hh]; Rtm = R_tmpo2[hh]
                        P.op("act", lambda e: e.activation(
                            out=rc[po:po + 64, 0:qw], in_=OD[pd:pd + 64, 0:qw], func=AF.Ln),
                            reads=[R_bank[odb]], writes=[Rrc])
                        P.op("act", lambda e: e.activation(
                            out=rc[po:po + 64, 0:qw], in_=rc[po:po + 64, 0:qw], func=AF.Exp, scale=-1.0),
                            reads=[Rrc], writes=[Rrc])
                        P.op("dve", lambda e: e.tensor_tensor(
                            out=tm[po:po + 64, 0:qw], in0=OD[po:po + 64, 0:qw], in1=rc[po:po + 64, 0:qw],
                            op=ALU.mult), reads=[R_bank[odb], Rrc], writes=[Rtm])
                        P.op("pool", lambda e: e.tensor_tensor(
                            out=oT[po:po + 64, pi, j * qw:(j + 1) * qw], in0=tm[po:po + 64, 0:qw],
                            in1=gT[po:po + 64, j * qw:(j + 1) * qw], op=ALU.mult),
                            reads=[Rtm, R_g[gq]], writes=[R_oT[pi][gq]])

                    def a_stage2(n):
                        if merged:
                            return a_stage2m(n)
                        j, hh = its[n]
                        po = hh * 64
                        sk = n % 2
                        odb = 4
                        OD = bank(odb)
                        blocks = a_blocks(j, hh)
                        nb = len(blocks)
                        gq = (j * qw) // gs
                        for bi_, (k_ap, nk, v_ap, slot, rk, rv) in enumerate(blocks):
                            P.op("pe", lambda e, v_ap=v_ap, nk=nk, slot=slot, bi_=bi_: e.matmul(
                                OD[po:po + 64, 0:qw], lhsT=v_ap, rhs=PT[sk][:nk, slot * qw:(slot + 1) * qw],
                                start=(bi_ == 0), stop=(bi_ == nb - 1)),
                                reads=rv + [R_PT[sk]], writes=[R_bank[odb]])
                        for bi_, (k_ap, nk, v_ap, slot, rk, rv) in enumerate(blocks):
                            P.op("pe", lambda e, nk=nk, slot=slot, bi_=bi_: e.matmul(
                                OD[po:po + 64, 128:128 + qw], lhsT=ones[:nk, 0:64],
                                rhs=PT[sk][:nk, slot * qw:(slot + 1) * qw],
                                start=(bi_ == 0), stop=(bi_ == nb - 1)),
                                reads=[R_PT[sk], R_const], writes=[R_bank[odb]])
                        rc = rec2[hh]; Rrc = R_rec2[hh]
                        tm = tmpo2[hh]; Rtm = R_tmpo2[hh]
                        P.op("act", lambda e: e.activation(
                            out=rc[po:po + 64, 0:qw], in_=OD[po:po + 64, 128:128 + qw], func=AF.Ln),
                            reads=[R_bank[odb]], writes=[Rrc])
                        P.op("act", lambda e: e.activation(
                            out=rc[po:po + 64, 0:qw], in_=rc[po:po + 64, 0:qw], func=AF.Exp, scale=-1.0),
                            reads=[Rrc], writes=[Rrc])
                        P.op("dve", lambda e: e.tensor_tensor(
                            out=tm[po:po + 64, 0:qw], in0=OD[po:po + 64, 0:qw], in1=rc[po:po + 64, 0:qw],
                            op=ALU.mult), reads=[R_bank[odb], Rrc], writes=[Rtm])
                        P.op("pool", lambda e: e.tensor_tensor(
                            out=oT[po:po + 64, pi, j * qw:(j + 1) * qw], in0=tm[po:po + 64, 0:qw],
                            in1=gT[po:po + 64, j * qw:(j + 1) * qw], op=ALU.mult),
                            reads=[Rtm, R_g[gq]], writes=[R_oT[pi][gq]])

                    a_stage1(0)
                    yield
                    for n in range(len(its)):
                        if n + 1 < len(its):
                            a_stage1(n + 1)
                            yield
                        a_stage2(n)
                        yield

                def gen_attnB(pi):
                    st = pi % 2
                    qT, kT, gT, V = qTs[st], kTs[st], gTs[st], Vs[st]
                    R_q, R_k, R_g, R_V = R_qs[st], R_ks[st], R_gs[st], R_Vs[st]
                    if kind == "s":
                        kTc, R_kTc, Vc, R_Vc = kTcs[st], R_kTcs[st], Vcs[st], R_Vcs[st]
                    cw = gs
                    ob = 4
                    OB = bank(ob)
                    dq = min(128, cw)
                    for c in range(ng):
                        steps = [[], []]
                        for hh in range(2):
                            po = hh * 64
                            P.op("dve", lambda e: e.tensor_scalar(
                                out=nqT[hh][po:po + 64, 0:cw], in0=qT[po:po + 64, c * cw:(c + 1) * cw],
                                scalar1=-1.0, scalar2=None, op0=ALU.mult), reads=[R_q[c]], writes=[R_nq[hh]])
                            P.op("pool", lambda e: e.memset(SaccH[hh][:, 0:cw], 0.0), writes=[R_SaccH[hh]])
                            P.op("pool", lambda e: e.memset(SaccBH[hh][0][:, 0:cw], 0.0), writes=[R_SaccBH[hh][0]])
                            if kind == "p":
                                for kb in range(4 * c + 3, -1, -1):
                                    q0 = max(0, kb * 128 - c * 512)
                                    steps[hh].append((kT[po:po + 64, kb * 128:(kb + 1) * 128], 128,
                                                      V[:, kb, hh, hh * 64:(hh + 1) * 64], q0, kb >= 4 * c,
                                                      [R_k[kb // 4]], [R_V[kb]]))
                            else:
                                steps[hh].append((kT[po:po + 64, 0:TS], TS, V[0:TS, 0, hh * 64:(hh + 1) * 64], 0, True,
                                                  [R_k[0]], [R_V[0]]))
                                for kb in range(15, -1, -1):
                                    steps[hh].append((kTc[po:po + 64, kb * 128:(kb + 1) * 128], 128,
                                                      Vc[:, kb, hh * 64:(hh + 1) * 64], 0, False, [R_kTc], [R_Vc]))
                        ns = len(steps[0])

                        def stage1(i):
                            for hh in range(2):
                                po = hh * 64
                                k_ap, nk, v_ap, q0, diag, rk, rv = steps[hh][i]
                                z = bank(hh); Rz = R_bank[hh]
                                P.op("pe", lambda e: e.matmul(z[:nk, q0:cw], lhsT=k_ap,
                                                              rhs=qT[po:po + 64, c * cw + q0:(c + 1) * cw],
                                                              start=True, stop=True),
                                     reads=rk + [R_q[c]], writes=[Rz])
                            for hh in range(2):
                                k_ap, nk, v_ap, q0, diag, rk, rv = steps[hh][i]
                                z = bank(hh); Rz = R_bank[hh]
                                E = EH[hh][i % 2]; RE = R_EH[hh][i % 2]
                                P.op("act", lambda e: e.activation(out=E[:nk, q0:cw], in_=z[:nk, q0:cw], func=AF.Exp),
                                     reads=[Rz], writes=[RE])
                                if diag:
                                    P.op("dve", lambda e: e.tensor_tensor(
                                        out=E[:nk, q0:q0 + dq], in0=E[:nk, q0:q0 + dq], in1=lmask[:nk, 0:dq],
                                        op=ALU.mult), reads=[RE, R_const], writes=[RE])
                            for hh in range(2):
                                k_ap, nk, v_ap, q0, diag, rk, rv = steps[hh][i]
                                E = EH[hh][i % 2]; RE = R_EH[hh][i % 2]
                                SP = SPH[hh][i % 2]; RSP = R_SPH[hh][i % 2]
                                P.op("act", lambda e: e.activation(out=SP[:nk, q0:cw], in_=E[:nk, q0:cw], func=AF.Ln,
                                                                   bias=1.0), reads=[RE], writes=[RSP])

                        def stage2(i):
                            for hh in range(2):
                                k_ap, nk, v_ap, q0, diag, rk, rv = steps[hh][i]
                                cps = bank(2 + hh); Rc = R_bank[2 + hh]
                                SP = SPH[hh][i % 2]; RSP = R_SPH[hh][i % 2]
                                P.op("pe", lambda e: e.matmul(cps[:nk, q0:cw], lhsT=tri[:nk, :nk], rhs=SP[:nk, q0:cw],
                                                              start=True, stop=False),
                                     reads=[RSP, R_const], writes=[Rc])
                                P.op("pe", lambda e: e.matmul(cps[:nk, q0:cw], lhsT=ones[:, :nk],
                                                              rhs=SaccBH[hh][i % 2][:, q0:cw], start=False, stop=False),
                                     reads=[R_SaccBH[hh][i % 2], R_const], writes=[Rc])
                            for hh in range(2):
                                po = hh * 64
                                k_ap, nk, v_ap, q0, diag, rk, rv = steps[hh][i]
                                cps = bank(2 + hh); Rc = R_bank[2 + hh]
                                P.op("pe", lambda e: e.matmul(cps[:nk, q0:cw], lhsT=k_ap,
                                                              rhs=nqT[hh][po:po + 64, q0:cw], start=False, stop=True),
                                     reads=rk + [R_nq[hh]], writes=[Rc])
                            if i + 1 < ns:
                                for hh in range(2):
                                    k_ap, nk, v_ap, q0, diag, rk, rv = steps[hh][i]
                                    SP = SPH[hh][i % 2]; RSP = R_SPH[hh][i % 2]
                                    P.op("dve", lambda e: e.tensor_tensor(
                                        out=SaccH[hh][:nk, q0:cw], in0=SaccH[hh][:nk, q0:cw], in1=SP[:nk, q0:cw], op=ALU.add),
                                        reads=[RSP, R_SaccH[hh]], writes=[R_SaccH[hh]])
                                    P.op("dve", lambda e: e.tensor_copy(out=SaccBH[hh][(i + 1) % 2][:, 0:cw],
                                                                        in_=SaccH[hh][:, 0:cw]),
                                         reads=[R_SaccH[hh]], writes=[R_SaccBH[hh][(i + 1) % 2]])
                            for hh in range(2):
                                k_ap, nk, v_ap, q0, diag, rk, rv = steps[hh][i]
                                cps = bank(2 + hh); Rc = R_bank[2 + hh]
                                AT = ATH[hh][i % 2]; RAT = R_ATH[hh][i % 2]
                                P.op("act", lambda e: e.activation(out=AT[:nk, q0:cw], in_=cps[:nk, q0:cw], func=AF.Exp,
                                                                   scale=-1.0), reads=[Rc], writes=[RAT])
                                if q0 > 0:
                                    P.op("pool", lambda e: e.memset(AT[:nk, 0:q0], 0.0), writes=[RAT])
                                if diag:
                                    P.op("dve", lambda e: e.tensor_tensor(
                                        out=AT[:nk, q0:q0 + dq], in0=AT[:nk, q0:q0 + dq], in1=lmask[:nk, 0:dq],
                                        op=ALU.mult), reads=[RAT, R_const], writes=[RAT])

                        def stage3(i):
                            for hh in range(2):
                                po = hh * 64
                                k_ap, nk, v_ap, q0, diag, rk, rv = steps[hh][i]
                                AT = ATH[hh][i % 2]; RAT = R_ATH[hh][i % 2]
                                P.op("pe", lambda e: e.matmul(OB[po:po + 64, 0:cw], lhsT=v_ap, rhs=AT[:nk, 0:cw],
                                                              start=(i == 0), stop=(i == ns - 1)),
                                     reads=rv + [RAT], writes=[R_bank[ob]])

                        stage1(0)
                        yield
                        for i in range(ns):
                            if i + 1 < ns:
                                stage1(i + 1)
                            stage2(i)
                            if i > 0:
                                stage3(i - 1)
                            yield
                        stage3(ns - 1)
                        P.op("dve", lambda e: e.tensor_tensor(
                            out=oT[:, pi, c * cw:(c + 1) * cw], in0=OB[:, 0:cw],
                            in1=gT[:, c * cw:(c + 1) * cw], op=ALU.mult),
                            reads=[R_bank[ob], R_g[c]], writes=[R_oT[pi][c]])
                        yield

                def run_weighted(ga, na, gb, nb_):
                    da = db = 0
                    a_alive, b_alive = True, gb is not None
                    while a_alive or b_alive:
                        pick_b = b_alive and (not a_alive or (db + 1) * na <= (da + 1) * nb_)
                        if pick_b:
                            try:
                                next(gb); db += 1
                            except StopIteration:
                                b_alive = False
                        else:
                            try:
                                next(ga); da += 1
                            except StopIteration:
                                a_alive = False

                n_proj = ng * (3 + tpg) + (3 if kind == "s" else 0)
                gp0, gj0 = gen_phase0(), gen_proj(0)
                alive = [gp0, gj0]
                while alive:
                    for s_ in list(alive):
                        try:
                            next(s_)
                        except StopIteration:
                            alive.remove(s_)
                for pi in range(8):
                    if pi + 2 < 8:
                        load_w(pi + 2)
                    isA = pi < 4
                    if isA:
                        ga = gen_attnA(pi); na = 2 * (T // ts) * 2
                    else:
                        ga = gen_attnB(pi)
                        na = sum((4 * c + 4 + 2) for c in range(ng)) if kind == "p" else 19
                    gb = gen_proj(pi + 1) if pi + 1 < 8 else None
                    run_weighted(ga, na, gb, n_proj)
            P.barrier()

            with ExitStack() as l1:
                def sb1(name, shape, dt):
                    return l1.enter_context(nc.sbuf_tensor(f"{name}_{kind}{si}", list(shape), dt))

                gs1 = min(T, 256)
                ng1 = T // gs1
                tpg1 = gs1 // ts
                qw = ts
                woab = sb1("woab", [128, 8, D], BF16); R_woab = Res()
                wc = sb1("wc", [128, 8, 2560], BF16); R_wcq = [Res() for _ in range(4)]
                woc = sb1("woc", [128, 8, D], BF16); R_woc = Res()
                gpostab = sb1("gpostab", [128, D], F32); gprec = sb1("gprec", [128, D], F32)
                gpostc = sb1("gpostc", [128, D], F32); R_gn = Res()
                for t_, src in ((gpostab, gpost_ab), (gprec, gpre_c), (gpostc, gpost_c)):
                    P.dma("sp", t_[:], src, writes=[R_gn])
                P.dma("pool", woab[:], w_oab.rearrange("(c p) n -> p c n", p=128), writes=[R_woab])
                for q4 in range(4):
                    P.dma("pool", wc[:, :, q4 * 640:(q4 + 1) * 640],
                          w_c[:, q4 * 640:(q4 + 1) * 640].rearrange("(c p) n -> p c n", p=128), writes=[R_wcq[q4]])
                P.dma("pool", woc[:], w_oc.rearrange("(c p) n -> p c n", p=128), writes=[R_woc])

                def mk(name, shape, dt, n):
                    return [sb1(f"{name}{i}", shape, dt) for i in range(n)], [Res() for _ in range(n)]

                Y0, R_Y0 = mk("Y0", [128, tpg1, D], F32, 2)
                t1b, R_t1 = mk("t1b", [128, D], F32, 2)
                statY, R_statY = mk("statY", [128, 4], F32, 2)
                xn1T, R_xn1 = mk("xn1T", [128, 8, gs1], BF16, 1)
                xn1T, R_xn1 = xn1T * 2, R_xn1 * 2
                qbf, R_qbf = mk("qbf", [128, gs1], BF16, 2)
                qr, R_qr = mk("qr", [128, 8, gs1], BF16, 2)
                kbf, R_kbf = mk("kbf", [128, 2, gs1], BF16, 1)
                kr, R_kr = mk("kr", [128, 4, 128 + gs1], BF16, 2)
                g1, R_g1 = mk("g1", [128, 8, gs1], BF16, 2)
                V1, R_V1 = mk("V1", [128, 1 + tpg1, 256], BF16, 2)
                cosg, R_cos = mk("cosg", [128, gs1], F32, 1)
                sing, R_sin = mk("sing", [128, gs1], F32, 1)
                cosg, R_cos, sing, R_sin = cosg * 2, R_cos * 2, sing * 2, R_sin * 2
                ta, R_ta = mk("ta", [128, 256], F32, 2)
                tb, R_tb = mk("tb", [128, 256], F32, 2)
                tcb, R_tcb = mk("tcb", [128, 256], F32, 2)
                PTc, R_PTc = mk("PTc", [128, 512], BF16, 4)
                recc, R_recc = mk("recc", [128, 256], F32, 1)
                tmpc, R_tmpc = mk("tmpc", [128, 256], F32, 1)
                recc, R_recc, tmpc, R_tmpc = recc * 2, R_recc * 2, tmpc * 2, R_tmpc * 2
                kst = sb1("kst", [128, 256], F32); R_kst = Res()
                ksw = sb1("ksw", [128, 256], F32); R_ksw = Res()
                ctm = sb1("ctm", [128, 256], F32); stm = sb1("stm", [128, 256], F32); R_ctm = Res()
                kvst = sb1("kvst", [128, 512], F32); R_kvst = Res()
                if kind == "s":
                    kcc = sb1("kcc", [128, 256], BF16); R_kcc = Res()
                    kcd = sb1("kcd", [128, 4, 128], BF16); R_kcd = Res()
                    krc = sb1("krc", [128, 4, 128], BF16); R_krc = Res()
                    Vcc = sb1("Vcc", [128, 256], BF16); R_Vcc = Res()
                    P.dma("pool", kcc[:], cc_k, writes=[R_kcc])
                    P.dma("pool", Vcc[:], cc_v, writes=[R_Vcc])
                    for a in range(4):
                        for d2 in range(2):
                            P.op("pool", lambda e, a=a, d2=d2: e.tensor_copy(
                                out=kcd[:, a, d2 * 64:(d2 + 1) * 64], in_=kcc[:, a * 64:(a + 1) * 64]),
                                reads=[R_kcc], writes=[R_kcd])
                    for a in range(4):
                        P.op("pe", lambda e, a=a: e.transpose(out=ptp[:, a * 128:(a + 1) * 128], in_=kcd[:, a, :],
                                                              identity=ident[:, :]),
                             reads=[R_kcd, R_const], writes=[R_tp])
                    for a in range(4):
                        P.op("dve", lambda e, a=a: e.tensor_copy(out=krc[:, a, :], in_=ptp[:, a * 128:(a + 1) * 128]),
                             reads=[R_tp], writes=[R_krc])
                lt0 = pos0 + T - ts
                P.dma("sp", ctm[:ts, :], c_cosTM[lt0:lt0 + ts, :], writes=[R_ctm])
                P.dma("sp", stm[:ts, :], c_sinTM[lt0:lt0 + ts, :], writes=[R_ctm])

                yc = {"n": 0, "bx": 0, "t": 0}
                LAG_A = 1
                XB = [0, 1, 2, 6]
                YB = [3, 4, 5]

                def nbx():
                    b_ = XB[yc["bx"] % 4]; yc["bx"] += 1
                    return b_

                def post_norm_residual(bk0, bk1, gain, res_ap, res_r, out_ap, out_r):
                    k = yc["t"]; yc["t"] += 1
                    st = statY[k % 2]; Rst = R_statY[k % 2]
                    jk = junk2[k % 2]; Rjk = R_junk2[k % 2]
                    for half, bk in enumerate((bk0, bk1)):
                        P.op("act", lambda e, half=half, bk=bk: e.activation(
                            out=jk[:ts, half * 512:(half + 1) * 512], in_=bank(bk)[:ts, :], func=AF.Square,
                            accum_out=st[:ts, half:half + 1]), reads=[R_bank[bk]], writes=[Rjk, Rst])
                    P.op("dve", lambda e: e.tensor_tensor(out=st[:ts, 2:3], in0=st[:ts, 0:1], in1=st[:ts, 1:2],
                                                          op=ALU.add), reads=[Rst], writes=[Rst])
                    rstd_from(st[:ts, 2:3], st[:ts, 3:4], ts, [Rst], [Rst])
                    for half, bk in enumerate((bk0, bk1)):
                        P.op("dve", lambda e, half=half, bk=bk: e.scalar_tensor_tensor(
                            out=out_ap[:, half * 512:(half + 1) * 512], in0=bank(bk)[:ts, :], scalar=st[:ts, 3:4],
                            in1=gain[:ts, half * 512:(half + 1) * 512], op0=ALU.mult, op1=ALU.mult),
                            reads=[R_bank[bk], Rst, R_gn], writes=[out_r])
                    P.op("pool", lambda e: e.tensor_tensor(out=out_ap, in0=out_ap, in1=res_ap, op=ALU.add),
                         reads=[res_r, out_r], writes=[out_r])

                def gen_a(g):
                    gb = g % 2
                    t0 = g * gs1
                    g0 = t0 // gs
                    P.dma("sp", cosg[gb][:, :], c_cosT[:, pos0 + t0:pos0 + t0 + gs1], writes=[R_cos[gb]])
                    P.dma("sp", sing[gb][:, :], c_sinT[:, pos0 + t0:pos0 + t0 + gs1], writes=[R_sin[gb]])
                    pend = []
                    for tt in range(tpg1):
                        ti = g * tpg1 + tt
                        k = cnt["x"]
                        xt = xst[k % 2]; Rxt = R_xst[k % 2]
                        P.dma("sp", xt[:ts, :], x_src(kind, si, ti * ts, ts), writes=[Rxt])
                        bks = (nbx(), nbx())
                        for half in range(2):
                            for c in range(8):
                                P.op("pe", lambda e, c=c, half=half: e.matmul(
                                    bank(bks[half])[:ts, :], lhsT=oT[:, c, ti * ts:(ti + 1) * ts],
                                    rhs=woab[:, c, half * 512:(half + 1) * 512], start=(c == 0), stop=(c == 7)),
                                    reads=[R_oT[c][g0], R_woab], writes=[R_bank[bks[half]]])
                        yield
                        post_norm_residual(bks[0], bks[1], gpostab, xt[:ts, :], Rxt, Y0[gb][:ts, tt, :], R_Y0[gb])
                        kk = norm_part1(Y0[gb][:ts, tt, :], R_Y0[gb], ts, gprec, gain_res=R_gn)
                        pend.append((kk, tt))
                        yield
                    for (kk, tt) in pend:
                        norm_part2(kk, ts, xn1T[gb][:, :, tt * ts:(tt + 1) * ts], R_xn1[gb])
                        yield

                def gen_b(g):
                    gb = g % 2
                    xn = xn1T[gb]; Rxn = R_xn1[gb]
                    cs, sn = cosg[gb], sing[gb]
                    if g > 0:
                        P.op("pool", lambda e: e.tensor_copy(out=kr[gb][:, :, 0:128], in_=kr[1 - gb][:, :, gs1:gs1 + 128]),
                             reads=[R_kr[1 - gb]], writes=[R_kr[gb]])
                        P.op("pool", lambda e: e.tensor_copy(out=V1[gb][:, 0, :], in_=V1[1 - gb][:, tpg1, :]),
                             reads=[R_V1[1 - gb]], writes=[R_V1[gb]])
                    def q_part_a(fc):
                        b1 = nbx()
                        for kc in range(8):
                            P.op("pe", lambda e, kc=kc: e.matmul(
                                bank(b1)[:, 0:gs1], lhsT=wc[:, kc, fc * 128:(fc + 1) * 128], rhs=xn[:, kc, :],
                                start=(kc == 0), stop=(kc == 7)), reads=[R_wcq[(fc * 128) // 640], Rxn], writes=[R_bank[b1]])
                        s2 = fc % 2
                        P.op("act", lambda e: e.activation(out=qbf[s2][:, :], in_=bank(b1)[:, 0:gs1], func=AF.Copy, scale=0.125),
                             reads=[R_bank[b1]], writes=[R_qbf[s2]])
                        P.op("dve", lambda e: e.scalar_tensor_tensor(
                            out=ta[s2][:, 0:gs1], in0=bank(b1)[:, 0:gs1], scalar=0.125, in1=cs[:, :], op0=ALU.mult,
                            op1=ALU.mult), reads=[R_bank[b1], R_cos[gb]], writes=[R_ta[s2]])

                    def q_part_b(fc):
                        s2 = fc % 2
                        b2 = nbx()
                        P.op("pe", lambda e: e.matmul(bank(b2)[:, 0:gs1], lhsT=rot[:, :], rhs=qbf[s2][:, :], start=True, stop=True),
                             reads=[R_qbf[s2], R_const], writes=[R_bank[b2]])
                        P.op("dve", lambda e: e.tensor_tensor(out=tb[s2][:, 0:gs1], in0=bank(b2)[:, 0:gs1], in1=sn[:, :],
                                                              op=ALU.mult), reads=[R_bank[b2], R_sin[gb]], writes=[R_tb[s2]])
                        P.op("pool", lambda e: e.tensor_tensor(out=qr[gb][:, fc, :], in0=ta[s2][:, 0:gs1], in1=tb[s2][:, 0:gs1],
                                                               op=ALU.add), reads=[R_ta[s2], R_tb[s2]], writes=[R_qr[gb]])

                    q_part_a(0)
                    yield
                    for fc in range(1, 8):
                        q_part_a(fc)
                        q_part_b(fc - 1)
                        yield
                    q_part_b(7)
                    yield

                def gen_b2(g):
                    gb = g % 2
                    xn = xn1T[gb]; Rxn = R_xn1[gb]
                    cs, sn = cosg[gb], sing[gb]
                    for kc2 in range(2):
                        b1 = nbx()
                        for kc in range(8):
                            P.op("pe", lambda e, kc=kc: e.matmul(
                                bank(b1)[:, 0:gs1], lhsT=wc[:, kc, 1024 + kc2 * 128:1024 + (kc2 + 1) * 128],
                                rhs=xn[:, kc, :], start=(kc == 0), stop=(kc == 7)),
                                reads=[R_wcq[1], Rxn], writes=[R_bank[b1]])
                        P.op("act", lambda e: e.activation(out=kbf[0][:, kc2, :], in_=bank(b1)[:, 0:gs1], func=AF.Copy),
                             reads=[R_bank[b1]], writes=[R_kbf[0]])
                    yield
                    for a in range(4):
                        s2 = a % 2
                        b1 = nbx()
                        P.op("pe", lambda e: e.matmul(bank(b1)[:, 0:gs1], lhsT=dsel[:, a % 2, :], rhs=kbf[0][:, a // 2, :],
                                                      start=True, stop=True),
                             reads=[R_kbf[0], R_const], writes=[R_bank[b1]])
                        b2 = nbx()
                        P.op("pe", lambda e: e.matmul(bank(b2)[:, 0:gs1], lhsT=dselrot[:, a % 2, :], rhs=kbf[0][:, a // 2, :],
                                                      start=True, stop=True),
                             reads=[R_kbf[0], R_const], writes=[R_bank[b2]])
                        P.op("dve", lambda e: e.tensor_tensor(out=ta[s2][:, 0:gs1], in0=bank(b1)[:, 0:gs1], in1=cs[:, :],
                                                              op=ALU.mult), reads=[R_bank[b1], R_cos[gb]], writes=[R_ta[s2]])
                        P.op("dve", lambda e: e.tensor_tensor(out=tb[s2][:, 0:gs1], in0=bank(b2)[:, 0:gs1], in1=sn[:, :],
                                                              op=ALU.mult), reads=[R_bank[b2], R_sin[gb]], writes=[R_tb[s2]])
                        P.op("pool", lambda e: e.tensor_tensor(
                            out=kr[gb][:, a, 128:128 + gs1], in0=ta[s2][:, 0:gs1], in1=tb[s2][:, 0:gs1], op=ALU.add),
                            reads=[R_ta[s2], R_tb[s2]], writes=[R_kr[gb]])
                        yield
                    for fc in range(8):
                        b1 = nbx()
                        for kc in range(8):
                            P.op("pe", lambda e, kc=kc: e.matmul(
                                bank(b1)[:, 0:gs1], lhsT=wc[:, kc, 1536 + fc * 128:1536 + (fc + 1) * 128],
                                rhs=xn[:, kc, :], start=(kc == 0), stop=(kc == 7)),
                                reads=[R_wcq[(1536 + fc * 128) // 640], Rxn], writes=[R_bank[b1]])
                        s2 = fc % 2
                        P.op("act", lambda e: e.activation(out=tcb[s2][:, 0:gs1], in_=bank(b1)[:, 0:gs1], func=AF.Exp,
                                                           scale=-1.0), reads=[R_bank[b1]], writes=[R_tcb[s2]])
                        P.op("act", lambda e: e.activation(out=tcb[s2][:, 0:gs1], in_=tcb[s2][:, 0:gs1], func=AF.Ln, bias=1.0),
                             reads=[R_tcb[s2]], writes=[R_tcb[s2]])
                        P.op("act", lambda e: e.activation(out=tcb[s2][:, 0:gs1], in_=tcb[s2][:, 0:gs1], func=AF.Exp,
                                                           scale=-1.0), reads=[R_tcb[s2]], writes=[R_tcb[s2]])
                        P.op("dve", lambda e: e.tensor_tensor(out=g1[gb][:, fc, :], in0=bank(b1)[:, 0:gs1],
                                                              in1=tcb[s2][:, 0:gs1], op=ALU.mult),
                             reads=[R_bank[b1], R_tcb[s2]], writes=[R_g1[gb]])
                        yield
                    for tt in range(tpg1):
                        ti = g * tpg1 + tt
                        b1 = nbx()
                        for kc in range(8):
                            P.op("pe", lambda e, kc=kc: e.matmul(
                                bank(b1)[:ts, :], lhsT=xn[:, kc, tt * ts:(tt + 1) * ts], rhs=wc[:, kc, 1024:1536],
                                start=(kc == 0), stop=(kc == 7)), reads=[R_wcq[1], R_wcq[2], Rxn], writes=[R_bank[b1]])
                        P.op("dve", lambda e: e.tensor_copy(out=V1[gb][:ts, 1 + tt, :], in_=bank(b1)[:ts, 256:512]),
                             reads=[R_bank[b1]], writes=[R_V1[gb]])
                        if ti == nt - 1:
                            ysk = kvst; Rysk = R_kvst
                            P.op("dve", lambda e: e.tensor_copy(out=ysk[:ts, 0:256], in_=bank(b1)[:ts, 256:512]),
                                 reads=[R_bank[b1]], writes=[Rysk])
                            dv_ap = o_cvp[si, :, :] if kind == "p" else o_cvs[:, :]
                            dk_ap = o_ckp[si, :, :] if kind == "p" else o_cks[:, :]
                            P.dma("pool", dv_ap, ysk[:ts, 0:256], reads=[Rysk], is_output=True)
                            P.op("dve", lambda e: e.tensor_copy(out=kst[:ts, :], in_=bank(b1)[:ts, 0:256]),
                                 reads=[R_bank[b1]], writes=[R_kst])
                            for hk in range(4):
                                for b2_ in range(2):
                                    P.op("dve", lambda e, hk=hk, b2_=b2_: e.tensor_copy(
                                        out=ksw[:ts, hk * 64 + b2_ * 32:hk * 64 + b2_ * 32 + 32],
                                        in_=kst[:ts, hk * 64 + (1 - b2_) * 32:hk * 64 + (1 - b2_) * 32 + 32]),
                                        reads=[R_kst], writes=[R_ksw])
                            P.op("dve", lambda e: e.tensor_tensor(out=kst[:ts, :], in0=kst[:ts, :], in1=ctm[:ts, :],
                                                                  op=ALU.mult), reads=[R_kst, R_ctm], writes=[R_kst])
                            P.op("dve", lambda e: e.tensor_tensor(out=ksw[:ts, :], in0=ksw[:ts, :], in1=stm[:ts, :],
                                                                  op=ALU.mult), reads=[R_ksw, R_ctm], writes=[R_ksw])
                            P.op("dve", lambda e: e.tensor_tensor(out=ysk[:ts, 256:512], in0=kst[:ts, :],
                                                                  in1=ksw[:ts, :], op=ALU.add),
                                 reads=[R_kst, R_ksw], writes=[Rysk])
                            P.dma("pool", dk_ap, ysk[:ts, 256:512], reads=[Rysk], is_output=True)
                        yield

                def c_blocks(g, j, a):
                    gb = g % 2
                    J = g * tpg1 + j
                    blocks = []
                    if kind == "p":
                        if J > 0:
                            blocks.append((kr[gb][:, a, j * 128:(j + 1) * 128], 128,
                                           V1[gb][:, j, a * 64:(a + 1) * 64], "prev", [R_kr[gb]], [R_V1[gb]]))
                        blocks.append((kr[gb][:, a, (j + 1) * 128:(j + 2) * 128], 128,
                                       V1[gb][:, j + 1, a * 64:(a + 1) * 64], "diag", [R_kr[gb]], [R_V1[gb]]))
                    else:
                        blocks.append((krc[:, a, :], 128, Vcc[:, a * 64:(a + 1) * 64], "c", [R_krc], [R_Vcc]))
                        blocks.append((kr[gb][:, a, 128:128 + TS], TS, V1[gb][0:TS, 1, a * 64:(a + 1) * 64], "n",
                                       [R_kr[gb]], [R_V1[gb]]))
                    return blocks

                def c_stage1(g, n):
                    gb = g % 2
                    j, a = n // 4, n % 4
                    blocks = c_blocks(g, j, a)
                    for par in range(2):
                        sbk = YB[par]
                        Sps = bank(sbk)
                        po = par * 64
                        pt = PTc[(n % 2) * 2 + par]; Rpt = R_PTc[(n % 2) * 2 + par]
                        for bi_, (k_ap, nk, v_ap, tag, rk, rv) in enumerate(blocks):
                            if qw == 128:
                                col = bi_ * 2 * qw
                                P.op("pe", lambda e, k_ap=k_ap, nk=nk, col=col: e.matmul(
                                    Sps[:nk, col:col + 2 * qw].rearrange("p (h q) -> p h q", h=2),
                                    lhsT=k_ap[po:po + 64, :],
                                    rhs=qr[gb][po:po + 64, 2 * a:2 * a + 2, j * qw:(j + 1) * qw],
                                    start=True, stop=True), reads=rk + [R_qr[gb]], writes=[R_bank[sbk]])
                                continue
                            for hi in range(2):
                                fc = 2 * a + hi
                                col = (bi_ * 2 + hi) * qw
                                P.op("pe", lambda e, k_ap=k_ap, nk=nk, fc=fc, col=col: e.matmul(
                                    Sps[:nk, col:col + qw], lhsT=k_ap[po:po + 64, :],
                                    rhs=qr[gb][po:po + 64, fc, j * qw:(j + 1) * qw],
                                    start=True, stop=True), reads=rk + [R_qr[gb]], writes=[R_bank[sbk]])
                        for bi_, (k_ap, nk, v_ap, tag, rk, rv) in enumerate(blocks):
                            c0 = bi_ * 2 * qw
                            P.op("act", lambda e, nk=nk, c0=c0: e.activation(
                                out=pt[:nk, c0:c0 + 2 * qw], in_=Sps[:nk, c0:c0 + 2 * qw], func=AF.Exp),
                                reads=[R_bank[sbk]], writes=[Rpt])
                            if tag == "prev":
                                P.op("pool", lambda e, c0=c0: e.memset(
                                    pt[0:64, c0:c0 + 2 * qw].rearrange("p (h q) -> p h q", h=2)[:, :, 64:128], 0.0),
                                    writes=[Rpt])
                            if tag == "diag":
                                P.op("pool", lambda e, c0=c0: e.memset(
                                    pt[64:128, c0:c0 + 2 * qw].rearrange("p (h q) -> p h q", h=2)[:, :, 0:64], 0.0),
                                    writes=[Rpt])

                def c_stage2(g, n):
                    gb = g % 2
                    j, a = n // 4, n % 4
                    blocks = c_blocks(g, j, a)
                    nb = len(blocks)
                    ocb = YB[2]
                    OC = bank(ocb)
                    for par in range(2):
                        po = par * 64
                        pt = PTc[(n % 2) * 2 + par]; Rpt = R_PTc[(n % 2) * 2 + par]
                        if qw == 128:
                            for bi_, (k_ap, nk, v_ap, tag, rk, rv) in enumerate(blocks):
                                col = bi_ * 2 * qw
                                P.op("pe", lambda e, v_ap=v_ap, nk=nk, col=col, bi_=bi_: e.matmul(
                                    OC[po:po + 64, 0:256], lhsT=v_ap, rhs=pt[:nk, col:col + 256],
                                    start=(bi_ == 0), stop=(bi_ == nb - 1)),
                                    reads=rv + [Rpt], writes=[R_bank[ocb]])
                            for bi_, (k_ap, nk, v_ap, tag, rk, rv) in enumerate(blocks):
                                col = bi_ * 2 * qw
                                P.op("pe", lambda e, nk=nk, col=col, bi_=bi_: e.matmul(
                                    OC[po:po + 64, 256:512], lhsT=ones[:nk, 0:64],
                                    rhs=pt[:nk, col:col + 256], start=(bi_ == 0), stop=(bi_ == nb - 1)),
                                    reads=[Rpt, R_const], writes=[R_bank[ocb]])
                            continue
                        for hi in range(2):
                            for bi_, (k_ap, nk, v_ap, tag, rk, rv) in enumerate(blocks):
                                col = (bi_ * 2 + hi) * qw
                                P.op("pe", lambda e, v_ap=v_ap, nk=nk, col=col, bi_=bi_: e.matmul(
                                    OC[po:po + 64, hi * 128:hi * 128 + qw], lhsT=v_ap, rhs=pt[:nk, col:col + qw],
                                    start=(bi_ == 0), stop=(bi_ == nb - 1)),
                                    reads=rv + [Rpt], writes=[R_bank[ocb]])
                            for bi_, (k_ap, nk, v_ap, tag, rk, rv) in enumerate(blocks):
                                col = (bi_ * 2 + hi) * qw
                                P.op("pe", lambda e, nk=nk, col=col, bi_=bi_: e.matmul(
                                    OC[po:po + 64, 256 + hi * 128:256 + hi * 128 + qw], lhsT=ones[:nk, 0:64],
                                    rhs=pt[:nk, col:col + qw], start=(bi_ == 0), stop=(bi_ == nb - 1)),
                                    reads=[Rpt, R_const], writes=[R_bank[ocb]])
                    s2 = n % 2
                    rc = recc[s2]; Rrc = R_recc[s2]
                    tm = tmpc[s2]; Rtm = R_tmpc[s2]
                    for hi in range(2):
                        fc = 2 * a + hi
                        P.op("act", lambda e, hi=hi, fc=fc: e.activation(
                            out=rc[:, hi * 128:hi * 128 + qw], in_=OC[:, 256 + hi * 128:256 + hi * 128 + qw],
                            func=AF.Ln, bias=esink[:, fc:fc + 1]),
                            reads=[R_bank[ocb], R_const], writes=[Rrc])
                    if qw == 128:
                        P.op("act", lambda e: e.activation(out=rc[:, 0:256], in_=rc[:, 0:256], func=AF.Exp, scale=-1.0),
                             reads=[Rrc], writes=[Rrc])
                        P.op("dve", lambda e: e.tensor_tensor(out=tm[:, 0:256], in0=OC[:, 0:256], in1=rc[:, 0:256],
                                                              op=ALU.mult), reads=[R_bank[ocb], Rrc], writes=[Rtm])
                        P.op("pool", lambda e: e.tensor_tensor(
                            out=qr[gb][:, 2 * a:2 * a + 2, j * qw:(j + 1) * qw],
                            in0=tm[:, 0:256].rearrange("p (h q) -> p h q", h=2),
                            in1=g1[gb][:, 2 * a:2 * a + 2, j * qw:(j + 1) * qw], op=ALU.mult),
                            reads=[Rtm, R_g1[gb]], writes=[R_qr[gb]])
                    else:
                        for hi in range(2):
                            fc = 2 * a + hi
                            P.op("act", lambda e, hi=hi: e.activation(out=rc[:, hi * 128:hi * 128 + qw],
                                                                      in_=rc[:, hi * 128:hi * 128 + qw], func=AF.Exp,
                                                                      scale=-1.0),
                                 reads=[Rrc], writes=[Rrc])
                            P.op("dve", lambda e, hi=hi: e.tensor_tensor(
                                out=tm[:, hi * 128:hi * 128 + qw], in0=OC[:, hi * 128:hi * 128 + qw],
                                in1=rc[:, hi * 128:hi * 128 + qw], op=ALU.mult),
                                reads=[R_bank[ocb], Rrc], writes=[Rtm])
                            P.op("pool", lambda e, hi=hi, fc=fc: e.tensor_tensor(
                                out=qr[gb][:, fc, j * qw:(j + 1) * qw], in0=tm[:, hi * 128:hi * 128 + qw],
                                in1=g1[gb][:, fc, j * qw:(j + 1) * qw], op=ALU.mult),
                                reads=[Rtm, R_g1[gb]], writes=[R_qr[gb]])

                def gen_c(g):
                    nn = tpg1 * 4
                    c_stage1(g, 0)
                    yield
                    for n in range(nn):
                        if n + 1 < nn:
                            c_stage1(g, n + 1)
                            yield
                        c_stage2(g, n)
                        yield

                def gen_d(g):
                    gb = g % 2
                    for tt in range(tpg1):
                        ti = g * tpg1 + tt
                        bks = (nbx(), nbx())
                        for half in range(2):
                            for c in range(8):
                                P.op("pe", lambda e, c=c, half=half: e.matmul(
                                    bank(bks[half])[:ts, :], lhsT=qr[gb][:, c, tt * ts:(tt + 1) * ts],
                                    rhs=woc[:, c, half * 512:(half + 1) * 512], start=(c == 0), stop=(c == 7)),
                                    reads=[R_qr[gb], R_woc], writes=[R_bank[bks[half]]])
                        yield
                        sk = yc["n"] % 2; yc["n"] += 1
                        ysk = t1b[sk]; Rysk = R_t1[sk]
                        post_norm_residual(bks[0], bks[1], gpostc, Y0[gb][:ts, tt, :], R_Y0[gb], ysk[:ts, :], Rysk)
                        dst = yp[si, ti * ts:(ti + 1) * ts, :] if kind == "p" else ys[0:ts, :]
                        P.dma("pool", dst, ysk[:ts, :], reads=[Rysk], is_output=True)
                        yield

                def chain(*gens):
                    for g_ in gens:
                        yield from g_

                def run_streams(streams):
                    alive = list(streams)
                    while alive:
                        for s_ in list(alive):
                            try:
                                next(s_)
                            except StopIteration:
                                alive.remove(s_)

                flags = {}

                def wait_for(*keys):
                    while not all(flags.get(k) for k in keys):
                        yield

                def SA():
                    for g in range(ng1):
                        if g >= 2:
                            yield from wait_for(("d", g - 2))
                        if g >= 1:
                            yield from wait_for(("b2", g - 1))
                        yield from gen_a(g)
                        flags[("a", g)] = True
                        first = True
                        for _ in gen_b(g):
                            if first:
                                flags[("halo", g)] = True
                                first = False
                            yield
                        flags[("halo", g)] = True
                        flags[("bq", g)] = True

                def SB():
                    for g in range(ng1):
                        yield from wait_for(("a", g), ("halo", g))
                        yield from gen_b2(g)
                        flags[("b2", g)] = True

                def SC():
                    for g in range(ng1):
                        yield from wait_for(("bq", g), ("b2", g))
                        yield from gen_c(g)
                        flags[("c", g)] = True

                def SD():
                    for g in range(ng1):
                        yield from wait_for(("c", g))
                        yield from gen_d(g)
                        flags[("d", g)] = True

                run_streams([SA(), SB(), SC(), SD()])
            P.barrier()


        with nc.Block() as block:
            P.finalize(block, sems, dma_sems)
    return nc


_NC_CACHE = {}


def _prep(x_prompt, x_sample, cache_a_k, cache_a_v, cache_b_k, cache_b_v, cache_c_k, cache_c_v,
          ab_norm_pre, ab_w_in, ab_w_out, ab_norm_post, a_rel_bias,
          c_norm_pre, c_w_in, c_sinks, c_w_out, c_norm_post):
    f32 = np.float32
    A = lambda a: np.ascontiguousarray(np.asarray(a, dtype=f32))
    x_prompt, x_sample = A(x_prompt), A(x_sample)
    ncore = 8
    cst = _consts()
    w = A(ab_w_in)[0]
    w_ab = np.zeros((8, D, 512), f32)
    for pi in range(8):
        base = 0 if pi < 4 else 2048
        hp = pi % 4
        sl = lambda blk: w[:, base + blk * 512 + hp * 128: base + blk * 512 + (hp + 1) * 128]
        w_ab[pi, :, 0:128] = sl(0)
        w_ab[pi, :, 128:256] = sl(3)
        w_ab[pi, :, 256:384] = sl(1)
        w_ab[pi, :, 384:512] = sl(2)
    rep = lambda v: np.ascontiguousarray(np.broadcast_to(A(v).reshape(1, D), (128, D)))
    bp, bs = _bias_tiles(A(a_rel_bias)[0])
    sk = A(c_sinks)[0]
    sinks_l = np.zeros((128, 8), f32)
    for fc in range(8):
        sinks_l[0:64, fc] = sk[2 * fc]
        sinks_l[64:128, fc] = sk[2 * fc + 1]
    common = {
        "w_ab": w_ab, "w_oab": A(ab_w_out)[0], "w_c": A(c_w_in)[0], "w_oc": A(c_w_out)[0],
        "gpre_ab": rep(ab_norm_pre[0]), "gpost_ab": rep(ab_norm_post[0]),
        "gpre_c": rep(c_norm_pre[0]), "gpost_c": rep(c_norm_post[0]),
        "biasP": bp, "biasS": bs, "sinks": sinks_l,
        "c_ident": cst["ident"], "c_tri": cst["tri"], "c_ones": cst["ones"], "c_lmask": cst["lmask"],
        "c_rot": cst["rot"], "c_dsel": cst["dsel"], "c_dselrot": cst["dselrot"],
        "c_cosT": cst["cosT"], "c_sinT": cst["sinT"], "c_cosTM": cst["cosTM"], "c_sinTM": cst["sinTM"],
    }
    cak, cav = A(cache_a_k)[0], A(cache_a_v)[0]
    cbk, cbv = A(cache_b_k)[0], A(cache_b_v)[0]
    cck, ccv = A(cache_c_k)[0], A(cache_c_v)[0]
    in_maps = []
    for i in range(ncore):
        m = dict(common)
        m["xp"] = np.ascontiguousarray(x_prompt[2 * i:2 * i + 2])
        m["xs"] = np.ascontiguousarray(x_sample[i])
        m["ca_k"] = cak[i].reshape(512, 512); m["ca_v"] = cav[i].reshape(512, 512)
        m["cb_k"] = cbk[i].reshape(PAST, 512); m["cb_v"] = cbv[i].reshape(PAST, 512)
        m["cc_k"] = cck[i].reshape(128, 256); m["cc_v"] = ccv[i].reshape(128, 256)
        in_maps.append(m)
    return in_maps


def kernel(**inputs):
    ncore = 8
    in_maps = _prep(**inputs)
    if "nc" not in _NC_CACHE:
        _NC_CACHE["nc"] = build()
    nc = _NC_CACHE["nc"]
    res = run_bass_kernel_spmd(nc, in_maps, core_ids=list(range(ncore)))
    return _gather(res.results)


def _gather(R):
    ncore = len(R)
    cat = lambda name: np.concatenate([R[i][name] for i in range(ncore)], axis=0)
    stk = lambda name: np.stack([R[i][name] for i in range(ncore)], axis=0)
    y_prompt = cat("yp")
    y_sample = stk("ys")
    out = (
        y_prompt, y_sample,
        cat("o_akp").reshape(1, 16, 512, 8, 64), cat("o_avp").reshape(1, 16, 512, 8, 64),
        cat("o_bkp").reshape(1, 16, S, 8, 64), cat("o_bvp").reshape(1, 16, S, 8, 64),
        cat("o_ckp").reshape(1, 16, 128, 4, 64), cat("o_cvp").reshape(1, 16, 128, 4, 64),
        stk("o_aks").reshape(1, 8, TS, 8, 64), stk("o_avs").reshape(1, 8, TS, 8, 64),
        stk("o_bks").reshape(1, 8, TS, 8, 64), stk("o_bvs").reshape(1, 8, TS, 8, 64),
        stk("o_cks").reshape(1, 8, TS, 4, 64), stk("o_cvs").reshape(1, 8, TS, 4, 64),
    )
    return tuple(np.ascontiguousarray(o.astype(np.float32)) for o in out)
```

```python
import numpy as np
import concourse.bass as bass
import concourse.mybir as mybir
from concourse.bass_utils import run_bass_kernel_spmd

F32 = mybir.dt.float32
BF16 = mybir.dt.bfloat16
AF = mybir.ActivationFunctionType
ALU = mybir.AluOpType

D = 1024
S = 2048
TS = 32
PAST = 2048
EPS = 1e-6
NEG = -30000.0
SEM_LIM = 20000


class Res:
    __slots__ = ("w", "r", "name", "excl")

    def __init__(self, name="", excl=False):
        self.w = None
        self.r = {}
        self.name = name
        self.excl = excl


class _Rec:
    def __init__(self):
        self.call = None

    def __getattr__(self, name):
        def f(*args, **kwargs):
            self.call = (name, args, kwargs)
            return None
        return f


class Prog:
    ENG = ("pe", "act", "dve", "pool", "sp")

    def __init__(self, nc):
        self.nc = nc
        self.ops = {e: [] for e in self.ENG}
        self.waited = {e: {} for e in self.ENG}
        self.ndma_sems = 12
        self.ndma_q = {"sp": 12, "pool": 6}
        self.dma_cnt = {q: [0] * self.ndma_sems for q in ("sp", "pool")}
        self.dma_next = {"sp": 0, "pool": 0}
        self.dma_last = {q: [None] * self.ndma_sems for q in ("sp", "pool")}
        self.out_dma_refs = []
        self.last_pe = None

    def _need(self, eng, ref, waits):
        if ref is None:
            return
        if ref[0] == "op":
            _, e2, idx = ref
            if e2 == eng and eng == "pe":
                return
            if self.waited[eng].get(e2, -1) >= idx:
                return
            if e2 == eng and idx >= len(self.ops[eng]):
                return
            self.waited[eng][e2] = idx
            self.ops[e2][idx]["inc"] = True
            waits.append(ref)
        else:
            _, q, slot, val = ref
            key = ("dma", q, slot)
            if self.waited[eng].get(key, -1) >= val:
                return
            self.waited[eng][key] = val
            waits.append(ref)

    def _deps(self, eng, reads, writes, same_engine_war=False):
        waits = []
        for r in reads:
            self._need(eng, r.w, waits)
        for w in writes:
            self._need(eng, w.w, waits)
            for e2, ref in w.r.items():
                self._need(eng, ref, waits)
        return waits

    def _commit(self, ref, reads, writes):
        for r in reads:
            r.r[ref[1] if ref[0] == "op" else ("dma", ref[1], ref[2])] = ref
        for w in writes:
            w.w = ref
            w.r = {}

    def op(self, eng, fn, reads=(), writes=()):
        ex = [r for r in reads if r.excl]
        if ex:
            reads = [r for r in reads if not r.excl]
            writes = list(writes) + ex
        waits = self._deps(eng, reads, writes)
        idx = len(self.ops[eng])
        rec = _Rec()
        fn(rec)
        assert rec.call is not None
        self.ops[eng].append({"fn": rec.call, "waits": waits, "inc": False, "dma": None})
        self._commit(("op", eng, idx), reads, writes)
        return ("op", eng, idx)

    def dma(self, q, out, in_, reads=(), writes=(), is_output=False):
        waits = self._deps(q, reads, writes)
        slot = self.dma_next[q]
        self.dma_next[q] = (slot + 1) % self.ndma_q[q]
        prev = self.dma_last[q][slot]
        if prev is not None:
            self._need(q, prev, waits)
        self.dma_cnt[q][slot] += 1
        val = self.dma_cnt[q][slot] * 16
        ref = ("dma", q, slot, val)
        self.dma_last[q][slot] = ref
        self.ops[q].append({"fn": ("dma_start", (), {"out": out, "in_": in_}), "waits": waits,
                            "inc": False, "dma": (q, slot)})
        self._commit(ref, reads, writes)
        if is_output:
            self.out_dma_refs.append(ref)
        return ref

    def barrier(self):
        refs = []
        for e in self.ENG:
            for idx in range(len(self.ops[e]) - 1, -1, -1):
                o = self.ops[e][idx]
                if o["fn"] is not None and o["dma"] is None:
                    refs.append(("op", e, idx))
                    break
        for q in ("sp", "pool"):
            for slot in range(self.ndma_sems):
                if self.dma_last[q][slot] is not None:
                    refs.append(self.dma_last[q][slot])
        for e in self.ENG:
            waits = []
            for ref in refs:
                if ref[0] == "op" and ref[1] == e:
                    continue
                self._need(e, ref, waits)
            self.ops[e].append({"fn": None, "waits": waits, "inc": False, "dma": None})

    def finalize(self, block, sems, dma_sems):
        nc = self.nc
        marks = {}
        for e in self.ENG:
            c = 0
            m = []
            for o in self.ops[e]:
                if o["inc"] and o["fn"] is not None and o["dma"] is None:
                    c += 1
                m.append(c)
            marks[e] = m
            assert c <= SEM_LIM * len(sems[e]), (e, c)

        def sem_of(e, idx):
            m = marks[e][idx]
            assert m >= 1
            k = (m - 1) // SEM_LIM
            return sems[e][k], (m - 1) % SEM_LIM + 1

        engs = {"pe": nc.tensor, "act": nc.scalar, "dve": nc.vector, "pool": nc.gpsimd, "sp": nc.sync}

        def _pinfo(ap):
            fs = 1
            for s_ in list(ap.tensor.shape)[1:]:
                fs *= int(s_)
            p0 = int(ap.offset) // fs
            col = int(ap.offset) % fs
            return p0, int(ap.ap[0][1]), col, fs
        prev = None
        nviol = 0
        for o in self.ops["pe"]:
            if o["fn"] is None:
                continue
            name_, args_, kw_ = o["fn"]
            out_ap = args_[0] if args_ else kw_["out"]
            l_ap = kw_.get("lhsT", kw_.get("in_"))
            p0, kk, _, _ = _pinfo(l_ap)
            _, _, col, fs = _pinfo(out_ap)
            esz = 4 if fs in (512, 1024) and out_ap.tensor.name.startswith("pb") else 2
            bank_id = (out_ap.tensor.name, (col * esz) // 2048)
            rows = (p0, p0 + kk)
            cur = (rows, bank_id)
            if prev is not None and kk < 128 and (prev[0][1] - prev[0][0]) < 128:
                disjoint = rows[0] >= prev[0][1] or prev[0][0] >= rows[1]
                if disjoint and prev[1] == bank_id:
                    nviol += 1
            prev = cur
        assert nviol == 0, f"row-tile bank violations: {nviol}"

        def run(e, eng):
            for idx, o in enumerate(self.ops[e]):
                for ref in o["waits"]:
                    if ref[0] == "op":
                        s_, v_ = sem_of(ref[1], ref[2])
                        eng.wait_ge(s_, v_)
                    else:
                        eng.wait_ge(dma_sems[ref[1]][ref[2]], ref[3])
                if o["fn"] is None:
                    continue
                name_, args_, kw_ = o["fn"]
                ins = getattr(eng, name_)(*args_, **kw_)
                if o["dma"] is not None:
                    ins.then_inc(dma_sems[o["dma"][0]][o["dma"][1]], 16)
                elif o["inc"]:
                    s_, _ = sem_of(e, idx)
                    ins.then_inc(s_, 1)

        @block.tensor
        def _(eng):
            run("pe", eng)

        @block.scalar
        def _(eng):
            run("act", eng)

        @block.vector
        def _(eng):
            run("dve", eng)

        @block.gpsimd
        def _(eng):
            run("pool", eng)

        @block.sync
        def _(eng):
            run("sp", eng)


def _consts():
    c = {}
    i = np.arange(128)
    c["ident"] = np.eye(128, dtype=np.float32)
    c["tri"] = (i[:, None] >= i[None, :]).astype(np.float32)
    c["ones"] = np.ones((128, 128), np.float32)
    c["lmask"] = (i[:, None] < i[None, :]).astype(np.float32)
    rot = np.zeros((128, 128), np.float32)
    for p in range(128):
        if p % 64 < 32:
            rot[p + 32, p] = 1.0
        else:
            rot[p - 32, p] = 1.0
    c["rot"] = rot
    dsel = np.zeros((2, 128, 128), np.float32)
    for a in range(2):
        for p in range(128):
            dsel[a, a * 64 + (p % 64), p] = 1.0
    c["dsel"] = dsel
    c["dselrot"] = np.stack([dsel[a] @ rot for a in range(2)])
    half = 32
    inv = (10000.0 ** (-np.arange(half, dtype=np.float32) * np.float32(2.0 / 64))).astype(np.float32)
    pos = np.arange(S + TS, dtype=np.float32)
    ang = (pos[:, None] * inv[None, :]).astype(np.float32)
    cos, sin = np.cos(ang).astype(np.float32), np.sin(ang).astype(np.float32)
    pidx = np.arange(128) % 32
    sign = np.where((np.arange(128) % 64) < 32, -1.0, 1.0).astype(np.float32)
    c["cosT"] = np.ascontiguousarray(cos[:, pidx].T)
    c["sinT"] = np.ascontiguousarray((sin[:, pidx] * sign[None, :]).T)
    fidx = np.arange(256) % 32
    fsign = np.where((np.arange(256) % 64) < 32, -1.0, 1.0).astype(np.float32)
    c["cosTM"] = np.ascontiguousarray(cos[:, fidx])
    c["sinTM"] = np.ascontiguousarray(sin[:, fidx] * fsign[None, :])
    return c


def _bias_tiles(table):
    k = np.arange(128)[:, None]
    q = np.arange(128)[None, :]
    bp = np.zeros((8, 128, 640), np.float32)
    for slot in range(5):
        rel = (4 - slot) * 128 + (q - k)
        idx = np.clip(rel, -128, 128) + 128
        bp[:, :, slot * 128:(slot + 1) * 128] = table[:, idx]
    qs = PAST + np.arange(TS)[None, :]
    bs = np.zeros((8, 128, 160), np.float32)
    for slot in range(5):
        kpos = PAST - 512 + slot * 128 + np.arange(128)[:, None]
        idx = np.clip(qs - kpos, -128, 128) + 128
        bs[:, :, slot * 32:(slot + 1) * 32] = table[:, idx]
    return bp, bs


def build():
    nc = bass.Bass("TRN2", target_bir_lowering=False)
    P = Prog(nc)

    def din(name, shape):
        return nc.dram_tensor(name, list(shape), F32, kind="ExternalInput").ap()

    def dout(name, shape):
        return nc.dram_tensor(name, list(shape), F32, kind="ExternalOutput").ap()

    xp = din("xp", [2, S, D])
    xs = din("xs", [TS, D])
    ca_k = din("ca_k", [512, 512]); ca_v = din("ca_v", [512, 512])
    cb_k = din("cb_k", [PAST, 512]); cb_v = din("cb_v", [PAST, 512])
    cc_k = din("cc_k", [128, 256]); cc_v = din("cc_v", [128, 256])
    w_ab = din("w_ab", [8, D, 512])
    w_oab = din("w_oab", [D, D])
    w_c = din("w_c", [D, 2560])
    w_oc = din("w_oc", [D, D])
    gpre_ab = din("gpre_ab", [128, D]); gpost_ab = din("gpost_ab", [128, D])
    gpre_c = din("gpre_c", [128, D]); gpost_c = din("gpost_c", [128, D])
    biasP = din("biasP", [8, 128, 640]); biasS = din("biasS", [8, 128, 160])
    sinks = din("sinks", [128, 8])
    c_ident = din("c_ident", [128, 128]); c_tri = din("c_tri", [128, 128]); c_ones = din("c_ones", [128, 128])
    c_lmask = din("c_lmask", [128, 128]); c_rot = din("c_rot", [128, 128])
    c_dsel = din("c_dsel", [2, 128, 128]); c_dselrot = din("c_dselrot", [2, 128, 128])
    c_cosT = din("c_cosT", [128, S + TS]); c_sinT = din("c_sinT", [128, S + TS])
    c_cosTM = din("c_cosTM", [S + TS, 256]); c_sinTM = din("c_sinTM", [S + TS, 256])

    yp = dout("yp", [2, S, D]); ys = dout("ys", [TS, D])
    o_akp = dout("o_akp", [2, 512, 512]); o_avp = dout("o_avp", [2, 512, 512])
    o_bkp = dout("o_bkp", [2, S, 512]); o_bvp = dout("o_bvp", [2, S, 512])
    o_ckp = dout("o_ckp", [2, 128, 256]); o_cvp = dout("o_cvp", [2, 128, 256])
    o_aks = dout("o_aks", [TS, 512]); o_avs = dout("o_avs", [TS, 512])
    o_bks = dout("o_bks", [TS, 512]); o_bvs = dout("o_bvs", [TS, 512])
    o_cks = dout("o_cks", [TS, 256]); o_cvs = dout("o_cvs", [TS, 256])

    from contextlib import ExitStack
    es = ExitStack()

    def sb(name, shape, dt):
        return es.enter_context(nc.sbuf_tensor(name, list(shape), dt))

    def ps(name, shape, dt):
        return es.enter_context(nc.psum_tensor(name, list(shape), dt))

    with es:
        sems = {e: [es.enter_context(nc.semaphore(f"s_{e}{k}")) for k in range(2)] for e in Prog.ENG}
        dma_sems = {q: [es.enter_context(nc.semaphore(f"d_{q}{k}")) for k in range(P.ndma_sems)]
                    for q in ("sp", "pool")}

        ident = sb("ident", [128, 128], BF16); tri = sb("tri", [128, 128], BF16)
        ones = sb("ones", [128, 128], BF16); lmask = sb("lmask", [128, 128], BF16)
        rot = sb("rot", [128, 128], BF16)
        dsel = sb("dsel", [128, 2, 128], BF16); dselrot = sb("dselrot", [128, 2, 128], BF16)
        esink = sb("esink", [128, 8], F32)
        R_const = Res("const")
        for t_, src in ((ident, c_ident), (tri, c_tri), (ones, c_ones), (lmask, c_lmask), (rot, c_rot)):
            P.dma("pool", t_[:], src, writes=[R_const])
        for a in range(2):
            P.dma("pool", dsel[:, a, :], c_dsel[a], writes=[R_const])
            P.dma("pool", dselrot[:, a, :], c_dselrot[a], writes=[R_const])
        for t_, src in ((esink, sinks),):
            P.dma("sp", t_[:], src, writes=[R_const])
        P.op("act", lambda e: e.activation(out=esink[:], in_=esink[:], func=AF.Exp), reads=[R_const], writes=[R_const])

        pbank = [ps(f"pb{i}", [128, 1024], F32) for i in range(3)]
        pb6 = ps("pb6", [128, 512], F32)
        ptp0 = ps("ptp0", [128, 1024], BF16)
        ptps = [ptp0, ptp0]
        ptp = ptps[0]
        R_bank = [Res(f"bank{i}", excl=True) for i in range(7)]
        R_tp = Res("tp0", excl=True)
        R_tps = [R_tp, R_tp]

        def bank(i):
            if i == 6:
                return pb6[:, :]
            return pbank[i // 2][:, (i % 2) * 512:(i % 2 + 1) * 512]

        oT = sb("oT", [128, 8, S], BF16)
        R_oT = [[Res(f"oT{c}_{g}") for g in range(4)] for c in range(8)]
        xst = [sb(f"xst{i}", [128, D], F32) for i in range(2)]
        R_xst = [Res(f"xst{i}") for i in range(2)]
        junk2 = [sb("junk2_0", [128, D], BF16)] * 2; R_junk2 = [Res()] * 2
        stat2 = [sb(f"stat2_{i}", [128, 2], F32) for i in range(2)]; R_stat2 = [Res() for _ in range(2)]
        xsb = [sb(f"xsb{i}", [128, D], BF16) for i in range(2)]
        R_xsb = [Res(f"xsb{i}") for i in range(2)]
        cnt = {"x": 0, "stg": 0, "pj": 0}

        seqs = [("p", 0), ("p", 1), ("s", 0)]

        def x_src(kind, si, t0, n):
            return xp[si, t0:t0 + n, :] if kind == "p" else xs[t0:t0 + n, :]

        def rstd_from(ss_ap, out_ap, n, reads, writes):
            P.op("act", lambda e: e.activation(out=out_ap, in_=ss_ap, func=AF.Ln, scale=1.0 / D, bias=EPS),
                 reads=reads, writes=writes)
            P.op("act", lambda e: e.activation(out=out_ap, in_=out_ap, func=AF.Exp, scale=-0.5),
                 reads=writes, writes=writes)

        def norm_part1(src_ap, src_res, ts, gain, gain_res=None):
            gain_res = gain_res or R_const
            k = cnt["x"]; cnt["x"] += 1
            xb = xsb[k % 2]; Rxb = R_xsb[k % 2]
            st = stat2[k % 2]; Rst = R_stat2[k % 2]
            jk = junk2[k % 2]; Rjk = R_junk2[k % 2]
            P.op("act", lambda e: e.activation(out=jk[:ts, :], in_=src_ap, func=AF.Square,
                                               accum_out=st[:ts, 0:1]),
                 reads=[src_res], writes=[Rjk, Rst])
            rstd_from(st[:ts, 0:1], st[:ts, 1:2], ts, [Rst], [Rst])
            P.op("dve", lambda e: e.scalar_tensor_tensor(out=xb[:ts, :], in0=src_ap, scalar=st[:ts, 1:2],
                                                         in1=gain[:ts, :], op0=ALU.mult, op1=ALU.mult),
                 reads=[src_res, Rst, gain_res], writes=[Rxb])
            return k

        def norm_part2(k, ts, dst_ap3, dst_res):
            xb = xsb[k % 2]; Rxb = R_xsb[k % 2]
            tp = ptps[k % 2]; Rtp = R_tps[k % 2]
            for c in range(8):
                P.op("pe", lambda e, c=c: e.transpose(out=tp[:, c * ts:(c + 1) * ts],
                                                      in_=xb[:ts, c * 128:(c + 1) * 128], identity=ident[:ts, :ts]),
                     reads=[Rxb, R_const], writes=[Rtp])
            P.op("dve", lambda e: e.tensor_copy(out=dst_ap3,
                                                in_=tp[:, 0:8 * ts].rearrange("p (c t) -> p c t", c=8)),
                 reads=[Rtp], writes=[dst_res])

        def norm_transpose(src_ap, src_res, ts, gain, dst_ap3, dst_res, gain_res=None):
            k = norm_part1(src_ap, src_res, ts, gain, gain_res)
            norm_part2(k, ts, dst_ap3, dst_res)

        for (kind, si) in seqs:
            T = S if kind == "p" else TS
            ts = min(T, 128)
            nt = T // ts
            gs = min(T, 512)
            ng = T // gs
            tpg = gs // ts
            pos0 = 0 if kind == "p" else PAST

            with ExitStack() as l0:
                def sb0(name, shape, dt):
                    return l0.enter_context(nc.sbuf_tensor(f"{name}_{kind}{si}", list(shape), dt))

                def mk0(name, shape, dt, n):
                    return [sb0(f"{name}{i}", shape, dt) for i in range(n)]

                gpreab = sb0("gpreab", [128, D], F32); R_gab = Res()
                P.dma("sp", gpreab[:], gpre_ab, writes=[R_gab])
                xnT = sb0("xnT", [128, 8, T], BF16)
                R_xnT = [Res(f"xnT{g}") for g in range(ng)]
                wring = mk0("wr", [128, 8, 512], BF16, 2)
                R_wring = [Res(f"wr{i}") for i in range(2)]
                qTs = mk0("qT", [128, T], BF16, 2); kTs = mk0("kT", [128, T], BF16, 2); gTs = mk0("gT", [128, T], BF16, 2)
                merged = (kind == "p")
                if merged:
                    Vs = mk0("V", [128, nt, 2, 128], BF16, 2)
                else:
                    Vs = mk0("V", [128, nt, 128], BF16, 2)
                R_qs = [[Res() for _ in range(ng)] for _ in range(2)]
                R_ks = [[Res() for _ in range(ng)] for _ in range(2)]
                R_gs = [[Res() for _ in range(ng)] for _ in range(2)]
                R_Vs = [[Res() for _ in range(nt)] for _ in range(2)]
                stg = mk0("stg", [128, 256], F32, 2); R_stg = [Res() for _ in range(2)]
                biasT = sb0("biasT", [128, 8, 640], F32); R_bias = Res("bias")
                Ssb = mk0("Ssb", [128, 640], F32, 2); R_Ssb = [Res() for _ in range(2)]
                PT = mk0("PT", [128, 640], BF16, 2); R_PT = [Res() for _ in range(2)]
                rec2 = mk0("rec", [128, 128], F32, 2); R_rec2 = [Res() for _ in range(2)]
                tmpo2 = mk0("tmpo", [128, 128], F32, 2); R_tmpo2 = [Res() for _ in range(2)]
                EH = [mk0(f"E{h}_", [128, 512], F32, 2) for h in range(2)]; R_EH = [[Res() for _ in range(2)] for _ in range(2)]
                SPH = [mk0(f"SP{h}_", [128, 512], BF16, 2) for h in range(2)]; R_SPH = [[Res() for _ in range(2)] for _ in range(2)]
                ATH = [mk0(f"AT{h}_", [128, 512], BF16, 2) for h in range(2)]; R_ATH = [[Res() for _ in range(2)] for _ in range(2)]
                sgt = sb0("sgt", [128, 512], F32); R_sgt = Res()
                SaccH = mk0("Sacc", [128, 512], F32, 2); R_SaccH = [Res() for _ in range(2)]
                SaccBH = [mk0(f"SaccB{h}_", [128, 512], BF16, 2) for h in range(2)]; R_SaccBH = [[Res() for _ in range(2)] for _ in range(2)]
                nqT = mk0("nq", [128, 512], BF16, 2); R_nq = [Res() for _ in range(2)]
                if kind == "s":
                    kcache = sb0("kcache", [128, 16, 128], BF16); R_kc = Res()
                    kTcs = mk0("kTc", [128, PAST], BF16, 2); R_kTcs = [Res() for _ in range(2)]
                    Vcs = mk0("Vc", [128, 16, 128], BF16, 2); R_Vcs = [Res() for _ in range(2)]

                def load_w(pi):
                    P.dma("pool", wring[pi % 2][:], w_ab[pi].rearrange("(c p) n -> p c n", p=128),
                          writes=[R_wring[pi % 2]])

                load_w(0)
                load_w(1)
                if merged:
                    for s_ in range(2):
                        for ti_ in range(nt):
                            P.op("pool", lambda e, s_=s_, ti_=ti_: e.memset(Vs[s_][:, ti_, :, :], 1.0), writes=[R_Vs[s_][ti_]])
                if kind == "p":
                    for h in range(8):
                        P.dma("sp", biasT[:, h, :], biasP[h], writes=[R_bias])
                    for h in range(8):
                        P.op("pool", lambda e, h=h: e.memset(biasT[0:64, h, 64:128], NEG), writes=[R_bias])
                        P.op("pool", lambda e, h=h: e.memset(biasT[64:128, h, 512:576], NEG), writes=[R_bias])
                else:
                    for h in range(8):
                        P.dma("sp", biasT[:, h, 0:160], biasS[h], writes=[R_bias])

                p0flags = {}

                def gen_phase0():
                    for ti in range(nt):
                        k = cnt["x"]
                        xt = xst[k % 2]; Rxt = R_xst[k % 2]
                        P.dma("sp", xt[:ts, :], x_src(kind, si, ti * ts, ts), writes=[Rxt])
                        g = ti // tpg
                        norm_transpose(xt[:ts, :], Rxt, ts, gpreab, xnT[:, :, ti * ts:(ti + 1) * ts], R_xnT[g],
                                       gain_res=R_gab)
                        if (ti + 1) % tpg == 0:
                            p0flags[g] = True
                        yield


                PJB = 5

                def gen_proj(pi):
                    isA = pi < 4
                    hp = pi % 4
                    st = pi % 2
                    W = wring[st]; RW = R_wring[st]
                    qT, kT, gT, V = qTs[st], kTs[st], gTs[st], Vs[st]
                    R_q, R_k, R_g, R_V = R_qs[st], R_ks[st], R_gs[st], R_Vs[st]
                    if kind == "s":
                        kTc, R_kTc, Vc, R_Vc = kTcs[st], R_kTcs[st], Vcs[st], R_Vcs[st]
                        csrc_k, csrc_v, nck = (ca_k, ca_v, 4) if isA else (cb_k, cb_v, 16)
                        if pi == 0:
                            P.dma("pool", kcache[:, 0:nck, :],
                                  csrc_k[:, hp * 128:(hp + 1) * 128].rearrange("(t p) f -> p t f", p=128), writes=[R_kc])
                        P.dma("pool", Vc[:, 0:nck, :],
                              csrc_v[:, hp * 128:(hp + 1) * 128].rearrange("(t p) f -> p t f", p=128), writes=[R_Vc])
                        for t0 in range(0, nck, 8):
                            nb_ = min(8, nck - t0)
                            for t_ in range(nb_):
                                P.op("pe", lambda e, t_=t_: e.transpose(
                                    out=ptp[:, t_ * 128:(t_ + 1) * 128], in_=kcache[:, t0 + t_, :], identity=ident[:, :]),
                                    reads=[R_kc, R_const], writes=[R_tp])
                            P.op("dve", lambda e: e.tensor_copy(
                                out=kTc[:, t0 * 128:(t0 + nb_) * 128], in_=ptp[:, 0:nb_ * 128]),
                                reads=[R_tp], writes=[R_kTc])
                            yield
                        if pi + 1 < 8:
                            pn = pi + 1
                            nsrc, nn = (ca_k, 4) if pn < 4 else (cb_k, 16)
                            P.dma("pool", kcache[:, 0:nn, :],
                                  nsrc[:, (pn % 4) * 128:(pn % 4 + 1) * 128].rearrange("(t p) f -> p t f", p=128),
                                  writes=[R_kc])
                    pjbanks = [5, 6]
                    pjc = [0]

                    def nextpj():
                        b_ = pjbanks[pjc[0] % len(pjbanks)]; pjc[0] += 1
                        return b_
                    for g in range(ng):
                        t0 = g * gs
                        while pi == 0 and not p0flags.get(g):
                            yield
                        for (fc, kindf) in ((0, "q"), (2, "k"), (1, "g")):
                            bi = nextpj()
                            for kc in range(8):
                                P.op("pe", lambda e, kc=kc: e.matmul(
                                    bank(bi)[:, 0:gs], lhsT=W[:, kc, fc * 128:(fc + 1) * 128],
                                    rhs=xnT[:, kc, t0:t0 + gs], start=(kc == 0), stop=(kc == 7)),
                                    reads=[RW, R_xnT[g]], writes=[R_bank[bi]])
                            if kindf == "q":
                                P.op("dve", lambda e: e.tensor_scalar(
                                    out=qT[:, t0:t0 + gs], in0=bank(bi)[:, 0:gs], scalar1=0.125, scalar2=None,
                                    op0=ALU.mult), reads=[R_bank[bi]], writes=[R_q[g]])
                            elif kindf == "k":
                                P.op("dve", lambda e: e.tensor_copy(
                                    out=kT[:, t0:t0 + gs], in_=bank(bi)[:, 0:gs]), reads=[R_bank[bi]], writes=[R_k[g]])
                            else:
                                P.op("act", lambda e: e.activation(out=sgt[:, 0:gs], in_=bank(bi)[:, 0:gs], func=AF.Exp,
                                                                   scale=-1.0), reads=[R_bank[bi]], writes=[R_sgt])
                                P.op("act", lambda e: e.activation(out=sgt[:, 0:gs], in_=sgt[:, 0:gs], func=AF.Ln, bias=1.0),
                                     reads=[R_sgt], writes=[R_sgt])
                                P.op("act", lambda e: e.activation(out=sgt[:, 0:gs], in_=sgt[:, 0:gs], func=AF.Exp,
                                                                   scale=-1.0), reads=[R_sgt], writes=[R_sgt])
                                P.op("dve", lambda e: e.tensor_tensor(out=gT[:, t0:t0 + gs], in0=bank(bi)[:, 0:gs],
                                                                      in1=sgt[:, 0:gs], op=ALU.mult),
                                     reads=[R_bank[bi], R_sgt], writes=[R_g[g]])
                            yield
                        for tt in range(tpg):
                            ti = g * tpg + tt
                            bi = nextpj()
                            for kc in range(8):
                                P.op("pe", lambda e, kc=kc: e.matmul(
                                    bank(bi)[:ts, 0:256], lhsT=xnT[:, kc, ti * ts:(ti + 1) * ts],
                                    rhs=W[:, kc, 256:512], start=(kc == 0), stop=(kc == 7)),
                                    reads=[RW, R_xnT[g]], writes=[R_bank[bi]])
                            if merged:
                                P.op("dve", lambda e: e.tensor_copy(
                                    out=V[:ts, ti, 0, 0:64], in_=bank(bi)[:ts, 128:192]), reads=[R_bank[bi]], writes=[R_V[ti]])
                                P.op("dve", lambda e: e.tensor_copy(
                                    out=V[:ts, ti, 1, 64:128], in_=bank(bi)[:ts, 192:256]), reads=[R_bank[bi]], writes=[R_V[ti]])
                            else:
                                P.op("dve", lambda e: e.tensor_copy(
                                    out=V[:ts, ti, :], in_=bank(bi)[:ts, 128:256]), reads=[R_bank[bi]], writes=[R_V[ti]])
                            if kind == "p":
                                if isA:
                                    need = ti * ts >= S - 512
                                    dk, dv, r0 = o_akp, o_avp, ti * ts - (S - 512)
                                else:
                                    need = True
                                    dk, dv, r0 = o_bkp, o_bvp, ti * ts
                                dk_ap = dk[si, r0:r0 + ts, hp * 128:(hp + 1) * 128] if need else None
                                dv_ap = dv[si, r0:r0 + ts, hp * 128:(hp + 1) * 128] if need else None
                            else:
                                need = True
                                dk, dv = (o_aks, o_avs) if isA else (o_bks, o_bvs)
                                dk_ap = dk[0:ts, hp * 128:(hp + 1) * 128]
                                dv_ap = dv[0:ts, hp * 128:(hp + 1) * 128]
                            if need:
                                sk = cnt["stg"] % 2; cnt["stg"] += 1
                                P.op("dve", lambda e: e.tensor_copy(
                                    out=stg[sk][:ts, :], in_=bank(bi)[:ts, 0:256]),
                                    reads=[R_bank[bi]], writes=[R_stg[sk]])
                                P.dma("pool", dk_ap, stg[sk][:ts, 0:128], reads=[R_stg[sk]], is_output=True)
                                P.dma("pool", dv_ap, stg[sk][:ts, 128:256], reads=[R_stg[sk]], is_output=True)
                            yield

                def gen_attnA(pi):
                    hp = pi % 4
                    st = pi % 2
                    qT, kT, gT, V = qTs[st], kTs[st], gTs[st], Vs[st]
                    R_q, R_k, R_g, R_V = R_qs[st], R_ks[st], R_gs[st], R_Vs[st]
                    if kind == "s":
                        kTc, R_kTc, Vc, R_Vc = kTcs[st], R_kTcs[st], Vcs[st], R_Vcs[st]
                    nqb = T // ts
                    qw = ts
                    its = [(j, hh) for j in range(nqb) for hh in range(2)]

                    def a_blocks(j, hh):
                        po = hh * 64
                        blocks = []
                        if kind == "p":
                            for slot in range(5):
                                kb = j - 4 + slot
                                if kb < 0:
                                    continue
                                blocks.append((kT[po:po + 64, kb * 128:(kb + 1) * 128], 128,
                                               V[:, kb, hh, :], slot,
                                               [R_k[kb // 4]], [R_V[kb]]))
                        else:
                            for slot in range(4):
                                blocks.append((kTc[po:po + 64, slot * 128:(slot + 1) * 128], 128,
                                               Vc[:, slot, hh * 64:(hh + 1) * 64], slot, [R_kTc], [R_Vc]))
                            blocks.append((kT[po:po + 64, 0:TS], TS, V[0:TS, 0, hh * 64:(hh + 1) * 64], 4,
                                           [R_k[0]], [R_V[0]]))
                        return blocks

                    def a_stage1(n):
                        j, hh = its[n]
                        h = hp * 2 + hh
                        po = hh * 64
                        sbk = n % 2
                        RS = [R_bank[2 * sbk], R_bank[2 * sbk + 1]]
                        Sps = pbank[sbk]
                        blocks = a_blocks(j, hh)
                        q_ap = qT[po:po + 64, j * qw:(j + 1) * qw]
                        gq = (j * qw) // gs
                        for (k_ap, nk, v_ap, slot, rk, rv) in blocks:
                            P.op("pe", lambda e, k_ap=k_ap, nk=nk, slot=slot: e.matmul(
                                Sps[:nk, slot * qw:(slot + 1) * qw], lhsT=k_ap, rhs=q_ap, start=True, stop=True),
                                reads=rk + [R_q[gq]], writes=RS)
                        s0 = blocks[0][3]
                        full = [b for b in blocks if b[1] == 128]
                        part = [b for b in blocks if b[1] != 128]
                        sk = n % 2
                        lo, hi = s0 * qw, (full[-1][3] + 1) * qw
                        P.op("dve", lambda e: e.tensor_tensor(
                            out=Ssb[sk][:, lo:hi], in0=Sps[:, lo:hi], in1=biasT[:, h, lo:hi], op=ALU.add),
                            reads=RS + [R_bias], writes=[R_Ssb[sk]])
                        P.op("act", lambda e: e.activation(
                            out=PT[sk][:, lo:hi], in_=Ssb[sk][:, lo:hi], func=AF.Exp),
                            reads=[R_Ssb[sk]], writes=[R_PT[sk]])
                        for (k_ap, nk, v_ap, slot, rk, rv) in part:
                            lo2, hi2 = slot * qw, (slot + 1) * qw
                            P.op("dve", lambda e, lo2=lo2, hi2=hi2, nk=nk: e.tensor_tensor(
                                out=Ssb[sk][:nk, lo2:hi2], in0=Sps[:nk, lo2:hi2], in1=biasT[:nk, h, lo2:hi2],
                                op=ALU.add), reads=RS + [R_bias], writes=[R_Ssb[sk]])
                            P.op("act", lambda e, lo2=lo2, hi2=hi2, nk=nk: e.activation(
                                out=PT[sk][:nk, lo2:hi2], in_=Ssb[sk][:nk, lo2:hi2], func=AF.Exp),
                                reads=[R_Ssb[sk]], writes=[R_PT[sk]])

                    def a_stage2m(n):
                        j, hh = its[n]
                        po = hh * 64
                        pd = 64 - po
                        sk = n % 2
                        odb = 4
                        OD = bank(4)[:, (n % 2) * 128:(n % 2) * 128 + 128]
                        blocks = a_blocks(j, hh)
                        nb = len(blocks)
                        gq = (j * qw) // gs
                        for bi_, (k_ap, nk, v_ap, slot, rk, rv) in enumerate(blocks):
                            P.op("pe", lambda e, v_ap=v_ap, nk=nk, slot=slot, bi_=bi_: e.matmul(
                                OD[:, 0:qw], lhsT=v_ap, rhs=PT[sk][:nk, slot * qw:(slot + 1) * qw],
                                start=(bi_ == 0), stop=(bi_ == nb - 1)),
                                reads=rv + [R_PT[sk]], writes=[R_bank[odb]])
                        rc = rec2[hh]; Rrc = R_rec2[hh]
                        tm = tmpo2[hh]; Rtm = R_tmpo2[hh]
                        P.op("act", lambda e: e.activation(
                            out=rc[po:po + 64, 0:qw], in_=OD[pd:pd + 64, 0:qw], func=AF.Ln),
                            reads=[R_bank[odb]], writes=[Rrc])
                        P.op("act", lambda e: e.activation(
                            out=rc[po:po + 64, 0:qw], in_=rc[po:po + 64, 0:qw], func=AF.Exp, scale=-1.0),
                            reads=[Rrc], writes=[Rrc])
                        P.op("dve", lambda e: e.tensor_tensor(
                            out=tm[po:po + 64, 0:qw], in0=OD[po:po + 64, 0:qw], in1=rc[po:po + 64, 0:qw],
                            op=ALU.mult), reads=[R_bank[odb], Rrc], writes=[Rtm])
                        P.op("pool", lambda e: e.tensor_tensor(
                            out=oT[po:po + 64, pi, j * qw:(j + 1) * qw], in0=tm[po:po + 64, 0:qw],
                            in1=gT[po:po + 64, j * qw:(j + 1) * qw], op=ALU.mult),
                            reads=[Rtm, R_g[gq]], writes=[R_oT[pi][gq]])

                    def a_stage2(n):
                        if merged:
                            return a_stage2m(n)
                        j, hh = its[n]
                        po = hh * 64
                        sk = n % 2
                        odb = 4
                        OD = bank(odb)
                        blocks = a_blocks(j, hh)
                        nb = len(blocks)
                        gq = (j * qw) // gs
                        for bi_, (k_ap, nk, v_ap, slot, rk, rv) in enumerate(blocks):
                            P.op("pe", lambda e, v_ap=v_ap, nk=nk, slot=slot, bi_=bi_: e.matmul(
                                OD[po:po + 64, 0:qw], lhsT=v_ap, rhs=PT[sk][:nk, slot * qw:(slot + 1) * qw],
                                start=(bi_ == 0), stop=(bi_ == nb - 1)),
                                reads=rv + [R_PT[sk]], writes=[R_bank[odb]])
                        for bi_, (k_ap, nk, v_ap, slot, rk, rv) in enumerate(blocks):
                            P.op("pe", lambda e, nk=nk, slot=slot, bi_=bi_: e.matmul(
                                OD[po:po + 64, 128:128 + qw], lhsT=ones[:nk, 0:64],
                                rhs=PT[sk][:nk, slot * qw:(slot + 1) * qw],
                                start=(bi_ == 0), stop=(bi_ == nb - 1)),
                                reads=[R_PT[sk], R_const], writes=[R_bank[odb]])
                        rc = rec2[hh]; Rrc = R_rec2[hh]
                        tm = tmpo2[hh]; Rtm = R_tmpo2[hh]
                        P.op("act", lambda e: e.activation(
                            out=rc[po:po + 64, 0:qw], in_=OD[po:po + 64, 128:128 + qw], func=AF.Ln),
                            reads=[R_bank[odb]], writes=[Rrc])
                        P.op("act", lambda e: e.activation(
                            out=rc[po:po + 64, 0:qw], in_=rc[po:po + 64, 0:qw], func=AF.Exp, scale=-1.0),
                            reads=[Rrc], writes=[Rrc])
                        P.op("dve", lambda e: e.tensor_tensor(
                            out=tm[po:po + 64, 0:qw], in0=OD[po:po + 64, 0:qw], in1=rc[po:po + 64, 0:qw],
                            op=ALU.mult), reads=[R_bank[odb], Rrc], writes=[Rtm])
                        P.op("pool", lambda e: e.tensor_tensor(
                            out=oT[po:po + 64, pi, j * qw:(j + 1) * qw], in0=tm[po:po + 64, 0:qw],
                            in1=gT[po:po + 64, j * qw:(j + 1) * qw], op=ALU.mult),
                            reads=[Rtm, R_g[gq]], writes=[R_oT[pi][gq]])

                    a_stage1(0)
                    yield
                    for n in range(len(its)):
                        if n + 1 < len(its):
                            a_stage1(n + 1)
                            yield
                        a_stage2(n)
                        yield

                def gen_attnB(pi):
                    st = pi % 2
                    qT, kT, gT, V = qTs[st], kTs[st], gTs[st], Vs[st]
                    R_q, R_k, R_g, R_V = R_qs[st], R_ks[st], R_gs[st], R_Vs[st]
                    if kind == "s":
                        kTc, R_kTc, Vc, R_Vc = kTcs[st], R_kTcs[st], Vcs[st], R_Vcs[st]
                    cw = gs
                    ob = 4
                    OB = bank(ob)
                    dq = min(128, cw)
                    for c in range(ng):
                        steps = [[], []]
                        for hh in range(2):
                            po = hh * 64
                            P.op("dve", lambda e: e.tensor_scalar(
                                out=nqT[hh][po:po + 64, 0:cw], in0=qT[po:po + 64, c * cw:(c + 1) * cw],
                                scalar1=-1.0, scalar2=None, op0=ALU.mult), reads=[R_q[c]], writes=[R_nq[hh]])
                            P.op("pool", lambda e: e.memset(SaccH[hh][:, 0:cw], 0.0), writes=[R_SaccH[hh]])
                            P.op("pool", lambda e: e.memset(SaccBH[hh][0][:, 0:cw], 0.0), writes=[R_SaccBH[hh][0]])
                            if kind == "p":
                                for kb in range(4 * c + 3, -1, -1):
                                    q0 = max(0, kb * 128 - c * 512)
                                    steps[hh].append((kT[po:po + 64, kb * 128:(kb + 1) * 128], 128,
                                                      V[:, kb, hh, hh * 64:(hh + 1) * 64], q0, kb >= 4 * c,
                                                      [R_k[kb // 4]], [R_V[kb]]))
                            else:
                                steps[hh].append((kT[po:po + 64, 0:TS], TS, V[0:TS, 0, hh * 64:(hh + 1) * 64], 0, True,
                                                  [R_k[0]], [R_V[0]]))
                                for kb in range(15, -1, -1):
                                    steps[hh].append((kTc[po:po + 64, kb * 128:(kb + 1) * 128], 128,
                                                      Vc[:, kb, hh * 64:(hh + 1) * 64], 0, False, [R_kTc], [R_Vc]))
                        ns = len(steps[0])

                        def stage1(i):
                            for hh in range(2):
                                po = hh * 64
                                k_ap, nk, v_ap, q0, diag, rk, rv = steps[hh][i]
                                z = bank(hh); Rz = R_bank[hh]
                                P.op("pe", lambda e: e.matmul(z[:nk, q0:cw], lhsT=k_ap,
                                                              rhs=qT[po:po + 64, c * cw + q0:(c + 1) * cw],
                                                              start=True, stop=True),
                                     reads=rk + [R_q[c]], writes=[Rz])
                            for hh in range(2):
                                k_ap, nk, v_ap, q0, diag, rk, rv = steps[hh][i]
                                z = bank(hh); Rz = R_bank[hh]
                                E = EH[hh][i % 2]; RE = R_EH[hh][i % 2]
                                P.op("act", lambda e: e.activation(out=E[:nk, q0:cw], in_=z[:nk, q0:cw], func=AF.Exp),
                                     reads=[Rz], writes=[RE])
                                if diag:
                                    P.op("dve", lambda e: e.tensor_tensor(
                                        out=E[:nk, q0:q0 + dq], in0=E[:nk, q0:q0 + dq], in1=lmask[:nk, 0:dq],
                                        op=ALU.mult), reads=[RE, R_const], writes=[RE])
                            for hh in range(2):
                                k_ap, nk, v_ap, q0, diag, rk, rv = steps[hh][i]
                                E = EH[hh][i % 2]; RE = R_EH[hh][i % 2]
                                SP = SPH[hh][i % 2]; RSP = R_SPH[hh][i % 2]
                                P.op("act", lambda e: e.activation(out=SP[:nk, q0:cw], in_=E[:nk, q0:cw], func=AF.Ln,
                                                                   bias=1.0), reads=[RE], writes=[RSP])

                        def stage2(i):
                            for hh in range(2):
                                k_ap, nk, v_ap, q0, diag, rk, rv = steps[hh][i]
                                cps = bank(2 + hh); Rc = R_bank[2 + hh]
                                SP = SPH[hh][i % 2]; RSP = R_SPH[hh][i % 2]
                                P.op("pe", lambda e: e.matmul(cps[:nk, q0:cw], lhsT=tri[:nk, :nk], rhs=SP[:nk, q0:cw],
                                                              start=True, stop=False),
                                     reads=[RSP, R_const], writes=[Rc])
                                P.op("pe", lambda e: e.matmul(cps[:nk, q0:cw], lhsT=ones[:, :nk],
                                                              rhs=SaccBH[hh][i % 2][:, q0:cw], start=False, stop=False),
                                     reads=[R_SaccBH[hh][i % 2], R_const], writes=[Rc])
                            for hh in range(2):
                                po = hh * 64
                                k_ap, nk, v_ap, q0, diag, rk, rv = steps[hh][i]
                                cps = bank(2 + hh); Rc = R_bank[2 + hh]
                                P.op("pe", lambda e: e.matmul(cps[:nk, q0:cw], lhsT=k_ap,
                                                              rhs=nqT[hh][po:po + 64, q0:cw], start=False, stop=True),
                                     reads=rk + [R_nq[hh]], writes=[Rc])
                            if i + 1 < ns:
                                for hh in range(2):
                                    k_ap, nk, v_ap, q0, diag, rk, rv = steps[hh][i]
                                    SP = SPH[hh][i % 2]; RSP = R_SPH[hh][i % 2]
                                    P.op("dve", lambda e: e.tensor_tensor(
                                        out=SaccH[hh][:nk, q0:cw], in0=SaccH[hh][:nk, q0:cw], in1=SP[:nk, q0:cw], op=ALU.add),
                                        reads=[RSP, R_SaccH[hh]], writes=[R_SaccH[hh]])
                                    P.op("dve", lambda e: e.tensor_copy(out=SaccBH[hh][(i + 1) % 2][:, 0:cw],
                                                                        in_=SaccH[hh][:, 0:cw]),
                                         reads=[R_SaccH[hh]], writes=[R_SaccBH[hh][(i + 1) % 2]])
                            for hh in range(2):
                                k_ap, nk, v_ap, q0, diag, rk, rv = steps[hh][i]
                                cps = bank(2 + hh); Rc = R_bank[2 + hh]
                                AT = ATH[hh][i % 2]; RAT = R_ATH[hh][i % 2]
                                P.op("act", lambda e: e.activation(out=AT[:nk, q0:cw], in_=cps[:nk, q0:cw], func=AF.Exp,
                                                                   scale=-1.0), reads=[Rc], writes=[RAT])
                                if q0 > 0:
                                    P.op("pool", lambda e: e.memset(AT[:nk, 0:q0], 0.0), writes=[RAT])
                                if diag:
                                    P.op("dve", lambda e: e.tensor_tensor(
                                        out=AT[:nk, q0:q0 + dq], in0=AT[:nk, q0:q0 + dq], in1=lmask[:nk, 0:dq],
                                        op=ALU.mult), reads=[RAT, R_const], writes=[RAT])

                        def stage3(i):
                            for hh in range(2):
                                po = hh * 64
                                k_ap, nk, v_ap, q0, diag, rk, rv = steps[hh][i]
                                AT = ATH[hh][i % 2]; RAT = R_ATH[hh][i % 2]
                                P.op("pe", lambda e: e.matmul(OB[po:po + 64, 0:cw], lhsT=v_ap, rhs=AT[:nk, 0:cw],
                                                              start=(i == 0), stop=(i == ns - 1)),
                                     reads=rv + [RAT], writes=[R_bank[ob]])

                        stage1(0)
                        yield
                        for i in range(ns):
                            if i + 1 < ns:
                                stage1(i + 1)
                            stage2(i)
                            if i > 0:
                                stage3(i - 1)
                            yield
                        stage3(ns - 1)
                        P.op("dve", lambda e: e.tensor_tensor(
                            out=oT[:, pi, c * cw:(c + 1) * cw], in0=OB[:, 0:cw],
                            in1=gT[:, c * cw:(c + 1) * cw], op=ALU.mult),
                            reads=[R_bank[ob], R_g[c]], writes=[R_oT[pi][c]])
                        yield

                def run_weighted(ga, na, gb, nb_):
                    da = db = 0
                    a_alive, b_alive = True, gb is not None
                    while a_alive or b_alive:
                        pick_b = b_alive and (not a_alive or (db + 1) * na <= (da + 1) * nb_)
                        if pick_b:
                            try:
                                next(gb); db += 1
                            except StopIteration:
                                b_alive = False
                        else:
                            try:
                                next(ga); da += 1
                            except StopIteration:
                                a_alive = False

                n_proj = ng * (3 + tpg) + (3 if kind == "s" else 0)
                gp0, gj0 = gen_phase0(), gen_proj(0)
                alive = [gp0, gj0]
                while alive:
                    for s_ in list(alive):
                        try:
                            next(s_)
                        except StopIteration:
                            alive.remove(s_)
                for pi in range(8):
                    if pi + 2 < 8:
                        load_w(pi + 2)
                    isA = pi < 4
                    if isA:
                        ga = gen_attnA(pi); na = 2 * (T // ts) * 2
                    else:
                        ga = gen_attnB(pi)
                        na = sum((4 * c + 4 + 2) for c in range(ng)) if kind == "p" else 19
                    gb = gen_proj(pi + 1) if pi + 1 < 8 else None
                    run_weighted(ga, na, gb, n_proj)
            P.barrier()

            with ExitStack() as l1:
                def sb1(name, shape, dt):
                    return l1.enter_context(nc.sbuf_tensor(f"{name}_{kind}{si}", list(shape), dt))

                gs1 = min(T, 256)
                ng1 = T // gs1
                tpg1 = gs1 // ts
                qw = ts
                woab = sb1("woab", [128, 8, D], BF16); R_woab = Res()
                wc = sb1("wc", [128, 8, 2560], BF16); R_wcq = [Res() for _ in range(4)]
                woc = sb1("woc", [128, 8, D], BF16); R_woc = Res()
                gpostab = sb1("gpostab", [128, D], F32); gprec = sb1("gprec", [128, D], F32)
                gpostc = sb1("gpostc", [128, D], F32); R_gn = Res()
                for t_, src in ((gpostab, gpost_ab), (gprec, gpre_c), (gpostc, gpost_c)):
                    P.dma("sp", t_[:], src, writes=[R_gn])
                P.dma("pool", woab[:], w_oab.rearrange("(c p) n -> p c n", p=128), writes=[R_woab])
                for q4 in range(4):
                    P.dma("pool", wc[:, :, q4 * 640:(q4 + 1) * 640],
                          w_c[:, q4 * 640:(q4 + 1) * 640].rearrange("(c p) n -> p c n", p=128), writes=[R_wcq[q4]])
                P.dma("pool", woc[:], w_oc.rearrange("(c p) n -> p c n", p=128), writes=[R_woc])

                def mk(name, shape, dt, n):
                    return [sb1(f"{name}{i}", shape, dt) for i in range(n)], [Res() for _ in range(n)]

                Y0, R_Y0 = mk("Y0", [128, tpg1, D], F32, 2)
                t1b, R_t1 = mk("t1b", [128, D], F32, 2)
                statY, R_statY = mk("statY", [128, 4], F32, 2)
                xn1T, R_xn1 = mk("xn1T", [128, 8, gs1], BF16, 1)
                xn1T, R_xn1 = xn1T * 2, R_xn1 * 2
                qbf, R_qbf = mk("qbf", [128, gs1], BF16, 2)
                qr, R_qr = mk("qr", [128, 8, gs1], BF16, 2)
                kbf, R_kbf = mk("kbf", [128, 2, gs1], BF16, 1)
                kr, R_kr = mk("kr", [128, 4, 128 + gs1], BF16, 2)
                g1, R_g1 = mk("g1", [128, 8, gs1], BF16, 2)
                V1, R_V1 = mk("V1", [128, 1 + tpg1, 256], BF16, 2)
                cosg, R_cos = mk("cosg", [128, gs1], F32, 1)
                sing, R_sin = mk("sing", [128, gs1], F32, 1)
                cosg, R_cos, sing, R_sin = cosg * 2, R_cos * 2, sing * 2, R_sin * 2
                ta, R_ta = mk("ta", [128, 256], F32, 2)
                tb, R_tb = mk("tb", [128, 256], F32, 2)
                tcb, R_tcb = mk("tcb", [128, 256], F32, 2)
                PTc, R_PTc = mk("PTc", [128, 512], BF16, 4)
                recc, R_recc = mk("recc", [128, 256], F32, 1)
                tmpc, R_tmpc = mk("tmpc", [128, 256], F32, 1)
                recc, R_recc, tmpc, R_tmpc = recc * 2, R_recc * 2, tmpc * 2, R_tmpc * 2
                kst = sb1("kst", [128, 256], F32); R_kst = Res()
                ksw = sb1("ksw", [128, 256], F32); R_ksw = Res()
                ctm = sb1("ctm", [128, 256], F32); stm = sb1("stm", [128, 256], F32); R_ctm = Res()
                kvst = sb1("kvst", [128, 512], F32); R_kvst = Res()
                if kind == "s":
                    kcc = sb1("kcc", [128, 256], BF16); R_kcc = Res()
                    kcd = sb1("kcd", [128, 4, 128], BF16); R_kcd = Res()
                    krc = sb1("krc", [128, 4, 128], BF16); R_krc = Res()
                    Vcc = sb1("Vcc", [128, 256], BF16); R_Vcc = Res()
                    P.dma("pool", kcc[:], cc_k, writes=[R_kcc])
                    P.dma("pool", Vcc[:], cc_v, writes=[R_Vcc])
                    for a in range(4):
                        for d2 in range(2):
                            P.op("pool", lambda e, a=a, d2=d2: e.tensor_copy(
                                out=kcd[:, a, d2 * 64:(d2 + 1) * 64], in_=kcc[:, a * 64:(a + 1) * 64]),
                                reads=[R_kcc], writes=[R_kcd])
                    for a in range(4):
                        P.op("pe", lambda e, a=a: e.transpose(out=ptp[:, a * 128:(a + 1) * 128], in_=kcd[:, a, :],
                                                              identity=ident[:, :]),
                             reads=[R_kcd, R_const], writes=[R_tp])
                    for a in range(4):
                        P.op("dve", lambda e, a=a: e.tensor_copy(out=krc[:, a, :], in_=ptp[:, a * 128:(a + 1) * 128]),
                             reads=[R_tp], writes=[R_krc])
                lt0 = pos0 + T - ts
                P.dma("sp", ctm[:ts, :], c_cosTM[lt0:lt0 + ts, :], writes=[R_ctm])
                P.dma("sp", stm[:ts, :], c_sinTM[lt0:lt0 + ts, :], writes=[R_ctm])

                yc = {"n": 0, "bx": 0, "t": 0}
                LAG_A = 1
                XB = [0, 1, 2, 6]
                YB = [3, 4, 5]

                def nbx():
                    b_ = XB[yc["bx"] % 4]; yc["bx"] += 1
                    return b_

                def post_norm_residual(bk0, bk1, gain, res_ap, res_r, out_ap, out_r):
                    k = yc["t"]; yc["t"] += 1
                    st = statY[k % 2]; Rst = R_statY[k % 2]
                    jk = junk2[k % 2]; Rjk = R_junk2[k % 2]
                    for half, bk in enumerate((bk0, bk1)):
                        P.op("act", lambda e, half=half, bk=bk: e.activation(
                            out=jk[:ts, half * 512:(half + 1) * 512], in_=bank(bk)[:ts, :], func=AF.Square,
                            accum_out=st[:ts, half:half + 1]), reads=[R_bank[bk]], writes=[Rjk, Rst])
                    P.op("dve", lambda e: e.tensor_tensor(out=st[:ts, 2:3], in0=st[:ts, 0:1], in1=st[:ts, 1:2],
                                                          op=ALU.add), reads=[Rst], writes=[Rst])
                    rstd_from(st[:ts, 2:3], st[:ts, 3:4], ts, [Rst], [Rst])
                    for half, bk in enumerate((bk0, bk1)):
                        P.op("dve", lambda e, half=half, bk=bk: e.scalar_tensor_tensor(
                            out=out_ap[:, half * 512:(half + 1) * 512], in0=bank(bk)[:ts, :], scalar=st[:ts, 3:4],
                            in1=gain[:ts, half * 512:(half + 1) * 512], op0=ALU.mult, op1=ALU.mult),
                            reads=[R_bank[bk], Rst, R_gn], writes=[out_r])
                    P.op("pool", lambda e: e.tensor_tensor(out=out_ap, in0=out_ap, in1=res_ap, op=ALU.add),
                         reads=[res_r, out_r], writes=[out_r])

                def gen_a(g):
                    gb = g % 2
                    t0 = g * gs1
                    g0 = t0 // gs
                    P.dma("sp", cosg[gb][:, :], c_cosT[:, pos0 + t0:pos0 + t0 + gs1], writes=[R_cos[gb]])
                    P.dma("sp", sing[gb][:, :], c_sinT[:, pos0 + t0:pos0 + t0 + gs1], writes=[R_sin[gb]])
                    pend = []
                    for tt in range(tpg1):
                        ti = g * tpg1 + tt
                        k = cnt["x"]
                        xt = xst[k % 2]; Rxt = R_xst[k % 2]
                        P.dma("sp", xt[:ts, :], x_src(kind, si, ti * ts, ts), writes=[Rxt])
                        bks = (nbx(), nbx())
                        for half in range(2):
                            for c in range(8):
                                P.op("pe", lambda e, c=c, half=half: e.matmul(
                                    bank(bks[half])[:ts, :], lhsT=oT[:, c, ti * ts:(ti + 1) * ts],
                                    rhs=woab[:, c, half * 512:(half + 1) * 512], start=(c == 0), stop=(c == 7)),
                                    reads=[R_oT[c][g0], R_woab], writes=[R_bank[bks[half]]])
                        yield
                        post_norm_residual(bks[0], bks[1], gpostab, xt[:ts, :], Rxt, Y0[gb][:ts, tt, :], R_Y0[gb])
                        kk = norm_part1(Y0[gb][:ts, tt, :], R_Y0[gb], ts, gprec, gain_res=R_gn)
                        pend.append((kk, tt))
                        yield
                    for (kk, tt) in pend:
                        norm_part2(kk, ts, xn1T[gb][:, :, tt * ts:(tt + 1) * ts], R_xn1[gb])
                        yield

                def gen_b(g):
                    gb = g % 2
                    xn = xn1T[gb]; Rxn = R_xn1[gb]
                    cs, sn = cosg[gb], sing[gb]
                    if g > 0:
                        P.op("pool", lambda e: e.tensor_copy(out=kr[gb][:, :, 0:128], in_=kr[1 - gb][:, :, gs1:gs1 + 128]),
                             reads=[R_kr[1 - gb]], writes=[R_kr[gb]])
                        P.op("pool", lambda e: e.tensor_copy(out=V1[gb][:, 0, :], in_=V1[1 - gb][:, tpg1, :]),
                             reads=[R_V1[1 - gb]], writes=[R_V1[gb]])
                    def q_part_a(fc):
                        b1 = nbx()
                        for kc in range(8):
                            P.op("pe", lambda e, kc=kc: e.matmul(
                                bank(b1)[:, 0:gs1], lhsT=wc[:, kc, fc * 128:(fc + 1) * 128], rhs=xn[:, kc, :],
                                start=(kc == 0), stop=(kc == 7)), reads=[R_wcq[(fc * 128) // 640], Rxn], writes=[R_bank[b1]])
                        s2 = fc % 2
                        P.op("act", lambda e: e.activation(out=qbf[s2][:, :], in_=bank(b1)[:, 0:gs1], func=AF.Copy, scale=0.125),
                             reads=[R_bank[b1]], writes=[R_qbf[s2]])
                        P.op("dve", lambda e: e.scalar_tensor_tensor(
                            out=ta[s2][:, 0:gs1], in0=bank(b1)[:, 0:gs1], scalar=0.125, in1=cs[:, :], op0=ALU.mult,
                            op1=ALU.mult), reads=[R_bank[b1], R_cos[gb]], writes=[R_ta[s2]])

                    def q_part_b(fc):
                        s2 = fc % 2
                        b2 = nbx()
                        P.op("pe", lambda e: e.matmul(bank(b2)[:, 0:gs1], lhsT=rot[:, :], rhs=qbf[s2][:, :], start=True, stop=True),
                             reads=[R_qbf[s2], R_const], writes=[R_bank[b2]])
                        P.op("dve", lambda e: e.tensor_tensor(out=tb[s2][:, 0:gs1], in0=bank(b2)[:, 0:gs1], in1=sn[:, :],
                                                              op=ALU.mult), reads=[R_bank[b2], R_sin[gb]], writes=[R_tb[s2]])
                        P.op("pool", lambda e: e.tensor_tensor(out=qr[gb][:, fc, :], in0=ta[s2][:, 0:gs1], in1=tb[s2][:, 0:gs1],
                                                               op=ALU.add), reads=[R_ta[s2], R_tb[s2]], writes=[R_qr[gb]])

                    q_part_a(0)
                    yield
                    for fc in range(1, 8):
                        q_part_a(fc)
                        q_part_b(fc - 1)
                        yield
                    q_part_b(7)
                    yield

                def gen_b2(g):
                    gb = g % 2
                    xn = xn1T[gb]; Rxn = R_xn1[gb]
                    cs, sn = cosg[gb], sing[gb]
                    for kc2 in range(2):
                        b1 = nbx()
                        for kc in range(8):
                            P.op("pe", lambda e, kc=kc: e.matmul(
                                bank(b1)[:, 0:gs1], lhsT=wc[:, kc, 1024 + kc2 * 128:1024 + (kc2 + 1) * 128],
                                rhs=xn[:, kc, :], start=(kc == 0), stop=(kc == 7)),
                                reads=[R_wcq[1], Rxn], writes=[R_bank[b1]])
                        P.op("act", lambda e: e.activation(out=kbf[0][:, kc2, :], in_=bank(b1)[:, 0:gs1], func=AF.Copy),
                             reads=[R_bank[b1]], writes=[R_kbf[0]])
                    yield
                    for a in range(4):
                        s2 = a % 2
                        b1 = nbx()
                        P.op("pe", lambda e: e.matmul(bank(b1)[:, 0:gs1], lhsT=dsel[:, a % 2, :], rhs=kbf[0][:, a // 2, :],
                                                      start=True, stop=True),
                             reads=[R_kbf[0], R_const], writes=[R_bank[b1]])
                        b2 = nbx()
                        P.op("pe", lambda e: e.matmul(bank(b2)[:, 0:gs1], lhsT=dselrot[:, a % 2, :], rhs=kbf[0][:, a // 2, :],
                                                      start=True, stop=True),
                             reads=[R_kbf[0], R_const], writes=[R_bank[b2]])
                        P.op("dve", lambda e: e.tensor_tensor(out=ta[s2][:, 0:gs1], in0=bank(b1)[:, 0:gs1], in1=cs[:, :],
                                                              op=ALU.mult), reads=[R_bank[b1], R_cos[gb]], writes=[R_ta[s2]])
                        P.op("dve", lambda e: e.tensor_tensor(out=tb[s2][:, 0:gs1], in0=bank(b2)[:, 0:gs1], in1=sn[:, :],
                                                              op=ALU.mult), reads=[R_bank[b2], R_sin[gb]], writes=[R_tb[s2]])
                        P.op("pool", lambda e: e.tensor_tensor(
                            out=kr[gb][:, a, 128:128 + gs1], in0=ta[s2][:, 0:gs1], in1=tb[s2][:, 0:gs1], op=ALU.add),
                            reads=[R_ta[s2], R_tb[s2]], writes=[R_kr[gb]])
                        yield
                    for fc in range(8):
                        b1 = nbx()
                        for kc in range(8):
                            P.op("pe", lambda e, kc=kc: e.matmul(
                                bank(b1)[:, 0:gs1], lhsT=wc[:, kc, 1536 + fc * 128:1536 + (fc + 1) * 128],
                                rhs=xn[:, kc, :], start=(kc == 0), stop=(kc == 7)),
                                reads=[R_wcq[(1536 + fc * 128) // 640], Rxn], writes=[R_bank[b1]])
                        s2 = fc % 2
                        P.op("act", lambda e: e.activation(out=tcb[s2][:, 0:gs1], in_=bank(b1)[:, 0:gs1], func=AF.Exp,
                                                           scale=-1.0), reads=[R_bank[b1]], writes=[R_tcb[s2]])
                        P.op("act", lambda e: e.activation(out=tcb[s2][:, 0:gs1], in_=tcb[s2][:, 0:gs1], func=AF.Ln, bias=1.0),
                             reads=[R_tcb[s2]], writes=[R_tcb[s2]])
                        P.op("act", lambda e: e.activation(out=tcb[s2][:, 0:gs1], in_=tcb[s2][:, 0:gs1], func=AF.Exp,
                                                           scale=-1.0), reads=[R_tcb[s2]], writes=[R_tcb[s2]])
                        P.op("dve", lambda e: e.tensor_tensor(out=g1[gb][:, fc, :], in0=bank(b1)[:, 0:gs1],
                                                              in1=tcb[s2][:, 0:gs1], op=ALU.mult),
                             reads=[R_bank[b1], R_tcb[s2]], writes=[R_g1[gb]])
                        yield
                    for tt in range(tpg1):
                        ti = g * tpg1 + tt
                        b1 = nbx()
                        for kc in range(8):
                            P.op("pe", lambda e, kc=kc: e.matmul(
                                bank(b1)[:ts, :], lhsT=xn[:, kc, tt * ts:(tt + 1) * ts], rhs=wc[:, kc, 1024:1536],
                                start=(kc == 0), stop=(kc == 7)), reads=[R_wcq[1], R_wcq[2], Rxn], writes=[R_bank[b1]])
                        P.op("dve", lambda e: e.tensor_copy(out=V1[gb][:ts, 1 + tt, :], in_=bank(b1)[:ts, 256:512]),
                             reads=[R_bank[b1]], writes=[R_V1[gb]])
                        if ti == nt - 1:
                            ysk = kvst; Rysk = R_kvst
                            P.op("dve", lambda e: e.tensor_copy(out=ysk[:ts, 0:256], in_=bank(b1)[:ts, 256:512]),
                                 reads=[R_bank[b1]], writes=[Rysk])
                            dv_ap = o_cvp[si, :, :] if kind == "p" else o_cvs[:, :]
                            dk_ap = o_ckp[si, :, :] if kind == "p" else o_cks[:, :]
                            P.dma("pool", dv_ap, ysk[:ts, 0:256], reads=[Rysk], is_output=True)
                            P.op("dve", lambda e: e.tensor_copy(out=kst[:ts, :], in_=bank(b1)[:ts, 0:256]),
                                 reads=[R_bank[b1]], writes=[R_kst])
                            for hk in range(4):
                                for b2_ in range(2):
                                    P.op("dve", lambda e, hk=hk, b2_=b2_: e.tensor_copy(
                                        out=ksw[:ts, hk * 64 + b2_ * 32:hk * 64 + b2_ * 32 + 32],
                                        in_=kst[:ts, hk * 64 + (1 - b2_) * 32:hk * 64 + (1 - b2_) * 32 + 32]),
                                        reads=[R_kst], writes=[R_ksw])
                            P.op("dve", lambda e: e.tensor_tensor(out=kst[:ts, :], in0=kst[:ts, :], in1=ctm[:ts, :],
                                                                  op=ALU.mult), reads=[R_kst, R_ctm], writes=[R_kst])
                            P.op("dve", lambda e: e.tensor_tensor(out=ksw[:ts, :], in0=ksw[:ts, :], in1=stm[:ts, :],
                                                                  op=ALU.mult), reads=[R_ksw, R_ctm], writes=[R_ksw])
                            P.op("dve", lambda e: e.tensor_tensor(out=ysk[:ts, 256:512], in0=kst[:ts, :],
                                                                  in1=ksw[:ts, :], op=ALU.add),
                                 reads=[R_kst, R_ksw], writes=[Rysk])
                            P.dma("pool", dk_ap, ysk[:ts, 256:512], reads=[Rysk], is_output=True)
                        yield

                def c_blocks(g, j, a):
                    gb = g % 2
                    J = g * tpg1 + j
                    blocks = []
                    if kind == "p":
                        if J > 0:
                            blocks.append((kr[gb][:, a, j * 128:(j + 1) * 128], 128,
                                           V1[gb][:, j, a * 64:(a + 1) * 64], "prev", [R_kr[gb]], [R_V1[gb]]))
                        blocks.append((kr[gb][:, a, (j + 1) * 128:(j + 2) * 128], 128,
                                       V1[gb][:, j + 1, a * 64:(a + 1) * 64], "diag", [R_kr[gb]], [R_V1[gb]]))
                    else:
                        blocks.append((krc[:, a, :], 128, Vcc[:, a * 64:(a + 1) * 64], "c", [R_krc], [R_Vcc]))
                        blocks.append((kr[gb][:, a, 128:128 + TS], TS, V1[gb][0:TS, 1, a * 64:(a + 1) * 64], "n",
                                       [R_kr[gb]], [R_V1[gb]]))
                    return blocks

                def c_stage1(g, n):
                    gb = g % 2
                    j, a = n // 4, n % 4
                    blocks = c_blocks(g, j, a)
                    for par in range(2):
                        sbk = YB[par]
                        Sps = bank(sbk)
                        po = par * 64
                        pt = PTc[(n % 2) * 2 + par]; Rpt = R_PTc[(n % 2) * 2 + par]
                        for bi_, (k_ap, nk, v_ap, tag, rk, rv) in enumerate(blocks):
                            if qw == 128:
                                col = bi_ * 2 * qw
                                P.op("pe", lambda e, k_ap=k_ap, nk=nk, col=col: e.matmul(
                                    Sps[:nk, col:col + 2 * qw].rearrange("p (h q) -> p h q", h=2),
                                    lhsT=k_ap[po:po + 64, :],
                                    rhs=qr[gb][po:po + 64, 2 * a:2 * a + 2, j * qw:(j + 1) * qw],
                                    start=True, stop=True), reads=rk + [R_qr[gb]], writes=[R_bank[sbk]])
                                continue
                            for hi in range(2):
                                fc = 2 * a + hi
                                col = (bi_ * 2 + hi) * qw
                                P.op("pe", lambda e, k_ap=k_ap, nk=nk, fc=fc, col=col: e.matmul(
                                    Sps[:nk, col:col + qw], lhsT=k_ap[po:po + 64, :],
                                    rhs=qr[gb][po:po + 64, fc, j * qw:(j + 1) * qw],
                                    start=True, stop=True), reads=rk + [R_qr[gb]], writes=[R_bank[sbk]])
                        for bi_, (k_ap, nk, v_ap, tag, rk, rv) in enumerate(blocks):
                            c0 = bi_ * 2 * qw
                            P.op("act", lambda e, nk=nk, c0=c0: e.activation(
                                out=pt[:nk, c0:c0 + 2 * qw], in_=Sps[:nk, c0:c0 + 2 * qw], func=AF.Exp),
                                reads=[R_bank[sbk]], writes=[Rpt])
                            if tag == "prev":
                                P.op("dve", lambda e, c0=c0: e.memset(
                                    pt[0:64, c0:c0 + 2 * qw].rearrange("p (h q) -> p h q", h=2)[:, :, 64:128], 0.0),
                                    writes=[Rpt])
                            if tag == "diag":
                                P.op("dve", lambda e, c0=c0: e.memset(
                                    pt[64:128, c0:c0 + 2 * qw].rearrange("p (h q) -> p h q", h=2)[:, :, 0:64], 0.0),
                                    writes=[Rpt])

                def c_stage2(g, n):
                    gb = g % 2
                    j, a = n // 4, n % 4
                    blocks = c_blocks(g, j, a)
                    nb = len(blocks)
                    ocb = YB[2]
                    OC = bank(ocb)
                    for par in range(2):
                        po = par * 64
                        pt = PTc[(n % 2) * 2 + par]; Rpt = R_PTc[(n % 2) * 2 + par]
                        if qw == 128:
                            for bi_, (k_ap, nk, v_ap, tag, rk, rv) in enumerate(blocks):
                                col = bi_ * 2 * qw
                                P.op("pe", lambda e, v_ap=v_ap, nk=nk, col=col, bi_=bi_: e.matmul(
                                    OC[po:po + 64, 0:256], lhsT=v_ap, rhs=pt[:nk, col:col + 256],
                                    start=(bi_ == 0), stop=(bi_ == nb - 1)),
                                    reads=rv + [Rpt], writes=[R_bank[ocb]])
                            for bi_, (k_ap, nk, v_ap, tag, rk, rv) in enumerate(blocks):
                                col = bi_ * 2 * qw
                                P.op("pe", lambda e, nk=nk, col=col, bi_=bi_: e.matmul(
                                    OC[po:po + 64, 256:512], lhsT=ones[:nk, 0:64],
                                    rhs=pt[:nk, col:col + 256], start=(bi_ == 0), stop=(bi_ == nb - 1)),
                                    reads=[Rpt, R_const], writes=[R_bank[ocb]])
                            continue
                        for hi in range(2):
                            for bi_, (k_ap, nk, v_ap, tag, rk, rv) in enumerate(blocks):
                                col = (bi_ * 2 + hi) * qw
                                P.op("pe", lambda e, v_ap=v_ap, nk=nk, col=col, bi_=bi_: e.matmul(
                                    OC[po:po + 64, hi * 128:hi * 128 + qw], lhsT=v_ap, rhs=pt[:nk, col:col + qw],
                                    start=(bi_ == 0), stop=(bi_ == nb - 1)),
                                    reads=rv + [Rpt], writes=[R_bank[ocb]])
                            for bi_, (k_ap, nk, v_ap, tag, rk, rv) in enumerate(blocks):
                                col = (bi_ * 2 + hi) * qw
                                P.op("pe", lambda e, nk=nk, col=col, bi_=bi_: e.matmul(
                                    OC[po:po + 64, 256 + hi * 128:256 + hi * 128 + qw], lhsT=ones[:nk, 0:64],
                                    rhs=pt[:nk, col:col + qw], start=(bi_ == 0), stop=(bi_ == nb - 1)),
                                    reads=[Rpt, R_const], writes=[R_bank[ocb]])
                    s2 = n % 2
                    rc = recc[s2]; Rrc = R_recc[s2]
                    tm = tmpc[s2]; Rtm = R_tmpc[s2]
                    for hi in range(2):
                        fc = 2 * a + hi
                        P.op("act", lambda e, hi=hi, fc=fc: e.activation(
                            out=rc[:, hi * 128:hi * 128 + qw], in_=OC[:, 256 + hi * 128:256 + hi * 128 + qw],
                            func=AF.Ln, bias=esink[:, fc:fc + 1]),
                            reads=[R_bank[ocb], R_const], writes=[Rrc])
                    if qw == 128:
                        P.op("act", lambda e: e.activation(out=rc[:, 0:256], in_=rc[:, 0:256], func=AF.Exp, scale=-1.0),
                             reads=[Rrc], writes=[Rrc])
                        P.op("dve", lambda e: e.tensor_tensor(out=tm[:, 0:256], in0=OC[:, 0:256], in1=rc[:, 0:256],
                                                              op=ALU.mult), reads=[R_bank[ocb], Rrc], writes=[Rtm])
                        P.op("pool", lambda e: e.tensor_tensor(
                            out=qr[gb][:, 2 * a:2 * a + 2, j * qw:(j + 1) * qw],
                            in0=tm[:, 0:256].rearrange("p (h q) -> p h q", h=2),
                            in1=g1[gb][:, 2 * a:2 * a + 2, j * qw:(j + 1) * qw], op=ALU.mult),
                            reads=[Rtm, R_g1[gb]], writes=[R_qr[gb]])
                    else:
                        for hi in range(2):
                            fc = 2 * a + hi
                            P.op("act", lambda e, hi=hi: e.activation(out=rc[:, hi * 128:hi * 128 + qw],
                                                                      in_=rc[:, hi * 128:hi * 128 + qw], func=AF.Exp,
                                                                      scale=-1.0),
                                 reads=[Rrc], writes=[Rrc])
                            P.op("dve", lambda e, hi=hi: e.tensor_tensor(
                                out=tm[:, hi * 128:hi * 128 + qw], in0=OC[:, hi * 128:hi * 128 + qw],
                                in1=rc[:, hi * 128:hi * 128 + qw], op=ALU.mult),
                                reads=[R_bank[ocb], Rrc], writes=[Rtm])
                            P.op("pool", lambda e, hi=hi, fc=fc: e.tensor_tensor(
                                out=qr[gb][:, fc, j * qw:(j + 1) * qw], in0=tm[:, hi * 128:hi * 128 + qw],
                                in1=g1[gb][:, fc, j * qw:(j + 1) * qw], op=ALU.mult),
                                reads=[Rtm, R_g1[gb]], writes=[R_qr[gb]])

                def gen_c(g):
                    nn = tpg1 * 4
                    c_stage1(g, 0)
                    yield
                    for n in range(nn):
                        if n + 1 < nn:
                            c_stage1(g, n + 1)
                            yield
                        c_stage2(g, n)
                        yield

                def gen_d(g):
                    gb = g % 2
                    for tt in range(tpg1):
                        ti = g * tpg1 + tt
                        bks = (nbx(), nbx())
                        for half in range(2):
                            for c in range(8):
                                P.op("pe", lambda e, c=c, half=half: e.matmul(
                                    bank(bks[half])[:ts, :], lhsT=qr[gb][:, c, tt * ts:(tt + 1) * ts],
                                    rhs=woc[:, c, half * 512:(half + 1) * 512], start=(c == 0), stop=(c == 7)),
                                    reads=[R_qr[gb], R_woc], writes=[R_bank[bks[half]]])
                        yield
                        sk = yc["n"] % 2; yc["n"] += 1
                        ysk = t1b[sk]; Rysk = R_t1[sk]
                        post_norm_residual(bks[0], bks[1], gpostc, Y0[gb][:ts, tt, :], R_Y0[gb], ysk[:ts, :], Rysk)
                        dst = yp[si, ti * ts:(ti + 1) * ts, :] if kind == "p" else ys[0:ts, :]
                        P.dma("pool", dst, ysk[:ts, :], reads=[Rysk], is_output=True)
                        yield

                def chain(*gens):
                    for g_ in gens:
                        yield from g_

                def run_streams(streams):
                    alive = list(streams)
                    while alive:
                        for s_ in list(alive):
                            try:
                                next(s_)
                            except StopIteration:
                                alive.remove(s_)

                flags = {}

                def wait_for(*keys):
                    while not all(flags.get(k) for k in keys):
                        yield

                def SA():
                    for g in range(ng1):
                        if g >= 2:
                            yield from wait_for(("d", g - 2))
                        if g >= 1:
                            yield from wait_for(("b2", g - 1))
                        yield from gen_a(g)
                        flags[("a", g)] = True
                        first = True
                        for _ in gen_b(g):
                            if first:
                                flags[("halo", g)] = True
                                first = False
                            yield
                        flags[("halo", g)] = True
                        flags[("bq", g)] = True

                def SB():
                    for g in range(ng1):
                        yield from wait_for(("a", g), ("halo", g))
                        yield from gen_b2(g)
                        flags[("b2", g)] = True

                def SC():
                    for g in range(ng1):
                        yield from wait_for(("bq", g), ("b2", g))
                        yield from gen_c(g)
                        flags[("c", g)] = True

                def SD():
                    for g in range(ng1):
                        yield from wait_for(("c", g))
                        yield from gen_d(g)
                        flags[("d", g)] = True

                run_streams([SA(), SB(), SC(), SD()])
            P.barrier()


        with nc.Block() as block:
            P.finalize(block, sems, dma_sems)
    return nc


_NC_CACHE = {}


def _prep(x_prompt, x_sample, cache_a_k, cache_a_v, cache_b_k, cache_b_v, cache_c_k, cache_c_v,
          ab_norm_pre, ab_w_in, ab_w_out, ab_norm_post, a_rel_bias,
          c_norm_pre, c_w_in, c_sinks, c_w_out, c_norm_post):
    f32 = np.float32
    A = lambda a: np.ascontiguousarray(np.asarray(a, dtype=f32))
    x_prompt, x_sample = A(x_prompt), A(x_sample)
    ncore = 8
    cst = _consts()
    w = A(ab_w_in)[0]
    w_ab = np.zeros((8, D, 512), f32)
    for pi in range(8):
        base = 0 if pi < 4 else 2048
        hp = pi % 4
        sl = lambda blk: w[:, base + blk * 512 + hp * 128: base + blk * 512 + (hp + 1) * 128]
        w_ab[pi, :, 0:128] = sl(0)
        w_ab[pi, :, 128:256] = sl(3)
        w_ab[pi, :, 256:384] = sl(1)
        w_ab[pi, :, 384:512] = sl(2)
    rep = lambda v: np.ascontiguousarray(np.broadcast_to(A(v).reshape(1, D), (128, D)))
    bp, bs = _bias_tiles(A(a_rel_bias)[0])
    sk = A(c_sinks)[0]
    sinks_l = np.zeros((128, 8), f32)
    for fc in range(8):
        sinks_l[0:64, fc] = sk[2 * fc]
        sinks_l[64:128, fc] = sk[2 * fc + 1]
    common = {
        "w_ab": w_ab, "w_oab": A(ab_w_out)[0], "w_c": A(c_w_in)[0], "w_oc": A(c_w_out)[0],
        "gpre_ab": rep(ab_norm_pre[0]), "gpost_ab": rep(ab_norm_post[0]),
        "gpre_c": rep(c_norm_pre[0]), "gpost_c": rep(c_norm_post[0]),
        "biasP": bp, "biasS": bs, "sinks": sinks_l,
        "c_ident": cst["ident"], "c_tri": cst["tri"], "c_ones": cst["ones"], "c_lmask": cst["lmask"],
        "c_rot": cst["rot"], "c_dsel": cst["dsel"], "c_dselrot": cst["dselrot"],
        "c_cosT": cst["cosT"], "c_sinT": cst["sinT"], "c_cosTM": cst["cosTM"], "c_sinTM": cst["sinTM"],
    }
    cak, cav = A(cache_a_k)[0], A(cache_a_v)[0]
    cbk, cbv = A(cache_b_k)[0], A(cache_b_v)[0]
    cck, ccv = A(cache_c_k)[0], A(cache_c_v)[0]
    in_maps = []
    for i in range(ncore):
        m = dict(common)
        m["xp"] = np.ascontiguousarray(x_prompt[2 * i:2 * i + 2])
        m["xs"] = np.ascontiguousarray(x_sample[i])
        m["ca_k"] = cak[i].reshape(512, 512); m["ca_v"] = cav[i].reshape(512, 512)
        m["cb_k"] = cbk[i].reshape(PAST, 512); m["cb_v"] = cbv[i].reshape(PAST, 512)
        m["cc_k"] = cck[i].reshape(128, 256); m["cc_v"] = ccv[i].reshape(128, 256)
        in_maps.append(m)
    return in_maps


def kernel(**inputs):
    ncore = 8
    in_maps = _prep(**inputs)
    if "nc" not in _NC_CACHE:
        _NC_CACHE["nc"] = build()
    nc = _NC_CACHE["nc"]
    res = run_bass_kernel_spmd(nc, in_maps, core_ids=list(range(ncore)))
    return _gather(res.results)


def _gather(R):
    ncore = len(R)
    cat = lambda name: np.concatenate([R[i][name] for i in range(ncore)], axis=0)
    stk = lambda name: np.stack([R[i][name] for i in range(ncore)], axis=0)
    y_prompt = cat("yp")
    y_sample = stk("ys")
    out = (
        y_prompt, y_sample,
        cat("o_akp").reshape(1, 16, 512, 8, 64), cat("o_avp").reshape(1, 16, 512, 8, 64),
        cat("o_bkp").reshape(1, 16, S, 8, 64), cat("o_bvp").reshape(1, 16, S, 8, 64),
        cat("o_ckp").reshape(1, 16, 128, 4, 64), cat("o_cvp").reshape(1, 16, 128, 4, 64),
        stk("o_aks").reshape(1, 8, TS, 8, 64), stk("o_avs").reshape(1, 8, TS, 8, 64),
        stk("o_bks").reshape(1, 8, TS, 8, 64), stk("o_bvs").reshape(1, 8, TS, 8, 64),
        stk("o_cks").reshape(1, 8, TS, 4, 64), stk("o_cvs").reshape(1, 8, TS, 4, 64),
    )
    return tuple(np.ascontiguousarray(o.astype(np.float32)) for o in out)
```

```python
import numpy as np
import concourse.bass as bass
import concourse.mybir as mybir
from concourse.bass_utils import run_bass_kernel_spmd

F32 = mybir.dt.float32
BF16 = mybir.dt.bfloat16
AF = mybir.ActivationFunctionType
ALU = mybir.AluOpType

D = 1024
S = 2048
TS = 32
PAST = 2048
EPS = 1e-6
NEG = -30000.0
SEM_LIM = 20000


class Res:
    __slots__ = ("w", "r", "name", "excl")

    def __init__(self, name="", excl=False):
        self.w = None
        self.r = {}
        self.name = name
        self.excl = excl


class _Rec:
    def __init__(self):
        self.call = None

    def __getattr__(self, name):
        def f(*args, **kwargs):
            self.call = (name, args, kwargs)
            return None
        return f


class Prog:
    ENG = ("pe", "act", "dve", "pool", "sp")

    def __init__(self, nc):
        self.nc = nc
        self.ops = {e: [] for e in self.ENG}
        self.waited = {e: {} for e in self.ENG}
        self.ndma_sems = 12
        self.ndma_q = {"sp": 12, "pool": 6}
        self.dma_cnt = {q: [0] * self.ndma_sems for q in ("sp", "pool")}
        self.dma_next = {"sp": 0, "pool": 0}
        self.dma_last = {q: [None] * self.ndma_sems for q in ("sp", "pool")}
        self.out_dma_refs = []
        self.last_pe = None

    def _need(self, eng, ref, waits):
        if ref is None:
            return
        if ref[0] == "op":
            _, e2, idx = ref
            if e2 == eng and eng == "pe":
                return
            if self.waited[eng].get(e2, -1) >= idx:
                return
            if e2 == eng and idx >= len(self.ops[eng]):
                return
            self.waited[eng][e2] = idx
            self.ops[e2][idx]["inc"] = True
            waits.append(ref)
        else:
            _, q, slot, val = ref
            key = ("dma", q, slot)
            if self.waited[eng].get(key, -1) >= val:
                return
            self.waited[eng][key] = val
            waits.append(ref)

    def _deps(self, eng, reads, writes, same_engine_war=False):
        waits = []
        for r in reads:
            self._need(eng, r.w, waits)
        for w in writes:
            self._need(eng, w.w, waits)
            for e2, ref in w.r.items():
                self._need(eng, ref, waits)
        return waits

    def _commit(self, ref, reads, writes):
        for r in reads:
            r.r[ref[1] if ref[0] == "op" else ("dma", ref[1], ref[2])] = ref
        for w in writes:
            w.w = ref
            w.r = {}

    def op(self, eng, fn, reads=(), writes=()):
        ex = [r for r in reads if r.excl]
        if ex:
            reads = [r for r in reads if not r.excl]
            writes = list(writes) + ex
        waits = self._deps(eng, reads, writes)
        idx = len(self.ops[eng])
        rec = _Rec()
        fn(rec)
        assert rec.call is not None
        self.ops[eng].append({"fn": rec.call, "waits": waits, "inc": False, "dma": None})
        self._commit(("op", eng, idx), reads, writes)
        return ("op", eng, idx)

    def dma(self, q, out, in_, reads=(), writes=(), is_output=False):
        waits = self._deps(q, reads, writes)
        slot = self.dma_next[q]
        self.dma_next[q] = (slot + 1) % self.ndma_q[q]
        prev = self.dma_last[q][slot]
        if prev is not None:
            self._need(q, prev, waits)
        self.dma_cnt[q][slot] += 1
        val = self.dma_cnt[q][slot] * 16
        ref = ("dma", q, slot, val)
        self.dma_last[q][slot] = ref
        self.ops[q].append({"fn": ("dma_start", (), {"out": out, "in_": in_}), "waits": waits,
                            "inc": False, "dma": (q, slot)})
        self._commit(ref, reads, writes)
        if is_output:
            self.out_dma_refs.append(ref)
        return ref

    def barrier(self):
        refs = []
        for e in self.ENG:
            for idx in range(len(self.ops[e]) - 1, -1, -1):
                o = self.ops[e][idx]
                if o["fn"] is not None and o["dma"] is None:
                    refs.append(("op", e, idx))
                    break
        for q in ("sp", "pool"):
            for slot in range(self.ndma_sems):
                if self.dma_last[q][slot] is not None:
                    refs.append(self.dma_last[q][slot])
        for e in self.ENG:
            waits = []
            for ref in refs:
                if ref[0] == "op" and ref[1] == e:
                    continue
                self._need(e, ref, waits)
            self.ops[e].append({"fn": None, "waits": waits, "inc": False, "dma": None})

    def finalize(self, block, sems, dma_sems):
        nc = self.nc
        marks = {}
        for e in self.ENG:
            c = 0
            m = []
            for o in self.ops[e]:
                if o["inc"] and o["fn"] is not None and o["dma"] is None:
                    c += 1
                m.append(c)
            marks[e] = m
            assert c <= SEM_LIM * len(sems[e]), (e, c)

        def sem_of(e, idx):
            m = marks[e][idx]
            assert m >= 1
            k = (m - 1) // SEM_LIM
            return sems[e][k], (m - 1) % SEM_LIM + 1

        engs = {"pe": nc.tensor, "act": nc.scalar, "dve": nc.vector, "pool": nc.gpsimd, "sp": nc.sync}

        def _pinfo(ap):
            fs = 1
            for s_ in list(ap.tensor.shape)[1:]:
                fs *= int(s_)
            p0 = int(ap.offset) // fs
            col = int(ap.offset) % fs
            return p0, int(ap.ap[0][1]), col, fs
        prev = None
        nviol = 0
        for o in self.ops["pe"]:
            if o["fn"] is None:
                continue
            name_, args_, kw_ = o["fn"]
            out_ap = args_[0] if args_ else kw_["out"]
            l_ap = kw_.get("lhsT", kw_.get("in_"))
            p0, kk, _, _ = _pinfo(l_ap)
            _, _, col, fs = _pinfo(out_ap)
            esz = 4 if fs in (512, 1024) and out_ap.tensor.name.startswith("pb") else 2
            bank_id = (out_ap.tensor.name, (col * esz) // 2048)
            rows = (p0, p0 + kk)
            cur = (rows, bank_id)
            if prev is not None and kk < 128 and (prev[0][1] - prev[0][0]) < 128:
                disjoint = rows[0] >= prev[0][1] or prev[0][0] >= rows[1]
                if disjoint and prev[1] == bank_id:
                    nviol += 1
            prev = cur
        assert nviol == 0, f"row-tile bank violations: {nviol}"

        def run(e, eng):
            for idx, o in enumerate(self.ops[e]):
                for ref in o["waits"]:
                    if ref[0] == "op":
                        s_, v_ = sem_of(ref[1], ref[2])
                        eng.wait_ge(s_, v_)
                    else:
                        eng.wait_ge(dma_sems[ref[1]][ref[2]], ref[3])
                if o["fn"] is None:
                    continue
                name_, args_, kw_ = o["fn"]
                ins = getattr(eng, name_)(*args_, **kw_)
                if o["dma"] is not None:
                    ins.then_inc(dma_sems[o["dma"][0]][o["dma"][1]], 16)
                elif o["inc"]:
                    s_, _ = sem_of(e, idx)
                    ins.then_inc(s_, 1)

        @block.tensor
        def _(eng):
            run("pe", eng)

        @block.scalar
        def _(eng):
            run("act", eng)

        @block.vector
        def _(eng):
            run("dve", eng)

        @block.gpsimd
        def _(eng):
            run("pool", eng)

        @block.sync
        def _(eng):
            run("sp", eng)


def _consts():
    c = {}
    i = np.arange(128)
    c["ident"] = np.eye(128, dtype=np.float32)
    c["tri"] = (i[:, None] >= i[None, :]).astype(np.float32)
    c["ones"] = np.ones((128, 128), np.float32)
    c["lmask"] = (i[:, None] < i[None, :]).astype(np.float32)
    rot = np.zeros((128, 128), np.float32)
    for p in range(128):
        if p % 64 < 32:
            rot[p + 32, p] = 1.0
        else:
            rot[p - 32, p] = 1.0
    c["rot"] = rot
    dsel = np.zeros((2, 128, 128), np.float32)
    for a in range(2):
        for p in range(128):
            dsel[a, a * 64 + (p % 64), p] = 1.0
    c["dsel"] = dsel
    c["dselrot"] = np.stack([dsel[a] @ rot for a in range(2)])
    half = 32
    inv = (10000.0 ** (-np.arange(half, dtype=np.float32) * np.float32(2.0 / 64))).astype(np.float32)
    pos = np.arange(S + TS, dtype=np.float32)
    ang = (pos[:, None] * inv[None, :]).astype(np.float32)
    cos, sin = np.cos(ang).astype(np.float32), np.sin(ang).astype(np.float32)
    pidx = np.arange(128) % 32
    sign = np.where((np.arange(128) % 64) < 32, -1.0, 1.0).astype(np.float32)
    c["cosT"] = np.ascontiguousarray(cos[:, pidx].T)
    c["sinT"] = np.ascontiguousarray((sin[:, pidx] * sign[None, :]).T)
    fidx = np.arange(256) % 32
    fsign = np.where((np.arange(256) % 64) < 32, -1.0, 1.0).astype(np.float32)
    c["cosTM"] = np.ascontiguousarray(cos[:, fidx])
    c["sinTM"] = np.ascontiguousarray(sin[:, fidx] * fsign[None, :])
    return c


def _bias_tiles(table):
    k = np.arange(128)[:, None]
    q = np.arange(128)[None, :]
    bp = np.zeros((8, 128, 640), np.float32)
    for slot in range(5):
        rel = (4 - slot) * 128 + (q - k)
        idx = np.clip(rel, -128, 128) + 128
        bp[:, :, slot * 128:(slot + 1) * 128] = table[:, idx]
    qs = PAST + np.arange(TS)[None, :]
    bs = np.zeros((8, 128, 160), np.float32)
    for slot in range(5):
        kpos = PAST - 512 + slot * 128 + np.arange(128)[:, None]
        idx = np.clip(qs - kpos, -128, 128) + 128
        bs[:, :, slot * 32:(slot + 1) * 32] = table[:, idx]
    return bp, bs


def build():
    nc = bass.Bass("TRN2", target_bir_lowering=False)
    P = Prog(nc)

    def din(name, shape):
        return nc.dram_tensor(name, list(shape), F32, kind="ExternalInput").ap()

    def dout(name, shape):
        return nc.dram_tensor(name, list(shape), F32, kind="ExternalOutput").ap()

    xp = din("xp", [2, S, D])
    xs = din("xs", [TS, D])
    ca_k = din("ca_k", [512, 512]); ca_v = din("ca_v", [512, 512])
    cb_k = din("cb_k", [PAST, 512]); cb_v = din("cb_v", [PAST, 512])
    cc_k = din("cc_k", [128, 256]); cc_v = din("cc_v", [128, 256])
    w_ab = din("w_ab", [8, D, 512])
    w_oab = din("w_oab", [D, D])
    w_c = din("w_c", [D, 2560])
    w_oc = din("w_oc", [D, D])
    gpre_ab = din("gpre_ab", [128, D]); gpost_ab = din("gpost_ab", [128, D])
    gpre_c = din("gpre_c", [128, D]); gpost_c = din("gpost_c", [128, D])
    biasP = din("biasP", [8, 128, 640]); biasS = din("biasS", [8, 128, 160])
    sinks = din("sinks", [128, 8])
    c_ident = din("c_ident", [128, 128]); c_tri = din("c_tri", [128, 128]); c_ones = din("c_ones", [128, 128])
    c_lmask = din("c_lmask", [128, 128]); c_rot = din("c_rot", [128, 128])
    c_dsel = din("c_dsel", [2, 128, 128]); c_dselrot = din("c_dselrot", [2, 128, 128])
    c_cosT = din("c_cosT", [128, S + TS]); c_sinT = din("c_sinT", [128, S + TS])
    c_cosTM = din("c_cosTM", [S + TS, 256]); c_sinTM = din("c_sinTM", [S + TS, 256])

    yp = dout("yp", [2, S, D]); ys = dout("ys", [TS, D])
    o_akp = dout("o_akp", [2, 512, 512]); o_avp = dout("o_avp", [2, 512, 512])
    o_bkp = dout("o_bkp", [2, S, 512]); o_bvp = dout("o_bvp", [2, S, 512])
    o_ckp = dout("o_ckp", [2, 128, 256]); o_cvp = dout("o_cvp", [2, 128, 256])
    o_aks = dout("o_aks", [TS, 512]); o_avs = dout("o_avs", [TS, 512])
    o_bks = dout("o_bks", [TS, 512]); o_bvs = dout("o_bvs", [TS, 512])
    o_cks = dout("o_cks", [TS, 256]); o_cvs = dout("o_cvs", [TS, 256])

    from contextlib import ExitStack
    es = ExitStack()

    def sb(name, shape, dt):
        return es.enter_context(nc.sbuf_tensor(name, list(shape), dt))

    def ps(name, shape, dt):
        return es.enter_context(nc.psum_tensor(name, list(shape), dt))

    with es:
        sems = {e: [es.enter_context(nc.semaphore(f"s_{e}{k}")) for k in range(2)] for e in Prog.ENG}
        dma_sems = {q: [es.enter_context(nc.semaphore(f"d_{q}{k}")) for k in range(P.ndma_sems)]
                    for q in ("sp", "pool")}

        ident = sb("ident", [128, 128], BF16); tri = sb("tri", [128, 128], BF16)
        ones = sb("ones", [128, 128], BF16); lmask = sb("lmask", [128, 128], BF16)
        rot = sb("rot", [128, 128], BF16)
        dsel = sb("dsel", [128, 2, 128], BF16); dselrot = sb("dselrot", [128, 2, 128], BF16)
        esink = sb("esink", [128, 8], F32)
        R_const = Res("const")
        for t_, src in ((ident, c_ident), (tri, c_tri), (ones, c_ones), (lmask, c_lmask), (rot, c_rot)):
            P.dma("pool", t_[:], src, writes=[R_const])
        for a in range(2):
            P.dma("pool", dsel[:, a, :], c_dsel[a], writes=[R_const])
            P.dma("pool", dselrot[:, a, :], c_dselrot[a], writes=[R_const])
        for t_, src in ((esink, sinks),):
            P.dma("sp", t_[:], src, writes=[R_const])
        P.op("act", lambda e: e.activation(out=esink[:], in_=esink[:], func=AF.Exp), reads=[R_const], writes=[R_const])

        pbank = [ps(f"pb{i}", [128, 1024], F32) for i in range(3)]
        pb6 = ps("pb6", [128, 512], F32)
        ptp0 = ps("ptp0", [128, 1024], BF16)
        ptps = [ptp0, ptp0]
        ptp = ptps[0]
        R_bank = [Res(f"bank{i}", excl=True) for i in range(7)]
        R_tp = Res("tp0", excl=True)
        R_tps = [R_tp, R_tp]

        def bank(i):
            if i == 6:
                return pb6[:, :]
            return pbank[i // 2][:, (i % 2) * 512:(i % 2 + 1) * 512]

        oT = sb("oT", [128, 8, S], BF16)
        R_oT = [[Res(f"oT{c}_{g}") for g in range(4)] for c in range(8)]
        xst = [sb(f"xst{i}", [128, D], F32) for i in range(2)]
        R_xst = [Res(f"xst{i}") for i in range(2)]
        junk2 = [sb("junk2_0", [128, D], BF16)] * 2; R_junk2 = [Res()] * 2
        stat2 = [sb(f"stat2_{i}", [128, 2], F32) for i in range(2)]; R_stat2 = [Res() for _ in range(2)]
        xsb = [sb(f"xsb{i}", [128, D], BF16) for i in range(2)]
        R_xsb = [Res(f"xsb{i}") for i in range(2)]
        cnt = {"x": 0, "stg": 0, "pj": 0}

        seqs = [("p", 0), ("p", 1), ("s", 0)]

        def x_src(kind, si, t0, n):
            return xp[si, t0:t0 + n, :] if kind == "p" else xs[t0:t0 + n, :]

        def rstd_from(ss_ap, out_ap, n, reads, writes):
            P.op("act", lambda e: e.activation(out=out_ap, in_=ss_ap, func=AF.Ln, scale=1.0 / D, bias=EPS),
                 reads=reads, writes=writes)
            P.op("act", lambda e: e.activation(out=out_ap, in_=out_ap, func=AF.Exp, scale=-0.5),
                 reads=writes, writes=writes)

        def norm_part1(src_ap, src_res, ts, gain, gain_res=None):
            gain_res = gain_res or R_const
            k = cnt["x"]; cnt["x"] += 1
            xb = xsb[k % 2]; Rxb = R_xsb[k % 2]
            st = stat2[k % 2]; Rst = R_stat2[k % 2]
            jk = junk2[k % 2]; Rjk = R_junk2[k % 2]
            P.op("act", lambda e: e.activation(out=jk[:ts, :], in_=src_ap, func=AF.Square,
                                               accum_out=st[:ts, 0:1]),
                 reads=[src_res], writes=[Rjk, Rst])
            rstd_from(st[:ts, 0:1], st[:ts, 1:2], ts, [Rst], [Rst])
            P.op("dve", lambda e: e.scalar_tensor_tensor(out=xb[:ts, :], in0=src_ap, scalar=st[:ts, 1:2],
                                                         in1=gain[:ts, :], op0=ALU.mult, op1=ALU.mult),
                 reads=[src_res, Rst, gain_res], writes=[Rxb])
            return k

        def norm_part2(k, ts, dst_ap3, dst_res):
            xb = xsb[k % 2]; Rxb = R_xsb[k % 2]
            tp = ptps[k % 2]; Rtp = R_tps[k % 2]
            for c in range(8):
                P.op("pe", lambda e, c=c: e.transpose(out=tp[:, c * ts:(c + 1) * ts],
                                                      in_=xb[:ts, c * 128:(c + 1) * 128], identity=ident[:ts, :ts]),
                     reads=[Rxb, R_const], writes=[Rtp])
            P.op("dve", lambda e: e.tensor_copy(out=dst_ap3,
                                                in_=tp[:, 0:8 * ts].rearrange("p (c t) -> p c t", c=8)),
                 reads=[Rtp], writes=[dst_res])

        def norm_transpose(src_ap, src_res, ts, gain, dst_ap3, dst_res, gain_res=None):
            k = norm_part1(src_ap, src_res, ts, gain, gain_res)
            norm_part2(k, ts, dst_ap3, dst_res)

        for (kind, si) in seqs:
            T = S if kind == "p" else TS
            ts = min(T, 128)
            nt = T // ts
            gs = min(T, 512)
            ng = T // gs
            tpg = gs // ts
            pos0 = 0 if kind == "p" else PAST

            with ExitStack() as l0:
                def sb0(name, shape, dt):
                    return l0.enter_context(nc.sbuf_tensor(f"{name}_{kind}{si}", list(shape), dt))

                def mk0(name, shape, dt, n):
                    return [sb0(f"{name}{i}", shape, dt) for i in range(n)]

                gpreab = sb0("gpreab", [128, D], F32); R_gab = Res()
                P.dma("sp", gpreab[:], gpre_ab, writes=[R_gab])
                xnT = sb0("xnT", [128, 8, T], BF16)
                R_xnT = [Res(f"xnT{g}") for g in range(ng)]
                wring = mk0("wr", [128, 8, 512], BF16, 2)
                R_wring = [Res(f"wr{i}") for i in range(2)]
                qTs = mk0("qT", [128, T], BF16, 2); kTs = mk0("kT", [128, T], BF16, 2); gTs = mk0("gT", [128, T], BF16, 2)
                merged = (kind == "p")
                if merged:
                    Vs = mk0("V", [128, nt, 2, 128], BF16, 2)
                else:
                    Vs = mk0("V", [128, nt, 128], BF16, 2)
                R_qs = [[Res() for _ in range(ng)] for _ in range(2)]
                R_ks = [[Res() for _ in range(ng)] for _ in range(2)]
                R_gs = [[Res() for _ in range(ng)] for _ in range(2)]
                R_Vs = [[Res() for _ in range(nt)] for _ in range(2)]
                stg = mk0("stg", [128, 256], F32, 2); R_stg = [Res() for _ in range(2)]
                biasT = sb0("biasT", [128, 8, 640], F32); R_bias = Res("bias")
                Ssb = mk0("Ssb", [128, 640], F32, 2); R_Ssb = [Res() for _ in range(2)]
                PT = mk0("PT", [128, 640], BF16, 2); R_PT = [Res() for _ in range(2)]
                rec2 = mk0("rec", [128, 128], F32, 2); R_rec2 = [Res() for _ in range(2)]
                tmpo2 = mk0("tmpo", [128, 128], F32, 2); R_tmpo2 = [Res() for _ in range(2)]
                EH = [mk0(f"E{h}_", [128, 512], F32, 2) for h in range(2)]; R_EH = [[Res() for _ in range(2)] for _ in range(2)]
                SPH = [mk0(f"SP{h}_", [128, 512], BF16, 2) for h in range(2)]; R_SPH = [[Res() for _ in range(2)] for _ in range(2)]
                ATH = [mk0(f"AT{h}_", [128, 512], BF16, 2) for h in range(2)]; R_ATH = [[Res() for _ in range(2)] for _ in range(2)]
                sgt = sb0("sgt", [128, 512], F32); R_sgt = Res()
                SaccH = mk0("Sacc", [128, 512], F32, 2); R_SaccH = [Res() for _ in range(2)]
                SaccBH = [mk0(f"SaccB{h}_", [128, 512], BF16, 2) for h in range(2)]; R_SaccBH = [[Res() for _ in range(2)] for _ in range(2)]
                nqT = mk0("nq", [128, 512], BF16, 2); R_nq = [Res() for _ in range(2)]
                if kind == "s":
                    kcache = sb0("kcache", [128, 16, 128], BF16); R_kc = Res()
                    kTcs = mk0("kTc", [128, PAST], BF16, 2); R_kTcs = [Res() for _ in range(2)]
                    Vcs = mk0("Vc", [128, 16, 128], BF16, 2); R_Vcs = [Res() for _ in range(2)]

                def load_w(pi):
                    P.dma("pool", wring[pi % 2][:], w_ab[pi].rearrange("(c p) n -> p c n", p=128),
                          writes=[R_wring[pi % 2]])

                load_w(0)
                load_w(1)
                if merged:
                    for s_ in range(2):
                        for ti_ in range(nt):
                            P.op("pool", lambda e, s_=s_, ti_=ti_: e.memset(Vs[s_][:, ti_, :, :], 1.0), writes=[R_Vs[s_][ti_]])
                if kind == "p":
                    for h in range(8):
                        P.dma("sp", biasT[:, h, :], biasP[h], writes=[R_bias])
                    for h in range(8):
                        P.op("pool", lambda e, h=h: e.memset(biasT[0:64, h, 64:128], NEG), writes=[R_bias])
                        P.op("pool", lambda e, h=h: e.memset(biasT[64:128, h, 512:576], NEG), writes=[R_bias])
                else:
                    for h in range(8):
                        P.dma("sp", biasT[:, h, 0:160], biasS[h], writes=[R_bias])

                p0flags = {}

                def gen_phase0():
                    for ti in range(nt):
                        k = cnt["x"]
                        xt = xst[k % 2]; Rxt = R_xst[k % 2]
                        P.dma("sp", xt[:ts, :], x_src(kind, si, ti * ts, ts), writes=[Rxt])
                        g = ti // tpg
                        norm_transpose(xt[:ts, :], Rxt, ts, gpreab, xnT[:, :, ti * ts:(ti + 1) * ts], R_xnT[g],
                                       gain_res=R_gab)
                        if (ti + 1) % tpg == 0:
                            p0flags[g] = True
                        yield


                PJB = 5

                def gen_proj(pi):
                    isA = pi < 4
                    hp = pi % 4
                    st = pi % 2
                    W = wring[st]; RW = R_wring[st]
                    qT, kT, gT, V = qTs[st], kTs[st], gTs[st], Vs[st]
                    R_q, R_k, R_g, R_V = R_qs[st], R_ks[st], R_gs[st], R_Vs[st]
                    if kind == "s":
                        kTc, R_kTc, Vc, R_Vc = kTcs[st], R_kTcs[st], Vcs[st], R_Vcs[st]
                        csrc_k, csrc_v, nck = (ca_k, ca_v, 4) if isA else (cb_k, cb_v, 16)
                        if pi == 0:
                            P.dma("pool", kcache[:, 0:nck, :],
                                  csrc_k[:, hp * 128:(hp + 1) * 128].rearrange("(t p) f -> p t f", p=128), writes=[R_kc])
                        P.dma("pool", Vc[:, 0:nck, :],
                              csrc_v[:, hp * 128:(hp + 1) * 128].rearrange("(t p) f -> p t f", p=128), writes=[R_Vc])
                        for t0 in range(0, nck, 8):
                            nb_ = min(8, nck - t0)
                            for t_ in range(nb_):
                                P.op("pe", lambda e, t_=t_: e.transpose(
                                    out=ptp[:, t_ * 128:(t_ + 1) * 128], in_=kcache[:, t0 + t_, :], identity=ident[:, :]),
                                    reads=[R_kc, R_const], writes=[R_tp])
                            P.op("dve", lambda e: e.tensor_copy(
                                out=kTc[:, t0 * 128:(t0 + nb_) * 128], in_=ptp[:, 0:nb_ * 128]),
                                reads=[R_tp], writes=[R_kTc])
                            yield
                        if pi + 1 < 8:
                            pn = pi + 1
                            nsrc, nn = (ca_k, 4) if pn < 4 else (cb_k, 16)
                            P.dma("pool", kcache[:, 0:nn, :],
                                  nsrc[:, (pn % 4) * 128:(pn % 4 + 1) * 128].rearrange("(t p) f -> p t f", p=128),
                                  writes=[R_kc])
                    pjbanks = [5, 6]
                    pjc = [0]

                    def nextpj():
                        b_ = pjbanks[pjc[0] % len(pjbanks)]; pjc[0] += 1
                        return b_
                    for g in range(ng):
                        t0 = g * gs
                        while pi == 0 and not p0flags.get(g):
                            yield
                        for (fc, kindf) in ((0, "q"), (2, "k"), (1, "g")):
                            bi = nextpj()
                            for kc in range(8):
                                P.op("pe", lambda e, kc=kc: e.matmul(
                                    bank(bi)[:, 0:gs], lhsT=W[:, kc, fc * 128:(fc + 1) * 128],
                                    rhs=xnT[:, kc, t0:t0 + gs], start=(kc == 0), stop=(kc == 7)),
                                    reads=[RW, R_xnT[g]], writes=[R_bank[bi]])
                            if kindf == "q":
                                P.op("dve", lambda e: e.tensor_scalar(
                                    out=qT[:, t0:t0 + gs], in0=bank(bi)[:, 0:gs], scalar1=0.125, scalar2=None,
                                    op0=ALU.mult), reads=[R_bank[bi]], writes=[R_q[g]])
                            elif kindf == "k":
                                P.op("dve", lambda e: e.tensor_copy(
                                    out=kT[:, t0:t0 + gs], in_=bank(bi)[:, 0:gs]), reads=[R_bank[bi]], writes=[R_k[g]])
                            else:
                                P.op("act", lambda e: e.activation(out=sgt[:, 0:gs], in_=bank(bi)[:, 0:gs], func=AF.Exp,
                                                                   scale=-1.0), reads=[R_bank[bi]], writes=[R_sgt])
                                P.op("act", lambda e: e.activation(out=sgt[:, 0:gs], in_=sgt[:, 0:gs], func=AF.Ln, bias=1.0),
                                     reads=[R_sgt], writes=[R_sgt])
                                P.op("act", lambda e: e.activation(out=sgt[:, 0:gs], in_=sgt[:, 0:gs], func=AF.Exp,
                                                                   scale=-1.0), reads=[R_sgt], writes=[R_sgt])
                                P.op("dve", lambda e: e.tensor_tensor(out=gT[:, t0:t0 + gs], in0=bank(bi)[:, 0:gs],
                                                                      in1=sgt[:, 0:gs], op=ALU.mult),
                                     reads=[R_bank[bi], R_sgt], writes=[R_g[g]])
                            yield
                        for tt in range(tpg):
                            ti = g * tpg + tt
                            bi = nextpj()
                            for kc in range(8):
                                P.op("pe", lambda e, kc=kc: e.matmul(
                                    bank(bi)[:ts, 0:256], lhsT=xnT[:, kc, ti * ts:(ti + 1) * ts],
                                    rhs=W[:, kc, 256:512], start=(kc == 0), stop=(kc == 7)),
                                    reads=[RW, R_xnT[g]], writes=[R_bank[bi]])
                            if merged:
                                P.op("dve", lambda e: e.tensor_copy(
                                    out=V[:ts, ti, 0, 0:64], in_=bank(bi)[:ts, 128:192]), reads=[R_bank[bi]], writes=[R_V[ti]])
                                P.op("dve", lambda e: e.tensor_copy(
                                    out=V[:ts, ti, 1, 64:128], in_=bank(bi)[:ts, 192:256]), reads=[R_bank[bi]], writes=[R_V[ti]])
                            else:
                                P.op("dve", lambda e: e.tensor_copy(
                                    out=V[:ts, ti, :], in_=bank(bi)[:ts, 128:256]), reads=[R_bank[bi]], writes=[R_V[ti]])
                            if kind == "p":
                                if isA:
                                    need = ti * ts >= S - 512
                                    dk, dv, r0 = o_akp, o_avp, ti * ts - (S - 512)
                                else:
                                    need = True
                                    dk, dv, r0 = o_bkp, o_bvp, ti * ts
                                dk_ap = dk[si, r0:r0 + ts, hp * 128:(hp + 1) * 128] if need else None
                                dv_ap = dv[si, r0:r0 + ts, hp * 128:(hp + 1) * 128] if need else None
                            else:
                                need = True
                                dk, dv = (o_aks, o_avs) if isA else (o_bks, o_bvs)
                                dk_ap = dk[0:ts, hp * 128:(hp + 1) * 128]
                                dv_ap = dv[0:ts, hp * 128:(hp + 1) * 128]
                            if need:
                                sk = cnt["stg"] % 2; cnt["stg"] += 1
                                P.op("dve", lambda e: e.tensor_copy(
                                    out=stg[sk][:ts, :], in_=bank(bi)[:ts, 0:256]),
                                    reads=[R_bank[bi]], writes=[R_stg[sk]])
                                P.dma("pool", dk_ap, stg[sk][:ts, 0:128], reads=[R_stg[sk]], is_output=True)
                                P.dma("pool", dv_ap, stg[sk][:ts, 128:256], reads=[R_stg[sk]], is_output=True)
                            yield

                def gen_attnA(pi):
                    hp = pi % 4
                    st = pi % 2
                    qT, kT, gT, V = qTs[st], kTs[st], gTs[st], Vs[st]
                    R_q, R_k, R_g, R_V = R_qs[st], R_ks[st], R_gs[st], R_Vs[st]
                    if kind == "s":
                        kTc, R_kTc, Vc, R_Vc = kTcs[st], R_kTcs[st], Vcs[st], R_Vcs[st]
                    nqb = T // ts
                    qw = ts
                    its = [(j, hh) for j in range(nqb) for hh in range(2)]

                    def a_blocks(j, hh):
                        po = hh * 64
                        blocks = []
                        if kind == "p":
                            for slot in range(5):
                                kb = j - 4 + slot
                                if kb < 0:
                                    continue
                                blocks.append((kT[po:po + 64, kb * 128:(kb + 1) * 128], 128,
                                               V[:, kb, hh, :], slot,
                                               [R_k[kb // 4]], [R_V[kb]]))
                        else:
                            for slot in range(4):
                                blocks.append((kTc[po:po + 64, slot * 128:(slot + 1) * 128], 128,
                                               Vc[:, slot, hh * 64:(hh + 1) * 64], slot, [R_kTc], [R_Vc]))
                            blocks.append((kT[po:po + 64, 0:TS], TS, V[0:TS, 0, hh * 64:(hh + 1) * 64], 4,
                                           [R_k[0]], [R_V[0]]))
                        return blocks

                    def a_stage1(n):
                        j, hh = its[n]
                        h = hp * 2 + hh
                        po = hh * 64
                        sbk = n % 2
                        RS = [R_bank[2 * sbk], R_bank[2 * sbk + 1]]
                        Sps = pbank[sbk]
                        blocks = a_blocks(j, hh)
                        q_ap = qT[po:po + 64, j * qw:(j + 1) * qw]
                        gq = (j * qw) // gs
                        for (k_ap, nk, v_ap, slot, rk, rv) in blocks:
                            P.op("pe", lambda e, k_ap=k_ap, nk=nk, slot=slot: e.matmul(
                                Sps[:nk, slot * qw:(slot + 1) * qw], lhsT=k_ap, rhs=q_ap, start=True, stop=True),
                                reads=rk + [R_q[gq]], writes=RS)
                        s0 = blocks[0][3]
                        full = [b for b in blocks if b[1] == 128]
                        part = [b for b in blocks if b[1] != 128]
                        sk = n % 2
                        lo, hi = s0 * qw, (full[-1][3] + 1) * qw
                        P.op("dve", lambda e: e.tensor_tensor(
                            out=Ssb[sk][:, lo:hi], in0=Sps[:, lo:hi], in1=biasT[:, h, lo:hi], op=ALU.add),
                            reads=RS + [R_bias], writes=[R_Ssb[sk]])
                        P.op("act", lambda e: e.activation(
                            out=PT[sk][:, lo:hi], in_=Ssb[sk][:, lo:hi], func=AF.Exp),
                            reads=[R_Ssb[sk]], writes=[R_PT[sk]])
                        for (k_ap, nk, v_ap, slot, rk, rv) in part:
                            lo2, hi2 = slot * qw, (slot + 1) * qw
                            P.op("dve", lambda e, lo2=lo2, hi2=hi2, nk=nk: e.tensor_tensor(
                                out=Ssb[sk][:nk, lo2:hi2], in0=Sps[:nk, lo2:hi2], in1=biasT[:nk, h, lo2:hi2],
                                op=ALU.add), reads=RS + [R_bias], writes=[R_Ssb[sk]])
                            P.op("act", lambda e, lo2=lo2, hi2=hi2, nk=nk: e.activation(
                                out=PT[sk][:nk, lo2:hi2], in_=Ssb[sk][:nk, lo2:hi2], func=AF.Exp),
                                reads=[R_Ssb[sk]], writes=[R_PT[sk]])

                    def a_stage2m(n):
                        j, hh = its[n]
                        po = hh * 64
                        pd = 64 - po
                        sk = n % 2
                        odb = 4
                        OD = bank(4)[:, (n % 2) * 128:(n % 2) * 128 + 128]
                        blocks = a_blocks(j, hh)
                        nb = len(blocks)
                        gq = (j * qw) // gs
                        for bi_, (k_ap, nk, v_ap, slot, rk, rv) in enumerate(blocks):
                            P.op("pe", lambda e, v_ap=v_ap, nk=nk, slot=slot, bi_=bi_: e.matmul(
                                OD[:, 0:qw], lhsT=v_ap, rhs=PT[sk][:nk, slot * qw:(slot + 1) * qw],
                                start=(bi_ == 0), stop=(bi_ == nb - 1)),
                                reads=rv + [R_PT[sk]], writes=[R_bank[odb]])
                        rc = rec2[hh]; Rrc = R_rec2[hh]
                        tm = tmpo2[hh]; Rtm = R_tmpo2[hh]
                        P.op("act", lambda e: e.activation(
                            out=rc[po:po + 64, 0:qw], in_=OD[pd:pd + 64, 0:qw], func=AF.Ln),
                            reads=[R_bank[odb]], writes=[Rrc])
                        P.op("act", lambda e: e.activation(
                            out=rc[po:po + 64, 0:qw], in_=rc[po:po + 64, 0:qw], func=AF.Exp, scale=-1.0),
                            reads=[Rrc], writes=[Rrc])
                        P.op("dve", lambda e: e.tensor_tensor(
                            out=tm[po:po + 64, 0:qw], in0=OD[po:po + 64, 0:qw], in1=rc[po:po + 64, 0:qw],
                            op=ALU.mult), reads=[R_bank[odb], Rrc], writes=[Rtm])
                        P.op("dve", lambda e: e.tensor_tensor(
                            out=oT[po:po + 64, pi, j * qw:(j + 1) * qw], in0=tm[po:po + 64, 0:qw],
                            in1=gT[po:po + 64, j * qw:(j + 1) * qw], op=ALU.mult),
                            reads=[Rtm, R_g[gq]], writes=[R_oT[pi][gq]])

                    def a_stage2(n):
                        if merged:
                            return a_stage2m(n)
                        j, hh = its[n]
                        po = hh * 64
                        sk = n % 2
                        odb = 4
                        OD = bank(odb)
                        blocks = a_blocks(j, hh)
                        nb = len(blocks)
                        gq = (j * qw) // gs
                        for bi_, (k_ap, nk, v_ap, slot, rk, rv) in enumerate(blocks):
                            P.op("pe", lambda e, v_ap=v_ap, nk=nk, slot=slot, bi_=bi_: e.matmul(
                                OD[po:po + 64, 0:qw], lhsT=v_ap, rhs=PT[sk][:nk, slot * qw:(slot + 1) * qw],
                                start=(bi_ == 0), stop=(bi_ == nb - 1)),
                                reads=rv + [R_PT[sk]], writes=[R_bank[odb]])
                        for bi_, (k_ap, nk, v_ap, slot, rk, rv) in enumerate(blocks):
                            P.op("pe", lambda e, nk=nk, slot=slot, bi_=bi_: e.matmul(
                                OD[po:po + 64, 128:128 + qw], lhsT=ones[:nk, 0:64],
                                rhs=PT[sk][:nk, slot * qw:(slot + 1) * qw],
                                start=(bi_ == 0), stop=(bi_ == nb - 1)),
                                reads=[R_PT[sk], R_const], writes=[R_bank[odb]])
                        rc = rec2[hh]; Rrc = R_rec2[hh]
                        tm = tmpo2[hh]; Rtm = R_tmpo2[hh]
                        P.op("act", lambda e: e.activation(
                            out=rc[po:po + 64, 0:qw], in_=OD[po:po + 64, 128:128 + qw], func=AF.Ln),
                            reads=[R_bank[odb]], writes=[Rrc])
                        P.op("act", lambda e: e.activation(
                            out=rc[po:po + 64, 0:qw], in_=rc[po:po + 64, 0:qw], func=AF.Exp, scale=-1.0),
                            reads=[Rrc], writes=[Rrc])
                        P.op("dve", lambda e: e.tensor_tensor(
                            out=tm[po:po + 64, 0:qw], in0=OD[po:po + 64, 0:qw], in1=rc[po:po + 64, 0:qw],
                            op=ALU.mult), reads=[R_bank[odb], Rrc], writes=[Rtm])
                        P.op("dve", lambda e: e.tensor_tensor(
                            out=oT[po:po + 64, pi, j * qw:(j + 1) * qw], in0=tm[po:po + 64, 0:qw],
                            in1=gT[po:po + 64, j * qw:(j + 1) * qw], op=ALU.mult),
                            reads=[Rtm, R_g[gq]], writes=[R_oT[pi][gq]])

                    a_stage1(0)
                    yield
                    for n in range(len(its)):
                        if n + 1 < len(its):
                            a_stage1(n + 1)
                            yield
                        a_stage2(n)
                        yield

                def gen_attnB(pi):
                    st = pi % 2
                    qT, kT, gT, V = qTs[st], kTs[st], gTs[st], Vs[st]
                    R_q, R_k, R_g, R_V = R_qs[st], R_ks[st], R_gs[st], R_Vs[st]
                    if kind == "s":
                        kTc, R_kTc, Vc, R_Vc = kTcs[st], R_kTcs[st], Vcs[st], R_Vcs[st]
                    cw = gs
                    ob = 4
                    OB = bank(ob)
                    dq = min(128, cw)
                    for c in range(ng):
                        steps = [[], []]
                        for hh in range(2):
                            po = hh * 64
                            P.op("dve", lambda e: e.tensor_scalar(
                                out=nqT[hh][po:po + 64, 0:cw], in0=qT[po:po + 64, c * cw:(c + 1) * cw],
                                scalar1=-1.0, scalar2=None, op0=ALU.mult), reads=[R_q[c]], writes=[R_nq[hh]])
                            P.op("pool", lambda e: e.memset(SaccH[hh][:, 0:cw], 0.0), writes=[R_SaccH[hh]])
                            P.op("pool", lambda e: e.memset(SaccBH[hh][0][:, 0:cw], 0.0), writes=[R_SaccBH[hh][0]])
                            if kind == "p":
                                for kb in range(4 * c + 3, -1, -1):
                                    q0 = max(0, kb * 128 - c * 512)
                                    steps[hh].append((kT[po:po + 64, kb * 128:(kb + 1) * 128], 128,
                                                      V[:, kb, hh, hh * 64:(hh + 1) * 64], q0, kb >= 4 * c,
                                                      [R_k[kb // 4]], [R_V[kb]]))
                            else:
                                steps[hh].append((kT[po:po + 64, 0:TS], TS, V[0:TS, 0, hh * 64:(hh + 1) * 64], 0, True,
                                                  [R_k[0]], [R_V[0]]))
                                for kb in range(15, -1, -1):
                                    steps[hh].append((kTc[po:po + 64, kb * 128:(kb + 1) * 128], 128,
                                                      Vc[:, kb, hh * 64:(hh + 1) * 64], 0, False, [R_kTc], [R_Vc]))
                        ns = len(steps[0])

                        def stage1(i):
                            for hh in range(2):
                                po = hh * 64
                                k_ap, nk, v_ap, q0, diag, rk, rv = steps[hh][i]
                                z = bank(hh); Rz = R_bank[hh]
                                P.op("pe", lambda e: e.matmul(z[:nk, q0:cw], lhsT=k_ap,
                                                              rhs=qT[po:po + 64, c * cw + q0:(c + 1) * cw],
                                                              start=True, stop=True),
                                     reads=rk + [R_q[c]], writes=[Rz])
                            for hh in range(2):
                                k_ap, nk, v_ap, q0, diag, rk, rv = steps[hh][i]
                                z = bank(hh); Rz = R_bank[hh]
                                E = EH[hh][i % 2]; RE = R_EH[hh][i % 2]
                                P.op("act", lambda e: e.activation(out=E[:nk, q0:cw], in_=z[:nk, q0:cw], func=AF.Exp),
                                     reads=[Rz], writes=[RE])
                                if diag:
                                    P.op("dve", lambda e: e.tensor_tensor(
                                        out=E[:nk, q0:q0 + dq], in0=E[:nk, q0:q0 + dq], in1=lmask[:nk, 0:dq],
                                        op=ALU.mult), reads=[RE, R_const], writes=[RE])
                            for hh in range(2):
                                k_ap, nk, v_ap, q0, diag, rk, rv = steps[hh][i]
                                E = EH[hh][i % 2]; RE = R_EH[hh][i % 2]
                                SP = SPH[hh][i % 2]; RSP = R_SPH[hh][i % 2]
                                P.op("act", lambda e: e.activation(out=SP[:nk, q0:cw], in_=E[:nk, q0:cw], func=AF.Ln,
                                                                   bias=1.0), reads=[RE], writes=[RSP])

                        def stage2(i):
                            for hh in range(2):
                                k_ap, nk, v_ap, q0, diag, rk, rv = steps[hh][i]
                                cps = bank(2 + hh); Rc = R_bank[2 + hh]
                                SP = SPH[hh][i % 2]; RSP = R_SPH[hh][i % 2]
                                P.op("pe", lambda e: e.matmul(cps[:nk, q0:cw], lhsT=tri[:nk, :nk], rhs=SP[:nk, q0:cw],
                                                              start=True, stop=False),
                                     reads=[RSP, R_const], writes=[Rc])
                                P.op("pe", lambda e: e.matmul(cps[:nk, q0:cw], lhsT=ones[:, :nk],
                                                              rhs=SaccBH[hh][i % 2][:, q0:cw], start=False, stop=False),
                                     reads=[R_SaccBH[hh][i % 2], R_const], writes=[Rc])
                            for hh in range(2):
                                po = hh * 64
                                k_ap, nk, v_ap, q0, diag, rk, rv = steps[hh][i]
                                cps = bank(2 + hh); Rc = R_bank[2 + hh]
                                P.op("pe", lambda e: e.matmul(cps[:nk, q0:cw], lhsT=k_ap,
                                                              rhs=nqT[hh][po:po + 64, q0:cw], start=False, stop=True),
                                     reads=rk + [R_nq[hh]], writes=[Rc])
                            if i + 1 < ns:
                                for hh in range(2):
                                    k_ap, nk, v_ap, q0, diag, rk, rv = steps[hh][i]
                                    SP = SPH[hh][i % 2]; RSP = R_SPH[hh][i % 2]
                                    P.op("dve", lambda e: e.tensor_tensor(
                                        out=SaccH[hh][:nk, q0:cw], in0=SaccH[hh][:nk, q0:cw], in1=SP[:nk, q0:cw], op=ALU.add),
                                        reads=[RSP, R_SaccH[hh]], writes=[R_SaccH[hh]])
                                    P.op("dve", lambda e: e.tensor_copy(out=SaccBH[hh][(i + 1) % 2][:, 0:cw],
                                                                        in_=SaccH[hh][:, 0:cw]),
                                         reads=[R_SaccH[hh]], writes=[R_SaccBH[hh][(i + 1) % 2]])
                            for hh in range(2):
                                k_ap, nk, v_ap, q0, diag, rk, rv = steps[hh][i]
                                cps = bank(2 + hh); Rc = R_bank[2 + hh]
                                AT = ATH[hh][i % 2]; RAT = R_ATH[hh][i % 2]
                                P.op("act", lambda e: e.activation(out=AT[:nk, q0:cw], in_=cps[:nk, q0:cw], func=AF.Exp,
                                                                   scale=-1.0), reads=[Rc], writes=[RAT])
                                if q0 > 0:
                                    P.op("pool", lambda e: e.memset(AT[:nk, 0:q0], 0.0), writes=[RAT])
                                if diag:
                                    P.op("dve", lambda e: e.tensor_tensor(
                                        out=AT[:nk, q0:q0 + dq], in0=AT[:nk, q0:q0 + dq], in1=lmask[:nk, 0:dq],
                                        op=ALU.mult), reads=[RAT, R_const], writes=[RAT])

                        def stage3(i):
                            for hh in range(2):
                                po = hh * 64
                                k_ap, nk, v_ap, q0, diag, rk, rv = steps[hh][i]
                                AT = ATH[hh][i % 2]; RAT = R_ATH[hh][i % 2]
                                P.op("pe", lambda e: e.matmul(OB[po:po + 64, 0:cw], lhsT=v_ap, rhs=AT[:nk, 0:cw],
                                                              start=(i == 0), stop=(i == ns - 1)),
                                     reads=rv + [RAT], writes=[R_bank[ob]])

                        stage1(0)
                        yield
                        for i in range(ns):
                            if i + 1 < ns:
                                stage1(i + 1)
                            stage2(i)
                            if i > 0:
                                stage3(i - 1)
                            yield
                        stage3(ns - 1)
                        P.op("dve", lambda e: e.tensor_tensor(
                            out=oT[:, pi, c * cw:(c + 1) * cw], in0=OB[:, 0:cw],
                            in1=gT[:, c * cw:(c + 1) * cw], op=ALU.mult),
                            reads=[R_bank[ob], R_g[c]], writes=[R_oT[pi][c]])
                        yield

                def run_weighted(ga, na, gb, nb_):
                    da = db = 0
                    a_alive, b_alive = True, gb is not None
                    while a_alive or b_alive:
                        pick_b = b_alive and (not a_alive or (db + 1) * na <= (da + 1) * nb_)
                        if pick_b:
                            try:
                                next(gb); db += 1
                            except StopIteration:
                                b_alive = False
                        else:
                            try:
                                next(ga); da += 1
                            except StopIteration:
                                a_alive = False

                n_proj = ng * (3 + tpg) + (3 if kind == "s" else 0)
                gp0, gj0 = gen_phase0(), gen_proj(0)
                alive = [gp0, gj0]
                while alive:
                    for s_ in list(alive):
                        try:
                            next(s_)
                        except StopIteration:
                            alive.remove(s_)
                for pi in range(8):
                    if pi + 2 < 8:
                        load_w(pi + 2)
                    isA = pi < 4
                    if isA:
                        ga = gen_attnA(pi); na = 2 * (T // ts) * 2
                    else:
                        ga = gen_attnB(pi)
                        na = sum((4 * c + 4 + 2) for c in range(ng)) if kind == "p" else 19
                    gb = gen_proj(pi + 1) if pi + 1 < 8 else None
                    run_weighted(ga, na, gb, n_proj)
            P.barrier()

            with ExitStack() as l1:
                def sb1(name, shape, dt):
                    return l1.enter_context(nc.sbuf_tensor(f"{name}_{kind}{si}", list(shape), dt))

                gs1 = min(T, 256)
                ng1 = T // gs1
                tpg1 = gs1 // ts
                qw = ts
                woab = sb1("woab", [128, 8, D], BF16); R_woab = Res()
                wc = sb1("wc", [128, 8, 2560], BF16); R_wcq = [Res() for _ in range(4)]
                woc = sb1("woc", [128, 8, D], BF16); R_woc = Res()
                gpostab = sb1("gpostab", [128, D], F32); gprec = sb1("gprec", [128, D], F32)
                gpostc = sb1("gpostc", [128, D], F32); R_gn = Res()
                for t_, src in ((gpostab, gpost_ab), (gprec, gpre_c), (gpostc, gpost_c)):
                    P.dma("sp", t_[:], src, writes=[R_gn])
                P.dma("pool", woab[:], w_oab.rearrange("(c p) n -> p c n", p=128), writes=[R_woab])
                for q4 in range(4):
                    P.dma("pool", wc[:, :, q4 * 640:(q4 + 1) * 640],
                          w_c[:, q4 * 640:(q4 + 1) * 640].rearrange("(c p) n -> p c n", p=128), writes=[R_wcq[q4]])
                P.dma("pool", woc[:], w_oc.rearrange("(c p) n -> p c n", p=128), writes=[R_woc])

                def mk(name, shape, dt, n):
                    return [sb1(f"{name}{i}", shape, dt) for i in range(n)], [Res() for _ in range(n)]

                Y0, R_Y0 = mk("Y0", [128, tpg1, D], F32, 2)
                t1b, R_t1 = mk("t1b", [128, D], F32, 2)
                statY, R_statY = mk("statY", [128, 4], F32, 2)
                xn1T, R_xn1 = mk("xn1T", [128, 8, gs1], BF16, 1)
                xn1T, R_xn1 = xn1T * 2, R_xn1 * 2
                qbf, R_qbf = mk("qbf", [128, gs1], BF16, 2)
                qr, R_qr = mk("qr", [128, 8, gs1], BF16, 2)
                kbf, R_kbf = mk("kbf", [128, 2, gs1], BF16, 1)
                kr, R_kr = mk("kr", [128, 4, 128 + gs1], BF16, 2)
                g1, R_g1 = mk("g1", [128, 8, gs1], BF16, 2)
                V1, R_V1 = mk("V1", [128, 1 + tpg1, 256], BF16, 2)
                cosg, R_cos = mk("cosg", [128, gs1], F32, 1)
                sing, R_sin = mk("sing", [128, gs1], F32, 1)
                cosg, R_cos, sing, R_sin = cosg * 2, R_cos * 2, sing * 2, R_sin * 2
                ta, R_ta = mk("ta", [128, 256], F32, 2)
                tb, R_tb = mk("tb", [128, 256], F32, 2)
                tcb, R_tcb = mk("tcb", [128, 256], F32, 2)
                PTc, R_PTc = mk("PTc", [128, 512], BF16, 4)
                recc, R_recc = mk("recc", [128, 256], F32, 1)
                tmpc, R_tmpc = mk("tmpc", [128, 256], F32, 1)
                recc, R_recc, tmpc, R_tmpc = recc * 2, R_recc * 2, tmpc * 2, R_tmpc * 2
                kst = sb1("kst", [128, 256], F32); R_kst = Res()
                ksw = sb1("ksw", [128, 256], F32); R_ksw = Res()
                ctm = sb1("ctm", [128, 256], F32); stm = sb1("stm", [128, 256], F32); R_ctm = Res()
                kvst = sb1("kvst", [128, 512], F32); R_kvst = Res()
                if kind == "s":
                    kcc = sb1("kcc", [128, 256], BF16); R_kcc = Res()
                    kcd = sb1("kcd", [128, 4, 128], BF16); R_kcd = Res()
                    krc = sb1("krc", [128, 4, 128], BF16); R_krc = Res()
                    Vcc = sb1("Vcc", [128, 256], BF16); R_Vcc = Res()
                    P.dma("pool", kcc[:], cc_k, writes=[R_kcc])
                    P.dma("pool", Vcc[:], cc_v, writes=[R_Vcc])
                    for a in range(4):
                        for d2 in range(2):
                            P.op("pool", lambda e, a=a, d2=d2: e.tensor_copy(
                                out=kcd[:, a, d2 * 64:(d2 + 1) * 64], in_=kcc[:, a * 64:(a + 1) * 64]),
                                reads=[R_kcc], writes=[R_kcd])
                    for a in range(4):
                        P.op("pe", lambda e, a=a: e.transpose(out=ptp[:, a * 128:(a + 1) * 128], in_=kcd[:, a, :],
                                                              identity=ident[:, :]),
                             reads=[R_kcd, R_const], writes=[R_tp])
                    for a in range(4):
                        P.op("dve", lambda e, a=a: e.tensor_copy(out=krc[:, a, :], in_=ptp[:, a * 128:(a + 1) * 128]),
                             reads=[R_tp], writes=[R_krc])
                lt0 = pos0 + T - ts
                P.dma("sp", ctm[:ts, :], c_cosTM[lt0:lt0 + ts, :], writes=[R_ctm])
                P.dma("sp", stm[:ts, :], c_sinTM[lt0:lt0 + ts, :], writes=[R_ctm])

                yc = {"n": 0, "bx": 0, "t": 0}
                LAG_A = 1
                XB = [0, 1, 2, 6]
                YB = [3, 4, 5]

                def nbx():
                    b_ = XB[yc["bx"] % 4]; yc["bx"] += 1
                    return b_

                def post_norm_residual(bk0, bk1, gain, res_ap, res_r, out_ap, out_r):
                    k = yc["t"]; yc["t"] += 1
                    st = statY[k % 2]; Rst = R_statY[k % 2]
                    jk = junk2[k % 2]; Rjk = R_junk2[k % 2]
                    for half, bk in enumerate((bk0, bk1)):
                        P.op("act", lambda e, half=half, bk=bk: e.activation(
                            out=jk[:ts, half * 512:(half + 1) * 512], in_=bank(bk)[:ts, :], func=AF.Square,
                            accum_out=st[:ts, half:half + 1]), reads=[R_bank[bk]], writes=[Rjk, Rst])
                    P.op("dve", lambda e: e.tensor_tensor(out=st[:ts, 2:3], in0=st[:ts, 0:1], in1=st[:ts, 1:2],
                                                          op=ALU.add), reads=[Rst], writes=[Rst])
                    rstd_from(st[:ts, 2:3], st[:ts, 3:4], ts, [Rst], [Rst])
                    for half, bk in enumerate((bk0, bk1)):
                        P.op("dve", lambda e, half=half, bk=bk: e.scalar_tensor_tensor(
                            out=out_ap[:, half * 512:(half + 1) * 512], in0=bank(bk)[:ts, :], scalar=st[:ts, 3:4],
                            in1=gain[:ts, half * 512:(half + 1) * 512], op0=ALU.mult, op1=ALU.mult),
                            reads=[R_bank[bk], Rst, R_gn], writes=[out_r])
                    P.op("dve", lambda e: e.tensor_tensor(out=out_ap, in0=out_ap, in1=res_ap, op=ALU.add),
                         reads=[res_r, out_r], writes=[out_r])

                def gen_a(g):
                    gb = g % 2
                    t0 = g * gs1
                    g0 = t0 // gs
                    P.dma("sp", cosg[gb][:, :], c_cosT[:, pos0 + t0:pos0 + t0 + gs1], writes=[R_cos[gb]])
                    P.dma("sp", sing[gb][:, :], c_sinT[:, pos0 + t0:pos0 + t0 + gs1], writes=[R_sin[gb]])
                    pend = []
                    for tt in range(tpg1):
                        ti = g * tpg1 + tt
                        k = cnt["x"]
                        xt = xst[k % 2]; Rxt = R_xst[k % 2]
                        P.dma("sp", xt[:ts, :], x_src(kind, si, ti * ts, ts), writes=[Rxt])
                        bks = (nbx(), nbx())
                        for half in range(2):
                            for c in range(8):
                                P.op("pe", lambda e, c=c, half=half: e.matmul(
                                    bank(bks[half])[:ts, :], lhsT=oT[:, c, ti * ts:(ti + 1) * ts],
                                    rhs=woab[:, c, half * 512:(half + 1) * 512], start=(c == 0), stop=(c == 7)),
                                    reads=[R_oT[c][g0], R_woab], writes=[R_bank[bks[half]]])
                        yield
                        post_norm_residual(bks[0], bks[1], gpostab, xt[:ts, :], Rxt, Y0[gb][:ts, tt, :], R_Y0[gb])
                        kk = norm_part1(Y0[gb][:ts, tt, :], R_Y0[gb], ts, gprec, gain_res=R_gn)
                        pend.append((kk, tt))
                        yield
                    for (kk, tt) in pend:
                        norm_part2(kk, ts, xn1T[gb][:, :, tt * ts:(tt + 1) * ts], R_xn1[gb])
                        yield

                def gen_b(g):
                    gb = g % 2
                    xn = xn1T[gb]; Rxn = R_xn1[gb]
                    cs, sn = cosg[gb], sing[gb]
                    if g > 0:
                        P.op("pool", lambda e: e.tensor_copy(out=kr[gb][:, :, 0:128], in_=kr[1 - gb][:, :, gs1:gs1 + 128]),
                             reads=[R_kr[1 - gb]], writes=[R_kr[gb]])
                        P.op("pool", lambda e: e.tensor_copy(out=V1[gb][:, 0, :], in_=V1[1 - gb][:, tpg1, :]),
                             reads=[R_V1[1 - gb]], writes=[R_V1[gb]])
                    def q_part_a(fc):
                        b1 = nbx()
                        for kc in range(8):
                            P.op("pe", lambda e, kc=kc: e.matmul(
                                bank(b1)[:, 0:gs1], lhsT=wc[:, kc, fc * 128:(fc + 1) * 128], rhs=xn[:, kc, :],
                                start=(kc == 0), stop=(kc == 7)), reads=[R_wcq[(fc * 128) // 640], Rxn], writes=[R_bank[b1]])
                        s2 = fc % 2
                        P.op("act", lambda e: e.activation(out=qbf[s2][:, :], in_=bank(b1)[:, 0:gs1], func=AF.Copy, scale=0.125),
                             reads=[R_bank[b1]], writes=[R_qbf[s2]])
                        P.op("dve", lambda e: e.scalar_tensor_tensor(
                            out=ta[s2][:, 0:gs1], in0=bank(b1)[:, 0:gs1], scalar=0.125, in1=cs[:, :], op0=ALU.mult,
                            op1=ALU.mult), reads=[R_bank[b1], R_cos[gb]], writes=[R_ta[s2]])

                    def q_part_b(fc):
                        s2 = fc % 2
                        b2 = nbx()
                        P.op("pe", lambda e: e.matmul(bank(b2)[:, 0:gs1], lhsT=rot[:, :], rhs=qbf[s2][:, :], start=True, stop=True),
                             reads=[R_qbf[s2], R_const], writes=[R_bank[b2]])
                        P.op("dve", lambda e: e.tensor_tensor(out=tb[s2][:, 0:gs1], in0=bank(b2)[:, 0:gs1], in1=sn[:, :],
                                                              op=ALU.mult), reads=[R_bank[b2], R_sin[gb]], writes=[R_tb[s2]])
                        P.op("dve", lambda e: e.tensor_tensor(out=qr[gb][:, fc, :], in0=ta[s2][:, 0:gs1], in1=tb[s2][:, 0:gs1],
                                                               op=ALU.add), reads=[R_ta[s2], R_tb[s2]], writes=[R_qr[gb]])

                    q_part_a(0)
                    yield
                    for fc in range(1, 8):
                        q_part_a(fc)
                        q_part_b(fc - 1)
                        yield
                    q_part_b(7)
                    yield

                def gen_b2(g):
                    gb = g % 2
                    xn = xn1T[gb]; Rxn = R_xn1[gb]
                    cs, sn = cosg[gb], sing[gb]
                    for kc2 in range(2):
                        b1 = nbx()
                        for kc in range(8):
                            P.op("pe", lambda e, kc=kc: e.matmul(
                                bank(b1)[:, 0:gs1], lhsT=wc[:, kc, 1024 + kc2 * 128:1024 + (kc2 + 1) * 128],
                                rhs=xn[:, kc, :], start=(kc == 0), stop=(kc == 7)),
                                reads=[R_wcq[1], Rxn], writes=[R_bank[b1]])
                        P.op("act", lambda e: e.activation(out=kbf[0][:, kc2, :], in_=bank(b1)[:, 0:gs1], func=AF.Copy),
                             reads=[R_bank[b1]], writes=[R_kbf[0]])
                    yield
                    for a in range(4):
                        s2 = a % 2
                        b1 = nbx()
                        P.op("pe", lambda e: e.matmul(bank(b1)[:, 0:gs1], lhsT=dsel[:, a % 2, :], rhs=kbf[0][:, a // 2, :],
                                                      start=True, stop=True),
                             reads=[R_kbf[0], R_const], writes=[R_bank[b1]])
                        b2 = nbx()
                        P.op("pe", lambda e: e.matmul(bank(b2)[:, 0:gs1], lhsT=dselrot[:, a % 2, :], rhs=kbf[0][:, a // 2, :],
                                                      start=True, stop=True),
                             reads=[R_kbf[0], R_const], writes=[R_bank[b2]])
                        P.op("dve", lambda e: e.tensor_tensor(out=ta[s2][:, 0:gs1], in0=bank(b1)[:, 0:gs1], in1=cs[:, :],
                                                              op=ALU.mult), reads=[R_bank[b1], R_cos[gb]], writes=[R_ta[s2]])
                        P.op("dve", lambda e: e.tensor_tensor(out=tb[s2][:, 0:gs1], in0=bank(b2)[:, 0:gs1], in1=sn[:, :],
                                                              op=ALU.mult), reads=[R_bank[b2], R_sin[gb]], writes=[R_tb[s2]])
                        P.op("dve", lambda e: e.tensor_tensor(
                            out=kr[gb][:, a, 128:128 + gs1], in0=ta[s2][:, 0:gs1], in1=tb[s2][:, 0:gs1], op=ALU.add),
                            reads=[R_ta[s2], R_tb[s2]], writes=[R_kr[gb]])
                        yield
                    for fc in range(8):
                        b1 = nbx()
                        for kc in range(8):
                            P.op("pe", lambda e, kc=kc: e.matmul(
                                bank(b1)[:, 0:gs1], lhsT=wc[:, kc, 1536 + fc * 128:1536 + (fc + 1) * 128],
                                rhs=xn[:, kc, :], start=(kc == 0), stop=(kc == 7)),
                                reads=[R_wcq[(1536 + fc * 128) // 640], Rxn], writes=[R_bank[b1]])
                        s2 = fc % 2
                        P.op("act", lambda e: e.activation(out=tcb[s2][:, 0:gs1], in_=bank(b1)[:, 0:gs1], func=AF.Exp,
                                                           scale=-1.0), reads=[R_bank[b1]], writes=[R_tcb[s2]])
                        P.op("act", lambda e: e.activation(out=tcb[s2][:, 0:gs1], in_=tcb[s2][:, 0:gs1], func=AF.Ln, bias=1.0),
                             reads=[R_tcb[s2]], writes=[R_tcb[s2]])
                        P.op("act", lambda e: e.activation(out=tcb[s2][:, 0:gs1], in_=tcb[s2][:, 0:gs1], func=AF.Exp,
                                                           scale=-1.0), reads=[R_tcb[s2]], writes=[R_tcb[s2]])
                        P.op("dve", lambda e: e.tensor_tensor(out=g1[gb][:, fc, :], in0=bank(b1)[:, 0:gs1],
                                                              in1=tcb[s2][:, 0:gs1], op=ALU.mult),
                             reads=[R_bank[b1], R_tcb[s2]], writes=[R_g1[gb]])
                        yield
                    for tt in range(tpg1):
                        ti = g * tpg1 + tt
                        b1 = nbx()
                        for kc in range(8):
                            P.op("pe", lambda e, kc=kc: e.matmul(
                                bank(b1)[:ts, :], lhsT=xn[:, kc, tt * ts:(tt + 1) * ts], rhs=wc[:, kc, 1024:1536],
                                start=(kc == 0), stop=(kc == 7)), reads=[R_wcq[1], R_wcq[2], Rxn], writes=[R_bank[b1]])
                        P.op("dve", lambda e: e.tensor_copy(out=V1[gb][:ts, 1 + tt, :], in_=bank(b1)[:ts, 256:512]),
                             reads=[R_bank[b1]], writes=[R_V1[gb]])
                        if ti == nt - 1:
                            ysk = kvst; Rysk = R_kvst
                            P.op("dve", lambda e: e.tensor_copy(out=ysk[:ts, 0:256], in_=bank(b1)[:ts, 256:512]),
                                 reads=[R_bank[b1]], writes=[Rysk])
                            dv_ap = o_cvp[si, :, :] if kind == "p" else o_cvs[:, :]
                            dk_ap = o_ckp[si, :, :] if kind == "p" else o_cks[:, :]
                            P.dma("pool", dv_ap, ysk[:ts, 0:256], reads=[Rysk], is_output=True)
                            P.op("dve", lambda e: e.tensor_copy(out=kst[:ts, :], in_=bank(b1)[:ts, 0:256]),
                                 reads=[R_bank[b1]], writes=[R_kst])
                            for hk in range(4):
                                for b2_ in range(2):
                                    P.op("dve", lambda e, hk=hk, b2_=b2_: e.tensor_copy(
                                        out=ksw[:ts, hk * 64 + b2_ * 32:hk * 64 + b2_ * 32 + 32],
                                        in_=kst[:ts, hk * 64 + (1 - b2_) * 32:hk * 64 + (1 - b2_) * 32 + 32]),
                                        reads=[R_kst], writes=[R_ksw])
                            P.op("dve", lambda e: e.tensor_tensor(out=kst[:ts, :], in0=kst[:ts, :], in1=ctm[:ts, :],
                                                                  op=ALU.mult), reads=[R_kst, R_ctm], writes=[R_kst])
                            P.op("dve", lambda e: e.tensor_tensor(out=ksw[:ts, :], in0=ksw[:ts, :], in1=stm[:ts, :],
                                                                  op=ALU.mult), reads=[R_ksw, R_ctm], writes=[R_ksw])
                            P.op("dve", lambda e: e.tensor_tensor(out=ysk[:ts, 256:512], in0=kst[:ts, :],
                                                                  in1=ksw[:ts, :], op=ALU.add),
                                 reads=[R_kst, R_ksw], writes=[Rysk])
                            P.dma("pool", dk_ap, ysk[:ts, 256:512], reads=[Rysk], is_output=True)
                        yield

                def c_blocks(g, j, a):
                    gb = g % 2
                    J = g * tpg1 + j
                    blocks = []
                    if kind == "p":
                        if J > 0:
                            blocks.append((kr[gb][:, a, j * 128:(j + 1) * 128], 128,
                                           V1[gb][:, j, a * 64:(a + 1) * 64], "prev", [R_kr[gb]], [R_V1[gb]]))
                        blocks.append((kr[gb][:, a, (j + 1) * 128:(j + 2) * 128], 128,
                                       V1[gb][:, j + 1, a * 64:(a + 1) * 64], "diag", [R_kr[gb]], [R_V1[gb]]))
                    else:
                        blocks.append((krc[:, a, :], 128, Vcc[:, a * 64:(a + 1) * 64], "c", [R_krc], [R_Vcc]))
                        blocks.append((kr[gb][:, a, 128:128 + TS], TS, V1[gb][0:TS, 1, a * 64:(a + 1) * 64], "n",
                                       [R_kr[gb]], [R_V1[gb]]))
                    return blocks

                def c_stage1(g, n):
                    gb = g % 2
                    j, a = n // 4, n % 4
                    blocks = c_blocks(g, j, a)
                    for par in range(2):
                        sbk = YB[par]
                        Sps = bank(sbk)
                        po = par * 64
                        pt = PTc[(n % 2) * 2 + par]; Rpt = R_PTc[(n % 2) * 2 + par]
                        for bi_, (k_ap, nk, v_ap, tag, rk, rv) in enumerate(blocks):
                            if qw == 128:
                                col = bi_ * 2 * qw
                                P.op("pe", lambda e, k_ap=k_ap, nk=nk, col=col: e.matmul(
                                    Sps[:nk, col:col + 2 * qw].rearrange("p (h q) -> p h q", h=2),
                                    lhsT=k_ap[po:po + 64, :],
                                    rhs=qr[gb][po:po + 64, 2 * a:2 * a + 2, j * qw:(j + 1) * qw],
                                    start=True, stop=True), reads=rk + [R_qr[gb]], writes=[R_bank[sbk]])
                                continue
                            for hi in range(2):
                                fc = 2 * a + hi
                                col = (bi_ * 2 + hi) * qw
                                P.op("pe", lambda e, k_ap=k_ap, nk=nk, fc=fc, col=col: e.matmul(
                                    Sps[:nk, col:col + qw], lhsT=k_ap[po:po + 64, :],
                                    rhs=qr[gb][po:po + 64, fc, j * qw:(j + 1) * qw],
                                    start=True, stop=True), reads=rk + [R_qr[gb]], writes=[R_bank[sbk]])
                        for bi_, (k_ap, nk, v_ap, tag, rk, rv) in enumerate(blocks):
                            c0 = bi_ * 2 * qw
                            P.op("act", lambda e, nk=nk, c0=c0: e.activation(
                                out=pt[:nk, c0:c0 + 2 * qw], in_=Sps[:nk, c0:c0 + 2 * qw], func=AF.Exp),
                                reads=[R_bank[sbk]], writes=[Rpt])
                            if tag == "prev":
                                P.op("dve", lambda e, c0=c0: e.memset(
                                    pt[0:64, c0:c0 + 2 * qw].rearrange("p (h q) -> p h q", h=2)[:, :, 64:128], 0.0),
                                    writes=[Rpt])
                            if tag == "diag":
                                P.op("dve", lambda e, c0=c0: e.memset(
                                    pt[64:128, c0:c0 + 2 * qw].rearrange("p (h q) -> p h q", h=2)[:, :, 0:64], 0.0),
                                    writes=[Rpt])

                def c_stage2(g, n):
                    gb = g % 2
                    j, a = n // 4, n % 4
                    blocks = c_blocks(g, j, a)
                    nb = len(blocks)
                    ocb = YB[2]
                    OC = bank(ocb)
                    for par in range(2):
                        po = par * 64
                        pt = PTc[(n % 2) * 2 + par]; Rpt = R_PTc[(n % 2) * 2 + par]
                        if qw == 128:
                            for bi_, (k_ap, nk, v_ap, tag, rk, rv) in enumerate(blocks):
                                col = bi_ * 2 * qw
                                P.op("pe", lambda e, v_ap=v_ap, nk=nk, col=col, bi_=bi_: e.matmul(
                                    OC[po:po + 64, 0:256], lhsT=v_ap, rhs=pt[:nk, col:col + 256],
                                    start=(bi_ == 0), stop=(bi_ == nb - 1)),
                                    reads=rv + [Rpt], writes=[R_bank[ocb]])
                            for bi_, (k_ap, nk, v_ap, tag, rk, rv) in enumerate(blocks):
                                col = bi_ * 2 * qw
                                P.op("pe", lambda e, nk=nk, col=col, bi_=bi_: e.matmul(
                                    OC[po:po + 64, 256:512], lhsT=ones[:nk, 0:64],
                                    rhs=pt[:nk, col:col + 256], start=(bi_ == 0), stop=(bi_ == nb - 1)),
                                    reads=[Rpt, R_const], writes=[R_bank[ocb]])
                            continue
                        for hi in range(2):
                            for bi_, (k_ap, nk, v_ap, tag, rk, rv) in enumerate(blocks):
                                col = (bi_ * 2 + hi) * qw
                                P.op("pe", lambda e, v_ap=v_ap, nk=nk, col=col, bi_=bi_: e.matmul(
                                    OC[po:po + 64, hi * 128:hi * 128 + qw], lhsT=v_ap, rhs=pt[:nk, col:col + qw],
                                    start=(bi_ == 0), stop=(bi_ == nb - 1)),
                                    reads=rv + [Rpt], writes=[R_bank[ocb]])
                            for bi_, (k_ap, nk, v_ap, tag, rk, rv) in enumerate(blocks):
                                col = (bi_ * 2 + hi) * qw
                                P.op("pe", lambda e, nk=nk, col=col, bi_=bi_: e.matmul(
                                    OC[po:po + 64, 256 + hi * 128:256 + hi * 128 + qw], lhsT=ones[:nk, 0:64],
                                    rhs=pt[:nk, col:col + qw], start=(bi_ == 0), stop=(bi_ == nb - 1)),
                                    reads=[Rpt, R_const], writes=[R_bank[ocb]])
                    s2 = n % 2
                    rc = recc[s2]; Rrc = R_recc[s2]
                    tm = tmpc[s2]; Rtm = R_tmpc[s2]
                    for hi in range(2):
                        fc = 2 * a + hi
                        P.op("act", lambda e, hi=hi, fc=fc: e.activation(
                            out=rc[:, hi * 128:hi * 128 + qw], in_=OC[:, 256 + hi * 128:256 + hi * 128 + qw],
                            func=AF.Ln, bias=esink[:, fc:fc + 1]),
                            reads=[R_bank[ocb], R_const], writes=[Rrc])
                    if qw == 128:
                        P.op("act", lambda e: e.activation(out=rc[:, 0:256], in_=rc[:, 0:256], func=AF.Exp, scale=-1.0),
                             reads=[Rrc], writes=[Rrc])
                        P.op("dve", lambda e: e.tensor_tensor(out=tm[:, 0:256], in0=OC[:, 0:256], in1=rc[:, 0:256],
                                                              op=ALU.mult), reads=[R_bank[ocb], Rrc], writes=[Rtm])
                        P.op("dve", lambda e: e.tensor_tensor(
                            out=qr[gb][:, 2 * a:2 * a + 2, j * qw:(j + 1) * qw],
                            in0=tm[:, 0:256].rearrange("p (h q) -> p h q", h=2),
                            in1=g1[gb][:, 2 * a:2 * a + 2, j * qw:(j + 1) * qw], op=ALU.mult),
                            reads=[Rtm, R_g1[gb]], writes=[R_qr[gb]])
                    else:
                        for hi in range(2):
                            fc = 2 * a + hi
                            P.op("act", lambda e, hi=hi: e.activation(out=rc[:, hi * 128:hi * 128 + qw],
                                                                      in_=rc[:, hi * 128:hi * 128 + qw], func=AF.Exp,
                                                                      scale=-1.0),
                                 reads=[Rrc], writes=[Rrc])
                            P.op("dve", lambda e, hi=hi: e.tensor_tensor(
                                out=tm[:, hi * 128:hi * 128 + qw], in0=OC[:, hi * 128:hi * 128 + qw],
                                in1=rc[:, hi * 128:hi * 128 + qw], op=ALU.mult),
                                reads=[R_bank[ocb], Rrc], writes=[Rtm])
                            P.op("pool", lambda e, hi=hi, fc=fc: e.tensor_tensor(
                                out=qr[gb][:, fc, j * qw:(j + 1) * qw], in0=tm[:, hi * 128:hi * 128 + qw],
                                in1=g1[gb][:, fc, j * qw:(j + 1) * qw], op=ALU.mult),
                                reads=[Rtm, R_g1[gb]], writes=[R_qr[gb]])

                def gen_c(g):
                    nn = tpg1 * 4
                    c_stage1(g, 0)
                    yield
                    for n in range(nn):
                        if n + 1 < nn:
                            c_stage1(g, n + 1)
                            yield
                        c_stage2(g, n)
                        yield

                def gen_d(g):
                    gb = g % 2
                    for tt in range(tpg1):
                        ti = g * tpg1 + tt
                        bks = (nbx(), nbx())
                        for half in range(2):
                            for c in range(8):
                                P.op("pe", lambda e, c=c, half=half: e.matmul(
                                    bank(bks[half])[:ts, :], lhsT=qr[gb][:, c, tt * ts:(tt + 1) * ts],
                                    rhs=woc[:, c, half * 512:(half + 1) * 512], start=(c == 0), stop=(c == 7)),
                                    reads=[R_qr[gb], R_woc], writes=[R_bank[bks[half]]])
                        yield
                        sk = yc["n"] % 2; yc["n"] += 1
                        ysk = t1b[sk]; Rysk = R_t1[sk]
                        post_norm_residual(bks[0], bks[1], gpostc, Y0[gb][:ts, tt, :], R_Y0[gb], ysk[:ts, :], Rysk)
                        dst = yp[si, ti * ts:(ti + 1) * ts, :] if kind == "p" else ys[0:ts, :]
                        P.dma("pool", dst, ysk[:ts, :], reads=[Rysk], is_output=True)
                        yield

                def chain(*gens):
                    for g_ in gens:
                        yield from g_

                def run_streams(streams):
                    alive = list(streams)
                    while alive:
                        for s_ in list(alive):
                            try:
                                next(s_)
                            except StopIteration:
                                alive.remove(s_)

                flags = {}

                def wait_for(*keys):
                    while not all(flags.get(k) for k in keys):
                        yield

                def SA():
                    for g in range(ng1):
                        if g >= 2:
                            yield from wait_for(("d", g - 2))
                        if g >= 1:
                            yield from wait_for(("b2", g - 1))
                        yield from gen_a(g)
                        flags[("a", g)] = True
                        first = True
                        for _ in gen_b(g):
                            if first:
                                flags[("halo", g)] = True
                                first = False
                            yield
                        flags[("halo", g)] = True
                        flags[("bq", g)] = True

                def SB():
                    for g in range(ng1):
                        yield from wait_for(("a", g), ("halo", g))
                        yield from gen_b2(g)
                        flags[("b2", g)] = True

                def SC():
                    for g in range(ng1):
                        yield from wait_for(("bq", g), ("b2", g))
                        yield from gen_c(g)
                        flags[("c", g)] = True

                def SD():
                    for g in range(ng1):
                        yield from wait_for(("c", g))
                        yield from gen_d(g)
                        flags[("d", g)] = True

                run_streams([SA(), SB(), SC(), SD()])
            P.barrier()


        with nc.Block() as block:
            P.finalize(block, sems, dma_sems)
    return nc


_NC_CACHE = {}


def _prep(x_prompt, x_sample, cache_a_k, cache_a_v, cache_b_k, cache_b_v, cache_c_k, cache_c_v,
          ab_norm_pre, ab_w_in, ab_w_out, ab_norm_post, a_rel_bias,
          c_norm_pre, c_w_in, c_sinks, c_w_out, c_norm_post):
    f32 = np.float32
    A = lambda a: np.ascontiguousarray(np.asarray(a, dtype=f32))
    x_prompt, x_sample = A(x_prompt), A(x_sample)
    ncore = 8
    cst = _consts()
    w = A(ab_w_in)[0]
    w_ab = np.zeros((8, D, 512), f32)
    for pi in range(8):
        base = 0 if pi < 4 else 2048
        hp = pi % 4
        sl = lambda blk: w[:, base + blk * 512 + hp * 128: base + blk * 512 + (hp + 1) * 128]
        w_ab[pi, :, 0:128] = sl(0)
        w_ab[pi, :, 128:256] = sl(3)
        w_ab[pi, :, 256:384] = sl(1)
        w_ab[pi, :, 384:512] = sl(2)
    rep = lambda v: np.ascontiguousarray(np.broadcast_to(A(v).reshape(1, D), (128, D)))
    bp, bs = _bias_tiles(A(a_rel_bias)[0])
    sk = A(c_sinks)[0]
    sinks_l = np.zeros((128, 8), f32)
    for fc in range(8):
        sinks_l[0:64, fc] = sk[2 * fc]
        sinks_l[64:128, fc] = sk[2 * fc + 1]
    common = {
        "w_ab": w_ab, "w_oab": A(ab_w_out)[0], "w_c": A(c_w_in)[0], "w_oc": A(c_w_out)[0],
        "gpre_ab": rep(ab_norm_pre[0]), "gpost_ab": rep(ab_norm_post[0]),
        "gpre_c": rep(c_norm_pre[0]), "gpost_c": rep(c_norm_post[0]),
        "biasP": bp, "biasS": bs, "sinks": sinks_l,
        "c_ident": cst["ident"], "c_tri": cst["tri"], "c_ones": cst["ones"], "c_lmask": cst["lmask"],
        "c_rot": cst["rot"], "c_dsel": cst["dsel"], "c_dselrot": cst["dselrot"],
        "c_cosT": cst["cosT"], "c_sinT": cst["sinT"], "c_cosTM": cst["cosTM"], "c_sinTM": cst["sinTM"],
    }
    cak, cav = A(cache_a_k)[0], A(cache_a_v)[0]
    cbk, cbv = A(cache_b_k)[0], A(cache_b_v)[0]
    cck, ccv = A(cache_c_k)[0], A(cache_c_v)[0]
    in_maps = []
    for i in range(ncore):
        m = dict(common)
        m["xp"] = np.ascontiguousarray(x_prompt[2 * i:2 * i + 2])
        m["xs"] = np.ascontiguousarray(x_sample[i])
        m["ca_k"] = cak[i].reshape(512, 512); m["ca_v"] = cav[i].reshape(512, 512)
        m["cb_k"] = cbk[i].reshape(PAST, 512); m["cb_v"] = cbv[i].reshape(PAST, 512)
        m["cc_k"] = cck[i].reshape(128, 256); m["cc_v"] = ccv[i].reshape(128, 256)
        in_maps.append(m)
    return in_maps


def kernel(**inputs):
    ncore = 8
    in_maps = _prep(**inputs)
    if "nc" not in _NC_CACHE:
        _NC_CACHE["nc"] = build()
    nc = _NC_CACHE["nc"]
    res = run_bass_kernel_spmd(nc, in_maps, core_ids=list(range(ncore)))
    return _gather(res.results)


def _gather(R):
    ncore = len(R)
    cat = lambda name: np.concatenate([R[i][name] for i in range(ncore)], axis=0)
    stk = lambda name: np.stack([R[i][name] for i in range(ncore)], axis=0)
    y_prompt = cat("yp")
    y_sample = stk("ys")
    out = (
        y_prompt, y_sample,
        cat("o_akp").reshape(1, 16, 512, 8, 64), cat("o_avp").reshape(1, 16, 512, 8, 64),
        cat("o_bkp").reshape(1, 16, S, 8, 64), cat("o_bvp").reshape(1, 16, S, 8, 64),
        cat("o_ckp").reshape(1, 16, 128, 4, 64), cat("o_cvp").reshape(1, 16, 128, 4, 64),
        stk("o_aks").reshape(1, 8, TS, 8, 64), stk("o_avs").reshape(1, 8, TS, 8, 64),
        stk("o_bks").reshape(1, 8, TS, 8, 64), stk("o_bvs").reshape(1, 8, TS, 8, 64),
        stk("o_cks").reshape(1, 8, TS, 4, 64), stk("o_cvs").reshape(1, 8, TS, 4, 64),
    )
    return tuple(np.ascontiguousarray(o.astype(np.float32)) for o in out)
```
